# Optimizing a Trainium2 kernel written in Bass

```python
import math
import jax, jax.numpy as jnp
from jax import lax
import numpy as np

D_MODEL = 1024
BATCH = 2
SEQ = 16384
DEPTH = 2

CTX_LEN = 256
GRID_W = 64
EPS = 1e-6
HEAD_DIM = 64
ATTN_Q_HEADS = 8
ATTN_KV_HEADS = 2
ATTN_GROUP = ATTN_Q_HEADS // ATTN_KV_HEADS
ATTN_Q_DIM = ATTN_Q_HEADS * HEAD_DIM
ATTN_KV_DIM = ATTN_KV_HEADS * HEAD_DIM
WINDOW = 128
BLOCK = 128
ROPE_BASE = 10000.0
GDN_HEADS = 4
GDN_DK = 128
GDN_DV = 128
GDN_QK_DIM = GDN_HEADS * GDN_DK
GDN_V_DIM = GDN_HEADS * GDN_DV
GDN_CONV = 3
CHUNK = 64
HY_SPLITS = (ATTN_Q_DIM, ATTN_KV_DIM, ATTN_KV_DIM, 2 * GDN_QK_DIM + GDN_V_DIM, GDN_V_DIM, 2 * GDN_HEADS, 2 * GDN_HEADS)
HY_IN = sum(HY_SPLITS)
MIX_WIDTH = ATTN_Q_DIM + GDN_V_DIM
SC_CONV = 3
FFN_HIDDEN = -(-8 * D_MODEL // (3 * 256)) * 256

kernel_name = 'hybrid_swa_gdn_shortconv_dit_block'


def rms_norm(x, g):
    xf = x.astype(jnp.float32)
    y = xf * lax.rsqrt(jnp.mean(xf * xf, axis=-1, keepdims=True) + EPS)
    return (y * g.astype(jnp.float32)).astype(x.dtype)


def l2_normalize(x):
    return x * lax.rsqrt(jnp.sum(x * x, axis=-1, keepdims=True) + EPS)


def modulate(h, shift, scale):
    return h * (1 + scale) + shift


def ada_params(cond, w, b):
    return jnp.split(jax.nn.silu(cond) @ w + b, 6, axis=-1)


def centred_depthwise_conv(x, w):
    pad = w.shape[0] // 2
    return lax.conv_general_dilated(x, w[:, None, :].astype(x.dtype), window_strides=(1,), padding=[(pad, pad)],
                                    dimension_numbers=('NWC', 'WIO', 'NWC'), feature_group_count=x.shape[-1])


def axial_rope(length):
    rows = length // GRID_W
    row = jnp.broadcast_to(jnp.arange(rows)[:, None], (rows, GRID_W)).reshape(length).astype(jnp.float32)
    col = jnp.broadcast_to(jnp.arange(GRID_W)[None, :], (rows, GRID_W)).reshape(length).astype(jnp.float32)
    n_freq = HEAD_DIM // 4
    inv_freq = ROPE_BASE ** (-jnp.arange(n_freq, dtype=jnp.float32) / n_freq)
    ang = jnp.concatenate([row[:, None] * inv_freq, col[:, None] * inv_freq], axis=-1)
    return jnp.cos(ang), jnp.sin(ang)


def apply_rope(x, cos, sin):
    half = HEAD_DIM // 2
    xf = x.astype(jnp.float32)
    x1, x2 = xf[..., :half], xf[..., half:]
    cs, sn = cos[None, :, None, :], sin[None, :, None, :]
    return jnp.concatenate([x1 * cs - x2 * sn, x2 * cs + x1 * sn], axis=-1).astype(x.dtype)


def windowed_attention(q, k, v, k_ctx, v_ctx, sink_logit):
    B, L = q.shape[:2]
    n_blocks = L // BLOCK
    span = BLOCK + 2 * WINDOW
    scale = HEAD_DIM ** -0.5
    pad = ((0, 0), (WINDOW, WINDOW), (0, 0), (0, 0))
    kp, vp = jnp.pad(k, pad), jnp.pad(v, pad)
    rel = jnp.arange(span)[None, :] - WINDOW - jnp.arange(BLOCK)[:, None]
    in_band = jnp.abs(rel) <= WINDOW
    s_sink = jnp.broadcast_to(sink_logit[None, :, :, None, None], (B, ATTN_KV_HEADS, ATTN_GROUP, BLOCK, 1))

    def one_block(blk):
        start = blk * BLOCK
        qb = lax.dynamic_slice_in_dim(q, start, BLOCK, axis=1)
        kb = lax.dynamic_slice_in_dim(kp, start, span, axis=1)
        vb = lax.dynamic_slice_in_dim(vp, start, span, axis=1)
        key_pos = start - WINDOW + jnp.arange(span)
        valid = in_band & ((key_pos >= 0) & (key_pos < L))[None, :]
        s_loc = jnp.einsum('bnkgd,bmkd->bkgnm', qb, kb).astype(jnp.float32) * scale
        s_loc = jnp.where(valid, s_loc, -jnp.inf)
        s_ctx = jnp.einsum('bnkgd,bckd->bkgnc', qb, k_ctx).astype(jnp.float32) * scale
        p = jax.nn.softmax(jnp.concatenate([s_loc, s_ctx, s_sink], axis=-1), axis=-1).astype(v.dtype)
        return (jnp.einsum('bkgnm,bmkd->bnkgd', p[..., :span], vb)
                + jnp.einsum('bkgnc,bckd->bnkgd', p[..., span:-1], v_ctx))

    out = lax.map(one_block, jnp.arange(n_blocks))
    return jnp.moveaxis(out, 0, 1).reshape(B, L, ATTN_Q_DIM)


def context_attention(q_ctx, k_ctx, v_ctx, sink_logit):
    B, Lc = q_ctx.shape[:2]
    s = jnp.einsum('bnkgd,bckd->bkgnc', q_ctx, k_ctx).astype(jnp.float32) * HEAD_DIM ** -0.5
    s_sink = jnp.broadcast_to(sink_logit[None, :, :, None, None], s.shape[:-1] + (1,))
    p = jax.nn.softmax(jnp.concatenate([s, s_sink], axis=-1), axis=-1).astype(v_ctx.dtype)
    return jnp.einsum('bkgnc,bckd->bnkgd', p[..., :-1], v_ctx).reshape(B, Lc, ATTN_Q_DIM)


def gated_delta_chunked(q, k, v, g, beta, state0):
    B, H, L, dk = q.shape
    dv = v.shape[-1]
    n = L // CHUNK
    q = q.reshape(B, H, n, CHUNK, dk)
    k = k.reshape(B, H, n, CHUNK, dk)
    v = v.reshape(B, H, n, CHUNK, dv)
    g = jnp.cumsum(g.reshape(B, H, n, CHUNK), axis=-1)
    beta = beta.reshape(B, H, n, CHUNK)
    row = jnp.arange(CHUNK)[:, None]
    col = jnp.arange(CHUNK)[None, :]
    decay = jnp.exp(jnp.where(row >= col, g[..., :, None] - g[..., None, :], -jnp.inf))
    k_beta = k * beta[..., None]
    strict = jnp.where(row > col, jnp.einsum('bhncd,bhnsd->bhncs', k_beta, k) * decay, 0.0)
    eye = jnp.broadcast_to(jnp.eye(CHUNK, dtype=q.dtype), strict.shape)
    t_mat = lax.linalg.triangular_solve(eye + strict, eye, left_side=True, lower=True, unit_diagonal=True)
    u = jnp.einsum('bhncs,bhnse->bhnce', t_mat, v * beta[..., None])
    w = jnp.einsum('bhncs,bhnsd->bhncd', t_mat, k_beta * jnp.exp(g)[..., None])
    qk = jnp.einsum('bhncd,bhnsd->bhncs', q, k) * decay
    q_g = q * jnp.exp(g)[..., None]
    g_last = g[..., -1]
    k_tail = k * jnp.exp(g_last[..., None] - g)[..., None]

    def step(S, xs):
        u_c, w_c, qk_c, qg_c, kt_c, gl_c = xs
        v_new = u_c - jnp.einsum('bhcd,bhde->bhce', w_c, S)
        o_c = jnp.einsum('bhcd,bhde->bhce', qg_c, S) + jnp.einsum('bhcs,bhse->bhce', qk_c, v_new)
        S = S * jnp.exp(gl_c)[..., None, None] + jnp.einsum('bhcd,bhce->bhde', kt_c, v_new)
        return S, o_c

    xs = tuple(jnp.moveaxis(t, 2, 0) for t in (u, w, qk, q_g, k_tail, g_last))
    S, o = lax.scan(step, state0, xs)
    return jnp.moveaxis(o, 0, 2).reshape(B, H, L, dv), S


def gdn_prepare(qkv, a_proj, b_proj, conv_w, a_log, dt_bias):
    B, L, _ = qkv.shape
    qkv = jax.nn.silu(centred_depthwise_conv(qkv, conv_w)).astype(jnp.float32)
    q, k, v = jnp.split(qkv, [GDN_QK_DIM, 2 * GDN_QK_DIM], axis=-1)
    q = l2_normalize(q.reshape(B, L, GDN_HEADS, GDN_DK).transpose(0, 2, 1, 3)) * GDN_DK ** -0.5
    k = l2_normalize(k.reshape(B, L, GDN_HEADS, GDN_DK).transpose(0, 2, 1, 3))
    v = v.reshape(B, L, GDN_HEADS, GDN_DV).transpose(0, 2, 1, 3)
    a = a_proj.astype(jnp.float32).reshape(B, L, 2, GDN_HEADS)
    b = b_proj.astype(jnp.float32).reshape(B, L, 2, GDN_HEADS)
    g = -jnp.exp(a_log.astype(jnp.float32)) * jax.nn.softplus(a + dt_bias.astype(jnp.float32))
    beta = jax.nn.sigmoid(b)
    return q, k, v, g.transpose(2, 0, 3, 1), beta.transpose(2, 0, 3, 1)


def bidirectional_gdn(lat, ctx):
    q, k, v, g, beta = lat
    qc, kc, vc, gc, bc = ctx
    zero = jnp.zeros((q.shape[0], GDN_HEADS, GDN_DK, GDN_DV), jnp.float32)
    rev = lambda t: jnp.flip(t, axis=2)
    oc_f, s_f = gated_delta_chunked(qc, kc, vc, gc[0], bc[0], zero)
    o_f, _ = gated_delta_chunked(q, k, v, g[0], beta[0], s_f)
    oc_b, s_b = gated_delta_chunked(rev(qc), rev(kc), rev(vc), rev(gc[1]), rev(bc[1]), zero)
    o_b, _ = gated_delta_chunked(rev(q), rev(k), rev(v), rev(g[1]), rev(beta[1]), s_b)
    return o_f + rev(o_b), oc_f + rev(oc_b)


def gdn_output(o, z, norm_g):
    B, H, L, dv = o.shape
    o = rms_norm(o.transpose(0, 2, 1, 3), norm_g)
    gate = jax.nn.silu(z.astype(jnp.float32)).reshape(B, L, H, dv)
    return (o * gate).reshape(B, L, H * dv).astype(z.dtype)


def split_hybrid(p):
    idx = np.cumsum(HY_SPLITS)[:-1].tolist()
    return jnp.split(p, idx, axis=-1)


def attn_gdn_mixer(h, hc, w_in, w_out, sink, conv_w, a_log, dt_bias, norm_g, ctx_needed):
    B, L, _ = h.shape
    Lc = hc.shape[1]
    q, k, v, qkv, z, a, b = split_hybrid(h @ w_in)
    qc, kc, vc, qkv_c, z_c, a_c, b_c = split_hybrid(hc @ w_in)
    cos, sin = axial_rope(L)
    q = apply_rope(q.reshape(B, L, ATTN_Q_HEADS, HEAD_DIM), cos, sin).reshape(B, L, ATTN_KV_HEADS, ATTN_GROUP, HEAD_DIM)
    k = apply_rope(k.reshape(B, L, ATTN_KV_HEADS, HEAD_DIM), cos, sin)
    v = v.reshape(B, L, ATTN_KV_HEADS, HEAD_DIM)
    kc = kc.reshape(B, Lc, ATTN_KV_HEADS, HEAD_DIM)
    vc = vc.reshape(B, Lc, ATTN_KV_HEADS, HEAD_DIM)
    sink_logit = sink.reshape(ATTN_KV_HEADS, ATTN_GROUP).astype(jnp.float32)
    attn_out = windowed_attention(q, k, v, kc, vc, sink_logit)
    o, oc = bidirectional_gdn(gdn_prepare(qkv, a, b, conv_w, a_log, dt_bias),
                              gdn_prepare(qkv_c, a_c, b_c, conv_w, a_log, dt_bias))
    y = jnp.concatenate([attn_out, gdn_output(o, z, norm_g)], axis=-1) @ w_out
    if not ctx_needed:
        return y, None
    attn_c = context_attention(qc.reshape(B, Lc, ATTN_KV_HEADS, ATTN_GROUP, HEAD_DIM), kc, vc, sink_logit)
    yc = jnp.concatenate([attn_c, gdn_output(oc, z_c, norm_g)], axis=-1) @ w_out
    return y, yc


def short_conv_mixer(h, w_in, conv_w, w_out):
    b_gate, c_gate, u = jnp.split(h @ w_in, 3, axis=-1)
    return (b_gate * centred_depthwise_conv(c_gate * u, conv_w)) @ w_out


def swiglu(h, w_gate, w_up, w_down):
    return (jax.nn.silu(h @ w_gate) * (h @ w_up)) @ w_down


def setup_inputs(seed: int = 0) -> dict:
    key = jax.random.key(seed)
    ks = jax.random.split(key, 24)
    f32 = jnp.float32
    d = D_MODEL
    n_even = (DEPTH + 1) // 2
    n_odd = DEPTH // 2
    normal = lambda k, shape, s: jax.random.normal(k, shape, f32) * s
    gain = lambda k, shape: 1.0 + 0.02 * jax.random.normal(k, shape, f32)
    dt = jnp.exp(jax.random.uniform(ks[15], (n_even, 2, GDN_HEADS), f32, math.log(1e-3), math.log(1e-1)))
    return {
        'x': normal(ks[0], (BATCH, SEQ, d), 1.0),
        'c': normal(ks[1], (BATCH, d), 1.0),
        'ctx': normal(ks[2], (BATCH, CTX_LEN, d), 1.0),
        'c_ctx': normal(ks[3], (d,), 1.0),
        'ada_w': normal(ks[4], (DEPTH, d, 6 * d), 0.5 * d ** -0.5),
        'ada_b': normal(ks[5], (DEPTH, 6 * d), 0.01),
        'pre_mix_g': gain(ks[6], (DEPTH, d)),
        'post_mix_g': gain(ks[7], (DEPTH, d)),
        'pre_ffn_g': gain(ks[8], (DEPTH, d)),
        'post_ffn_g': gain(ks[9], (DEPTH, d)),
        'hy_w_in': normal(ks[10], (n_even, d, HY_IN), d ** -0.5),
        'hy_w_out': normal(ks[11], (n_even, MIX_WIDTH, d), MIX_WIDTH ** -0.5),
        'attn_sink': normal(ks[12], (n_even, ATTN_Q_HEADS), 1.0),
        'gdn_conv_w': normal(ks[13], (n_even, GDN_CONV, 2 * GDN_QK_DIM + GDN_V_DIM), GDN_CONV ** -0.5),
        'gdn_a_log': jnp.log(jax.random.uniform(ks[14], (n_even, 2, GDN_HEADS), f32, 1.0, 16.0)),
        'gdn_dt_bias': dt + jnp.log(-jnp.expm1(-dt)),
        'gdn_norm_g': gain(ks[16], (n_even, GDN_DV)),
        'sc_w_in': normal(ks[17], (n_odd, d, 3 * d), d ** -0.5),
        'sc_conv_w': normal(ks[18], (n_odd, SC_CONV, d), SC_CONV ** -0.5),
        'sc_w_out': normal(ks[19], (n_odd, d, d), d ** -0.5),
        'ffn_w_gate': normal(ks[20], (DEPTH, d, FFN_HIDDEN), d ** -0.5),
        'ffn_w_up': normal(ks[21], (DEPTH, d, FFN_HIDDEN), d ** -0.5),
        'ffn_w_down': normal(ks[22], (DEPTH, FFN_HIDDEN, d), FFN_HIDDEN ** -0.5),
    }


def reference(x, c, ctx, c_ctx, ada_w, ada_b, pre_mix_g, post_mix_g, pre_ffn_g, post_ffn_g,
              hy_w_in, hy_w_out, attn_sink, gdn_conv_w, gdn_a_log, gdn_dt_bias, gdn_norm_g,
              sc_w_in, sc_conv_w, sc_w_out, ffn_w_gate, ffn_w_up, ffn_w_down):
    for l in range(DEPTH):
        ctx_needed = any(j % 2 == 0 for j in range(l + 1, DEPTH))
        sh1, sc1, g1, sh2, sc2, g2 = ada_params(c, ada_w[l], ada_b[l])
        sh1, sc1, g1, sh2, sc2, g2 = (t[:, None, :] for t in (sh1, sc1, g1, sh2, sc2, g2))
        h = modulate(rms_norm(x, pre_mix_g[l]), sh1, sc1)
        if l % 2 == 0 or ctx_needed:
            csh1, csc1, cg1, csh2, csc2, cg2 = ada_params(c_ctx, ada_w[l], ada_b[l])
            hc = modulate(rms_norm(ctx, pre_mix_g[l]), csh1, csc1)
        if l % 2 == 0:
            e = l // 2
            y, yc = attn_gdn_mixer(h, hc, hy_w_in[e], hy_w_out[e], attn_sink[e], gdn_conv_w[e],
                                   gdn_a_log[e], gdn_dt_bias[e], gdn_norm_g[e], ctx_needed)
        else:
            o = l // 2
            y = short_conv_mixer(h, sc_w_in[o], sc_conv_w[o], sc_w_out[o])
            if ctx_needed:
                yc = short_conv_mixer(hc, sc_w_in[o], sc_conv_w[o], sc_w_out[o])
        x = x + g1 * rms_norm(y, post_mix_g[l])
        f = swiglu(modulate(rms_norm(x, pre_ffn_g[l]), sh2, sc2), ffn_w_gate[l], ffn_w_up[l], ffn_w_down[l])
        x = x + g2 * rms_norm(f, post_ffn_g[l])
        if ctx_needed:
            ctx = ctx + cg1 * rms_norm(yc, post_mix_g[l])
            fc = swiglu(modulate(rms_norm(ctx, pre_ffn_g[l]), csh2, csc2), ffn_w_gate[l], ffn_w_up[l], ffn_w_down[l])
            ctx = ctx + cg2 * rms_norm(fc, post_ffn_g[l])
    return x
```

```python
import numpy as np
import concourse.bass as bass
import concourse.mybir as mybir

F32 = mybir.dt.float32
BF16 = mybir.dt.bfloat16
AF = mybir.ActivationFunctionType
ALU = mybir.AluOpType
AX = mybir.AxisListType

EPOCH = 12000
_UID = [0]


class Dep:
    __slots__ = ("w", "r", "dsem", "dval", "name", "uid")

    def __init__(self, name=""):
        _UID[0] += 1
        self.uid = _UID[0]
        self.w = None
        self.r = {}
        self.dsem = None
        self.dval = 0
        self.name = name


class Sched:
    def __init__(self, nc, stack):
        self.nc = nc
        self.stack = stack
        self.semstack = stack
        self.eng = {"pe": nc.tensor, "dve": nc.vector, "act": nc.scalar, "pool": nc.gpsimd, "sp": nc.sync}
        self.sem = {}
        self.cnt = {}
        self.order = {}
        self.seen = {k: {} for k in self.eng}
        self.nsem = 0
        for k in self.eng:
            self.sem[k] = self.new_sem(k)
            self.cnt[k] = 0
            self.order[k] = 0
        self.ninst = 0
        self.nwait = 0
        self.last_tok = {}
        self.dma_deps = []
        self.free_dsems = []

    def new_sem(self, tag):
        self.nsem += 1
        return self.semstack.enter_context(self.nc.semaphore("s%d_%s" % (self.nsem, tag)))

    def _need(self, e, tok):
        if tok is None:
            return
        pk, order, sem, val = tok
        if pk == "pe" and e == "pe":
            return
        if self.seen[e].get(pk, -1) >= order:
            return
        self.eng[e].wait_ge(sem, val)
        self.nwait += 1
        self.seen[e][pk] = order

    def _waits(self, e, reads, writes):
        for d in reads:
            self._need(e, d.w)
        for d in writes:
            self._need(e, d.w)
            for t in list(d.r.values()):
                self._need(e, t)

    def op(self, e, fn, reads=(), writes=(), sig=True):
        self._waits(e, reads, writes)
        ins = fn(self.eng[e])
        self.ninst += 1
        self.order[e] += 1
        if sig:
            self.cnt[e] += 1
            ins.then_inc(self.sem[e], 1)
            tok = (e, self.order[e], self.sem[e], self.cnt[e])
            self.last_tok[e] = tok
        else:
            tok = (e, self.order[e], self.sem[e], self.cnt[e] + 1)
        for d in writes:
            d.w = tok
            d.r = {}
        for d in reads:
            if d not in writes:
                d.r[e] = tok
        if sig and self.cnt[e] >= EPOCH:
            self.sem[e] = self.new_sem(e)
            self.cnt[e] = 0
        return ins

    def dma(self, q, out, in_, reads=(), writes=(), own=None, **kw):
        self._waits(q, reads, writes)
        own = own or writes[0]
        if own.dsem is not None and own.dval > 0:
            self._need(q, ("dma%d" % own.uid, own.dval, own.dsem, own.dval))
        if own.dsem is None:
            if self.free_dsems:
                own.dsem, own.dval = self.free_dsems.pop()
            else:
                own.dsem = self.new_sem("d")
                own.dval = 0
            self.dma_deps.append(own)
        own.dval += 16
        ins = self.eng[q].dma_start(out=out, in_=in_, **kw)
        ins.then_inc(own.dsem, 16)
        self.ninst += 1
        tok = ("dma%d" % own.uid, own.dval, own.dsem, own.dval)
        for d in writes:
            d.w = tok
            d.r = {}
        for d in reads:
            d.r[tok[0]] = tok
        return ins

    def barrier(self, engines=None):
        for e in (engines or list(self.eng)):
            for p, tok in self.last_tok.items():
                if not (p == e and e == "pe"):
                    self._need(e, tok)
            for d in self.dma_deps:
                self._need(e, ("dma%d" % d.uid, d.dval, d.dsem, d.dval))

    def retire(self):
        self.barrier()
        for d in self.dma_deps:
            self.free_dsems.append((d.dsem, d.dval))
            d.dsem = None
        self.dma_deps = []

    def allgather(self, in_ap, out_ap, groups):
        self.barrier()
        csem = self.new_sem("cc")
        ins = self.nc.gpsimd.collective_compute("AllGather", mybir.AluOpType.bypass, replica_groups=groups,
                                                ins=[in_ap.opt()], outs=[out_ap.opt()])
        ins.then_inc(csem, 1)
        self.ninst += 1
        for e in self.eng:
            self.eng[e].wait_ge(csem, 1)

    def wait_all(self, e, deps):
        for d in deps:
            self._need(e, d.w)
            for t in list(d.r.values()):
                self._need(e, t)


class SBT:
    def __init__(self, S, name, shape, dtype, psum=False, ndeps=1):
        nc = S.nc
        S.ntile = getattr(S, "ntile", 0) + 1
        name = "t%d_%s" % (S.ntile, name)
        if psum:
            self.t = S.stack.enter_context(nc.psum_tensor(name, shape, dtype))
        else:
            self.t = S.stack.enter_context(nc.sbuf_tensor(name, shape, dtype))
        self.deps = [Dep(name + str(i)) for i in range(ndeps)]
        self.d = self.deps[0]
        self.shape = shape

    def __getitem__(self, idx):
        return self.t[idx]


SBT_ = SBT


class DramT:
    def __init__(self, nc, name, shape, dtype, kind="Internal", blk=512):
        self.ap = nc.dram_tensor(name, shape, dtype, kind=kind).ap()
        self.blk = blk
        self.deps = [Dep("%s_%d" % (name, i)) for i in range((shape[-1] + blk - 1) // blk)]

    def dd(self, c0, c1):
        return self.deps[c0 // self.blk:(c1 - 1) // self.blk + 1]


import os
import numpy as np
import ml_dtypes
from contextlib import ExitStack

BF = ml_dtypes.bfloat16
D = 1024
T = 4096
NT = T // 128
HAL = 128
CTX = 256
EXT = T + 2 * HAL + CTX
COL_HL, COL_HR, COL_CTX = T, T + HAL, T + 2 * HAL
FF = 2816
NJ = FF // 128
EPS = 1e-6


class Ring:
    def __init__(self, S, name, shape, dtype, n, psum=False):
        self.tiles = [SBT(S, "%s%d" % (name, i), shape, dtype, psum=psum) for i in range(n)]
        self.i = 0

    def next(self):
        t = self.tiles[self.i % len(self.tiles)]
        self.i += 1
        return t


def rope_tables(s):
    pos_own = s * T + np.arange(T)
    pos_l = s * T - HAL + np.arange(HAL)
    pos_r = (s + 1) * T + np.arange(HAL)
    pos = np.concatenate([pos_own, pos_l, pos_r]).astype(np.int64)
    row = (pos // 64).astype(np.float32)
    col = (pos % 64).astype(np.float32)
    inv = (np.float32(10000.0) ** (-np.arange(16, dtype=np.float32) / np.float32(16))).astype(np.float32)
    ang = np.concatenate([row[:, None] * inv[None, :], col[:, None] * inv[None, :]], axis=1).astype(np.float32)
    cs = np.cos(ang).astype(np.float32).T
    sn = np.sin(ang).astype(np.float32).T
    cosT = np.concatenate([cs, cs, cs, cs], axis=0)
    sinT = np.concatenate([-sn, sn, -sn, sn], axis=0)
    return np.ascontiguousarray(cosT), np.ascontiguousarray(sinT)


def make_consts(s):
    c = {}
    c["ident_bf"] = np.eye(128, dtype=np.float32).astype(BF)
    c["ident_f"] = np.eye(128, dtype=np.float32)
    sw = np.zeros((128, 128), np.float32)
    for m in range(128):
        k = (m // 64) * 64 + ((m % 64) + 32) % 64
        sw[k, m] = 1.0
    c["swap_bf"] = sw.astype(BF)
    c["ones_bf"] = np.ones((128, 128), np.float32).astype(BF)
    c["ones_f"] = np.ones((128, 128), np.float32)
    m = np.arange(128)[:, None]
    n = np.arange(128)[None, :]
    mL = (m >= n).astype(np.float32)
    mR = (m <= n).astype(np.float32)
    masks = np.zeros((128, 4, 512), np.float32)
    masks[:, 0] = np.tile(mL, (1, 4))
    masks[:, 1] = np.tile(mR, (1, 4))
    masks[:, 2] = np.tile(mL, (1, 4)) * (0.0 if s == 0 else 1.0)
    masks[:, 3] = np.tile(mR, (1, 4)) * (0.0 if s == 3 else 1.0)
    c["amask"] = masks.astype(BF)
    gm = np.zeros((128, 4, 128), np.float32)
    gm[:, 0] = (n <= m)
    gm[:, 1] = (n < m)
    gm[:, 2] = (n >= m)
    gm[:, 3] = (n > m)
    c["gmask"] = gm
    cm = np.zeros((128, 4, 128), np.float32)
    cm[:, 0] = (m <= n)
    cm[:, 1] = (m > n)
    cm[:, 2] = (m >= n)
    cm[:, 3] = (m < n)
    c["cmat"] = cm
    lm = np.zeros((128, 14, 128), np.float32)
    for li, mm_ in enumerate((1, 2, 4, 8, 16, 32, 64)):
        low = ((m // mm_) % 2 == 1) & ((n // mm_) == (m // mm_) - 1)
        lm[:, li * 2] = low
        lm[:, li * 2 + 1] = low.T
    c["lvlmask"] = lm.astype(BF)
    hv = np.ones((128, 2), np.float32)
    if s == 0:
        hv[:, 0] = 0
    if s == 3:
        hv[:, 1] = 0
    c["hv"] = hv
    cosT, sinT = rope_tables(s)
    c["cosT"] = cosT
    c["sinT"] = sinT
    return c


def prep_core_inputs(inp, b, s):
    x = inp["x"]
    d = {}
    xo = x[b, s * T:(s + 1) * T, :]
    d["xT"] = np.ascontiguousarray(xo.T.reshape(8, 128, T))
    xh = np.zeros((2 * HAL, D), np.float32)
    if s > 0:
        xh[:HAL] = x[b, s * T - HAL:s * T]
    if s < 3:
        xh[HAL:] = x[b, (s + 1) * T:(s + 1) * T + HAL]
    d["xhT"] = np.ascontiguousarray(xh.T.reshape(8, 128, 2 * HAL))
    d["ctxT"] = np.ascontiguousarray(inp["ctx"][b].T.reshape(8, 128, CTX))
    cv = np.stack([inp["c"][b], inp["c_ctx"]], axis=1)
    d["cvec"] = np.ascontiguousarray(cv.reshape(8, 128, 2).transpose(1, 0, 2))
    d.update(make_consts(s))
    return d


def prep_shared_inputs(inp):
    d = {}
    d["ada_w"] = inp["ada_w"]
    d["ada_bT"] = np.ascontiguousarray(inp["ada_b"].reshape(2, 48, 128).transpose(2, 0, 1))
    g = np.stack([inp["pre_mix_g"], inp["post_mix_g"], inp["pre_ffn_g"], inp["post_ffn_g"]], axis=0)
    d["gains"] = np.ascontiguousarray(g.reshape(4, 2, 8, 128).transpose(3, 0, 1, 2))
    w = inp["hy_w_in"][0]
    wq = w[:, 0:512].reshape(D, 2, 4, 64).transpose(0, 2, 1, 3).reshape(D, 512)
    d["wq"] = np.ascontiguousarray(wq)
    d["wk"] = np.ascontiguousarray(w[:, 512:640])
    d["wvab"] = np.ascontiguousarray(np.concatenate([w[:, 640:768], w[:, 2816:2832]], axis=1))
    d["wg"] = np.ascontiguousarray(w[:, 768:2304])
    d["wz"] = np.ascontiguousarray(w[:, 2304:2816])
    d["hy_w_out"] = inp["hy_w_out"][0]
    sk = inp["attn_sink"][0].reshape(2, 4)
    d["sinkrow"] = np.ascontiguousarray(np.broadcast_to(sk[None, :, :, None], (64, 2, 4, 128)).reshape(64, 2, 512))
    d["gconv"] = np.ascontiguousarray(inp["gdn_conv_w"][0].reshape(3, 12, 128).transpose(2, 1, 0))
    d["galog"] = np.ascontiguousarray(np.broadcast_to(inp["gdn_a_log"][0].reshape(1, 8), (128, 8)))
    d["gdtb"] = np.ascontiguousarray(np.broadcast_to(inp["gdn_dt_bias"][0].reshape(1, 8), (128, 8)))
    d["gnormg"] = np.ascontiguousarray(inp["gdn_norm_g"][0].reshape(128, 1))
    d["sc_w_in"] = inp["sc_w_in"][0]
    d["sconv"] = np.ascontiguousarray(inp["sc_conv_w"][0].reshape(3, 8, 128).transpose(2, 1, 0))
    d["sc_w_out"] = inp["sc_w_out"][0]
    d["ffn_w_gate"] = inp["ffn_w_gate"]
    d["ffn_w_up"] = inp["ffn_w_up"]
    d["ffn_w_down"] = inp["ffn_w_down"]
    return d


INPUT_SPECS = {
    "xT": ([8, 128, T], F32), "xhT": ([8, 128, 2 * HAL], F32), "ctxT": ([8, 128, CTX], F32), "cvec": ([128, 8, 2], F32),
    "ident_bf": ([128, 128], BF16), "ident_f": ([128, 128], F32), "swap_bf": ([128, 128], BF16),
    "ones_bf": ([128, 128], BF16), "ones_f": ([128, 128], F32), "amask": ([128, 4, 512], BF16),
    "gmask": ([128, 4, 128], F32), "lvlmask": ([128, 14, 128], BF16), "cmat": ([128, 4, 128], F32), "hv": ([128, 2], F32),
    "cosT": ([128, T + 2 * HAL], F32), "sinT": ([128, T + 2 * HAL], F32),
    "ada_w": ([2, D, 6 * D], F32), "ada_bT": ([128, 2, 48], F32), "gains": ([128, 4, 2, 8], F32),
    "wq": ([D, 512], F32), "wk": ([D, 128], F32), "wvab": ([D, 144], F32), "wg": ([D, 1536], F32), "wz": ([D, 512], F32),
    "hy_w_out": ([D, D], F32), "sinkrow": ([64, 2, 512], F32), "gconv": ([128, 12, 3], F32),
    "galog": ([128, 8], F32), "gdtb": ([128, 8], F32), "gnormg": ([128, 1], F32),
    "sc_w_in": ([D, 3 * D], F32), "sconv": ([128, 8, 3], F32), "sc_w_out": ([D, D], F32),
    "sel": ([128, 4], F32), "xall": ([128, 4, 8, 2, 128], F32), "sctx_in": ([128, 8, 128], F32),
    "R_in0": ([128, 4, T], BF16), "R_in1": ([128, 4, T], BF16), "o0_in0": ([128, 4, T], F32), "o0_in1": ([128, 4, T], F32),
    "zs_in": ([128, 4, T], BF16), "attn_in": ([64, 8, T], BF16), "x2_in": ([128, 8, T], F32),
    "cu_ext": ([128, 8, T + 2], BF16), "bg_in": ([128, 8, T], BF16), "selL": ([128, 4], F32), "selR": ([128, 4], F32),
    "ffn_w_gate": ([2, D, FF], F32), "ffn_w_up": ([2, D, FF], F32), "ffn_w_down": ([2, FF, D], F32),
}


class Ctx:
    pass


def declare_inputs(nc, names):
    C = Ctx()
    C.inp = {}
    for n in names:
        shp, dt = INPUT_SPECS[n]
        C.inp[n] = nc.dram_tensor(n, shp, dt, kind="ExternalInput").ap()
    return C


def load_const(S, C, name, q="sp"):
    shp, dt = INPUT_SPECS[name]
    t = SBT(S, "c_" + name, shp, dt)
    S.dma(q, t[:], C.inp[name], writes=[t.d])
    return t


def load_weight_bf16(S, C, dst, dst_cols, src_ap, ncols, stage_ring, eng="pool"):
    for c0 in range(0, ncols, 256):
        n = min(256, ncols - c0)
        st = stage_ring.next()
        S.dma("sp", st[:, :, :n], src_ap[:, c0:c0 + n].rearrange("(kc p) n -> p kc n", p=128), writes=[st.d])
        S.op(eng, lambda e: e.tensor_copy(out=dst[:, :, dst_cols + c0:dst_cols + c0 + n], in_=st[:, :, :n]),
             reads=[st.d], writes=[dst.d])


def phase0(S, C):
    nc = S.nc
    cv = SBT(S, "cv", [128, 8, 2], F32)
    S.dma("sp", cv[:], C.inp["cvec"], writes=[cv.d])
    sc = SBT(S, "silu_c", [128, 8, 2], F32)
    S.op("act", lambda e: e.activation(out=sc[:], in_=cv[:], func=AF.Silu), reads=[cv.d], writes=[sc.d])
    adab = load_const(S, C, "ada_bT")
    gains = load_const(S, C, "gains")
    C.gains = gains
    C.mod = [SBT(S, "mod%d" % l, [128, 48, 2], F32) for l in range(2)]
    with ExitStack() as ph:
        old = S.stack
        S.stack = ph
        stage = Ring(S, "adast", [128, 8, 1536], F32, 2)
        psm = SBT(S, "ps_mod", [128, 48, 2], F32, psum=True)
        for l in range(2):
            for qq in range(4):
                st = stage.next()
                S.dma("sp", st[:], C.inp["ada_w"][l, :, qq * 1536:(qq + 1) * 1536].rearrange("(kc p) n -> p kc n", p=128),
                      writes=[st.d])
                for jj in range(12):
                    j = qq * 12 + jj
                    for kc in range(8):
                        S.op("pe", lambda e: e.matmul(psm[:, j, :], lhsT=st[:, kc, jj * 128:(jj + 1) * 128], rhs=sc[:, kc, :],
                                                      start=(kc == 0), stop=(kc == 7)),
                             reads=[st.d, sc.d], writes=[psm.d], sig=(kc == 7))
            mod = C.mod[l]
            S.op("dve", lambda e: e.tensor_tensor(out=mod[:], in0=psm[:], in1=adab[:, l, :, None].to_broadcast([128, 48, 2]),
                                                  op=ALU.add), reads=[psm.d, adab.d], writes=[mod.d])
        S.barrier()
        S.stack = old
    C.vec = {}
    for l in range(2):
        mod = C.mod[l]
        for (nm, gi, mi) in (("a1", 0, 1), ("a2", 2, 4)):
            for col, sfx in ((0, ""), (1, "c")):
                t = SBT(S, "%s%s_%d" % (nm, sfx, l), [128, 8], F32)
                S.op("dve", lambda e: e.scalar_tensor_tensor(out=t[:], in0=mod[:, mi * 8:(mi + 1) * 8, col], scalar=1.0,
                                                             in1=gains[:, gi, l, :], op0=ALU.add, op1=ALU.mult),
                     reads=[mod.d, gains.d], writes=[t.d])
                C.vec[(nm + sfx, l)] = t
        for (nm, gi, mi) in (("gg1", 1, 2), ("gg2", 3, 5)):
            t = SBT(S, "%s_%d" % (nm, l), [128, 8], F32)
            S.op("dve", lambda e: e.tensor_tensor(out=t[:], in0=mod[:, mi * 8:(mi + 1) * 8, 0], in1=gains[:, gi, l, :],
                                                  op=ALU.mult), reads=[mod.d, gains.d], writes=[t.d])
            C.vec[(nm, l)] = t
        for (nm, mi) in (("b1", 0), ("b2", 3)):
            for col, sfx in ((0, ""), (1, "c")):
                t = SBT(S, "%s%s_%d" % (nm, sfx, l), [128, 8], F32)
                S.op("dve", lambda e: e.tensor_copy(out=t[:], in_=mod[:, mi * 8:(mi + 1) * 8, col]),
                     reads=[mod.d], writes=[t.d])
                C.vec[(nm + sfx, l)] = t


def norm_mod_block(S, C, xb, ntok, a, b, hT, R):
    sq = R["sq"].next()
    for c in range(8):
        S.op("dve", lambda e: e.tensor_tensor(out=sq[:, c, :ntok], in0=xb[:, c, :ntok], in1=xb[:, c, :ntok], op=ALU.mult),
             reads=[xb.d], writes=[sq.d])
    psn = R["ps"].next()
    for c in range(8):
        S.op("pe", lambda e: e.matmul(psn[:, :ntok], lhsT=C.ones_bf[:], rhs=sq[:, c, :ntok], start=(c == 0), stop=(c == 7)),
             reads=[C.ones_bf.d, sq.d], writes=[psn.d], sig=(c == 7))
    rstd = R["rstd"].next()
    S.op("act", lambda e: e.activation(out=rstd[:, :ntok], in_=psn[:, :ntok], func=AF.Ln, bias=C.eps_t[:, 0:1], scale=1.0 / D),
         reads=[psn.d, C.eps_t.d], writes=[rstd.d])
    S.op("act", lambda e: e.activation(out=rstd[:, :ntok], in_=rstd[:, :ntok], func=AF.Exp, scale=-0.5),
         reads=[rstd.d], writes=[rstd.d])
    for c in range(8):
        tmp = R["tmp"].next()
        S.op("dve", lambda e: e.scalar_tensor_tensor(out=tmp[:, :ntok], in0=xb[:, c, :ntok], scalar=a[:, c:c + 1],
                                                     in1=rstd[:, :ntok], op0=ALU.mult, op1=ALU.mult),
             reads=[xb.d, a.d, rstd.d], writes=[tmp.d])
        S.op("act", lambda e: e.activation(out=hT[:, c, :ntok], in_=tmp[:, :ntok], func=AF.Identity, bias=b[:, c:c + 1], scale=1.0),
             reads=[tmp.d, b.d], writes=[hT.d])
    return rstd


def setup_common(S, C):
    C.ones_bf = load_const(S, C, "ones_bf")
    C.ident_bf = load_const(S, C, "ident_bf")
    C.eps_t = SBT(S, "eps_t", [128, 1], F32)
    S.op("pool", lambda e: e.memset(C.eps_t[:], EPS), writes=[C.eps_t.d])


def phase1(S, C, dbg=None):
    nc = S.nc
    l = 0
    C.kT = SBT(S, "kT", [128, EXT], BF16)
    C.V = SBT(S, "V", [128, 36, 128], BF16)
    C.ab = SBT(S, "ab", [128, 36, 16], F32)
    with ExitStack() as ph:
        old = S.stack
        S.stack = ph
        swap = load_const(S, C, "swap_bf")
        hv = load_const(S, C, "hv")
        wq = SBT(S, "wq", [128, 8, 512], BF16)
        wk = SBT(S, "wk", [128, 8, 128], BF16)
        wvab = SBT(S, "wvab", [128, 8, 144], BF16)
        wg = SBT(S, "wg", [128, 8, 1536], BF16)
        wz = SBT(S, "wz", [128, 8, 512], BF16)
        stage = Ring(S, "wst", [128, 8, 256], F32, 2)
        load_weight_bf16(S, C, wk, 0, C.inp["wk"], 128, stage)
        load_weight_bf16(S, C, wvab, 0, C.inp["wvab"], 144, stage)
        load_weight_bf16(S, C, wq, 0, C.inp["wq"], 512, stage)
        load_weight_bf16(S, C, wg, 0, C.inp["wg"], 1536, stage)
        load_weight_bf16(S, C, wz, 0, C.inp["wz"], 512, stage)
        R = {"sq": Ring(S, "sq", [128, 8, 512], BF16, 1), "ps": Ring(S, "psn", [128, 512], F32, 1, psum=True),
             "rstd": Ring(S, "rstd", [128, 512], F32, 2), "tmp": Ring(S, "ntmp", [128, 512], F32, 3)}
        xring = Ring(S, "xblk", [128, 8, 512], F32, 2)
        hring = Ring(S, "hT", [128, 8, 512], BF16, 2)
        psr = Ring(S, "psp", [128, 512], F32, 4, psum=True)
        psrot = Ring(S, "psrot", [128, 512], F32, 1, psum=True)
        pst = Ring(S, "pst", [128, 512], F32, 2, psum=True)
        cosr = Ring(S, "cosb", [128, 512], F32, 2)
        sinr = Ring(S, "sinb", [128, 512], F32, 2)
        t1r = Ring(S, "t1", [128, 512], F32, 2)
        q32r = Ring(S, "q32", [128, 512], F32, 2)
        t2r = Ring(S, "t2", [128, 512], F32, 2)
        qsr = Ring(S, "qs", [128, 512], BF16, 2)
        qblk = Ring(S, "qblk", [128, 4, 512], BF16, 2)
        preblk = Ring(S, "preblk", [128, 12, 512], BF16, 1)
        zblk = Ring(S, "zblk", [128, 4, 512], BF16, 1)
        blocks = [("ctx", 0), ("halo", 0)] + [("own", i) for i in range(T // 512)]
        import os
        CUT = int(os.environ.get("CUT", "99"))
        if CUT < 99:
            blocks = blocks[:1]
        for kind, bi in blocks:
            ntok = 512 if kind == "own" else 256
            if kind == "own":
                src = C.inp["xT"][:, :, bi * 512:(bi + 1) * 512]
                col0 = bi * 512
                a, b = C.vec[("a1", l)], C.vec[("b1", l)]
            elif kind == "halo":
                src = C.inp["xhT"]
                col0 = COL_HL
                a, b = C.vec[("a1", l)], C.vec[("b1", l)]
            else:
                src = C.inp["ctxT"]
                col0 = COL_CTX
                a, b = C.vec[("a1c", l)], C.vec[("b1c", l)]
            xb = xring.next()
            S.dma("sp", xb[:, :, :ntok], src.rearrange("c p t -> p c t"), writes=[xb.d])
            hT = hring.next()
            norm_mod_block(S, C, xb, ntok, a, b, hT, R)
            if CUT <= 1:
                continue
            if dbg is not None and kind == "own" and bi == 0 and "h" in dbg:
                S.dma("sp", dbg["h"].ap, hT[:], reads=[hT.d], writes=dbg["h"].deps, own=hT.d)
            rope = kind != "ctx"
            if rope:
                cb = cosr.next()
                sb = sinr.next()
                S.dma("sp", cb[:, :ntok], C.inp["cosT"][:, col0:col0 + ntok], writes=[cb.d])
                S.dma("sp", sb[:, :ntok], C.inp["sinT"][:, col0:col0 + ntok], writes=[sb.d])

            def proj(w, j):
                ps = psr.next()
                for kc in range(8):
                    S.op("pe", lambda e: e.matmul(ps[:, :ntok], lhsT=w[:, kc, j * 128:(j + 1) * 128], rhs=hT[:, kc, :ntok],
                                                  start=(kc == 0), stop=(kc == 7)),
                         reads=[w.d, hT.d], writes=[ps.d], sig=(kc == 7))
                return ps

            def rope_evac(ps, out_ap, out_dep):
                q32 = q32r.next()
                S.op("act", lambda e: e.copy(out=q32[:, :ntok], in_=ps[:, :ntok]), reads=[ps.d], writes=[q32.d])
                t1 = t1r.next()
                S.op("dve", lambda e: e.tensor_tensor(out=t1[:, :ntok], in0=q32[:, :ntok], in1=cb[:, :ntok], op=ALU.mult),
                     reads=[q32.d, cb.d], writes=[t1.d])
                qs = qsr.next()
                S.op("dve", lambda e: e.tensor_copy(out=qs[:, :ntok], in_=q32[:, :ntok]), reads=[q32.d], writes=[qs.d])
                pr = psrot.next()
                S.op("pe", lambda e: e.matmul(pr[:, :ntok], lhsT=swap[:], rhs=qs[:, :ntok], start=True, stop=True),
                     reads=[swap.d, qs.d], writes=[pr.d])
                t2 = t2r.next()
                S.op("dve", lambda e: e.tensor_tensor(out=t2[:, :ntok], in0=pr[:, :ntok], in1=sb[:, :ntok], op=ALU.mult),
                     reads=[pr.d, sb.d], writes=[t2.d])
                S.op("dve", lambda e: e.tensor_tensor(out=out_ap, in0=t1[:, :ntok], in1=t2[:, :ntok], op=ALU.add),
                     reads=[t1.d, t2.d], writes=[out_dep])

            ps = proj(wk, 0)
            if rope:
                rope_evac(ps, C.kT[:, col0:col0 + ntok], C.kT.d)
            else:
                S.op("act", lambda e: e.copy(out=C.kT[:, col0:col0 + ntok], in_=ps[:, :ntok]), reads=[ps.d], writes=[C.kT.d])
            if CUT <= 2:
                continue
            if kind == "own":
                qb = qblk.next()
                for j in range(4):
                    ps = proj(wq, j)
                    rope_evac(ps, qb[:, j, :], qb.d)
                S.dma("pool", C.q_d.ap[:, :, col0:col0 + 512], qb[:], reads=[qb.d], writes=C.q_d.dd(col0, col0 + 512), own=qb.d)
                zb = zblk.next()
                for j in range(4):
                    ps = proj(wz, j)
                    S.op("act", lambda e: e.activation(out=zb[:, j, :], in_=ps[:, :], func=AF.Silu), reads=[ps.d], writes=[zb.d])
                S.dma("pool", C.zs_d.ap[:, :, col0:col0 + 512], zb[:], reads=[zb.d], writes=C.zs_d.dd(col0, col0 + 512), own=zb.d)
            pb = preblk.next()
            for j in range(12):
                ps = proj(wg, j)
                if kind == "halo":
                    for hh in range(2):
                        S.op("dve", lambda e: e.tensor_scalar(out=pb[:, j, hh * 128:(hh + 1) * 128], in0=ps[:, hh * 128:(hh + 1) * 128],
                                                              scalar1=hv[:, hh:hh + 1], scalar2=None, op0=ALU.mult),
                             reads=[ps.d, hv.d], writes=[pb.d])
                else:
                    eng = "act" if j % 2 else "dve"
                    if eng == "act":
                        S.op("act", lambda e: e.copy(out=pb[:, j, :ntok], in_=ps[:, :ntok]), reads=[ps.d], writes=[pb.d])
                    else:
                        S.op("dve", lambda e: e.tensor_copy(out=pb[:, j, :ntok], in_=ps[:, :ntok]), reads=[ps.d], writes=[pb.d])
            if kind == "own":
                S.dma("pool", C.pre_d.ap[:, :, 128 + col0:128 + col0 + 512], pb[:], reads=[pb.d], writes=C.pre_d.dd(128 + col0, 128 + col0 + 512), own=pb.d)
            elif kind == "halo":
                S.dma("pool", C.pre_d.ap[:, :, 0:128], pb[:, :, 0:128], reads=[pb.d], writes=C.pre_d.dd(0, 128), own=pb.d)
                S.dma("pool", C.pre_d.ap[:, :, 128 + T:128 + T + 128], pb[:, :, 128:256], reads=[pb.d], writes=C.pre_d.dd(128 + T, 256 + T), own=pb.d)
            else:
                S.dma("pool", C.prec_d.ap[:, :, 1:257], pb[:, :, 0:256], reads=[pb.d], writes=C.prec_d.deps, own=pb.d)
            if CUT <= 3:
                continue
            for tt in range(ntok // 128):
                if kind == "own":
                    ti = bi * 4 + tt
                elif kind == "halo":
                    ti = 32 + tt
                else:
                    ti = 34 + tt
                pt = pst.next()
                for kc in range(8):
                    S.op("pe", lambda e: e.matmul(pt[:, 0:144], lhsT=hT[:, kc, tt * 128:(tt + 1) * 128], rhs=wvab[:, kc, :],
                                                  start=(kc == 0), stop=(kc == 7)),
                         reads=[wvab.d, hT.d], writes=[pt.d], sig=(kc == 7))
                VAR = os.environ.get("VAR", "")
                if "a" not in VAR:
                    S.op("act", lambda e: e.copy(out=C.V[:, ti, :], in_=pt[:, 0:128]), reads=[pt.d], writes=[C.V.d])
                if "b" not in VAR:
                    if True:
                        S.op("act", lambda e: e.copy(out=C.ab[:, ti, :], in_=pt[:, 128:144]), reads=[pt.d], writes=[C.ab.d])
                    else:
                        S.op("dve", lambda e: e.tensor_copy(out=C.ab[:, ti, :], in_=pt[:, 128:144]), reads=[pt.d] + ([C.V.d] if "s" in VAR else []), writes=[C.ab.d])
        S.barrier()
        S.stack = old


def phase2(S, C):
    with ExitStack() as ph:
        old = S.stack
        S.stack = ph
        amask = load_const(S, C, "amask")
        sinkrow = load_const(S, C, "sinkrow")
        esink = SBT(S, "esink", [64, 2, 512], F32)
        S.op("act", lambda e: e.activation(out=esink[:], in_=sinkrow[:], func=AF.Exp), reads=[sinkrow.d], writes=[esink.d])
        qring = Ring(S, "qld", [128, 4, 512], BF16, 2)
        pss = Ring(S, "pss", [128, 512], F32, 4, psum=True)
        pso = Ring(S, "pso", [64, 512], F32, 2, psum=True)
        psd = Ring(S, "psd", [64, 512], F32, 2, psum=True)
        ptr = Ring(S, "pT", [128, 512], BF16, 5)
        denr = Ring(S, "den", [64, 512], F32, 2)
        outr = Ring(S, "aout", [64, 2, 4, 512], BF16, 2)
        def kcols(ti):
            if ti < 32:
                return ti * 128
            return T + (ti - 32) * 128
        tasks = []
        for qblk_i in range(T // 512):
            for sub in range(4):
                qb = qblk_i * 4 + sub
                left = (32, 2) if qb == 0 else (qb - 1, 0)
                right = (33, 3) if qb == NT - 1 else (qb + 1, 1)
                keyblocks = [left, (qb, None), right, (34, None), (35, None)]
                for kap in range(2):
                    for n, (kt, mi) in enumerate(keyblocks):
                        tasks.append((qblk_i, sub, kap, n, kt, mi))
        state = {"qt": {}, "pT": {}, "po": None, "pd": None, "ot": {}}

        def emit_S(t):
            qblk_i, sub, kap, n, kt, mi = tasks[t]
            if qblk_i not in state["qt"]:
                qt = qring.next()
                S.dma("sp", qt[:], C.q_d.ap[:, :, qblk_i * 512:(qblk_i + 1) * 512], reads=C.q_d.dd(qblk_i * 512, qblk_i * 512 + 512), writes=[qt.d])
                state["qt"][qblk_i] = qt
            qt = state["qt"][qblk_i]
            ps = pss.next()
            kc0 = kcols(kt)
            S.op("pe", lambda e: e.matmul(ps[:].rearrange("p (g n) -> p g n", g=4),
                                          lhsT=C.kT[64 * kap:64 * kap + 64, kc0:kc0 + 128],
                                          rhs=qt[64 * kap:64 * kap + 64, :, sub * 128:(sub + 1) * 128],
                                          start=True, stop=True),
                 reads=[C.kT.d, qt.d], writes=[ps.d])
            pT = ptr.next()
            S.op("act", lambda e: e.activation(out=pT[:], in_=ps[:], func=AF.Exp, scale=0.125), reads=[ps.d], writes=[pT.d])
            if mi is not None:
                S.op("dve", lambda e: e.tensor_tensor(out=pT[:], in0=pT[:], in1=amask[:, mi, :], op=ALU.mult),
                     reads=[pT.d, amask.d], writes=[pT.d])
            state["pT"][t] = pT

        def emit_PV(t):
            qblk_i, sub, kap, n, kt, mi = tasks[t]
            pT = state["pT"].pop(t)
            if n == 0:
                state["po"], state["pd"] = pso.next(), psd.next()
            po, pd = state["po"], state["pd"]
            if qblk_i not in state["ot"]:
                state["ot"][qblk_i] = outr.next()
            ot = state["ot"][qblk_i]
            S.op("pe", lambda e: e.matmul(po[:], lhsT=C.V[:, kt, 64 * kap:64 * kap + 64], rhs=pT[:], start=(n == 0), stop=(n == 4)),
                 reads=[C.V.d, pT.d], writes=[po.d], sig=(n == 4))
            S.op("pe", lambda e: e.matmul(pd[:], lhsT=C.ones_bf[:, 0:64], rhs=pT[:], start=(n == 0), stop=(n == 4)),
                 reads=[C.ones_bf.d, pT.d], writes=[pd.d], sig=(n == 4))
            if n == 4:
                den = denr.next()
                S.op("dve", lambda e: e.tensor_tensor(out=den[:], in0=pd[:], in1=esink[:, kap, :], op=ALU.add),
                     reads=[pd.d, esink.d], writes=[den.d])
                S.op("dve", lambda e: e.reciprocal(out=den[:], in_=den[:]), reads=[den.d], writes=[den.d])
                S.op("dve", lambda e: e.tensor_tensor(out=ot[:, kap, :, sub * 128:(sub + 1) * 128],
                                                      in0=po[:].rearrange("p (g n) -> p g n", g=4),
                                                      in1=den[:].rearrange("p (g n) -> p g n", g=4), op=ALU.mult),
                     reads=[po.d, den.d], writes=[ot.d])
                if sub == 3 and kap == 1:
                    S.dma("pool", C.attn_d.ap[:, :, qblk_i * 512:(qblk_i + 1) * 512], ot[:].rearrange("p k g n -> p (k g) n"),
                          reads=[ot.d], writes=C.attn_d.dd(qblk_i * 512, qblk_i * 512 + 512), own=ot.d)
        LOOK = 2
        for t in range(min(LOOK, len(tasks))):
            emit_S(t)
        for t in range(len(tasks)):
            if t + LOOK < len(tasks):
                emit_S(t + LOOK)
            emit_PV(t)
        S.barrier()
        S.stack = old


class Slot:
    def __init__(self, ap, d):
        self.ap = ap
        self.d = d

    def __getitem__(self, idx):
        return self.ap[idx]


class SlotRing:
    def __init__(self, S, name, dtype, nslots, psum=False):
        per = 4 if dtype == F32 else 8
        if not psum:
            per = 1
        self.slots = []
        for b in range((nslots + per - 1) // per):
            t = SBT(S, "%s%d" % (name, b), [128, per, 128], dtype, psum=psum, ndeps=per)
            for i in range(per):
                self.slots.append(Slot(t[:, i, :], t.deps[0] if psum else t.deps[i]))
        self.slots = self.slots[:nslots]
        self.i = 0

    def next(self):
        s = self.slots[self.i % len(self.slots)]
        self.i += 1
        return s


class PairRing:
    def __init__(self, S, name, dtype, n):
        t = SBT(S, name, [128, n, 2, 128], dtype, psum=True)
        self.slots = [Slot(t[:, i, :, :], t.d) for i in range(n)]
        self.i = 0

    def next(self):
        s = self.slots[self.i % len(self.slots)]
        self.i += 1
        return s


def gdn_gates(S, C, G):
    ab = C.ab
    dtb = load_const(S, C, "gdtb")
    alog = load_const(S, C, "galog")
    cmat = load_const(S, C, "cmat")
    G.one_t = SBT(S, "one_t", [128, 1], F32)
    S.op("pool", lambda e: e.memset(G.one_t[:], 1.0), writes=[G.one_t.d])
    eal = SBT(S, "eal", [128, 8], F32)
    S.op("act", lambda e: e.activation(out=eal[:], in_=alog[:], func=AF.Exp), reads=[alog.d], writes=[eal.d])
    xg = SBT(S, "xg", [128, 36, 8], F32)
    S.op("dve", lambda e: e.tensor_tensor(out=xg[:], in0=ab[:, :, 0:8], in1=dtb[:, None, :].to_broadcast([128, 36, 8]), op=ALU.add),
         reads=[ab.d, dtb.d], writes=[xg.d])
    S.op("act", lambda e: e.activation(out=xg[:], in_=xg[:], func=AF.Exp), reads=[xg.d], writes=[xg.d])
    S.op("act", lambda e: e.activation(out=xg[:], in_=xg[:], func=AF.Ln, bias=G.one_t[:, 0:1], scale=1.0),
         reads=[xg.d, G.one_t.d], writes=[xg.d])
    g = SBT(S, "gg", [128, 36, 8], F32)
    S.op("dve", lambda e: e.scalar_tensor_tensor(out=g[:], in0=xg[:], scalar=-1.0, in1=eal[:, None, :].to_broadcast([128, 36, 8]),
                                                 op0=ALU.mult, op1=ALU.mult), reads=[xg.d, eal.d], writes=[g.d])
    G.beta = SBT(S, "beta", [128, 36, 8], F32)
    S.op("act", lambda e: e.activation(out=G.beta[:], in_=ab[:, :, 8:16], func=AF.Sigmoid), reads=[ab.d], writes=[G.beta.d])
    G.GT = SBT(S, "GT", [128, 36, 16], F32)
    _pst = ExitStack()
    _old = S.stack
    S.stack = _pst
    psg = Ring(S, "psg", [128, 18, 16], F32, 1, psum=True)
    S.stack = _old
    for half in range(2):
        pg = psg.next()
        for tl in range(18):
            ti = half * 18 + tl
            for m in range(4):
                dirn = m // 2
                S.op("pe", lambda e: e.matmul(pg[:, tl, m * 4:(m + 1) * 4], lhsT=cmat[:, m, :], rhs=g[:, ti, dirn * 4:(dirn + 1) * 4],
                                              start=True, stop=True),
                     reads=[cmat.d, g.d], writes=[pg.d], sig=(tl == 17 and m == 3))
        S.op("act", lambda e: e.copy(out=G.GT[:, half * 18:(half + 1) * 18, :], in_=pg[:]), reads=[pg.d], writes=[G.GT.d])
    S.barrier()
    _pst.close()
    G.negG = SBT(S, "negG", [128, 36, 16], F32)
    S.op("dve", lambda e: e.tensor_scalar(out=G.negG[:], in0=G.GT[:], scalar1=-1.0, scalar2=None, op0=ALU.mult),
         reads=[G.GT.d], writes=[G.negG.d])
    G.eGT = SBT(S, "eGT", [128, 36, 16], F32)
    S.op("act", lambda e: e.activation(out=G.eGT[:], in_=G.GT[:], func=AF.Exp), reads=[G.GT.d], writes=[G.eGT.d])
    G.bEG = SBT(S, "bEG", [128, 36, 8], F32)
    for dirn in range(2):
        S.op("dve", lambda e: e.tensor_tensor(out=G.bEG[:, :, dirn * 4:(dirn + 1) * 4], in0=G.beta[:, :, dirn * 4:(dirn + 1) * 4],
                                              in1=G.eGT[:, :, dirn * 8:dirn * 8 + 4], op=ALU.mult),
             reads=[G.beta.d, G.eGT.d], writes=[G.bEG.d])


def gdn_setup(S, C, G):
    G.ident_f = load_const(S, C, "ident_f")
    G.ones_f = load_const(S, C, "ones_f")
    G.gmask = load_const(S, C, "gmask")
    G.lvl = load_const(S, C, "lvlmask")
    gconv = load_const(S, C, "gconv")
    G.diag = SBT(S, "diag", [128, 36, 128], BF16)
    for c in range(12):
        for k in range(3):
            S.op("dve" if (c + k) % 2 else "pool",
                 lambda e: e.tensor_scalar(out=G.diag[:, c * 3 + k, :], in0=C.ident_bf[:], scalar1=gconv[:, c, k:k + 1], scalar2=None,
                                           op0=ALU.mult), reads=[C.ident_bf.d, gconv.d], writes=[G.diag.d])
    G.psC = SlotRing(S, "psC", F32, 8, psum=True)
    G.psFs = [SlotRing(S, "psF%d" % i, F32, 4, psum=True) for i in range(4)]
    G.psBs = [SlotRing(S, "psB%d" % i, BF16, 8, psum=True) for i in range(2)]
    G.psF = G.psFs[0]
    G.S32 = SBT(S, "S32", [128, 8, 128], F32, ndeps=8)
    G.P32 = SBT(S, "P32", [128, 8, 128], F32, ndeps=8)
    G.Sbf = SBT(S, "Sbf", [128, 8, 128], BF16, ndeps=8)
    G.Pbf = SBT(S, "Pbf", [128, 8, 128], BF16, ndeps=8)


def gdn_pass(S, C, G, dirn, heads, tiles, with_P, with_out, pre_src, tag, psF, psB):
    f32r = lambda nm, n=2: SlotRing(S, tag + nm, F32, n)
    bfr = lambda nm, n=2: SlotRing(S, tag + nm, BF16, n)
    prer = Ring(S, tag + "prew", [128, 12, 130], BF16, 1)
    csr = Ring(S, tag + "cs", [128, 128], F32, 2)
    sqr = Ring(S, tag + "sq", [128, 128], BF16, 2)
    rnr = Ring(S, tag + "rn", [128, 128], F32, 2)
    qkb = Ring(S, tag + "qkb", [128, 8, 128], BF16, 1)
    vbk = Ring(S, tag + "vbk", [128, 4, 128], BF16, 1)
    oblk = Ring(S, tag + "oblk", [128, 4, 128], F32, 1)
    rblk = Ring(S, tag + "rblk", [128, 4, 128], BF16, 1)
    r_ng, r_gs, r_dm, r_e, r_eg, r_ta, r_tq = (f32r(n) for n in ("ng", "gs", "dm", "e", "eg", "ta", "tq"))
    r_q = bfr("Q", 2)
    r_p = bfr("P", 2)
    r_m = bfr("M", 5)
    r_y = bfr("Y", 4)
    r_a, r_b, r_qk, r_qkT, r_kbg, r_kt, r_vb, r_qg, r_w, r_vn, r_vnP = (bfr(n) for n in
                                                                      ("A", "B", "qk", "qkT", "kbg", "kt", "vb", "qg", "w", "vn", "vnP"))
    gi = dirn * 8
    mi = dirn * 2
    last_col = 127 if dirn == 0 else 0
    hlo, hhi = min(heads), max(heads) + 1
    for bi in range(len(tiles)):
        btiles = tiles[bi:bi + 1]
        lo = btiles[0][1]
        ntok = 128
        src_ap, src_deps = pre_src(lo, ntok)
        pw = prer.next()
        S.dma("sp", pw[:, :, :ntok + 2], src_ap, reads=src_deps, writes=[pw.d])
        qk = qkb.next()
        vb_ = vbk.next()
        for c in [h for h in heads] + [4 + h for h in heads] + [8 + h for h in heads]:
            pc = G.psC.next()
            for k in range(3):
                S.op("pe", lambda e: e.matmul(pc[:, :ntok], lhsT=G.diag[:, c * 3 + k, :], rhs=pw[:, c, k:k + ntok],
                                              start=(k == 0), stop=(k == 2)),
                     reads=[G.diag.d, pw.d], writes=[pc.d], sig=(k == 2))
            if c >= 8:
                S.op("act", lambda e: e.activation(out=vb_[:, c - 8, :ntok], in_=pc[:, :ntok], func=AF.Silu), reads=[pc.d], writes=[vb_.d])
                yield
                continue
            cs = csr.next()
            S.op("act", lambda e: e.activation(out=cs[:, :ntok], in_=pc[:, :ntok], func=AF.Silu), reads=[pc.d], writes=[cs.d])
            sq = sqr.next()
            S.op("dve", lambda e: e.tensor_tensor(out=sq[:, :ntok], in0=cs[:, :ntok], in1=cs[:, :ntok], op=ALU.mult),
                 reads=[cs.d], writes=[sq.d])
            yield
            pn = G.psC.next()
            S.op("pe", lambda e: e.matmul(pn[:, :ntok], lhsT=C.ones_bf[:], rhs=sq[:, :ntok], start=True, stop=True),
                 reads=[C.ones_bf.d, sq.d], writes=[pn.d])
            rn = rnr.next()
            S.op("act", lambda e: e.activation(out=rn[:, :ntok], in_=pn[:, :ntok], func=AF.Ln, bias=C.eps_t[:, 0:1], scale=1.0),
                 reads=[pn.d, C.eps_t.d], writes=[rn.d])
            S.op("act", lambda e: e.activation(out=rn[:, :ntok], in_=rn[:, :ntok], func=AF.Exp, scale=-0.5), reads=[rn.d], writes=[rn.d])
            scl = (128.0 ** -0.5) if c < 4 else 1.0
            S.op("dve", lambda e: e.scalar_tensor_tensor(out=qk[:, c, :ntok], in0=cs[:, :ntok], scalar=scl, in1=rn[:, :ntok],
                                                         op0=ALU.mult, op1=ALU.mult), reads=[cs.d, rn.d], writes=[qk.d])
            yield
        ob = oblk.next() if with_out else None
        rb = rblk.next() if with_out else None
        for ti, col in btiles:
            c0 = col - lo
            cols = slice(c0, c0 + 128)
            for h in heads:
                sidx = dirn * 4 + h
                kT_t = qk[:, 4 + h, cols]
                qT_t = qk[:, h, cols]
                vT_t = vb_[:, h, cols]
                Gcol = G.GT[:, ti, gi + h:gi + h + 1]
                nGcol = G.negG[:, ti, gi + h:gi + h + 1]
                betac = G.beta[:, ti, dirn * 4 + h:dirn * 4 + h + 1]
                bEGc = G.bEG[:, ti, dirn * 4 + h:dirn * 4 + h + 1]
                etailc = G.eGT[:, ti, gi + 4 + h:gi + 4 + h + 1]
                kk = psF.next()
                S.op("pe", lambda e: e.matmul(kk[:], lhsT=kT_t, rhs=kT_t, start=True, stop=True), reads=[qk.d], writes=[kk.d])
                yield
                qkp = psF.next()
                S.op("pe", lambda e: e.matmul(qkp[:], lhsT=qT_t, rhs=kT_t, start=True, stop=True), reads=[qk.d], writes=[qkp.d])
                yield
                ktok = psB.next()
                S.op("pe", lambda e: e.transpose(ktok[:], kT_t, C.ident_bf[:]), reads=[qk.d, C.ident_bf.d], writes=[ktok.d])
                yield
                vtok = psB.next()
                S.op("pe", lambda e: e.transpose(vtok[:], vT_t, C.ident_bf[:]), reads=[vb_.d, C.ident_bf.d], writes=[vtok.d])
                ng = r_ng.next()
                S.op("dve", lambda e: e.tensor_scalar(out=ng[:], in0=G.ones_f[:], scalar1=nGcol, scalar2=None, op0=ALU.mult),
                     reads=[G.ones_f.d, G.negG.d], writes=[ng.d])
                yield
                gbc = psF.next()
                S.op("pe", lambda e: e.matmul(gbc[:], lhsT=ng[:], rhs=G.ident_f[:], start=True, stop=True),
                     reads=[ng.d, G.ident_f.d], writes=[gbc.d])
                gs = r_gs.next()
                S.op("dve", lambda e: e.tensor_copy(out=gs[:], in_=gbc[:]), reads=[gbc.d], writes=[gs.d])
                dm = r_dm.next()
                S.op("dve", lambda e: e.tensor_scalar(out=dm[:], in0=gs[:], scalar1=Gcol, scalar2=0.0, op0=ALU.add, op1=ALU.min),
                     reads=[gs.d, G.GT.d], writes=[dm.d])
                E = r_e.next()
                S.op("act", lambda e: e.activation(out=E[:], in_=dm[:], func=AF.Exp), reads=[dm.d], writes=[E.d])
                eg = r_eg.next()
                S.op("act", lambda e: e.activation(out=eg[:], in_=gs[:], func=AF.Exp, scale=-1.0), reads=[gs.d], writes=[eg.d])
                ta = r_ta.next()
                S.op("dve", lambda e: e.tensor_tensor(out=ta[:], in0=kk[:], in1=E[:], op=ALU.mult), reads=[kk.d, E.d], writes=[ta.d])
                A = r_q.next()
                S.op("dve", lambda e: e.scalar_tensor_tensor(out=A[:], in0=ta[:], scalar=betac, in1=G.gmask[:, mi + 1, :],
                                                             op0=ALU.mult, op1=ALU.mult),
                     reads=[ta.d, G.beta.d, G.gmask.d], writes=[A.d])
                tq = r_tq.next()
                S.op("dve", lambda e: e.tensor_tensor(out=tq[:], in0=qkp[:], in1=E[:], op=ALU.mult), reads=[qkp.d, E.d], writes=[tq.d])
                qkm = r_qk.next()
                S.op("dve", lambda e: e.tensor_tensor(out=qkm[:], in0=tq[:], in1=G.gmask[:, mi, :], op=ALU.mult),
                     reads=[tq.d, G.gmask.d], writes=[qkm.d])
                yield
                bps = psB.next()
                S.op("pe", lambda e: e.transpose(bps[:], A[:], C.ident_bf[:]), reads=[A.d, C.ident_bf.d], writes=[bps.d])
                Bm = r_p.next()
                S.op("act", lambda e: e.copy(out=Bm[:], in_=bps[:]), reads=[bps.d], writes=[Bm.d])
                yield
                qps = psB.next()
                S.op("pe", lambda e: e.transpose(qps[:], qkm[:], C.ident_bf[:]), reads=[qkm.d, C.ident_bf.d], writes=[qps.d])
                qkT = r_qkT.next()
                S.op("act", lambda e: e.copy(out=qkT[:], in_=qps[:]), reads=[qps.d], writes=[qkT.d])
                kbg = r_kbg.next()
                S.op("act", lambda e: e.activation(out=kbg[:], in_=ktok[:], func=AF.Identity, scale=bEGc), reads=[ktok.d, G.bEG.d], writes=[kbg.d])
                kt = r_kt.next()
                S.op("act", lambda e: e.activation(out=kt[:], in_=ktok[:], func=AF.Identity, scale=etailc), reads=[ktok.d, G.eGT.d], writes=[kt.d])
                vbt = r_vb.next()
                S.op("act", lambda e: e.activation(out=vbt[:], in_=vtok[:], func=AF.Identity, scale=betac), reads=[vtok.d, G.beta.d], writes=[vbt.d])
                qg = r_qg.next()
                S.op("dve", lambda e: e.tensor_tensor(out=qg[:], in0=qT_t, in1=eg[:], op=ALU.mult), reads=[qk.d, eg.d], writes=[qg.d])
                lo_i, up_i = (0, 1) if dirn == 0 else (1, 0)
                Tt = r_m.next()
                Xt = r_m.next()
                tmp1 = r_y.next()
                S.op("dve", lambda e: e.tensor_tensor(out=tmp1[:], in0=A[:], in1=G.lvl[:, lo_i, :], op=ALU.mult),
                     reads=[A.d, G.lvl.d], writes=[tmp1.d])
                S.op("dve", lambda e: e.tensor_tensor(out=Tt[:], in0=C.ident_bf[:], in1=tmp1[:], op=ALU.subtract),
                     reads=[C.ident_bf.d, tmp1.d], writes=[Tt.d])
                tmp2 = r_y.next()
                S.op("dve", lambda e: e.tensor_tensor(out=tmp2[:], in0=Bm[:], in1=G.lvl[:, up_i, :], op=ALU.mult),
                     reads=[Bm.d, G.lvl.d], writes=[tmp2.d])
                S.op("dve", lambda e: e.tensor_tensor(out=Xt[:], in0=C.ident_bf[:], in1=tmp2[:], op=ALU.subtract),
                     reads=[C.ident_bf.d, tmp2.d], writes=[Xt.d])
                for li in range(1, 7):
                    last = (li == 6)
                    yield
                    yp = psF.next()
                    S.op("pe", lambda e: e.matmul(yp[:], lhsT=A[:], rhs=Xt[:], start=True, stop=True), reads=[A.d, Xt.d], writes=[yp.d])
                    Y = r_y.next()
                    S.op("dve", lambda e: e.tensor_tensor(out=Y[:], in0=yp[:], in1=G.lvl[:, li * 2 + up_i, :], op=ALU.mult),
                         reads=[yp.d, G.lvl.d], writes=[Y.d])
                    yield
                    zp = psF.next()
                    S.op("pe", lambda e: e.matmul(zp[:], lhsT=Tt[:], rhs=Y[:], start=True, stop=True), reads=[Tt.d, Y.d], writes=[zp.d])
                    Xn = r_m.next()
                    S.op("dve", lambda e: e.tensor_tensor(out=Xn[:], in0=Xt[:], in1=zp[:], op=ALU.subtract), reads=[zp.d, Xt.d], writes=[Xn.d])
                    if not last:
                        tp_ = psB.next()
                        S.op("pe", lambda e: e.transpose(tp_[:], Xn[:], C.ident_bf[:]), reads=[Xn.d, C.ident_bf.d], writes=[tp_.d])
                        Tn = r_m.next()
                        S.op("act", lambda e: e.copy(out=Tn[:], in_=tp_[:]), reads=[tp_.d], writes=[Tn.d])
                        Tt = Tn
                    Xt = Xn
                M = Xt
                TT = M
                yield
                wps = psF.next()
                S.op("pe", lambda e: e.matmul(wps[:], lhsT=kbg[:], rhs=TT[:], start=True, stop=True), reads=[kbg.d, TT.d], writes=[wps.d])
                wn = r_w.next()
                S.op("act", lambda e: e.activation(out=wn[:], in_=wps[:], func=AF.Identity, scale=-1.0), reads=[wps.d], writes=[wn.d])
                Sb, Sd = G.Sbf[:, sidx, :], G.Sbf.deps[sidx]
                Pb, Pd = G.Pbf[:, sidx, :], G.Pbf.deps[sidx]
                S3, S3d = G.S32[:, sidx, :], G.S32.deps[sidx]
                P3, P3d = G.P32[:, sidx, :], G.P32.deps[sidx]
                yield
                vps = psF.next()
                S.op("pe", lambda e: e.matmul(vps[:], lhsT=TT[:], rhs=vbt[:], start=True, stop=False), reads=[TT.d, vbt.d], writes=[vps.d], sig=False)
                S.op("pe", lambda e: e.matmul(vps[:], lhsT=wn[:], rhs=Sb, start=False, stop=True), reads=[wn.d, Sd], writes=[vps.d])
                vn = r_vn.next()
                S.op("act", lambda e: e.copy(out=vn[:], in_=vps[:]), reads=[vps.d], writes=[vn.d])
                if with_P:
                    yield
                    vpp = psF.next()
                    S.op("pe", lambda e: e.matmul(vpp[:], lhsT=wn[:], rhs=Pb, start=True, stop=True), reads=[wn.d, Pd], writes=[vpp.d])
                    vnP = r_vnP.next()
                    S.op("dve", lambda e: e.tensor_copy(out=vnP[:], in_=vpp[:]), reads=[vpp.d], writes=[vnP.d])
                if with_out:
                    yield
                    ops_ = psF.next()
                    S.op("pe", lambda e: e.matmul(ops_[:], lhsT=Sb, rhs=qg[:], start=True, stop=False), reads=[Sd, qg.d], writes=[ops_.d], sig=False)
                    S.op("pe", lambda e: e.matmul(ops_[:], lhsT=vn[:], rhs=qkT[:], start=False, stop=True), reads=[vn.d, qkT.d], writes=[ops_.d])
                    S.op("act", lambda e: e.copy(out=ob[:, h, cols], in_=ops_[:]), reads=[ops_.d], writes=[ob.d])
                    yield
                    rps = psF.next()
                    S.op("pe", lambda e: e.matmul(rps[:], lhsT=Pb, rhs=qg[:], start=True, stop=False), reads=[Pd, qg.d], writes=[rps.d], sig=False)
                    S.op("pe", lambda e: e.matmul(rps[:], lhsT=vnP[:], rhs=qkT[:], start=False, stop=True), reads=[vnP.d, qkT.d], writes=[rps.d])
                    S.op("dve", lambda e: e.tensor_copy(out=rb[:, h, cols], in_=rps[:]), reads=[rps.d], writes=[rb.d])
                egl = eg[:, last_col:last_col + 1]
                yield
                sps = psF.next()
                S.op("pe", lambda e: e.matmul(sps[:], lhsT=kt[:], rhs=vn[:], start=True, stop=True), reads=[kt.d, vn.d], writes=[sps.d])
                S.op("dve", lambda e: e.scalar_tensor_tensor(out=S3, in0=S3, scalar=egl, in1=sps[:], op0=ALU.mult, op1=ALU.add),
                     reads=[S3d, eg.d, sps.d], writes=[S3d])
                S.op("act", lambda e: e.copy(out=Sb, in_=S3), reads=[S3d], writes=[Sd])
                if with_P:
                    yield
                    pps = psF.next()
                    S.op("pe", lambda e: e.matmul(pps[:], lhsT=kt[:], rhs=vnP[:], start=True, stop=True), reads=[kt.d, vnP.d], writes=[pps.d])
                    S.op("dve", lambda e: e.scalar_tensor_tensor(out=P3, in0=P3, scalar=egl, in1=pps[:], op0=ALU.mult, op1=ALU.add),
                         reads=[P3d, eg.d, pps.d], writes=[P3d])
                    S.op("act", lambda e: e.copy(out=Pb, in_=P3), reads=[P3d], writes=[Pd])
                yield
        if with_out:
            S.dma("pool", C.o0_d[dirn].ap[:, hlo:hhi, lo:lo + ntok], ob[:, hlo:hhi, :ntok], reads=[ob.d], writes=C.o0_d[dirn].dd(lo, lo + ntok), own=ob.d)
            S.dma("pool", C.R_d[dirn].ap[:, hlo:hhi, lo:lo + ntok], rb[:, hlo:hhi, :ntok], reads=[rb.d], writes=C.R_d[dirn].dd(lo, lo + ntok), own=rb.d)


def phase3(S, C):
    G = Ctx()
    C.sctx = SBT(S, "sctx", [128, 8, 128], F32)
    with ExitStack() as ph:
        old = S.stack
        S.stack = ph
        gdn_gates(S, C, G)
        gdn_setup(S, C, G)
        zt = SBT(S, "zt", [128, 12, 2], BF16)
        S.op("pool", lambda e: e.memset(zt[:], 0.0), writes=[zt.d])
        S.dma("pool", C.prec_d.ap[:, :, 0:1], zt[:, :, 0:1], reads=[zt.d], writes=C.prec_d.deps, own=zt.d, allow_slow_non_contiguous=True)
        S.dma("pool", C.prec_d.ap[:, :, 257:258], zt[:, :, 1:2], reads=[zt.d], writes=C.prec_d.deps, own=zt.d, allow_slow_non_contiguous=True)
        for i in range(8):
            S.op("pool", lambda e: e.memset(G.S32[:, i, :], 0.0), writes=[G.S32.deps[i]])
            S.op("pool", lambda e: e.memset(G.Sbf[:, i, :], 0.0), writes=[G.Sbf.deps[i]])

        def ctx_src(lo, ntok):
            return C.prec_d.ap[:, :, lo:lo + ntok + 2], C.prec_d.deps

        def lat_src(lo, ntok):
            return C.pre_d.ap[:, :, 128 + lo - 1:128 + lo + ntok + 1], C.pre_d.dd(128 + lo - 1, 128 + lo + ntok + 1)

        def run(gens):
            gens = list(gens)
            while gens:
                for g in list(gens):
                    try:
                        next(g)
                    except StopIteration:
                        gens.remove(g)
        with ExitStack() as ph2:
            S.stack = ph2
            HG = [(0, 1), (2, 3)]
            run([gdn_pass(S, C, G, d_, HG[k_], [(34, 0), (35, 128)] if d_ == 0 else [(35, 128), (34, 0)], False, False, ctx_src,
                          "c%d%d" % (d_, k_), G.psFs[d_ * 2 + k_], G.psBs[d_]) for d_ in range(2) for k_ in range(2)])
            for i in range(8):
                S.op("dve", lambda e: e.tensor_copy(out=C.sctx[:, i, :], in_=G.S32[:, i, :]), reads=[G.S32.deps[i]], writes=[C.sctx.d])
                S.op("pool", lambda e: e.memset(G.S32[:, i, :], 0.0), writes=[G.S32.deps[i]])
                S.op("pool", lambda e: e.memset(G.Sbf[:, i, :], 0.0), writes=[G.Sbf.deps[i]])
                S.op("pool", lambda e: e.tensor_copy(out=G.P32[:, i, :], in_=G.ident_f[:]), reads=[G.ident_f.d], writes=[G.P32.deps[i]])
                S.op("pool", lambda e: e.tensor_copy(out=G.Pbf[:, i, :], in_=C.ident_bf[:]), reads=[C.ident_bf.d], writes=[G.Pbf.deps[i]])
            S.barrier()
            S.stack = ph
        NTL = int(os.environ.get("NTL", str(NT)))
        with ExitStack() as ph2:
            S.stack = ph2
            ft = [(ti, ti * 128) for ti in range(NTL)]
            bt = [(ti, ti * 128) for ti in reversed(range(NT - NTL, NT))]
            gens = [gdn_pass(S, C, G, d_, HG[k_], ft if d_ == 0 else bt, True, True, lat_src,
                             "l%d%d" % (d_, k_), G.psFs[d_ * 2 + k_], G.psBs[d_]) for d_ in range(2) for k_ in range(2)]
            if getattr(C, "preconverted", False):
                gens.append(conv_gen(S, C))
            run(gens)
            xs = SBT(S, "xs", [128, 8, 2, 128], F32)
            for i in range(8):
                S.op("act", lambda e: e.copy(out=xs[:, i, 0, :], in_=G.S32[:, i, :]), reads=[G.S32.deps[i]], writes=[xs.d])
                pp = G.psF.next()
                S.op("pe", lambda e: e.matmul(pp[:], lhsT=G.P32[:, i, :], rhs=G.ident_f[:], start=True, stop=True),
                     reads=[G.P32.deps[i], G.ident_f.d], writes=[pp.d])
                S.op("act", lambda e: e.copy(out=xs[:, i, 1, :], in_=pp[:]), reads=[pp.d], writes=[xs.d])
            S.dma("pool", C.xch_d.ap, xs[:], reads=[xs.d], writes=C.xch_d.deps, own=xs.d)
            S.dma("pool", C.sctx_d.ap, C.sctx[:], reads=[C.sctx.d], writes=C.sctx_d.deps, own=C.sctx.d)
            S.barrier()
            S.stack = ph
        S.stack = old


def convert_ffn_weights(S, C, l):
    with ExitStack() as ph:
        old = S.stack
        S.stack = ph
        st = Ring(S, "cvst", [128, 22, 128], F32, 2)
        sb = Ring(S, "cvsb", [128, 22, 128], BF16, 2)
        k = 0
        for j in range(NJ):
            for gi, nm in enumerate(("ffn_w_gate", "ffn_w_up")):
                s_, b_ = st.next(), sb.next()
                S.dma("sp", s_[:, 0:8, :], C.inp[nm][l, :, j * 128:(j + 1) * 128].rearrange("(kc p) n -> p kc n", p=128), writes=[s_.d])
                S.op(("pool", "dve", "act")[k % 3], (lambda e: e.tensor_copy(out=b_[:, 0:8, :], in_=s_[:, 0:8, :])) if k % 3 != 2 else
                     (lambda e: e.copy(out=b_[:, 0:8, :], in_=s_[:, 0:8, :])), reads=[s_.d], writes=[b_.d])
                S.dma("pool", C.wgu_d.ap[j, :, gi, :, :], b_[:, 0:8, :], reads=[b_.d], writes=C.wgu_d.deps, own=b_.d)
                k += 1
        for n in range(8):
            s_, b_ = st.next(), sb.next()
            S.dma("sp", s_[:], C.inp["ffn_w_down"][l, :, n * 128:(n + 1) * 128].rearrange("(j p) n -> p j n", p=128), writes=[s_.d])
            S.op(("pool", "dve", "act")[k % 3], (lambda e: e.tensor_copy(out=b_[:], in_=s_[:])) if k % 3 != 2 else
                 (lambda e: e.copy(out=b_[:], in_=s_[:])), reads=[s_.d], writes=[b_.d])
            S.dma("pool", C.wd_d.ap[n], b_[:], reads=[b_.d], writes=C.wd_d.deps, own=b_.d)
            k += 1
        S.barrier()
        S.stack = old


def rstd_of(S, C, yT, ntok, R):
    sq = R["sq"].next()
    for c in range(8):
        S.op("dve", lambda e: e.tensor_tensor(out=sq[:, c, :ntok], in0=yT[:, c, :ntok], in1=yT[:, c, :ntok], op=ALU.mult),
             reads=[yT.d], writes=[sq.d])
    psn = R["ps"].next()
    for c in range(8):
        S.op("pe", lambda e: e.matmul(psn[:, :ntok], lhsT=C.ones_bf[:], rhs=sq[:, c, :ntok], start=(c == 0), stop=(c == 7)),
             reads=[C.ones_bf.d, sq.d], writes=[psn.d], sig=(c == 7))
    rstd = R["rstd"].next()
    S.op("act", lambda e: e.activation(out=rstd[:, :ntok], in_=psn[:, :ntok], func=AF.Ln, bias=C.eps_t[:, 0:1], scale=1.0 / D),
         reads=[psn.d, C.eps_t.d], writes=[rstd.d])
    S.op("act", lambda e: e.activation(out=rstd[:, :ntok], in_=rstd[:, :ntok], func=AF.Exp, scale=-0.5), reads=[rstd.d], writes=[rstd.d])
    return rstd


def postnorm_residual(S, C, xb, yT, gg, R):
    rstd = rstd_of(S, C, yT, 512, R)
    for c in range(8):
        tmp = R["tmp"].next()
        S.op("dve", lambda e: e.scalar_tensor_tensor(out=tmp[:], in0=yT[:, c, :], scalar=gg[:, c:c + 1], in1=rstd[:], op0=ALU.mult, op1=ALU.mult),
             reads=[yT.d, gg.d, rstd.d], writes=[tmp.d])
        S.op("dve", lambda e: e.tensor_tensor(out=xb[:, c, :], in0=xb[:, c, :], in1=tmp[:], op=ALU.add), reads=[xb.d, tmp.d], writes=[xb.d])


def ffn_block(S, C, xb, l, R, F):
    hT = F["hT"]
    norm_mod_block(S, C, xb, 512, C.vec[("a2", l)], C.vec[("b2", l)], hT, R)
    actT = F["actT"]
    for j in range(NJ):
        w = F["wgu"].next()
        wgu_d = C.wgu_d[l] if isinstance(C.wgu_d, list) else C.wgu_d
        S.dma("sp", w[:], wgu_d.ap[j], reads=wgu_d.deps, writes=[w.d])
        pg, pu = F["psg"].next(), F["psu"].next()
        for gi, ps in ((0, pg), (1, pu)):
            for kc in range(8):
                S.op("pe", lambda e: e.matmul(ps[:], lhsT=w[:, gi, kc, :], rhs=hT[:, kc, :], start=(kc == 0), stop=(kc == 7)),
                     reads=[w.d, hT.d], writes=[ps.d], sig=(kc == 7))
        sg = F["sg"].next()
        S.op("act", lambda e: e.activation(out=sg[:], in_=pg[:], func=AF.Silu), reads=[pg.d], writes=[sg.d])
        S.op("dve", lambda e: e.tensor_tensor(out=actT[:, j, :], in0=pu[:], in1=sg[:], op=ALU.mult), reads=[pu.d, sg.d], writes=[actT.d])
    fT = F["fT"]
    for n in range(8):
        w = F["wd"].next()
        wd_d = C.wd_d[l] if isinstance(C.wd_d, list) else C.wd_d
        S.dma("sp", w[:], wd_d.ap[n], reads=wd_d.deps, writes=[w.d])
        ps = F["psg"].next()
        for j in range(NJ):
            S.op("pe", lambda e: e.matmul(ps[:], lhsT=w[:, j, :], rhs=actT[:, j, :], start=(j == 0), stop=(j == NJ - 1)),
                 reads=[w.d, actT.d], writes=[ps.d], sig=(j == NJ - 1))
        S.op("act", lambda e: e.copy(out=fT[:, n, :], in_=ps[:]), reads=[ps.d], writes=[fT.d])
    postnorm_residual(S, C, xb, fT, C.vec[("gg2", l)], R)


def common_rings(S, deep=False):
    R = {"sq": Ring(S, "sq", [128, 8, 512], BF16, 1), "ps": Ring(S, "psn", [128, 512], F32, 1, psum=True),
         "rstd": Ring(S, "rstd", [128, 512], F32, 2), "tmp": Ring(S, "ntmp", [128, 512], F32, 3)}
    F = {"hT": SBT(S, "hT", [128, 8, 512], BF16), "actT": SBT(S, "actT", [128, NJ, 512], BF16),
         "fT": SBT(S, "fT", [128, 8, 512], F32), "wgu": Ring(S, "wgu", [128, 2, 8, 128], BF16, 4 if deep else 2),
         "wd": Ring(S, "wd", [128, NJ, 128], BF16, 3 if deep else 2), "psg": Ring(S, "psg", [128, 512], F32, 3, psum=True),
         "psu": Ring(S, "psu", [128, 512], F32, 3, psum=True), "sg": Ring(S, "sg", [128, 512], F32, 3 if deep else 2)}
    return R, F


def phaseB(S, C):
    if not getattr(C, "preconverted", False):
        convert_ffn_weights(S, C, 0)
    ident_f = load_const(S, C, "ident_f")
    gnormg = load_const(S, C, "gnormg")
    sel = load_const(S, C, "sel")
    Sst = SBT(S, "Sst", [128, 8, 128], BF16)
    with ExitStack() as ph:
        old = S.stack
        S.stack = ph
        xall = SBT(S, "xall", [128, 4, 8, 2, 128], F32)
        S.dma("sp", xall[:], C.inp["xall"], writes=[xall.d])
        sctx = SBT(S, "sctxB", [128, 8, 128], F32)
        S.dma("sp", sctx[:], C.inp["sctx_in"], writes=[sctx.d])
        cand = SBT(S, "cand", [128, 4, 128], F32, ndeps=4)
        acc = SBT(S, "accS", [128, 128], F32)
        psc = Ring(S, "psc", [128, 128], F32, 2, psum=True)
        for sidx in range(8):
            dirn = sidx // 4
            order = [0, 1, 2, 3] if dirn == 0 else [3, 2, 1, 0]
            first = order[0]
            S.op("dve", lambda e: e.tensor_copy(out=cand[:, first, :], in_=sctx[:, sidx, :]), reads=[sctx.d], writes=[cand.deps[first]])
            for a_, b_ in zip(order[:-1], order[1:]):
                ps = psc.next()
                S.op("pe", lambda e: e.matmul(ps[:], lhsT=xall[:, a_, sidx, 1, :], rhs=cand[:, a_, :], start=True, stop=True),
                     reads=[xall.d, cand.deps[a_]], writes=[ps.d])
                S.op("dve", lambda e: e.tensor_tensor(out=cand[:, b_, :], in0=ps[:], in1=xall[:, a_, sidx, 0, :], op=ALU.add),
                     reads=[ps.d, xall.d], writes=[cand.deps[b_]])
            S.op("dve", lambda e: e.tensor_scalar(out=acc[:], in0=cand[:, 0, :], scalar1=sel[:, 0:1], scalar2=None, op0=ALU.mult),
                 reads=[cand.deps[0], sel.d], writes=[acc.d])
            for s_ in range(1, 4):
                S.op("dve", lambda e: e.scalar_tensor_tensor(out=acc[:], in0=cand[:, s_, :], scalar=sel[:, s_:s_ + 1], in1=acc[:],
                                                             op0=ALU.mult, op1=ALU.add), reads=[cand.deps[s_], sel.d, acc.d], writes=[acc.d])
            S.op("dve", lambda e: e.tensor_copy(out=Sst[:, sidx, :], in_=acc[:]), reads=[acc.d], writes=[Sst.d])
        S.barrier()
        S.stack = old
    with ExitStack() as ph:
        old = S.stack
        S.stack = ph
        wst = Ring(S, "wst", [128, 8, 128], F32, 2)
        woa = SBT(S, "woa", [64, 8, 1024], BF16)
        wog = SBT(S, "wog", [128, 4, 1024], BF16)
        for c0 in range(0, 1024, 128):
            st = wst.next()
            S.dma("sp", st[0:64, :, :], C.inp["hy_w_out"][0:512, c0:c0 + 128].rearrange("(kg d) n -> d kg n", d=64), writes=[st.d])
            S.op("pool", lambda e: e.tensor_copy(out=woa[:, :, c0:c0 + 128], in_=st[0:64, :, :]), reads=[st.d], writes=[woa.d])
            st = wst.next()
            S.dma("sp", st[:, 0:4, :], C.inp["hy_w_out"][512:1024, c0:c0 + 128].rearrange("(h e) n -> e h n", e=128), writes=[st.d])
            S.op("pool", lambda e: e.tensor_copy(out=wog[:, :, c0:c0 + 128], in_=st[:, 0:4, :]), reads=[st.d], writes=[wog.d])
        cvs = Ring(S, "cvs2", [128, 8, 128], F32, 2)
        cvb = Ring(S, "cvb2", [128, 8, 128], BF16, 2)
        for j in range(0 if getattr(C, "preconverted", False) else 24):
            s_, b_ = cvs.next(), cvb.next()
            S.dma("sp", s_[:], C.inp["sc_w_in"][:, j * 128:(j + 1) * 128].rearrange("(kc p) n -> p kc n", p=128), writes=[s_.d])
            S.op("pool" if j % 2 else "dve", lambda e: e.tensor_copy(out=b_[:], in_=s_[:]), reads=[s_.d], writes=[b_.d])
            S.dma("pool", C.wsc_d.ap[j], b_[:], reads=[b_.d], writes=C.wsc_d.deps, own=b_.d)
        wscr = Ring(S, "wscr", [128, 8, 128], BF16, 3)
        R, F = common_rings(S)
        xring = Ring(S, "xblk", [128, 8, 512], F32, 1)
        rf = Ring(S, "rfb", [128, 2, 512], BF16, 2)
        o0 = Ring(S, "o0b", [128, 2, 512], F32, 2)
        zs = Ring(S, "zsb", [128, 4, 512], BF16, 1)
        at = Ring(S, "atb", [64, 8, 512], BF16, 1)
        mixT = SBT(S, "mixT", [128, 4, 512], BF16)
        o32 = Ring(S, "o32", [128, 512], F32, 2)
        yT = F["fT"]
        cur = Ring(S, "cub", [128, 512], BF16, 3)
        bgr_ = Ring(S, "bgb", [128, 512], BF16, 3)
        csb = Ring(S, "csb", [128, 512], F32, 2)
        pso = F["psu"]
        for bi in range(int(os.environ.get("NBLK", T // 512))):
            cs_ = slice(bi * 512, (bi + 1) * 512)
            xb = xring.next()
            S.dma("sp", xb[:], C.inp["xT"][:, :, cs_].rearrange("c p t -> p c t"), writes=[xb.d])
            z_, a_ = zs.next(), at.next()
            S.dma("sp", z_[:], C.inp["zs_in"][:, :, cs_], writes=[z_.d])
            S.dma("sp", a_[:], C.inp["attn_in"][:, :, cs_], writes=[a_.d])
            for h in range(4):
                r_, o_ = rf.next(), o0.next()
                for dirn in range(2):
                    S.dma("sp", r_[:, dirn], C.inp["R_in%d" % dirn][:, h, cs_], writes=[r_.d])
                    S.dma("sp", o_[:, dirn], C.inp["o0_in%d" % dirn][:, h, cs_], writes=[o_.d])
                ps = pso.next()
                S.op("pe", lambda e: e.matmul(ps[:], lhsT=Sst[:, h, :], rhs=r_[:, 0, :], start=True, stop=False), reads=[Sst.d, r_.d], writes=[ps.d], sig=False)
                S.op("pe", lambda e: e.matmul(ps[:], lhsT=Sst[:, 4 + h, :], rhs=r_[:, 1, :], start=False, stop=True), reads=[Sst.d, r_.d], writes=[ps.d])
                o = o32.next()
                S.op("dve", lambda e: e.tensor_tensor(out=o[:], in0=ps[:], in1=o_[:, 0, :], op=ALU.add), reads=[ps.d, o_.d], writes=[o.d])
                S.op("dve", lambda e: e.tensor_tensor(out=o[:], in0=o[:], in1=o_[:, 1, :], op=ALU.add), reads=[o.d, o_.d], writes=[o.d])
                sq = R["sq"].next()
                S.op("dve", lambda e: e.tensor_tensor(out=sq[:, 0, :], in0=o[:], in1=o[:], op=ALU.mult), reads=[o.d], writes=[sq.d])
                pn = R["ps"].next()
                S.op("pe", lambda e: e.matmul(pn[:], lhsT=C.ones_bf[:], rhs=sq[:, 0, :], start=True, stop=True), reads=[C.ones_bf.d, sq.d], writes=[pn.d])
                rn = R["rstd"].next()
                S.op("act", lambda e: e.activation(out=rn[:], in_=pn[:], func=AF.Ln, bias=C.eps_t[:, 0:1], scale=1.0 / 128), reads=[pn.d, C.eps_t.d], writes=[rn.d])
                S.op("act", lambda e: e.activation(out=rn[:], in_=rn[:], func=AF.Exp, scale=-0.5), reads=[rn.d], writes=[rn.d])
                tmp = R["tmp"].next()
                S.op("dve", lambda e: e.scalar_tensor_tensor(out=tmp[:], in0=o[:], scalar=gnormg[:, 0:1], in1=rn[:], op0=ALU.mult, op1=ALU.mult),
                     reads=[o.d, gnormg.d, rn.d], writes=[tmp.d])
                S.op("dve", lambda e: e.tensor_tensor(out=mixT[:, h, :], in0=tmp[:], in1=z_[:, h, :], op=ALU.mult), reads=[tmp.d, z_.d], writes=[mixT.d])
            for n in range(8):
                ps = F["psg"].next()
                for kg in range(8):
                    S.op("pe", lambda e: e.matmul(ps[:], lhsT=woa[:, kg, n * 128:(n + 1) * 128], rhs=a_[:, kg, :], start=(kg == 0), stop=False),
                         reads=[woa.d, a_.d], writes=[ps.d], sig=False)
                for h in range(4):
                    S.op("pe", lambda e: e.matmul(ps[:], lhsT=wog[:, h, n * 128:(n + 1) * 128], rhs=mixT[:, h, :], start=False, stop=(h == 3)),
                         reads=[wog.d, mixT.d], writes=[ps.d], sig=(h == 3))
                S.op("act", lambda e: e.copy(out=yT[:, n, :], in_=ps[:]), reads=[ps.d], writes=[yT.d])
            postnorm_residual(S, C, xb, yT, C.vec[("gg1", 0)], R)
            ffn_block(S, C, xb, 0, R, F)
            S.dma("pool", C.x2_d.ap[:, :, cs_], xb[:], reads=[xb.d], writes=C.x2_d.dd(bi * 512, bi * 512 + 512), own=xb.d)
            hT = F["hT"]
            norm_mod_block(S, C, xb, 512, C.vec[("a1", 1)], C.vec[("b1", 1)], hT, R)
            for c in range(8):
                def proj(j):
                    ps = F["psg"].next() if j < 16 else F["psu"].next()
                    wsc = wscr.next()
                    S.dma("sp", wsc[:], C.wsc_d.ap[j], reads=C.wsc_d.deps, writes=[wsc.d])
                    for kc in range(8):
                        S.op("pe", lambda e: e.matmul(ps[:], lhsT=wsc[:, kc, :], rhs=hT[:, kc, :], start=(kc == 0), stop=(kc == 7)),
                             reads=[wsc.d, hT.d], writes=[ps.d], sig=(kc == 7))
                    return ps
                pb_ = proj(c)
                bgb = bgr_.next()
                S.op("act", lambda e: e.copy(out=bgb[:], in_=pb_[:]), reads=[pb_.d], writes=[bgb.d])
                S.dma("pool", C.bg_d.ap[:, c, cs_], bgb[:], reads=[bgb.d], writes=C.bg_d.dd(bi * 512, bi * 512 + 512), own=bgb.d)
                pc_ = proj(8 + c)
                cs32 = csb.next()
                S.op("act", lambda e: e.copy(out=cs32[:], in_=pc_[:]), reads=[pc_.d], writes=[cs32.d])
                pu_ = proj(16 + c)
                cub = cur.next()
                S.op("dve", lambda e: e.tensor_tensor(out=cub[:], in0=pu_[:], in1=cs32[:], op=ALU.mult), reads=[pu_.d, cs32.d], writes=[cub.d])
                cuo = getattr(C, "cu_off", 0)
                S.dma("pool", C.cu_d.ap[:, c, cuo + bi * 512:cuo + bi * 512 + 512], cub[:], reads=[cub.d],
                      writes=C.cu_d.dd(bi * 512, bi * 512 + 512), own=cub.d)
                if getattr(C, "edge", None) is not None and bi in (0, T // 512 - 1):
                    ecol = 0 if bi == 0 else 511
                    eidx = c * 2 + (0 if bi == 0 else 1)
                    S.op("pool", lambda e: e.tensor_copy(out=C.edge[:, eidx:eidx + 1], in_=cub[:, ecol:ecol + 1]),
                         reads=[cub.d], writes=[C.edge.d])
                    if T // 512 == 1:
                        S.op("pool", lambda e: e.tensor_copy(out=C.edge[:, c * 2 + 1:c * 2 + 2], in_=cub[:, 511:512]),
                             reads=[cub.d], writes=[C.edge.d])
        S.barrier()
        S.stack = old


def phaseC(S, C):
    if not getattr(C, "preconverted", False):
        convert_ffn_weights(S, C, 1)
    with ExitStack() as ph:
        old = S.stack
        S.stack = ph
        sconv = load_const(S, C, "sconv")
        diag = SBT(S, "diagc", [128, 24, 128], BF16)
        for c in range(8):
            for k in range(3):
                S.op("dve", lambda e: e.tensor_scalar(out=diag[:, c * 3 + k, :], in0=C.ident_bf[:], scalar1=sconv[:, c, k:k + 1], scalar2=None,
                                                      op0=ALU.mult), reads=[C.ident_bf.d, sconv.d], writes=[diag.d])
        wso = SBT(S, "wso", [128, 8, 1024], BF16)
        with ExitStack() as phw:
            S.stack = phw
            wst = Ring(S, "wst", [128, 8, 256], F32, 2)
            load_weight_bf16(S, C, wso, 0, C.inp["sc_w_out"], 1024, wst)
            S.barrier()
            S.stack = ph
        R, F = common_rings(S, deep=True)
        xring = Ring(S, "xblk", [128, 8, 512], F32, 2)
        cuw = Ring(S, "cuw", [128, 8, 514], BF16, 1)
        bgr = Ring(S, "bgr", [128, 8, 512], BF16, 1)
        mT = SBT(S, "mT", [128, 8, 512], BF16)
        yT = F["fT"]
        for bi in range(int(os.environ.get("NBLK", T // 512))):
            cs_ = slice(bi * 512, (bi + 1) * 512)
            xb = xring.next()
            S.dma("sp", xb[:], C.inp["x2_in"][:, :, cs_], writes=[xb.d])
            cw, bg = cuw.next(), bgr.next()
            S.dma("sp", cw[:], C.inp["cu_ext"][:, :, bi * 512:bi * 512 + 514], writes=[cw.d])
            S.dma("sp", bg[:], C.inp["bg_in"][:, :, cs_], writes=[bg.d])
            for c in range(8):
                ps = F["psu"].next()
                for k in range(3):
                    S.op("pe", lambda e: e.matmul(ps[:], lhsT=diag[:, c * 3 + k, :], rhs=cw[:, c, k:k + 512], start=(k == 0), stop=(k == 2)),
                         reads=[diag.d, cw.d], writes=[ps.d], sig=(k == 2))
                S.op("dve", lambda e: e.tensor_tensor(out=mT[:, c, :], in0=ps[:], in1=bg[:, c, :], op=ALU.mult), reads=[ps.d, bg.d], writes=[mT.d])
            for n in range(8):
                ps = F["psg"].next()
                for kc in range(8):
                    S.op("pe", lambda e: e.matmul(ps[:], lhsT=wso[:, kc, n * 128:(n + 1) * 128], rhs=mT[:, kc, :], start=(kc == 0), stop=(kc == 7)),
                         reads=[wso.d, mT.d], writes=[ps.d], sig=(kc == 7))
                S.op("act", lambda e: e.copy(out=yT[:, n, :], in_=ps[:]), reads=[ps.d], writes=[yT.d])
            postnorm_residual(S, C, xb, yT, C.vec[("gg1", 1)], R)
            ffn_block(S, C, xb, 1, R, F)
            S.dma("pool", C.out_d.ap[:, :, cs_], xb[:], reads=[xb.d], writes=C.out_d.dd(bi * 512, bi * 512 + 512), own=xb.d)
        S.barrier()
        S.stack = old


def halo_exchange(S, C, groups):
    S.dma("pool", C.edge_d.ap, C.edge[:], reads=[C.edge.d], writes=C.edge_d.deps, own=C.edge.d)
    S.allgather(C.edge_d.ap, C.edges_g.ap, groups)
    with ExitStack() as ph:
        old = S.stack
        S.stack = ph
        selL = load_const(S, C, "selL")
        selR = load_const(S, C, "selR")
        eg = SBT_(S, "egath", [128, 4, 8, 2], BF16)
        S.dma("sp", eg[:].rearrange("p r c k -> p r (c k)"), C.edges_g.ap.rearrange("(r p) n -> p r n", p=128), writes=[eg.d])
        for nm, sel, k, col in (("hl", selL, 1, 0), ("hr", selR, 0, T + 1)):
            acc = SBT_(S, nm + "acc", [128, 8], F32)
            S.op("dve", lambda e: e.tensor_scalar(out=acc[:], in0=eg[:, 0, :, k], scalar1=sel[:, 0:1], scalar2=None, op0=ALU.mult),
                 reads=[eg.d, sel.d], writes=[acc.d])
            for r in range(1, 4):
                S.op("dve", lambda e: e.scalar_tensor_tensor(out=acc[:], in0=eg[:, r, :, k], scalar=sel[:, r:r + 1], in1=acc[:],
                                                             op0=ALU.mult, op1=ALU.add), reads=[eg.d, sel.d, acc.d], writes=[acc.d])
            hb = SBT_(S, nm + "bf", [128, 8, 1], BF16)
            S.op("dve", lambda e: e.tensor_copy(out=hb[:, :, 0], in_=acc[:]), reads=[acc.d], writes=[hb.d])
            S.dma("pool", C.cu_d.ap[:, :, col:col + 1], hb[:], reads=[hb.d], writes=C.cu_d.deps, own=hb.d, allow_slow_non_contiguous=True)
        S.barrier()
        S.stack = old


def conv_gen(S, C):
    st = Ring(S, "pcs", [128, 8, 128], F32, 2)
    sb = Ring(S, "pcb", [128, 8, 128], BF16, 2)
    for l in range(2):
        for j in range(NJ):
            for gi, nm in enumerate(("ffn_w_gate", "ffn_w_up")):
                s_, b_ = st.next(), sb.next()
                S.dma("sp", s_[:], C.inp[nm][l, :, j * 128:(j + 1) * 128].rearrange("(kc p) n -> p kc n", p=128), writes=[s_.d])
                S.op("pool", lambda e: e.tensor_copy(out=b_[:], in_=s_[:]), reads=[s_.d], writes=[b_.d])
                S.dma("pool", C.wgu_d[l].ap[j, :, gi, :, :], b_[:], reads=[b_.d], writes=C.wgu_d[l].deps, own=b_.d)
                yield
        for n in range(8):
            for j0 in range(0, NJ, 8):
                nj = min(8, NJ - j0)
                s_, b_ = st.next(), sb.next()
                S.dma("sp", s_[:, :nj, :], C.inp["ffn_w_down"][l, j0 * 128:(j0 + nj) * 128, n * 128:(n + 1) * 128].rearrange("(j p) n -> p j n", p=128),
                      writes=[s_.d])
                S.op("pool", lambda e: e.tensor_copy(out=b_[:, :nj, :], in_=s_[:, :nj, :]), reads=[s_.d], writes=[b_.d])
                S.dma("pool", C.wd_d[l].ap[n, :, j0:j0 + nj, :], b_[:, :nj, :], reads=[b_.d], writes=C.wd_d[l].deps, own=b_.d)
                yield
    for j in range(24):
        s_, b_ = st.next(), sb.next()
        S.dma("sp", s_[:], C.inp["sc_w_in"][:, j * 128:(j + 1) * 128].rearrange("(kc p) n -> p kc n", p=128), writes=[s_.d])
        S.op("pool", lambda e: e.tensor_copy(out=b_[:], in_=s_[:]), reads=[s_.d], writes=[b_.d])
        S.dma("pool", C.wsc_d.ap[j], b_[:], reads=[b_.d], writes=C.wsc_d.deps, own=b_.d)
        yield


import sys, time
from concourse.bass_utils import run_bass_kernel_spmd

GROUPS = [[0, 1, 2, 3], [4, 5, 6, 7]]
NAMES_F = ["xT", "xhT", "ctxT", "cvec", "ident_bf", "ident_f", "swap_bf", "ones_bf", "ones_f", "amask", "gmask", "lvlmask", "cmat", "hv",
           "cosT", "sinT", "ada_w", "ada_bT", "gains", "wq", "wk", "wvab", "wg", "wz", "sinkrow", "gconv", "galog", "gdtb",
           "hy_w_out", "gnormg", "sel", "selL", "selR", "sc_w_in", "ffn_w_gate", "ffn_w_up", "ffn_w_down", "sconv", "sc_w_out"]


def build_fused():
    nc = bass.Bass("TRN2", target_bir_lowering=False)
    C = declare_inputs(nc, NAMES_F)
    C.q_d = DramT(nc, "q_d", [128, 4, T], BF16)
    C.zs_d = DramT(nc, "zs_d", [128, 4, T], BF16)
    C.pre_d = DramT(nc, "pre_d", [128, 12, T + 256], BF16)
    C.prec_d = DramT(nc, "prec_d", [128, 12, 258], BF16)
    C.attn_d = DramT(nc, "attn_d", [64, 8, T], BF16)
    C.o0_d = [DramT(nc, "o0_d%d" % i, [128, 4, T], F32) for i in range(2)]
    C.R_d = [DramT(nc, "R_d%d" % i, [128, 4, T], BF16) for i in range(2)]
    C.xch_d = DramT(nc, "xch_d", [128, 8, 2, 128], F32)
    C.sctx_d = DramT(nc, "sctx_d", [128, 8, 128], F32)
    C.xall_g = DramT(nc, "xall_g", [4 * 128, 2048], F32)
    C.wgu_d = [DramT(nc, "wgu_d%d" % l, [NJ, 128, 2, 8, 128], BF16, blk=10 ** 9) for l in range(2)]
    C.wd_d = [DramT(nc, "wd_d%d" % l, [8, 128, NJ, 128], BF16, blk=10 ** 9) for l in range(2)]
    C.preconverted = True
    C.wsc_d = DramT(nc, "wsc_d", [24, 128, 8, 128], BF16, blk=10 ** 9)
    C.x2_d = DramT(nc, "x2_d", [128, 8, T], F32)
    C.cu_d = DramT(nc, "cu_d", [128, 8, T + 2], BF16)
    C.bg_d = DramT(nc, "bg_d", [128, 8, T], BF16)
    C.edge_d = DramT(nc, "edge_d", [128, 16], BF16)
    C.edges_g = DramT(nc, "edges_g", [4 * 128, 16], BF16)
    C.out_d = DramT(nc, "out_d", [128, 8, T], F32, kind="ExternalOutput")
    C.cu_off = 1
    with ExitStack() as st:
        S = Sched(nc, st)
        setup_common(S, C)
        phase0(S, C)
        S.retire()
        with ExitStack() as stA:
            S.stack = stA
            phase1(S, C, None)
            S.retire()
            phase2(S, C)
            S.retire()
            phase3(S, C)
            S.retire()
            S.stack = st
        S.allgather(C.xch_d.ap.rearrange("p a b c -> p (a b c)"), C.xall_g.ap, GROUPS)
        C.inp["xall"] = C.xall_g.ap.rearrange("(r p) n -> p r n", p=128)
        C.inp["sctx_in"] = C.sctx_d.ap
        for i in range(2):
            C.inp["R_in%d" % i] = C.R_d[i].ap
            C.inp["o0_in%d" % i] = C.o0_d[i].ap
        C.inp["zs_in"] = C.zs_d.ap
        C.inp["attn_in"] = C.attn_d.ap
        with ExitStack() as stB:
            S.stack = stB
            C.edge = SBT_(S, "edge", [128, 16], BF16)
            phaseB(S, C)
            S.retire()
            halo_exchange(S, C, GROUPS)
            S.retire()
            S.stack = st
        C.inp["x2_in"] = C.x2_d.ap
        C.inp["cu_ext"] = C.cu_d.ap
        C.inp["bg_in"] = C.bg_d.ap
        with ExitStack() as stC:
            S.stack = stC
            phaseC(S, C)
            S.stack = st
        S.barrier()
        pass
    return nc


def kernel(**inp):
    inp = {k: np.asarray(v) for k, v in inp.items()}
    shared = prep_shared_inputs(inp)
    cores = list(range(8))
    maps = []
    for core in cores:
        b, s = core // 4, core % 4
        d = prep_core_inputs(inp, b, s)
        d.update(shared)
        for nm, idx in (("sel", s), ("selL", s - 1), ("selR", s + 1)):
            v = np.zeros((128, 4), np.float32)
            if 0 <= idx <= 3:
                v[:, idx] = 1.0
            d[nm] = v
        maps.append({k: d[k] for k in NAMES_F})
    nc = build_fused()
    res = run_bass_kernel_spmd(nc, maps, core_ids=cores).results
    out = np.zeros((2, 4 * T, D), np.float32)
    for c in cores:
        b, s = c // 4, c % 4
        o = np.asarray(res[c]["out_d"])
        out[b, s * T:(s + 1) * T, :] = o.transpose(2, 1, 0).reshape(T, D)
    return out
```

```python
import numpy as np
import concourse.bass as bass
import concourse.mybir as mybir

F32 = mybir.dt.float32
BF16 = mybir.dt.bfloat16
AF = mybir.ActivationFunctionType
ALU = mybir.AluOpType
AX = mybir.AxisListType

EPOCH = 12000
_UID = [0]


class Dep:
    __slots__ = ("w", "r", "dsem", "dval", "name", "uid")

    def __init__(self, name=""):
        _UID[0] += 1
        self.uid = _UID[0]
        self.w = None
        self.r = {}
        self.dsem = None
        self.dval = 0
        self.name = name


class Sched:
    def __init__(self, nc, stack):
        self.nc = nc
        self.stack = stack
        self.semstack = stack
        self.eng = {"pe": nc.tensor, "dve": nc.vector, "act": nc.scalar, "pool": nc.gpsimd, "sp": nc.sync}
        self.sem = {}
        self.cnt = {}
        self.order = {}
        self.seen = {k: {} for k in self.eng}
        self.nsem = 0
        for k in self.eng:
            self.sem[k] = self.new_sem(k)
            self.cnt[k] = 0
            self.order[k] = 0
        self.ninst = 0
        self.nwait = 0
        self.last_tok = {}
        self.dma_deps = []
        self.free_dsems = []

    def new_sem(self, tag):
        self.nsem += 1
        return self.semstack.enter_context(self.nc.semaphore("s%d_%s" % (self.nsem, tag)))

    def _need(self, e, tok):
        if tok is None:
            return
        pk, order, sem, val = tok
        if pk == "pe" and e == "pe":
            return
        if self.seen[e].get(pk, -1) >= order:
            return
        self.eng[e].wait_ge(sem, val)
        self.nwait += 1
        self.seen[e][pk] = order

    def _waits(self, e, reads, writes):
        for d in reads:
            self._need(e, d.w)
        for d in writes:
            self._need(e, d.w)
            for t in list(d.r.values()):
                self._need(e, t)

    def op(self, e, fn, reads=(), writes=(), sig=True):
        self._waits(e, reads, writes)
        ins = fn(self.eng[e])
        self.ninst += 1
        self.order[e] += 1
        if sig:
            self.cnt[e] += 1
            ins.then_inc(self.sem[e], 1)
            tok = (e, self.order[e], self.sem[e], self.cnt[e])
            self.last_tok[e] = tok
        else:
            tok = (e, self.order[e], self.sem[e], self.cnt[e] + 1)
        for d in writes:
            d.w = tok
            d.r = {}
        for d in reads:
            if d not in writes:
                d.r[e] = tok
        if sig and self.cnt[e] >= EPOCH:
            self.sem[e] = self.new_sem(e)
            self.cnt[e] = 0
        return ins

    def dma(self, q, out, in_, reads=(), writes=(), own=None, **kw):
        self._waits(q, reads, writes)
        own = own or writes[0]
        if own.dsem is not None and own.dval > 0:
            self._need(q, ("dma%d" % own.uid, own.dval, own.dsem, own.dval))
        if own.dsem is None:
            if self.free_dsems:
                own.dsem, own.dval = self.free_dsems.pop()
            else:
                own.dsem = self.new_sem("d")
                own.dval = 0
            self.dma_deps.append(own)
        own.dval += 16
        ins = self.eng[q].dma_start(out=out, in_=in_, **kw)
        ins.then_inc(own.dsem, 16)
        self.ninst += 1
        tok = ("dma%d" % own.uid, own.dval, own.dsem, own.dval)
        for d in writes:
            d.w = tok
            d.r = {}
        for d in reads:
            d.r[tok[0]] = tok
        return ins

    def barrier(self, engines=None):
        for e in (engines or list(self.eng)):
            for p, tok in self.last_tok.items():
                if not (p == e and e == "pe"):
                    self._need(e, tok)
            for d in self.dma_deps:
                self._need(e, ("dma%d" % d.uid, d.dval, d.dsem, d.dval))

    def retire(self):
        self.barrier()
        for d in self.dma_deps:
            self.free_dsems.append((d.dsem, d.dval))
            d.dsem = None
        self.dma_deps = []

    def allgather(self, in_ap, out_ap, groups):
        self.barrier()
        csem = self.new_sem("cc")
        ins = self.nc.gpsimd.collective_compute("AllGather", mybir.AluOpType.bypass, replica_groups=groups,
                                                ins=[in_ap.opt()], outs=[out_ap.opt()])
        ins.then_inc(csem, 1)
        self.ninst += 1
        for e in self.eng:
            self.eng[e].wait_ge(csem, 1)

    def wait_all(self, e, deps):
        for d in deps:
            self._need(e, d.w)
            for t in list(d.r.values()):
                self._need(e, t)


class SBT:
    def __init__(self, S, name, shape, dtype, psum=False, ndeps=1):
        nc = S.nc
        S.ntile = getattr(S, "ntile", 0) + 1
        name = "t%d_%s" % (S.ntile, name)
        if psum:
            self.t = S.stack.enter_context(nc.psum_tensor(name, shape, dtype))
        else:
            self.t = S.stack.enter_context(nc.sbuf_tensor(name, shape, dtype))
        self.deps = [Dep(name + str(i)) for i in range(ndeps)]
        self.d = self.deps[0]
        self.shape = shape

    def __getitem__(self, idx):
        return self.t[idx]


SBT_ = SBT


class DramT:
    def __init__(self, nc, name, shape, dtype, kind="Internal", blk=512):
        self.ap = nc.dram_tensor(name, shape, dtype, kind=kind).ap()
        self.blk = blk
        self.deps = [Dep("%s_%d" % (name, i)) for i in range((shape[-1] + blk - 1) // blk)]

    def dd(self, c0, c1):
        return self.deps[c0 // self.blk:(c1 - 1) // self.blk + 1]


import os
import numpy as np
import ml_dtypes
from contextlib import ExitStack

BF = ml_dtypes.bfloat16
D = 1024
T = 4096
NT = T // 128
HAL = 128
CTX = 256
EXT = T + 2 * HAL + CTX
COL_HL, COL_HR, COL_CTX = T, T + HAL, T + 2 * HAL
FF = 2816
NJ = FF // 128
EPS = 1e-6


class Ring:
    def __init__(self, S, name, shape, dtype, n, psum=False):
        self.tiles = [SBT(S, "%s%d" % (name, i), shape, dtype, psum=psum) for i in range(n)]
        self.i = 0

    def next(self):
        t = self.tiles[self.i % len(self.tiles)]
        self.i += 1
        return t


def rope_tables(s):
    pos_own = s * T + np.arange(T)
    pos_l = s * T - HAL + np.arange(HAL)
    pos_r = (s + 1) * T + np.arange(HAL)
    pos = np.concatenate([pos_own, pos_l, pos_r]).astype(np.int64)
    row = (pos // 64).astype(np.float32)
    col = (pos % 64).astype(np.float32)
    inv = (np.float32(10000.0) ** (-np.arange(16, dtype=np.float32) / np.float32(16))).astype(np.float32)
    ang = np.concatenate([row[:, None] * inv[None, :], col[:, None] * inv[None, :]], axis=1).astype(np.float32)
    cs = np.cos(ang).astype(np.float32).T
    sn = np.sin(ang).astype(np.float32).T
    cosT = np.concatenate([cs, cs, cs, cs], axis=0)
    sinT = np.concatenate([-sn, sn, -sn, sn], axis=0)
    return np.ascontiguousarray(cosT), np.ascontiguousarray(sinT)


def make_consts(s):
    c = {}
    c["ident_bf"] = np.eye(128, dtype=np.float32).astype(BF)
    c["ident_f"] = np.eye(128, dtype=np.float32)
    sw = np.zeros((128, 128), np.float32)
    for m in range(128):
        k = (m // 64) * 64 + ((m % 64) + 32) % 64
        sw[k, m] = 1.0
    c["swap_bf"] = sw.astype(BF)
    c["ones_bf"] = np.ones((128, 128), np.float32).astype(BF)
    c["ones_f"] = np.ones((128, 128), np.float32)
    m = np.arange(128)[:, None]
    n = np.arange(128)[None, :]
    mL = (m >= n).astype(np.float32)
    mR = (m <= n).astype(np.float32)
    masks = np.zeros((128, 4, 512), np.float32)
    masks[:, 0] = np.tile(mL, (1, 4))
    masks[:, 1] = np.tile(mR, (1, 4))
    masks[:, 2] = np.tile(mL, (1, 4)) * (0.0 if s == 0 else 1.0)
    masks[:, 3] = np.tile(mR, (1, 4)) * (0.0 if s == 3 else 1.0)
    c["amask"] = masks.astype(BF)
    gm = np.zeros((128, 4, 128), np.float32)
    gm[:, 0] = (n <= m)
    gm[:, 1] = (n < m)
    gm[:, 2] = (n >= m)
    gm[:, 3] = (n > m)
    c["gmask"] = gm
    cm = np.zeros((128, 4, 128), np.float32)
    cm[:, 0] = (m <= n)
    cm[:, 1] = (m > n)
    cm[:, 2] = (m >= n)
    cm[:, 3] = (m < n)
    c["cmat"] = cm
    lm = np.zeros((128, 14, 128), np.float32)
    for li, mm_ in enumerate((1, 2, 4, 8, 16, 32, 64)):
        low = ((m // mm_) % 2 == 1) & ((n // mm_) == (m // mm_) - 1)
        lm[:, li * 2] = low
        lm[:, li * 2 + 1] = low.T
    c["lvlmask"] = lm.astype(BF)
    hv = np.ones((128, 2), np.float32)
    if s == 0:
        hv[:, 0] = 0
    if s == 3:
        hv[:, 1] = 0
    c["hv"] = hv
    cosT, sinT = rope_tables(s)
    c["cosT"] = cosT
    c["sinT"] = sinT
    return c


def prep_core_inputs(inp, b, s):
    x = inp["x"]
    d = {}
    xo = x[b, s * T:(s + 1) * T, :]
    d["xT"] = np.ascontiguousarray(xo.T.reshape(8, 128, T))
    xh = np.zeros((2 * HAL, D), np.float32)
    if s > 0:
        xh[:HAL] = x[b, s * T - HAL:s * T]
    if s < 3:
        xh[HAL:] = x[b, (s + 1) * T:(s + 1) * T + HAL]
    d["xhT"] = np.ascontiguousarray(xh.T.reshape(8, 128, 2 * HAL))
    d["ctxT"] = np.ascontiguousarray(inp["ctx"][b].T.reshape(8, 128, CTX))
    cv = np.stack([inp["c"][b], inp["c_ctx"]], axis=1)
    d["cvec"] = np.ascontiguousarray(cv.reshape(8, 128, 2).transpose(1, 0, 2))
    d.update(make_consts(s))
    return d


def prep_shared_inputs(inp):
    d = {}
    d["ada_w"] = inp["ada_w"]
    d["ada_bT"] = np.ascontiguousarray(inp["ada_b"].reshape(2, 48, 128).transpose(2, 0, 1))
    g = np.stack([inp["pre_mix_g"], inp["post_mix_g"], inp["pre_ffn_g"], inp["post_ffn_g"]], axis=0)
    d["gains"] = np.ascontiguousarray(g.reshape(4, 2, 8, 128).transpose(3, 0, 1, 2))
    w = inp["hy_w_in"][0]
    wq = w[:, 0:512].reshape(D, 2, 4, 64).transpose(0, 2, 1, 3).reshape(D, 512)
    d["wq"] = np.ascontiguousarray(wq)
    d["wk"] = np.ascontiguousarray(w[:, 512:640])
    d["wvab"] = np.ascontiguousarray(np.concatenate([w[:, 640:768], w[:, 2816:2832]], axis=1))
    d["wg"] = np.ascontiguousarray(w[:, 768:2304])
    d["wz"] = np.ascontiguousarray(w[:, 2304:2816])
    d["hy_w_out"] = inp["hy_w_out"][0]
    sk = inp["attn_sink"][0].reshape(2, 4)
    d["sinkrow"] = np.ascontiguousarray(np.broadcast_to(sk[None, :, :, None], (64, 2, 4, 128)).reshape(64, 2, 512))
    d["gconv"] = np.ascontiguousarray(inp["gdn_conv_w"][0].reshape(3, 12, 128).transpose(2, 1, 0))
    d["galog"] = np.ascontiguousarray(np.broadcast_to(inp["gdn_a_log"][0].reshape(1, 8), (128, 8)))
    d["gdtb"] = np.ascontiguousarray(np.broadcast_to(inp["gdn_dt_bias"][0].reshape(1, 8), (128, 8)))
    d["gnormg"] = np.ascontiguousarray(inp["gdn_norm_g"][0].reshape(128, 1))
    d["sc_w_in"] = inp["sc_w_in"][0]
    d["sconv"] = np.ascontiguousarray(inp["sc_conv_w"][0].reshape(3, 8, 128).transpose(2, 1, 0))
    d["sc_w_out"] = inp["sc_w_out"][0]
    d["ffn_w_gate"] = inp["ffn_w_gate"]
    d["ffn_w_up"] = inp["ffn_w_up"]
    d["ffn_w_down"] = inp["ffn_w_down"]
    return d


INPUT_SPECS = {
    "xT": ([8, 128, T], F32), "xhT": ([8, 128, 2 * HAL], F32), "ctxT": ([8, 128, CTX], F32), "cvec": ([128, 8, 2], F32),
    "ident_bf": ([128, 128], BF16), "ident_f": ([128, 128], F32), "swap_bf": ([128, 128], BF16),
    "ones_bf": ([128, 128], BF16), "ones_f": ([128, 128], F32), "amask": ([128, 4, 512], BF16),
    "gmask": ([128, 4, 128], F32), "lvlmask": ([128, 14, 128], BF16), "cmat": ([128, 4, 128], F32), "hv": ([128, 2], F32),
    "cosT": ([128, T + 2 * HAL], F32), "sinT": ([128, T + 2 * HAL], F32),
    "ada_w": ([2, D, 6 * D], F32), "ada_bT": ([128, 2, 48], F32), "gains": ([128, 4, 2, 8], F32),
    "wq": ([D, 512], F32), "wk": ([D, 128], F32), "wvab": ([D, 144], F32), "wg": ([D, 1536], F32), "wz": ([D, 512], F32),
    "hy_w_out": ([D, D], F32), "sinkrow": ([64, 2, 512], F32), "gconv": ([128, 12, 3], F32),
    "galog": ([128, 8], F32), "gdtb": ([128, 8], F32), "gnormg": ([128, 1], F32),
    "sc_w_in": ([D, 3 * D], F32), "sconv": ([128, 8, 3], F32), "sc_w_out": ([D, D], F32),
    "sel": ([128, 4], F32), "xall": ([128, 4, 8, 2, 128], F32), "sctx_in": ([128, 8, 128], F32),
    "R_in0": ([128, 4, T], BF16), "R_in1": ([128, 4, T], BF16), "o0_in0": ([128, 4, T], F32), "o0_in1": ([128, 4, T], F32),
    "zs_in": ([128, 4, T], BF16), "attn_in": ([64, 8, T], BF16), "x2_in": ([128, 8, T], F32),
    "cu_ext": ([128, 8, T + 2], BF16), "bg_in": ([128, 8, T], BF16), "selL": ([128, 4], F32), "selR": ([128, 4], F32),
    "ffn_w_gate": ([2, D, FF], F32), "ffn_w_up": ([2, D, FF], F32), "ffn_w_down": ([2, FF, D], F32),
}


class Ctx:
    pass


def declare_inputs(nc, names):
    C = Ctx()
    C.inp = {}
    for n in names:
        shp, dt = INPUT_SPECS[n]
        C.inp[n] = nc.dram_tensor(n, shp, dt, kind="ExternalInput").ap()
    return C


def load_const(S, C, name, q="sp"):
    shp, dt = INPUT_SPECS[name]
    t = SBT(S, "c_" + name, shp, dt)
    S.dma(q, t[:], C.inp[name], writes=[t.d])
    return t


def load_weight_bf16(S, C, dst, dst_cols, src_ap, ncols, stage_ring, eng="pool"):
    for c0 in range(0, ncols, 256):
        n = min(256, ncols - c0)
        st = stage_ring.next()
        S.dma("sp", st[:, :, :n], src_ap[:, c0:c0 + n].rearrange("(kc p) n -> p kc n", p=128), writes=[st.d])
        S.op(eng, lambda e: e.tensor_copy(out=dst[:, :, dst_cols + c0:dst_cols + c0 + n], in_=st[:, :, :n]),
             reads=[st.d], writes=[dst.d])


def phase0(S, C):
    nc = S.nc
    cv = SBT(S, "cv", [128, 8, 2], F32)
    S.dma("sp", cv[:], C.inp["cvec"], writes=[cv.d])
    sc = SBT(S, "silu_c", [128, 8, 2], F32)
    S.op("act", lambda e: e.activation(out=sc[:], in_=cv[:], func=AF.Silu), reads=[cv.d], writes=[sc.d])
    adab = load_const(S, C, "ada_bT")
    gains = load_const(S, C, "gains")
    C.gains = gains
    C.mod = [SBT(S, "mod%d" % l, [128, 48, 2], F32) for l in range(2)]
    with ExitStack() as ph:
        old = S.stack
        S.stack = ph
        stage = Ring(S, "adast", [128, 8, 1536], F32, 2)
        psm = SBT(S, "ps_mod", [128, 48, 2], F32, psum=True)
        for l in range(2):
            for qq in range(4):
                st = stage.next()
                S.dma("sp", st[:], C.inp["ada_w"][l, :, qq * 1536:(qq + 1) * 1536].rearrange("(kc p) n -> p kc n", p=128),
                      writes=[st.d])
                for jj in range(12):
                    j = qq * 12 + jj
                    for kc in range(8):
                        S.op("pe", lambda e: e.matmul(psm[:, j, :], lhsT=st[:, kc, jj * 128:(jj + 1) * 128], rhs=sc[:, kc, :],
                                                      start=(kc == 0), stop=(kc == 7)),
                             reads=[st.d, sc.d], writes=[psm.d], sig=(kc == 7))
            mod = C.mod[l]
            S.op("dve", lambda e: e.tensor_tensor(out=mod[:], in0=psm[:], in1=adab[:, l, :, None].to_broadcast([128, 48, 2]),
                                                  op=ALU.add), reads=[psm.d, adab.d], writes=[mod.d])
        S.barrier()
        S.stack = old
    C.vec = {}
    for l in range(2):
        mod = C.mod[l]
        for (nm, gi, mi) in (("a1", 0, 1), ("a2", 2, 4)):
            for col, sfx in ((0, ""), (1, "c")):
                t = SBT(S, "%s%s_%d" % (nm, sfx, l), [128, 8], F32)
                S.op("dve", lambda e: e.scalar_tensor_tensor(out=t[:], in0=mod[:, mi * 8:(mi + 1) * 8, col], scalar=1.0,
                                                             in1=gains[:, gi, l, :], op0=ALU.add, op1=ALU.mult),
                     reads=[mod.d, gains.d], writes=[t.d])
                C.vec[(nm + sfx, l)] = t
        for (nm, gi, mi) in (("gg1", 1, 2), ("gg2", 3, 5)):
            t = SBT(S, "%s_%d" % (nm, l), [128, 8], F32)
            S.op("dve", lambda e: e.tensor_tensor(out=t[:], in0=mod[:, mi * 8:(mi + 1) * 8, 0], in1=gains[:, gi, l, :],
                                                  op=ALU.mult), reads=[mod.d, gains.d], writes=[t.d])
            C.vec[(nm, l)] = t
        for (nm, mi) in (("b1", 0), ("b2", 3)):
            for col, sfx in ((0, ""), (1, "c")):
                t = SBT(S, "%s%s_%d" % (nm, sfx, l), [128, 8], F32)
                S.op("dve", lambda e: e.tensor_copy(out=t[:], in_=mod[:, mi * 8:(mi + 1) * 8, col]),
                     reads=[mod.d], writes=[t.d])
                C.vec[(nm + sfx, l)] = t


def norm_mod_block(S, C, xb, ntok, a, b, hT, R):
    sq = R["sq"].next()
    for c in range(8):
        S.op("dve", lambda e: e.tensor_tensor(out=sq[:, c, :ntok], in0=xb[:, c, :ntok], in1=xb[:, c, :ntok], op=ALU.mult),
             reads=[xb.d], writes=[sq.d])
    psn = R["ps"].next()
    for c in range(8):
        S.op("pe", lambda e: e.matmul(psn[:, :ntok], lhsT=C.ones_bf[:], rhs=sq[:, c, :ntok], start=(c == 0), stop=(c == 7)),
             reads=[C.ones_bf.d, sq.d], writes=[psn.d], sig=(c == 7))
    rstd = R["rstd"].next()
    S.op("act", lambda e: e.activation(out=rstd[:, :ntok], in_=psn[:, :ntok], func=AF.Ln, bias=C.eps_t[:, 0:1], scale=1.0 / D),
         reads=[psn.d, C.eps_t.d], writes=[rstd.d])
    S.op("act", lambda e: e.activation(out=rstd[:, :ntok], in_=rstd[:, :ntok], func=AF.Exp, scale=-0.5),
         reads=[rstd.d], writes=[rstd.d])
    for c in range(8):
        tmp = R["tmp"].next()
        S.op("dve", lambda e: e.scalar_tensor_tensor(out=tmp[:, :ntok], in0=xb[:, c, :ntok], scalar=a[:, c:c + 1],
                                                     in1=rstd[:, :ntok], op0=ALU.mult, op1=ALU.mult),
             reads=[xb.d, a.d, rstd.d], writes=[tmp.d])
        S.op("act", lambda e: e.activation(out=hT[:, c, :ntok], in_=tmp[:, :ntok], func=AF.Identity, bias=b[:, c:c + 1], scale=1.0),
             reads=[tmp.d, b.d], writes=[hT.d])
    return rstd


def setup_common(S, C):
    C.ones_bf = load_const(S, C, "ones_bf")
    C.ident_bf = load_const(S, C, "ident_bf")
    C.eps_t = SBT(S, "eps_t", [128, 1], F32)
    S.op("pool", lambda e: e.memset(C.eps_t[:], EPS), writes=[C.eps_t.d])


def phase1(S, C, dbg=None):
    nc = S.nc
    l = 0
    C.kT = SBT(S, "kT", [128, EXT], BF16)
    C.V = SBT(S, "V", [128, 36, 128], BF16)
    C.ab = SBT(S, "ab", [128, 36, 16], F32)
    with ExitStack() as ph:
        old = S.stack
        S.stack = ph
        swap = load_const(S, C, "swap_bf")
        hv = load_const(S, C, "hv")
        wq = SBT(S, "wq", [128, 8, 512], BF16)
        wk = SBT(S, "wk", [128, 8, 128], BF16)
        wvab = SBT(S, "wvab", [128, 8, 144], BF16)
        wg = SBT(S, "wg", [128, 8, 1536], BF16)
        wz = SBT(S, "wz", [128, 8, 512], BF16)
        stage = Ring(S, "wst", [128, 8, 256], F32, 2)
        load_weight_bf16(S, C, wk, 0, C.inp["wk"], 128, stage)
        load_weight_bf16(S, C, wvab, 0, C.inp["wvab"], 144, stage)
        load_weight_bf16(S, C, wq, 0, C.inp["wq"], 512, stage)
        load_weight_bf16(S, C, wg, 0, C.inp["wg"], 1536, stage)
        load_weight_bf16(S, C, wz, 0, C.inp["wz"], 512, stage)
        R = {"sq": Ring(S, "sq", [128, 8, 512], BF16, 1), "ps": Ring(S, "psn", [128, 512], F32, 1, psum=True),
             "rstd": Ring(S, "rstd", [128, 512], F32, 2), "tmp": Ring(S, "ntmp", [128, 512], F32, 3)}
        xring = Ring(S, "xblk", [128, 8, 512], F32, 2)
        hring = Ring(S, "hT", [128, 8, 512], BF16, 2)
        psr = Ring(S, "psp", [128, 512], F32, 4, psum=True)
        psrot = Ring(S, "psrot", [128, 512], F32, 1, psum=True)
        pst = Ring(S, "pst", [128, 512], F32, 2, psum=True)
        cosr = Ring(S, "cosb", [128, 512], F32, 2)
        sinr = Ring(S, "sinb", [128, 512], F32, 2)
        t1r = Ring(S, "t1", [128, 512], F32, 2)
        q32r = Ring(S, "q32", [128, 512], F32, 2)
        t2r = Ring(S, "t2", [128, 512], F32, 2)
        qsr = Ring(S, "qs", [128, 512], BF16, 2)
        qblk = Ring(S, "qblk", [128, 4, 512], BF16, 2)
        preblk = Ring(S, "preblk", [128, 12, 512], BF16, 1)
        zblk = Ring(S, "zblk", [128, 4, 512], BF16, 1)
        blocks = [("ctx", 0), ("halo", 0)] + [("own", i) for i in range(T // 512)]
        import os
        CUT = int(os.environ.get("CUT", "99"))
        if CUT < 99:
            blocks = blocks[:1]
        for kind, bi in blocks:
            ntok = 512 if kind == "own" else 256
            if kind == "own":
                src = C.inp["xT"][:, :, bi * 512:(bi + 1) * 512]
                col0 = bi * 512
                a, b = C.vec[("a1", l)], C.vec[("b1", l)]
            elif kind == "halo":
                src = C.inp["xhT"]
                col0 = COL_HL
                a, b = C.vec[("a1", l)], C.vec[("b1", l)]
            else:
                src = C.inp["ctxT"]
                col0 = COL_CTX
                a, b = C.vec[("a1c", l)], C.vec[("b1c", l)]
            xb = xring.next()
            S.dma("sp", xb[:, :, :ntok], src.rearrange("c p t -> p c t"), writes=[xb.d])
            hT = hring.next()
            norm_mod_block(S, C, xb, ntok, a, b, hT, R)
            if CUT <= 1:
                continue
            if dbg is not None and kind == "own" and bi == 0 and "h" in dbg:
                S.dma("sp", dbg["h"].ap, hT[:], reads=[hT.d], writes=dbg["h"].deps, own=hT.d)
            rope = kind != "ctx"
            if rope:
                cb = cosr.next()
                sb = sinr.next()
                S.dma("sp", cb[:, :ntok], C.inp["cosT"][:, col0:col0 + ntok], writes=[cb.d])
                S.dma("sp", sb[:, :ntok], C.inp["sinT"][:, col0:col0 + ntok], writes=[sb.d])

            def proj(w, j):
                ps = psr.next()
                for kc in range(8):
                    S.op("pe", lambda e: e.matmul(ps[:, :ntok], lhsT=w[:, kc, j * 128:(j + 1) * 128], rhs=hT[:, kc, :ntok],
                                                  start=(kc == 0), stop=(kc == 7)),
                         reads=[w.d, hT.d], writes=[ps.d], sig=(kc == 7))
                return ps

            def rope_evac(ps, out_ap, out_dep):
                q32 = q32r.next()
                S.op("act", lambda e: e.copy(out=q32[:, :ntok], in_=ps[:, :ntok]), reads=[ps.d], writes=[q32.d])
                t1 = t1r.next()
                S.op("dve", lambda e: e.tensor_tensor(out=t1[:, :ntok], in0=q32[:, :ntok], in1=cb[:, :ntok], op=ALU.mult),
                     reads=[q32.d, cb.d], writes=[t1.d])
                qs = qsr.next()
                S.op("dve", lambda e: e.tensor_copy(out=qs[:, :ntok], in_=q32[:, :ntok]), reads=[q32.d], writes=[qs.d])
                pr = psrot.next()
                S.op("pe", lambda e: e.matmul(pr[:, :ntok], lhsT=swap[:], rhs=qs[:, :ntok], start=True, stop=True),
                     reads=[swap.d, qs.d], writes=[pr.d])
                t2 = t2r.next()
                S.op("dve", lambda e: e.tensor_tensor(out=t2[:, :ntok], in0=pr[:, :ntok], in1=sb[:, :ntok], op=ALU.mult),
                     reads=[pr.d, sb.d], writes=[t2.d])
                S.op("dve", lambda e: e.tensor_tensor(out=out_ap, in0=t1[:, :ntok], in1=t2[:, :ntok], op=ALU.add),
                     reads=[t1.d, t2.d], writes=[out_dep])

            ps = proj(wk, 0)
            if rope:
                rope_evac(ps, C.kT[:, col0:col0 + ntok], C.kT.d)
            else:
                S.op("act", lambda e: e.copy(out=C.kT[:, col0:col0 + ntok], in_=ps[:, :ntok]), reads=[ps.d], writes=[C.kT.d])
            if CUT <= 2:
                continue
            if kind == "own":
                qb = qblk.next()
                for j in range(4):
                    ps = proj(wq, j)
                    rope_evac(ps, qb[:, j, :], qb.d)
                S.dma("pool", C.q_d.ap[:, :, col0:col0 + 512], qb[:], reads=[qb.d], writes=C.q_d.dd(col0, col0 + 512), own=qb.d)
                zb = zblk.next()
                for j in range(4):
                    ps = proj(wz, j)
                    S.op("act", lambda e: e.activation(out=zb[:, j, :], in_=ps[:, :], func=AF.Silu), reads=[ps.d], writes=[zb.d])
                S.dma("pool", C.zs_d.ap[:, :, col0:col0 + 512], zb[:], reads=[zb.d], writes=C.zs_d.dd(col0, col0 + 512), own=zb.d)
            pb = preblk.next()
            for j in range(12):
                ps = proj(wg, j)
                if kind == "halo":
                    for hh in range(2):
                        S.op("dve", lambda e: e.tensor_scalar(out=pb[:, j, hh * 128:(hh + 1) * 128], in0=ps[:, hh * 128:(hh + 1) * 128],
                                                              scalar1=hv[:, hh:hh + 1], scalar2=None, op0=ALU.mult),
                             reads=[ps.d, hv.d], writes=[pb.d])
                else:
                    eng = "act" if j % 2 else "dve"
                    if eng == "act":
                        S.op("act", lambda e: e.copy(out=pb[:, j, :ntok], in_=ps[:, :ntok]), reads=[ps.d], writes=[pb.d])
                    else:
                        S.op("dve", lambda e: e.tensor_copy(out=pb[:, j, :ntok], in_=ps[:, :ntok]), reads=[ps.d], writes=[pb.d])
            if kind == "own":
                S.dma("pool", C.pre_d.ap[:, :, 128 + col0:128 + col0 + 512], pb[:], reads=[pb.d], writes=C.pre_d.dd(128 + col0, 128 + col0 + 512), own=pb.d)
            elif kind == "halo":
                S.dma("pool", C.pre_d.ap[:, :, 0:128], pb[:, :, 0:128], reads=[pb.d], writes=C.pre_d.dd(0, 128), own=pb.d)
                S.dma("pool", C.pre_d.ap[:, :, 128 + T:128 + T + 128], pb[:, :, 128:256], reads=[pb.d], writes=C.pre_d.dd(128 + T, 256 + T), own=pb.d)
            else:
                S.dma("pool", C.prec_d.ap[:, :, 1:257], pb[:, :, 0:256], reads=[pb.d], writes=C.prec_d.deps, own=pb.d)
            if CUT <= 3:
                continue
            for tt in range(ntok // 128):
                if kind == "own":
                    ti = bi * 4 + tt
                elif kind == "halo":
                    ti = 32 + tt
                else:
                    ti = 34 + tt
                pt = pst.next()
                for kc in range(8):
                    S.op("pe", lambda e: e.matmul(pt[:, 0:144], lhsT=hT[:, kc, tt * 128:(tt + 1) * 128], rhs=wvab[:, kc, :],
                                                  start=(kc == 0), stop=(kc == 7)),
                         reads=[wvab.d, hT.d], writes=[pt.d], sig=(kc == 7))
                VAR = os.environ.get("VAR", "")
                if "a" not in VAR:
                    S.op("act", lambda e: e.copy(out=C.V[:, ti, :], in_=pt[:, 0:128]), reads=[pt.d], writes=[C.V.d])
                if "b" not in VAR:
                    if True:
                        S.op("act", lambda e: e.copy(out=C.ab[:, ti, :], in_=pt[:, 128:144]), reads=[pt.d], writes=[C.ab.d])
                    else:
                        S.op("dve", lambda e: e.tensor_copy(out=C.ab[:, ti, :], in_=pt[:, 128:144]), reads=[pt.d] + ([C.V.d] if "s" in VAR else []), writes=[C.ab.d])
        S.barrier()
        S.stack = old


def phase2(S, C):
    with ExitStack() as ph:
        old = S.stack
        S.stack = ph
        amask = load_const(S, C, "amask")
        sinkrow = load_const(S, C, "sinkrow")
        esink = SBT(S, "esink", [64, 2, 512], F32)
        S.op("act", lambda e: e.activation(out=esink[:], in_=sinkrow[:], func=AF.Exp), reads=[sinkrow.d], writes=[esink.d])
        qring = Ring(S, "qld", [128, 4, 512], BF16, 2)
        pss = Ring(S, "pss", [128, 512], F32, 4, psum=True)
        pso = Ring(S, "pso", [64, 512], F32, 2, psum=True)
        psd = Ring(S, "psd", [64, 512], F32, 2, psum=True)
        ptr = Ring(S, "pT", [128, 512], BF16, 5)
        denr = Ring(S, "den", [64, 512], F32, 2)
        outr = Ring(S, "aout", [64, 2, 4, 512], BF16, 2)
        def kcols(ti):
            if ti < 32:
                return ti * 128
            return T + (ti - 32) * 128
        tasks = []
        for qblk_i in range(T // 512):
            for sub in range(4):
                qb = qblk_i * 4 + sub
                left = (32, 2) if qb == 0 else (qb - 1, 0)
                right = (33, 3) if qb == NT - 1 else (qb + 1, 1)
                keyblocks = [left, (qb, None), right, (34, None), (35, None)]
                for kap in range(2):
                    for n, (kt, mi) in enumerate(keyblocks):
                        tasks.append((qblk_i, sub, kap, n, kt, mi))
        state = {"qt": {}, "pT": {}, "po": None, "pd": None, "ot": {}}

        def emit_S(t):
            qblk_i, sub, kap, n, kt, mi = tasks[t]
            if qblk_i not in state["qt"]:
                qt = qring.next()
                S.dma("sp", qt[:], C.q_d.ap[:, :, qblk_i * 512:(qblk_i + 1) * 512], reads=C.q_d.dd(qblk_i * 512, qblk_i * 512 + 512), writes=[qt.d])
                state["qt"][qblk_i] = qt
            qt = state["qt"][qblk_i]
            ps = pss.next()
            kc0 = kcols(kt)
            S.op("pe", lambda e: e.matmul(ps[:].rearrange("p (g n) -> p g n", g=4),
                                          lhsT=C.kT[64 * kap:64 * kap + 64, kc0:kc0 + 128],
                                          rhs=qt[64 * kap:64 * kap + 64, :, sub * 128:(sub + 1) * 128],
                                          start=True, stop=True),
                 reads=[C.kT.d, qt.d], writes=[ps.d])
            pT = ptr.next()
            S.op("act", lambda e: e.activation(out=pT[:], in_=ps[:], func=AF.Exp, scale=0.125), reads=[ps.d], writes=[pT.d])
            if mi is not None:
                S.op("dve", lambda e: e.tensor_tensor(out=pT[:], in0=pT[:], in1=amask[:, mi, :], op=ALU.mult),
                     reads=[pT.d, amask.d], writes=[pT.d])
            state["pT"][t] = pT

        def emit_PV(t):
            qblk_i, sub, kap, n, kt, mi = tasks[t]
            pT = state["pT"].pop(t)
            if n == 0:
                state["po"], state["pd"] = pso.next(), psd.next()
            po, pd = state["po"], state["pd"]
            if qblk_i not in state["ot"]:
                state["ot"][qblk_i] = outr.next()
            ot = state["ot"][qblk_i]
            S.op("pe", lambda e: e.matmul(po[:], lhsT=C.V[:, kt, 64 * kap:64 * kap + 64], rhs=pT[:], start=(n == 0), stop=(n == 4)),
                 reads=[C.V.d, pT.d], writes=[po.d], sig=(n == 4))
            S.op("pe", lambda e: e.matmul(pd[:], lhsT=C.ones_bf[:, 0:64], rhs=pT[:], start=(n == 0), stop=(n == 4)),
                 reads=[C.ones_bf.d, pT.d], writes=[pd.d], sig=(n == 4))
            if n == 4:
                den = denr.next()
                S.op("dve", lambda e: e.tensor_tensor(out=den[:], in0=pd[:], in1=esink[:, kap, :], op=ALU.add),
                     reads=[pd.d, esink.d], writes=[den.d])
                S.op("dve", lambda e: e.reciprocal(out=den[:], in_=den[:]), reads=[den.d], writes=[den.d])
                S.op("dve", lambda e: e.tensor_tensor(out=ot[:, kap, :, sub * 128:(sub + 1) * 128],
                                                      in0=po[:].rearrange("p (g n) -> p g n", g=4),
                                                      in1=den[:].rearrange("p (g n) -> p g n", g=4), op=ALU.mult),
                     reads=[po.d, den.d], writes=[ot.d])
                if sub == 3 and kap == 1:
                    S.dma("pool", C.attn_d.ap[:, :, qblk_i * 512:(qblk_i + 1) * 512], ot[:].rearrange("p k g n -> p (k g) n"),
                          reads=[ot.d], writes=C.attn_d.dd(qblk_i * 512, qblk_i * 512 + 512), own=ot.d)
        LOOK = 2
        for t in range(min(LOOK, len(tasks))):
            emit_S(t)
        for t in range(len(tasks)):
            if t + LOOK < len(tasks):
                emit_S(t + LOOK)
            emit_PV(t)
        S.barrier()
        S.stack = old


class Slot:
    def __init__(self, ap, d):
        self.ap = ap
        self.d = d

    def __getitem__(self, idx):
        return self.ap[idx]


class SlotRing:
    def __init__(self, S, name, dtype, nslots, psum=False):
        per = 4 if dtype == F32 else 8
        if not psum:
            per = 1
        self.slots = []
        for b in range((nslots + per - 1) // per):
            t = SBT(S, "%s%d" % (name, b), [128, per, 128], dtype, psum=psum, ndeps=per)
            for i in range(per):
                self.slots.append(Slot(t[:, i, :], t.deps[0] if psum else t.deps[i]))
        self.slots = self.slots[:nslots]
        self.i = 0

    def next(self):
        s = self.slots[self.i % len(self.slots)]
        self.i += 1
        return s


class PairRing:
    def __init__(self, S, name, dtype, n):
        t = SBT(S, name, [128, n, 2, 128], dtype, psum=True)
        self.slots = [Slot(t[:, i, :, :], t.d) for i in range(n)]
        self.i = 0

    def next(self):
        s = self.slots[self.i % len(self.slots)]
        self.i += 1
        return s


def gdn_gates(S, C, G):
    ab = C.ab
    dtb = load_const(S, C, "gdtb")
    alog = load_const(S, C, "galog")
    cmat = load_const(S, C, "cmat")
    G.one_t = SBT(S, "one_t", [128, 1], F32)
    S.op("pool", lambda e: e.memset(G.one_t[:], 1.0), writes=[G.one_t.d])
    eal = SBT(S, "eal", [128, 8], F32)
    S.op("act", lambda e: e.activation(out=eal[:], in_=alog[:], func=AF.Exp), reads=[alog.d], writes=[eal.d])
    xg = SBT(S, "xg", [128, 36, 8], F32)
    S.op("dve", lambda e: e.tensor_tensor(out=xg[:], in0=ab[:, :, 0:8], in1=dtb[:, None, :].to_broadcast([128, 36, 8]), op=ALU.add),
         reads=[ab.d, dtb.d], writes=[xg.d])
    S.op("act", lambda e: e.activation(out=xg[:], in_=xg[:], func=AF.Exp), reads=[xg.d], writes=[xg.d])
    S.op("act", lambda e: e.activation(out=xg[:], in_=xg[:], func=AF.Ln, bias=G.one_t[:, 0:1], scale=1.0),
         reads=[xg.d, G.one_t.d], writes=[xg.d])
    g = SBT(S, "gg", [128, 36, 8], F32)
    S.op("dve", lambda e: e.scalar_tensor_tensor(out=g[:], in0=xg[:], scalar=-1.0, in1=eal[:, None, :].to_broadcast([128, 36, 8]),
                                                 op0=ALU.mult, op1=ALU.mult), reads=[xg.d, eal.d], writes=[g.d])
    G.beta = SBT(S, "beta", [128, 36, 8], F32)
    S.op("act", lambda e: e.activation(out=G.beta[:], in_=ab[:, :, 8:16], func=AF.Sigmoid), reads=[ab.d], writes=[G.beta.d])
    G.GT = SBT(S, "GT", [128, 36, 16], F32)
    _pst = ExitStack()
    _old = S.stack
    S.stack = _pst
    psg = Ring(S, "psg", [128, 18, 16], F32, 1, psum=True)
    S.stack = _old
    for half in range(2):
        pg = psg.next()
        for tl in range(18):
            ti = half * 18 + tl
            for m in range(4):
                dirn = m // 2
                S.op("pe", lambda e: e.matmul(pg[:, tl, m * 4:(m + 1) * 4], lhsT=cmat[:, m, :], rhs=g[:, ti, dirn * 4:(dirn + 1) * 4],
                                              start=True, stop=True),
                     reads=[cmat.d, g.d], writes=[pg.d], sig=(tl == 17 and m == 3))
        S.op("act", lambda e: e.copy(out=G.GT[:, half * 18:(half + 1) * 18, :], in_=pg[:]), reads=[pg.d], writes=[G.GT.d])
    S.barrier()
    _pst.close()
    G.negG = SBT(S, "negG", [128, 36, 16], F32)
    S.op("dve", lambda e: e.tensor_scalar(out=G.negG[:], in0=G.GT[:], scalar1=-1.0, scalar2=None, op0=ALU.mult),
         reads=[G.GT.d], writes=[G.negG.d])
    G.eGT = SBT(S, "eGT", [128, 36, 16], F32)
    S.op("act", lambda e: e.activation(out=G.eGT[:], in_=G.GT[:], func=AF.Exp), reads=[G.GT.d], writes=[G.eGT.d])
    G.bEG = SBT(S, "bEG", [128, 36, 8], F32)
    for dirn in range(2):
        S.op("dve", lambda e: e.tensor_tensor(out=G.bEG[:, :, dirn * 4:(dirn + 1) * 4], in0=G.beta[:, :, dirn * 4:(dirn + 1) * 4],
                                              in1=G.eGT[:, :, dirn * 8:dirn * 8 + 4], op=ALU.mult),
             reads=[G.beta.d, G.eGT.d], writes=[G.bEG.d])


def gdn_setup(S, C, G):
    G.ident_f = load_const(S, C, "ident_f")
    G.ones_f = load_const(S, C, "ones_f")
    G.gmask = load_const(S, C, "gmask")
    G.lvl = load_const(S, C, "lvlmask")
    gconv = load_const(S, C, "gconv")
    G.diag = SBT(S, "diag", [128, 36, 128], BF16)
    for c in range(12):
        for k in range(3):
            S.op("dve" if (c + k) % 2 else "pool",
                 lambda e: e.tensor_scalar(out=G.diag[:, c * 3 + k, :], in0=C.ident_bf[:], scalar1=gconv[:, c, k:k + 1], scalar2=None,
                                           op0=ALU.mult), reads=[C.ident_bf.d, gconv.d], writes=[G.diag.d])
    G.psC = SlotRing(S, "psC", F32, 8, psum=True)
    G.psFs = [SlotRing(S, "psF%d" % i, F32, 4, psum=True) for i in range(4)]
    G.psBs = [SlotRing(S, "psB%d" % i, BF16, 8, psum=True) for i in range(2)]
    G.psF = G.psFs[0]
    G.S32 = SBT(S, "S32", [128, 8, 128], F32, ndeps=8)
    G.P32 = SBT(S, "P32", [128, 8, 128], F32, ndeps=8)
    G.Sbf = SBT(S, "Sbf", [128, 8, 128], BF16, ndeps=8)
    G.Pbf = SBT(S, "Pbf", [128, 8, 128], BF16, ndeps=8)


def gdn_pass(S, C, G, dirn, heads, tiles, with_P, with_out, pre_src, tag, psF, psB):
    f32r = lambda nm, n=2: SlotRing(S, tag + nm, F32, n)
    bfr = lambda nm, n=2: SlotRing(S, tag + nm, BF16, n)
    prer = Ring(S, tag + "prew", [128, 12, 130], BF16, 1)
    csr = Ring(S, tag + "cs", [128, 128], F32, 2)
    sqr = Ring(S, tag + "sq", [128, 128], BF16, 2)
    rnr = Ring(S, tag + "rn", [128, 128], F32, 2)
    qkb = Ring(S, tag + "qkb", [128, 8, 128], BF16, 1)
    vbk = Ring(S, tag + "vbk", [128, 4, 128], BF16, 1)
    oblk = Ring(S, tag + "oblk", [128, 4, 128], F32, 1)
    rblk = Ring(S, tag + "rblk", [128, 4, 128], BF16, 1)
    r_ng, r_gs, r_dm, r_e, r_eg, r_ta, r_tq = (f32r(n) for n in ("ng", "gs", "dm", "e", "eg", "ta", "tq"))
    r_q = bfr("Q", 2)
    r_p = bfr("P", 2)
    r_m = bfr("M", 5)
    r_y = bfr("Y", 4)
    r_a, r_b, r_qk, r_qkT, r_kbg, r_kt, r_vb, r_qg, r_w, r_vn, r_vnP = (bfr(n) for n in
                                                                      ("A", "B", "qk", "qkT", "kbg", "kt", "vb", "qg", "w", "vn", "vnP"))
    gi = dirn * 8
    mi = dirn * 2
    last_col = 127 if dirn == 0 else 0
    hlo, hhi = min(heads), max(heads) + 1
    for bi in range(len(tiles)):
        btiles = tiles[bi:bi + 1]
        lo = btiles[0][1]
        ntok = 128
        src_ap, src_deps = pre_src(lo, ntok)
        pw = prer.next()
        S.dma("sp", pw[:, :, :ntok + 2], src_ap, reads=src_deps, writes=[pw.d])
        qk = qkb.next()
        vb_ = vbk.next()
        for c in [h for h in heads] + [4 + h for h in heads] + [8 + h for h in heads]:
            pc = G.psC.next()
            for k in range(3):
                S.op("pe", lambda e: e.matmul(pc[:, :ntok], lhsT=G.diag[:, c * 3 + k, :], rhs=pw[:, c, k:k + ntok],
                                              start=(k == 0), stop=(k == 2)),
                     reads=[G.diag.d, pw.d], writes=[pc.d], sig=(k == 2))
            if c >= 8:
                S.op("act", lambda e: e.activation(out=vb_[:, c - 8, :ntok], in_=pc[:, :ntok], func=AF.Silu), reads=[pc.d], writes=[vb_.d])
                yield
                continue
            cs = csr.next()
            S.op("act", lambda e: e.activation(out=cs[:, :ntok], in_=pc[:, :ntok], func=AF.Silu), reads=[pc.d], writes=[cs.d])
            sq = sqr.next()
            S.op("dve", lambda e: e.tensor_tensor(out=sq[:, :ntok], in0=cs[:, :ntok], in1=cs[:, :ntok], op=ALU.mult),
                 reads=[cs.d], writes=[sq.d])
            yield
            pn = G.psC.next()
            S.op("pe", lambda e: e.matmul(pn[:, :ntok], lhsT=C.ones_bf[:], rhs=sq[:, :ntok], start=True, stop=True),
                 reads=[C.ones_bf.d, sq.d], writes=[pn.d])
            rn = rnr.next()
            S.op("act", lambda e: e.activation(out=rn[:, :ntok], in_=pn[:, :ntok], func=AF.Ln, bias=C.eps_t[:, 0:1], scale=1.0),
                 reads=[pn.d, C.eps_t.d], writes=[rn.d])
            S.op("act", lambda e: e.activation(out=rn[:, :ntok], in_=rn[:, :ntok], func=AF.Exp, scale=-0.5), reads=[rn.d], writes=[rn.d])
            scl = (128.0 ** -0.5) if c < 4 else 1.0
            S.op("dve", lambda e: e.scalar_tensor_tensor(out=qk[:, c, :ntok], in0=cs[:, :ntok], scalar=scl, in1=rn[:, :ntok],
                                                         op0=ALU.mult, op1=ALU.mult), reads=[cs.d, rn.d], writes=[qk.d])
            yield
        ob = oblk.next() if with_out else None
        rb = rblk.next() if with_out else None
        for ti, col in btiles:
            c0 = col - lo
            cols = slice(c0, c0 + 128)
            for h in heads:
                sidx = dirn * 4 + h
                kT_t = qk[:, 4 + h, cols]
                qT_t = qk[:, h, cols]
                vT_t = vb_[:, h, cols]
                Gcol = G.GT[:, ti, gi + h:gi + h + 1]
                nGcol = G.negG[:, ti, gi + h:gi + h + 1]
                betac = G.beta[:, ti, dirn * 4 + h:dirn * 4 + h + 1]
                bEGc = G.bEG[:, ti, dirn * 4 + h:dirn * 4 + h + 1]
                etailc = G.eGT[:, ti, gi + 4 + h:gi + 4 + h + 1]
                kk = psF.next()
                S.op("pe", lambda e: e.matmul(kk[:], lhsT=kT_t, rhs=kT_t, start=True, stop=True), reads=[qk.d], writes=[kk.d])
                yield
                qkp = psF.next()
                S.op("pe", lambda e: e.matmul(qkp[:], lhsT=qT_t, rhs=kT_t, start=True, stop=True), reads=[qk.d], writes=[qkp.d])
                yield
                ktok = psB.next()
                S.op("pe", lambda e: e.transpose(ktok[:], kT_t, C.ident_bf[:]), reads=[qk.d, C.ident_bf.d], writes=[ktok.d])
                yield
                vtok = psB.next()
                S.op("pe", lambda e: e.transpose(vtok[:], vT_t, C.ident_bf[:]), reads=[vb_.d, C.ident_bf.d], writes=[vtok.d])
                ng = r_ng.next()
                S.op("dve", lambda e: e.tensor_scalar(out=ng[:], in0=G.ones_f[:], scalar1=nGcol, scalar2=None, op0=ALU.mult),
                     reads=[G.ones_f.d, G.negG.d], writes=[ng.d])
                yield
                gbc = psF.next()
                S.op("pe", lambda e: e.matmul(gbc[:], lhsT=ng[:], rhs=G.ident_f[:], start=True, stop=True),
                     reads=[ng.d, G.ident_f.d], writes=[gbc.d])
                gs = r_gs.next()
                S.op("dve", lambda e: e.tensor_copy(out=gs[:], in_=gbc[:]), reads=[gbc.d], writes=[gs.d])
                dm = r_dm.next()
                S.op("dve", lambda e: e.tensor_scalar(out=dm[:], in0=gs[:], scalar1=Gcol, scalar2=0.0, op0=ALU.add, op1=ALU.min),
                     reads=[gs.d, G.GT.d], writes=[dm.d])
                E = r_e.next()
                S.op("act", lambda e: e.activation(out=E[:], in_=dm[:], func=AF.Exp), reads=[dm.d], writes=[E.d])
                eg = r_eg.next()
                S.op("act", lambda e: e.activation(out=eg[:], in_=gs[:], func=AF.Exp, scale=-1.0), reads=[gs.d], writes=[eg.d])
                ta = r_ta.next()
                S.op("dve", lambda e: e.tensor_tensor(out=ta[:], in0=kk[:], in1=E[:], op=ALU.mult), reads=[kk.d, E.d], writes=[ta.d])
                A = r_q.next()
                S.op("dve", lambda e: e.scalar_tensor_tensor(out=A[:], in0=ta[:], scalar=betac, in1=G.gmask[:, mi + 1, :],
                                                             op0=ALU.mult, op1=ALU.mult),
                     reads=[ta.d, G.beta.d, G.gmask.d], writes=[A.d])
                tq = r_tq.next()
                S.op("dve", lambda e: e.tensor_tensor(out=tq[:], in0=qkp[:], in1=E[:], op=ALU.mult), reads=[qkp.d, E.d], writes=[tq.d])
                qkm = r_qk.next()
                S.op("dve", lambda e: e.tensor_tensor(out=qkm[:], in0=tq[:], in1=G.gmask[:, mi, :], op=ALU.mult),
                     reads=[tq.d, G.gmask.d], writes=[qkm.d])
                yield
                bps = psB.next()
                S.op("pe", lambda e: e.transpose(bps[:], A[:], C.ident_bf[:]), reads=[A.d, C.ident_bf.d], writes=[bps.d])
                Bm = r_p.next()
                S.op("act", lambda e: e.copy(out=Bm[:], in_=bps[:]), reads=[bps.d], writes=[Bm.d])
                yield
                qps = psB.next()
                S.op("pe", lambda e: e.transpose(qps[:], qkm[:], C.ident_bf[:]), reads=[qkm.d, C.ident_bf.d], writes=[qps.d])
                qkT = r_qkT.next()
                S.op("act", lambda e: e.copy(out=qkT[:], in_=qps[:]), reads=[qps.d], writes=[qkT.d])
                kbg = r_kbg.next()
                S.op("act", lambda e: e.activation(out=kbg[:], in_=ktok[:], func=AF.Identity, scale=bEGc), reads=[ktok.d, G.bEG.d], writes=[kbg.d])
                kt = r_kt.next()
                S.op("act", lambda e: e.activation(out=kt[:], in_=ktok[:], func=AF.Identity, scale=etailc), reads=[ktok.d, G.eGT.d], writes=[kt.d])
                vbt = r_vb.next()
                S.op("act", lambda e: e.activation(out=vbt[:], in_=vtok[:], func=AF.Identity, scale=betac), reads=[vtok.d, G.beta.d], writes=[vbt.d])
                qg = r_qg.next()
                S.op("dve", lambda e: e.tensor_tensor(out=qg[:], in0=qT_t, in1=eg[:], op=ALU.mult), reads=[qk.d, eg.d], writes=[qg.d])
                lo_i, up_i = (0, 1) if dirn == 0 else (1, 0)
                Tt = r_m.next()
                Xt = r_m.next()
                tmp1 = r_y.next()
                S.op("dve", lambda e: e.tensor_tensor(out=tmp1[:], in0=A[:], in1=G.lvl[:, lo_i, :], op=ALU.mult),
                     reads=[A.d, G.lvl.d], writes=[tmp1.d])
                S.op("dve", lambda e: e.tensor_tensor(out=Tt[:], in0=C.ident_bf[:], in1=tmp1[:], op=ALU.subtract),
                     reads=[C.ident_bf.d, tmp1.d], writes=[Tt.d])
                tmp2 = r_y.next()
                S.op("dve", lambda e: e.tensor_tensor(out=tmp2[:], in0=Bm[:], in1=G.lvl[:, up_i, :], op=ALU.mult),
                     reads=[Bm.d, G.lvl.d], writes=[tmp2.d])
                S.op("dve", lambda e: e.tensor_tensor(out=Xt[:], in0=C.ident_bf[:], in1=tmp2[:], op=ALU.subtract),
                     reads=[C.ident_bf.d, tmp2.d], writes=[Xt.d])
                for li in range(1, 7):
                    last = (li == 6)
                    yield
                    yp = psF.next()
                    S.op("pe", lambda e: e.matmul(yp[:], lhsT=A[:], rhs=Xt[:], start=True, stop=True), reads=[A.d, Xt.d], writes=[yp.d])
                    Y = r_y.next()
                    S.op("dve", lambda e: e.tensor_tensor(out=Y[:], in0=yp[:], in1=G.lvl[:, li * 2 + up_i, :], op=ALU.mult),
                         reads=[yp.d, G.lvl.d], writes=[Y.d])
                    if not last:
                        yield
                        ypp = psF.next()
                        S.op("pe", lambda e: e.matmul(ypp[:], lhsT=Bm[:], rhs=Tt[:], start=True, stop=True), reads=[Bm.d, Tt.d], writes=[ypp.d])
                        Y2 = r_y.next()
                        S.op("dve", lambda e: e.tensor_tensor(out=Y2[:], in0=ypp[:], in1=G.lvl[:, li * 2 + lo_i, :], op=ALU.mult),
                             reads=[ypp.d, G.lvl.d], writes=[Y2.d])
                    yield
                    zp = psF.next()
                    S.op("pe", lambda e: e.matmul(zp[:], lhsT=Tt[:], rhs=Y[:], start=True, stop=True), reads=[Tt.d, Y.d], writes=[zp.d])
                    Xn = r_m.next()
                    S.op("dve", lambda e: e.tensor_tensor(out=Xn[:], in0=Xt[:], in1=zp[:], op=ALU.subtract), reads=[zp.d, Xt.d], writes=[Xn.d])
                    if not last:
                        yield
                        zpp = psF.next()
                        S.op("pe", lambda e: e.matmul(zpp[:], lhsT=Xt[:], rhs=Y2[:], start=True, stop=True), reads=[Xt.d, Y2.d], writes=[zpp.d])
                        Tn = r_m.next()
                        S.op("dve", lambda e: e.tensor_tensor(out=Tn[:], in0=Tt[:], in1=zpp[:], op=ALU.subtract), reads=[zpp.d, Tt.d], writes=[Tn.d])
                        Tt = Tn
                    Xt = Xn
                M = Xt
                TT = M
                yield
                wps = psF.next()
                S.op("pe", lambda e: e.matmul(wps[:], lhsT=kbg[:], rhs=TT[:], start=True, stop=True), reads=[kbg.d, TT.d], writes=[wps.d])
                wn = r_w.next()
                S.op("act", lambda e: e.activation(out=wn[:], in_=wps[:], func=AF.Identity, scale=-1.0), reads=[wps.d], writes=[wn.d])
                Sb, Sd = G.Sbf[:, sidx, :], G.Sbf.deps[sidx]
                Pb, Pd = G.Pbf[:, sidx, :], G.Pbf.deps[sidx]
                S3, S3d = G.S32[:, sidx, :], G.S32.deps[sidx]
                P3, P3d = G.P32[:, sidx, :], G.P32.deps[sidx]
                yield
                vps = psF.next()
                S.op("pe", lambda e: e.matmul(vps[:], lhsT=TT[:], rhs=vbt[:], start=True, stop=False), reads=[TT.d, vbt.d], writes=[vps.d], sig=False)
                S.op("pe", lambda e: e.matmul(vps[:], lhsT=wn[:], rhs=Sb, start=False, stop=True), reads=[wn.d, Sd], writes=[vps.d])
                vn = r_vn.next()
                S.op("act", lambda e: e.copy(out=vn[:], in_=vps[:]), reads=[vps.d], writes=[vn.d])
                if with_P:
                    yield
                    vpp = psF.next()
                    S.op("pe", lambda e: e.matmul(vpp[:], lhsT=wn[:], rhs=Pb, start=True, stop=True), reads=[wn.d, Pd], writes=[vpp.d])
                    vnP = r_vnP.next()
                    S.op("dve", lambda e: e.tensor_copy(out=vnP[:], in_=vpp[:]), reads=[vpp.d], writes=[vnP.d])
                if with_out:
                    yield
                    ops_ = psF.next()
                    S.op("pe", lambda e: e.matmul(ops_[:], lhsT=Sb, rhs=qg[:], start=True, stop=False), reads=[Sd, qg.d], writes=[ops_.d], sig=False)
                    S.op("pe", lambda e: e.matmul(ops_[:], lhsT=vn[:], rhs=qkT[:], start=False, stop=True), reads=[vn.d, qkT.d], writes=[ops_.d])
                    S.op("act", lambda e: e.copy(out=ob[:, h, cols], in_=ops_[:]), reads=[ops_.d], writes=[ob.d])
                    yield
                    rps = psF.next()
                    S.op("pe", lambda e: e.matmul(rps[:], lhsT=Pb, rhs=qg[:], start=True, stop=False), reads=[Pd, qg.d], writes=[rps.d], sig=False)
                    S.op("pe", lambda e: e.matmul(rps[:], lhsT=vnP[:], rhs=qkT[:], start=False, stop=True), reads=[vnP.d, qkT.d], writes=[rps.d])
                    S.op("dve", lambda e: e.tensor_copy(out=rb[:, h, cols], in_=rps[:]), reads=[rps.d], writes=[rb.d])
                egl = eg[:, last_col:last_col + 1]
                yield
                sps = psF.next()
                S.op("pe", lambda e: e.matmul(sps[:], lhsT=kt[:], rhs=vn[:], start=True, stop=True), reads=[kt.d, vn.d], writes=[sps.d])
                S.op("dve", lambda e: e.scalar_tensor_tensor(out=S3, in0=S3, scalar=egl, in1=sps[:], op0=ALU.mult, op1=ALU.add),
                     reads=[S3d, eg.d, sps.d], writes=[S3d])
                S.op("act", lambda e: e.copy(out=Sb, in_=S3), reads=[S3d], writes=[Sd])
                if with_P:
                    yield
                    pps = psF.next()
                    S.op("pe", lambda e: e.matmul(pps[:], lhsT=kt[:], rhs=vnP[:], start=True, stop=True), reads=[kt.d, vnP.d], writes=[pps.d])
                    S.op("dve", lambda e: e.scalar_tensor_tensor(out=P3, in0=P3, scalar=egl, in1=pps[:], op0=ALU.mult, op1=ALU.add),
                         reads=[P3d, eg.d, pps.d], writes=[P3d])
                    S.op("act", lambda e: e.copy(out=Pb, in_=P3), reads=[P3d], writes=[Pd])
                yield
        if with_out:
            S.dma("pool", C.o0_d[dirn].ap[:, hlo:hhi, lo:lo + ntok], ob[:, hlo:hhi, :ntok], reads=[ob.d], writes=C.o0_d[dirn].dd(lo, lo + ntok), own=ob.d)
            S.dma("pool", C.R_d[dirn].ap[:, hlo:hhi, lo:lo + ntok], rb[:, hlo:hhi, :ntok], reads=[rb.d], writes=C.R_d[dirn].dd(lo, lo + ntok), own=rb.d)


def phase3(S, C):
    G = Ctx()
    C.sctx = SBT(S, "sctx", [128, 8, 128], F32)
    with ExitStack() as ph:
        old = S.stack
        S.stack = ph
        gdn_gates(S, C, G)
        gdn_setup(S, C, G)
        zt = SBT(S, "zt", [128, 12, 2], BF16)
        S.op("pool", lambda e: e.memset(zt[:], 0.0), writes=[zt.d])
        S.dma("pool", C.prec_d.ap[:, :, 0:1], zt[:, :, 0:1], reads=[zt.d], writes=C.prec_d.deps, own=zt.d, allow_slow_non_contiguous=True)
        S.dma("pool", C.prec_d.ap[:, :, 257:258], zt[:, :, 1:2], reads=[zt.d], writes=C.prec_d.deps, own=zt.d, allow_slow_non_contiguous=True)
        for i in range(8):
            S.op("pool", lambda e: e.memset(G.S32[:, i, :], 0.0), writes=[G.S32.deps[i]])
            S.op("pool", lambda e: e.memset(G.Sbf[:, i, :], 0.0), writes=[G.Sbf.deps[i]])

        def ctx_src(lo, ntok):
            return C.prec_d.ap[:, :, lo:lo + ntok + 2], C.prec_d.deps

        def lat_src(lo, ntok):
            return C.pre_d.ap[:, :, 128 + lo - 1:128 + lo + ntok + 1], C.pre_d.dd(128 + lo - 1, 128 + lo + ntok + 1)

        def run(gens):
            gens = list(gens)
            while gens:
                for g in list(gens):
                    try:
                        next(g)
                    except StopIteration:
                        gens.remove(g)
        with ExitStack() as ph2:
            S.stack = ph2
            HG = [(0, 1), (2, 3)]
            run([gdn_pass(S, C, G, d_, HG[k_], [(34, 0), (35, 128)] if d_ == 0 else [(35, 128), (34, 0)], False, False, ctx_src,
                          "c%d%d" % (d_, k_), G.psFs[d_ * 2 + k_], G.psBs[d_]) for d_ in range(2) for k_ in range(2)])
            for i in range(8):
                S.op("dve", lambda e: e.tensor_copy(out=C.sctx[:, i, :], in_=G.S32[:, i, :]), reads=[G.S32.deps[i]], writes=[C.sctx.d])
                S.op("pool", lambda e: e.memset(G.S32[:, i, :], 0.0), writes=[G.S32.deps[i]])
                S.op("pool", lambda e: e.memset(G.Sbf[:, i, :], 0.0), writes=[G.Sbf.deps[i]])
                S.op("pool", lambda e: e.tensor_copy(out=G.P32[:, i, :], in_=G.ident_f[:]), reads=[G.ident_f.d], writes=[G.P32.deps[i]])
                S.op("pool", lambda e: e.tensor_copy(out=G.Pbf[:, i, :], in_=C.ident_bf[:]), reads=[C.ident_bf.d], writes=[G.Pbf.deps[i]])
            S.barrier()
            S.stack = ph
        NTL = int(os.environ.get("NTL", str(NT)))
        with ExitStack() as ph2:
            S.stack = ph2
            ft = [(ti, ti * 128) for ti in range(NTL)]
            bt = [(ti, ti * 128) for ti in reversed(range(NT - NTL, NT))]
            gens = [gdn_pass(S, C, G, d_, HG[k_], ft if d_ == 0 else bt, True, True, lat_src,
                             "l%d%d" % (d_, k_), G.psFs[d_ * 2 + k_], G.psBs[d_]) for d_ in range(2) for k_ in range(2)]
            if getattr(C, "preconverted", False):
                gens.append(conv_gen(S, C))
            run(gens)
            xs = SBT(S, "xs", [128, 8, 2, 128], F32)
            for i in range(8):
                S.op("act", lambda e: e.copy(out=xs[:, i, 0, :], in_=G.S32[:, i, :]), reads=[G.S32.deps[i]], writes=[xs.d])
                pp = G.psF.next()
                S.op("pe", lambda e: e.matmul(pp[:], lhsT=G.P32[:, i, :], rhs=G.ident_f[:], start=True, stop=True),
                     reads=[G.P32.deps[i], G.ident_f.d], writes=[pp.d])
                S.op("act", lambda e: e.copy(out=xs[:, i, 1, :], in_=pp[:]), reads=[pp.d], writes=[xs.d])
            S.dma("pool", C.xch_d.ap, xs[:], reads=[xs.d], writes=C.xch_d.deps, own=xs.d)
            S.dma("pool", C.sctx_d.ap, C.sctx[:], reads=[C.sctx.d], writes=C.sctx_d.deps, own=C.sctx.d)
            S.barrier()
            S.stack = ph
        S.stack = old


def convert_ffn_weights(S, C, l):
    with ExitStack() as ph:
        old = S.stack
        S.stack = ph
        st = Ring(S, "cvst", [128, 22, 128], F32, 2)
        sb = Ring(S, "cvsb", [128, 22, 128], BF16, 2)
        k = 0
        for j in range(NJ):
            for gi, nm in enumerate(("ffn_w_gate", "ffn_w_up")):
                s_, b_ = st.next(), sb.next()
                S.dma("sp", s_[:, 0:8, :], C.inp[nm][l, :, j * 128:(j + 1) * 128].rearrange("(kc p) n -> p kc n", p=128), writes=[s_.d])
                S.op(("pool", "dve", "act")[k % 3], (lambda e: e.tensor_copy(out=b_[:, 0:8, :], in_=s_[:, 0:8, :])) if k % 3 != 2 else
                     (lambda e: e.copy(out=b_[:, 0:8, :], in_=s_[:, 0:8, :])), reads=[s_.d], writes=[b_.d])
                S.dma("pool", C.wgu_d.ap[j, :, gi, :, :], b_[:, 0:8, :], reads=[b_.d], writes=C.wgu_d.deps, own=b_.d)
                k += 1
        for n in range(8):
            s_, b_ = st.next(), sb.next()
            S.dma("sp", s_[:], C.inp["ffn_w_down"][l, :, n * 128:(n + 1) * 128].rearrange("(j p) n -> p j n", p=128), writes=[s_.d])
            S.op(("pool", "dve", "act")[k % 3], (lambda e: e.tensor_copy(out=b_[:], in_=s_[:])) if k % 3 != 2 else
                 (lambda e: e.copy(out=b_[:], in_=s_[:])), reads=[s_.d], writes=[b_.d])
            S.dma("pool", C.wd_d.ap[n], b_[:], reads=[b_.d], writes=C.wd_d.deps, own=b_.d)
            k += 1
        S.barrier()
        S.stack = old


def rstd_of(S, C, yT, ntok, R):
    sq = R["sq"].next()
    for c in range(8):
        S.op("dve", lambda e: e.tensor_tensor(out=sq[:, c, :ntok], in0=yT[:, c, :ntok], in1=yT[:, c, :ntok], op=ALU.mult),
             reads=[yT.d], writes=[sq.d])
    psn = R["ps"].next()
    for c in range(8):
        S.op("pe", lambda e: e.matmul(psn[:, :ntok], lhsT=C.ones_bf[:], rhs=sq[:, c, :ntok], start=(c == 0), stop=(c == 7)),
             reads=[C.ones_bf.d, sq.d], writes=[psn.d], sig=(c == 7))
    rstd = R["rstd"].next()
    S.op("act", lambda e: e.activation(out=rstd[:, :ntok], in_=psn[:, :ntok], func=AF.Ln, bias=C.eps_t[:, 0:1], scale=1.0 / D),
         reads=[psn.d, C.eps_t.d], writes=[rstd.d])
    S.op("act", lambda e: e.activation(out=rstd[:, :ntok], in_=rstd[:, :ntok], func=AF.Exp, scale=-0.5), reads=[rstd.d], writes=[rstd.d])
    return rstd


def postnorm_residual(S, C, xb, yT, gg, R):
    rstd = rstd_of(S, C, yT, 512, R)
    for c in range(8):
        tmp = R["tmp"].next()
        S.op("dve", lambda e: e.scalar_tensor_tensor(out=tmp[:], in0=yT[:, c, :], scalar=gg[:, c:c + 1], in1=rstd[:], op0=ALU.mult, op1=ALU.mult),
             reads=[yT.d, gg.d, rstd.d], writes=[tmp.d])
        S.op("dve", lambda e: e.tensor_tensor(out=xb[:, c, :], in0=xb[:, c, :], in1=tmp[:], op=ALU.add), reads=[xb.d, tmp.d], writes=[xb.d])


def ffn_block(S, C, xb, l, R, F):
    hT = F["hT"]
    norm_mod_block(S, C, xb, 512, C.vec[("a2", l)], C.vec[("b2", l)], hT, R)
    actT = F["actT"]
    for j in range(NJ):
        w = F["wgu"].next()
        wgu_d = C.wgu_d[l] if isinstance(C.wgu_d, list) else C.wgu_d
        S.dma("sp", w[:], wgu_d.ap[j], reads=wgu_d.deps, writes=[w.d])
        pg, pu = F["psg"].next(), F["psu"].next()
        for gi, ps in ((0, pg), (1, pu)):
            for kc in range(8):
                S.op("pe", lambda e: e.matmul(ps[:], lhsT=w[:, gi, kc, :], rhs=hT[:, kc, :], start=(kc == 0), stop=(kc == 7)),
                     reads=[w.d, hT.d], writes=[ps.d], sig=(kc == 7))
        sg = F["sg"].next()
        S.op("act", lambda e: e.activation(out=sg[:], in_=pg[:], func=AF.Silu), reads=[pg.d], writes=[sg.d])
        S.op("dve", lambda e: e.tensor_tensor(out=actT[:, j, :], in0=pu[:], in1=sg[:], op=ALU.mult), reads=[pu.d, sg.d], writes=[actT.d])
    fT = F["fT"]
    for n in range(8):
        w = F["wd"].next()
        wd_d = C.wd_d[l] if isinstance(C.wd_d, list) else C.wd_d
        S.dma("sp", w[:], wd_d.ap[n], reads=wd_d.deps, writes=[w.d])
        ps = F["psg"].next()
        for j in range(NJ):
            S.op("pe", lambda e: e.matmul(ps[:], lhsT=w[:, j, :], rhs=actT[:, j, :], start=(j == 0), stop=(j == NJ - 1)),
                 reads=[w.d, actT.d], writes=[ps.d], sig=(j == NJ - 1))
        S.op("act", lambda e: e.copy(out=fT[:, n, :], in_=ps[:]), reads=[ps.d], writes=[fT.d])
    postnorm_residual(S, C, xb, fT, C.vec[("gg2", l)], R)


def common_rings(S, deep=False):
    R = {"sq": Ring(S, "sq", [128, 8, 512], BF16, 1), "ps": Ring(S, "psn", [128, 512], F32, 1, psum=True),
         "rstd": Ring(S, "rstd", [128, 512], F32, 2), "tmp": Ring(S, "ntmp", [128, 512], F32, 3)}
    F = {"hT": SBT(S, "hT", [128, 8, 512], BF16), "actT": SBT(S, "actT", [128, NJ, 512], BF16),
         "fT": SBT(S, "fT", [128, 8, 512], F32), "wgu": Ring(S, "wgu", [128, 2, 8, 128], BF16, 4 if deep else 2),
         "wd": Ring(S, "wd", [128, NJ, 128], BF16, 3 if deep else 2), "psg": Ring(S, "psg", [128, 512], F32, 3, psum=True),
         "psu": Ring(S, "psu", [128, 512], F32, 3, psum=True), "sg": Ring(S, "sg", [128, 512], F32, 3 if deep else 2)}
    return R, F


def phaseB(S, C):
    if not getattr(C, "preconverted", False):
        convert_ffn_weights(S, C, 0)
    ident_f = load_const(S, C, "ident_f")
    gnormg = load_const(S, C, "gnormg")
    sel = load_const(S, C, "sel")
    Sst = SBT(S, "Sst", [128, 8, 128], BF16)
    with ExitStack() as ph:
        old = S.stack
        S.stack = ph
        xall = SBT(S, "xall", [128, 4, 8, 2, 128], F32)
        S.dma("sp", xall[:], C.inp["xall"], writes=[xall.d])
        sctx = SBT(S, "sctxB", [128, 8, 128], F32)
        S.dma("sp", sctx[:], C.inp["sctx_in"], writes=[sctx.d])
        cand = SBT(S, "cand", [128, 4, 128], F32, ndeps=4)
        acc = SBT(S, "accS", [128, 128], F32)
        psc = Ring(S, "psc", [128, 128], F32, 2, psum=True)
        for sidx in range(8):
            dirn = sidx // 4
            order = [0, 1, 2, 3] if dirn == 0 else [3, 2, 1, 0]
            first = order[0]
            S.op("dve", lambda e: e.tensor_copy(out=cand[:, first, :], in_=sctx[:, sidx, :]), reads=[sctx.d], writes=[cand.deps[first]])
            for a_, b_ in zip(order[:-1], order[1:]):
                ps = psc.next()
                S.op("pe", lambda e: e.matmul(ps[:], lhsT=xall[:, a_, sidx, 1, :], rhs=cand[:, a_, :], start=True, stop=True),
                     reads=[xall.d, cand.deps[a_]], writes=[ps.d])
                S.op("dve", lambda e: e.tensor_tensor(out=cand[:, b_, :], in0=ps[:], in1=xall[:, a_, sidx, 0, :], op=ALU.add),
                     reads=[ps.d, xall.d], writes=[cand.deps[b_]])
            S.op("dve", lambda e: e.tensor_scalar(out=acc[:], in0=cand[:, 0, :], scalar1=sel[:, 0:1], scalar2=None, op0=ALU.mult),
                 reads=[cand.deps[0], sel.d], writes=[acc.d])
            for s_ in range(1, 4):
                S.op("dve", lambda e: e.scalar_tensor_tensor(out=acc[:], in0=cand[:, s_, :], scalar=sel[:, s_:s_ + 1], in1=acc[:],
                                                             op0=ALU.mult, op1=ALU.add), reads=[cand.deps[s_], sel.d, acc.d], writes=[acc.d])
            S.op("dve", lambda e: e.tensor_copy(out=Sst[:, sidx, :], in_=acc[:]), reads=[acc.d], writes=[Sst.d])
        S.barrier()
        S.stack = old
    with ExitStack() as ph:
        old = S.stack
        S.stack = ph
        wst = Ring(S, "wst", [128, 8, 128], F32, 2)
        woa = SBT(S, "woa", [64, 8, 1024], BF16)
        wog = SBT(S, "wog", [128, 4, 1024], BF16)
        for c0 in range(0, 1024, 128):
            st = wst.next()
            S.dma("sp", st[0:64, :, :], C.inp["hy_w_out"][0:512, c0:c0 + 128].rearrange("(kg d) n -> d kg n", d=64), writes=[st.d])
            S.op("pool", lambda e: e.tensor_copy(out=woa[:, :, c0:c0 + 128], in_=st[0:64, :, :]), reads=[st.d], writes=[woa.d])
            st = wst.next()
            S.dma("sp", st[:, 0:4, :], C.inp["hy_w_out"][512:1024, c0:c0 + 128].rearrange("(h e) n -> e h n", e=128), writes=[st.d])
            S.op("pool", lambda e: e.tensor_copy(out=wog[:, :, c0:c0 + 128], in_=st[:, 0:4, :]), reads=[st.d], writes=[wog.d])
        cvs = Ring(S, "cvs2", [128, 8, 128], F32, 2)
        cvb = Ring(S, "cvb2", [128, 8, 128], BF16, 2)
        for j in range(0 if getattr(C, "preconverted", False) else 24):
            s_, b_ = cvs.next(), cvb.next()
            S.dma("sp", s_[:], C.inp["sc_w_in"][:, j * 128:(j + 1) * 128].rearrange("(kc p) n -> p kc n", p=128), writes=[s_.d])
            S.op("pool" if j % 2 else "dve", lambda e: e.tensor_copy(out=b_[:], in_=s_[:]), reads=[s_.d], writes=[b_.d])
            S.dma("pool", C.wsc_d.ap[j], b_[:], reads=[b_.d], writes=C.wsc_d.deps, own=b_.d)
        wscr = Ring(S, "wscr", [128, 8, 128], BF16, 3)
        R, F = common_rings(S)
        xring = Ring(S, "xblk", [128, 8, 512], F32, 1)
        rf = Ring(S, "rfb", [128, 2, 512], BF16, 2)
        o0 = Ring(S, "o0b", [128, 2, 512], F32, 2)
        zs = Ring(S, "zsb", [128, 4, 512], BF16, 1)
        at = Ring(S, "atb", [64, 8, 512], BF16, 1)
        mixT = SBT(S, "mixT", [128, 4, 512], BF16)
        o32 = Ring(S, "o32", [128, 512], F32, 2)
        yT = F["fT"]
        cur = Ring(S, "cub", [128, 512], BF16, 3)
        bgr_ = Ring(S, "bgb", [128, 512], BF16, 3)
        csb = Ring(S, "csb", [128, 512], F32, 2)
        pso = F["psu"]
        for bi in range(int(os.environ.get("NBLK", T // 512))):
            cs_ = slice(bi * 512, (bi + 1) * 512)
            xb = xring.next()
            S.dma("sp", xb[:], C.inp["xT"][:, :, cs_].rearrange("c p t -> p c t"), writes=[xb.d])
            z_, a_ = zs.next(), at.next()
            S.dma("sp", z_[:], C.inp["zs_in"][:, :, cs_], writes=[z_.d])
            S.dma("sp", a_[:], C.inp["attn_in"][:, :, cs_], writes=[a_.d])
            for h in range(4):
                r_, o_ = rf.next(), o0.next()
                for dirn in range(2):
                    S.dma("sp", r_[:, dirn], C.inp["R_in%d" % dirn][:, h, cs_], writes=[r_.d])
                    S.dma("sp", o_[:, dirn], C.inp["o0_in%d" % dirn][:, h, cs_], writes=[o_.d])
                ps = pso.next()
                S.op("pe", lambda e: e.matmul(ps[:], lhsT=Sst[:, h, :], rhs=r_[:, 0, :], start=True, stop=False), reads=[Sst.d, r_.d], writes=[ps.d], sig=False)
                S.op("pe", lambda e: e.matmul(ps[:], lhsT=Sst[:, 4 + h, :], rhs=r_[:, 1, :], start=False, stop=True), reads=[Sst.d, r_.d], writes=[ps.d])
                o = o32.next()
                S.op("dve", lambda e: e.tensor_tensor(out=o[:], in0=ps[:], in1=o_[:, 0, :], op=ALU.add), reads=[ps.d, o_.d], writes=[o.d])
                S.op("dve", lambda e: e.tensor_tensor(out=o[:], in0=o[:], in1=o_[:, 1, :], op=ALU.add), reads=[o.d, o_.d], writes=[o.d])
                sq = R["sq"].next()
                S.op("dve", lambda e: e.tensor_tensor(out=sq[:, 0, :], in0=o[:], in1=o[:], op=ALU.mult), reads=[o.d], writes=[sq.d])
                pn = R["ps"].next()
                S.op("pe", lambda e: e.matmul(pn[:], lhsT=C.ones_bf[:], rhs=sq[:, 0, :], start=True, stop=True), reads=[C.ones_bf.d, sq.d], writes=[pn.d])
                rn = R["rstd"].next()
                S.op("act", lambda e: e.activation(out=rn[:], in_=pn[:], func=AF.Ln, bias=C.eps_t[:, 0:1], scale=1.0 / 128), reads=[pn.d, C.eps_t.d], writes=[rn.d])
                S.op("act", lambda e: e.activation(out=rn[:], in_=rn[:], func=AF.Exp, scale=-0.5), reads=[rn.d], writes=[rn.d])
                tmp = R["tmp"].next()
                S.op("dve", lambda e: e.scalar_tensor_tensor(out=tmp[:], in0=o[:], scalar=gnormg[:, 0:1], in1=rn[:], op0=ALU.mult, op1=ALU.mult),
                     reads=[o.d, gnormg.d, rn.d], writes=[tmp.d])
                S.op("dve", lambda e: e.tensor_tensor(out=mixT[:, h, :], in0=tmp[:], in1=z_[:, h, :], op=ALU.mult), reads=[tmp.d, z_.d], writes=[mixT.d])
            for n in range(8):
                ps = F["psg"].next()
                for kg in range(8):
                    S.op("pe", lambda e: e.matmul(ps[:], lhsT=woa[:, kg, n * 128:(n + 1) * 128], rhs=a_[:, kg, :], start=(kg == 0), stop=False),
                         reads=[woa.d, a_.d], writes=[ps.d], sig=False)
                for h in range(4):
                    S.op("pe", lambda e: e.matmul(ps[:], lhsT=wog[:, h, n * 128:(n + 1) * 128], rhs=mixT[:, h, :], start=False, stop=(h == 3)),
                         reads=[wog.d, mixT.d], writes=[ps.d], sig=(h == 3))
                S.op("act", lambda e: e.copy(out=yT[:, n, :], in_=ps[:]), reads=[ps.d], writes=[yT.d])
            postnorm_residual(S, C, xb, yT, C.vec[("gg1", 0)], R)
            ffn_block(S, C, xb, 0, R, F)
            S.dma("pool", C.x2_d.ap[:, :, cs_], xb[:], reads=[xb.d], writes=C.x2_d.dd(bi * 512, bi * 512 + 512), own=xb.d)
            hT = F["hT"]
            norm_mod_block(S, C, xb, 512, C.vec[("a1", 1)], C.vec[("b1", 1)], hT, R)
            for c in range(8):
                def proj(j):
                    ps = F["psg"].next() if j < 16 else F["psu"].next()
                    wsc = wscr.next()
                    S.dma("sp", wsc[:], C.wsc_d.ap[j], reads=C.wsc_d.deps, writes=[wsc.d])
                    for kc in range(8):
                        S.op("pe", lambda e: e.matmul(ps[:], lhsT=wsc[:, kc, :], rhs=hT[:, kc, :], start=(kc == 0), stop=(kc == 7)),
                             reads=[wsc.d, hT.d], writes=[ps.d], sig=(kc == 7))
                    return ps
                pb_ = proj(c)
                bgb = bgr_.next()
                S.op("act", lambda e: e.copy(out=bgb[:], in_=pb_[:]), reads=[pb_.d], writes=[bgb.d])
                S.dma("pool", C.bg_d.ap[:, c, cs_], bgb[:], reads=[bgb.d], writes=C.bg_d.dd(bi * 512, bi * 512 + 512), own=bgb.d)
                pc_ = proj(8 + c)
                cs32 = csb.next()
                S.op("act", lambda e: e.copy(out=cs32[:], in_=pc_[:]), reads=[pc_.d], writes=[cs32.d])
                pu_ = proj(16 + c)
                cub = cur.next()
                S.op("dve", lambda e: e.tensor_tensor(out=cub[:], in0=pu_[:], in1=cs32[:], op=ALU.mult), reads=[pu_.d, cs32.d], writes=[cub.d])
                cuo = getattr(C, "cu_off", 0)
                S.dma("pool", C.cu_d.ap[:, c, cuo + bi * 512:cuo + bi * 512 + 512], cub[:], reads=[cub.d],
                      writes=C.cu_d.dd(bi * 512, bi * 512 + 512), own=cub.d)
                if getattr(C, "edge", None) is not None and bi in (0, T // 512 - 1):
                    ecol = 0 if bi == 0 else 511
                    eidx = c * 2 + (0 if bi == 0 else 1)
                    S.op("pool", lambda e: e.tensor_copy(out=C.edge[:, eidx:eidx + 1], in_=cub[:, ecol:ecol + 1]),
                         reads=[cub.d], writes=[C.edge.d])
                    if T // 512 == 1:
                        S.op("pool", lambda e: e.tensor_copy(out=C.edge[:, c * 2 + 1:c * 2 + 2], in_=cub[:, 511:512]),
                             reads=[cub.d], writes=[C.edge.d])
        S.barrier()
        S.stack = old


def phaseC(S, C):
    if not getattr(C, "preconverted", False):
        convert_ffn_weights(S, C, 1)
    with ExitStack() as ph:
        old = S.stack
        S.stack = ph
        sconv = load_const(S, C, "sconv")
        diag = SBT(S, "diagc", [128, 24, 128], BF16)
        for c in range(8):
            for k in range(3):
                S.op("dve", lambda e: e.tensor_scalar(out=diag[:, c * 3 + k, :], in0=C.ident_bf[:], scalar1=sconv[:, c, k:k + 1], scalar2=None,
                                                      op0=ALU.mult), reads=[C.ident_bf.d, sconv.d], writes=[diag.d])
        wso = SBT(S, "wso", [128, 8, 1024], BF16)
        with ExitStack() as phw:
            S.stack = phw
            wst = Ring(S, "wst", [128, 8, 256], F32, 2)
            load_weight_bf16(S, C, wso, 0, C.inp["sc_w_out"], 1024, wst)
            S.barrier()
            S.stack = ph
        R, F = common_rings(S, deep=True)
        xring = Ring(S, "xblk", [128, 8, 512], F32, 2)
        cuw = Ring(S, "cuw", [128, 8, 514], BF16, 1)
        bgr = Ring(S, "bgr", [128, 8, 512], BF16, 1)
        mT = SBT(S, "mT", [128, 8, 512], BF16)
        yT = F["fT"]
        for bi in range(int(os.environ.get("NBLK", T // 512))):
            cs_ = slice(bi * 512, (bi + 1) * 512)
            xb = xring.next()
            S.dma("sp", xb[:], C.inp["x2_in"][:, :, cs_], writes=[xb.d])
            cw, bg = cuw.next(), bgr.next()
            S.dma("sp", cw[:], C.inp["cu_ext"][:, :, bi * 512:bi * 512 + 514], writes=[cw.d])
            S.dma("sp", bg[:], C.inp["bg_in"][:, :, cs_], writes=[bg.d])
            for c in range(8):
                ps = F["psu"].next()
                for k in range(3):
                    S.op("pe", lambda e: e.matmul(ps[:], lhsT=diag[:, c * 3 + k, :], rhs=cw[:, c, k:k + 512], start=(k == 0), stop=(k == 2)),
                         reads=[diag.d, cw.d], writes=[ps.d], sig=(k == 2))
                S.op("dve", lambda e: e.tensor_tensor(out=mT[:, c, :], in0=ps[:], in1=bg[:, c, :], op=ALU.mult), reads=[ps.d, bg.d], writes=[mT.d])
            for n in range(8):
                ps = F["psg"].next()
                for kc in range(8):
                    S.op("pe", lambda e: e.matmul(ps[:], lhsT=wso[:, kc, n * 128:(n + 1) * 128], rhs=mT[:, kc, :], start=(kc == 0), stop=(kc == 7)),
                         reads=[wso.d, mT.d], writes=[ps.d], sig=(kc == 7))
                S.op("act", lambda e: e.copy(out=yT[:, n, :], in_=ps[:]), reads=[ps.d], writes=[yT.d])
            postnorm_residual(S, C, xb, yT, C.vec[("gg1", 1)], R)
            ffn_block(S, C, xb, 1, R, F)
            S.dma("pool", C.out_d.ap[:, :, cs_], xb[:], reads=[xb.d], writes=C.out_d.dd(bi * 512, bi * 512 + 512), own=xb.d)
        S.barrier()
        S.stack = old


def halo_exchange(S, C, groups):
    S.dma("pool", C.edge_d.ap, C.edge[:], reads=[C.edge.d], writes=C.edge_d.deps, own=C.edge.d)
    S.allgather(C.edge_d.ap, C.edges_g.ap, groups)
    with ExitStack() as ph:
        old = S.stack
        S.stack = ph
        selL = load_const(S, C, "selL")
        selR = load_const(S, C, "selR")
        eg = SBT_(S, "egath", [128, 4, 8, 2], BF16)
        S.dma("sp", eg[:].rearrange("p r c k -> p r (c k)"), C.edges_g.ap.rearrange("(r p) n -> p r n", p=128), writes=[eg.d])
        for nm, sel, k, col in (("hl", selL, 1, 0), ("hr", selR, 0, T + 1)):
            acc = SBT_(S, nm + "acc", [128, 8], F32)
            S.op("dve", lambda e: e.tensor_scalar(out=acc[:], in0=eg[:, 0, :, k], scalar1=sel[:, 0:1], scalar2=None, op0=ALU.mult),
                 reads=[eg.d, sel.d], writes=[acc.d])
            for r in range(1, 4):
                S.op("dve", lambda e: e.scalar_tensor_tensor(out=acc[:], in0=eg[:, r, :, k], scalar=sel[:, r:r + 1], in1=acc[:],
                                                             op0=ALU.mult, op1=ALU.add), reads=[eg.d, sel.d, acc.d], writes=[acc.d])
            hb = SBT_(S, nm + "bf", [128, 8, 1], BF16)
            S.op("dve", lambda e: e.tensor_copy(out=hb[:, :, 0], in_=acc[:]), reads=[acc.d], writes=[hb.d])
            S.dma("pool", C.cu_d.ap[:, :, col:col + 1], hb[:], reads=[hb.d], writes=C.cu_d.deps, own=hb.d, allow_slow_non_contiguous=True)
        S.barrier()
        S.stack = old


def conv_gen(S, C):
    st = Ring(S, "pcs", [128, 8, 128], F32, 2)
    sb = Ring(S, "pcb", [128, 8, 128], BF16, 2)
    for l in range(2):
        for j in range(NJ):
            for gi, nm in enumerate(("ffn_w_gate", "ffn_w_up")):
                s_, b_ = st.next(), sb.next()
                S.dma("sp", s_[:], C.inp[nm][l, :, j * 128:(j + 1) * 128].rearrange("(kc p) n -> p kc n", p=128), writes=[s_.d])
                S.op("pool", lambda e: e.tensor_copy(out=b_[:], in_=s_[:]), reads=[s_.d], writes=[b_.d])
                S.dma("pool", C.wgu_d[l].ap[j, :, gi, :, :], b_[:], reads=[b_.d], writes=C.wgu_d[l].deps, own=b_.d)
                for _sp in range(12):
                    yield
        for n in range(8):
            for j0 in range(0, NJ, 8):
                nj = min(8, NJ - j0)
                s_, b_ = st.next(), sb.next()
                S.dma("sp", s_[:, :nj, :], C.inp["ffn_w_down"][l, j0 * 128:(j0 + nj) * 128, n * 128:(n + 1) * 128].rearrange("(j p) n -> p j n", p=128),
                      writes=[s_.d])
                S.op("pool", lambda e: e.tensor_copy(out=b_[:, :nj, :], in_=s_[:, :nj, :]), reads=[s_.d], writes=[b_.d])
                S.dma("pool", C.wd_d[l].ap[n, :, j0:j0 + nj, :], b_[:, :nj, :], reads=[b_.d], writes=C.wd_d[l].deps, own=b_.d)
                for _sp in range(12):
                    yield
    for j in range(24):
        s_, b_ = st.next(), sb.next()
        S.dma("sp", s_[:], C.inp["sc_w_in"][:, j * 128:(j + 1) * 128].rearrange("(kc p) n -> p kc n", p=128), writes=[s_.d])
        S.op("pool", lambda e: e.tensor_copy(out=b_[:], in_=s_[:]), reads=[s_.d], writes=[b_.d])
        S.dma("pool", C.wsc_d.ap[j], b_[:], reads=[b_.d], writes=C.wsc_d.deps, own=b_.d)
        for _sp in range(12):
            yield


import sys, time
from concourse.bass_utils import run_bass_kernel_spmd

GROUPS = [[0, 1, 2, 3], [4, 5, 6, 7]]
NAMES_F = ["xT", "xhT", "ctxT", "cvec", "ident_bf", "ident_f", "swap_bf", "ones_bf", "ones_f", "amask", "gmask", "lvlmask", "cmat", "hv",
           "cosT", "sinT", "ada_w", "ada_bT", "gains", "wq", "wk", "wvab", "wg", "wz", "sinkrow", "gconv", "galog", "gdtb",
           "hy_w_out", "gnormg", "sel", "selL", "selR", "sc_w_in", "ffn_w_gate", "ffn_w_up", "ffn_w_down", "sconv", "sc_w_out"]


def build_fused():
    nc = bass.Bass("TRN2", target_bir_lowering=False)
    C = declare_inputs(nc, NAMES_F)
    C.q_d = DramT(nc, "q_d", [128, 4, T], BF16)
    C.zs_d = DramT(nc, "zs_d", [128, 4, T], BF16)
    C.pre_d = DramT(nc, "pre_d", [128, 12, T + 256], BF16)
    C.prec_d = DramT(nc, "prec_d", [128, 12, 258], BF16)
    C.attn_d = DramT(nc, "attn_d", [64, 8, T], BF16)
    C.o0_d = [DramT(nc, "o0_d%d" % i, [128, 4, T], F32) for i in range(2)]
    C.R_d = [DramT(nc, "R_d%d" % i, [128, 4, T], BF16) for i in range(2)]
    C.xch_d = DramT(nc, "xch_d", [128, 8, 2, 128], F32)
    C.sctx_d = DramT(nc, "sctx_d", [128, 8, 128], F32)
    C.xall_g = DramT(nc, "xall_g", [4 * 128, 2048], F32)
    C.wgu_d = [DramT(nc, "wgu_d%d" % l, [NJ, 128, 2, 8, 128], BF16, blk=10 ** 9) for l in range(2)]
    C.wd_d = [DramT(nc, "wd_d%d" % l, [8, 128, NJ, 128], BF16, blk=10 ** 9) for l in range(2)]
    C.preconverted = True
    C.wsc_d = DramT(nc, "wsc_d", [24, 128, 8, 128], BF16, blk=10 ** 9)
    C.x2_d = DramT(nc, "x2_d", [128, 8, T], F32)
    C.cu_d = DramT(nc, "cu_d", [128, 8, T + 2], BF16)
    C.bg_d = DramT(nc, "bg_d", [128, 8, T], BF16)
    C.edge_d = DramT(nc, "edge_d", [128, 16], BF16)
    C.edges_g = DramT(nc, "edges_g", [4 * 128, 16], BF16)
    C.out_d = DramT(nc, "out_d", [128, 8, T], F32, kind="ExternalOutput")
    C.cu_off = 1
    with ExitStack() as st:
        S = Sched(nc, st)
        setup_common(S, C)
        phase0(S, C)
        S.retire()
        with ExitStack() as stA:
            S.stack = stA
            phase1(S, C, None)
            S.retire()
            phase2(S, C)
            S.retire()
            phase3(S, C)
            S.retire()
            S.stack = st
        S.allgather(C.xch_d.ap.rearrange("p a b c -> p (a b c)"), C.xall_g.ap, GROUPS)
        C.inp["xall"] = C.xall_g.ap.rearrange("(r p) n -> p r n", p=128)
        C.inp["sctx_in"] = C.sctx_d.ap
        for i in range(2):
            C.inp["R_in%d" % i] = C.R_d[i].ap
            C.inp["o0_in%d" % i] = C.o0_d[i].ap
        C.inp["zs_in"] = C.zs_d.ap
        C.inp["attn_in"] = C.attn_d.ap
        with ExitStack() as stB:
            S.stack = stB
            C.edge = SBT_(S, "edge", [128, 16], BF16)
            phaseB(S, C)
            S.retire()
            halo_exchange(S, C, GROUPS)
            S.retire()
            S.stack = st
        C.inp["x2_in"] = C.x2_d.ap
        C.inp["cu_ext"] = C.cu_d.ap
        C.inp["bg_in"] = C.bg_d.ap
        with ExitStack() as stC:
            S.stack = stC
            phaseC(S, C)
            S.stack = st
        S.barrier()
        pass
    return nc


def kernel(**inp):
    inp = {k: np.asarray(v) for k, v in inp.items()}
    shared = prep_shared_inputs(inp)
    cores = list(range(8))
    maps = []
    for core in cores:
        b, s = core // 4, core % 4
        d = prep_core_inputs(inp, b, s)
        d.update(shared)
        for nm, idx in (("sel", s), ("selL", s - 1), ("selR", s + 1)):
            v = np.zeros((128, 4), np.float32)
            if 0 <= idx <= 3:
                v[:, idx] = 1.0
            d[nm] = v
        maps.append({k: d[k] for k in NAMES_F})
    nc = build_fused()
    res = run_bass_kernel_spmd(nc, maps, core_ids=cores).results
    out = np.zeros((2, 4 * T, D), np.float32)
    for c in cores:
        b, s = c // 4, c % 4
        o = np.asarray(res[c]["out_d"])
        out[b, s * T:(s + 1) * T, :] = o.transpose(2, 1, 0).reshape(T, D)
    return out
```

```python
import numpy as np
import concourse.bass as bass
import concourse.mybir as mybir

F32 = mybir.dt.float32
BF16 = mybir.dt.bfloat16
AF = mybir.ActivationFunctionType
ALU = mybir.AluOpType
AX = mybir.AxisListType

EPOCH = 12000
_UID = [0]


class Dep:
    __slots__ = ("w", "r", "dsem", "dval", "name", "uid")

    def __init__(self, name=""):
        _UID[0] += 1
        self.uid = _UID[0]
        self.w = None
        self.r = {}
        self.dsem = None
        self.dval = 0
        self.name = name


class Sched:
    def __init__(self, nc, stack):
        self.nc = nc
        self.stack = stack
        self.semstack = stack
        self.eng = {"pe": nc.tensor, "dve": nc.vector, "act": nc.scalar, "pool": nc.gpsimd, "sp": nc.sync}
        self.sem = {}
        self.cnt = {}
        self.order = {}
        self.seen = {k: {} for k in self.eng}
        self.nsem = 0
        for k in self.eng:
            self.sem[k] = self.new_sem(k)
            self.cnt[k] = 0
            self.order[k] = 0
        self.ninst = 0
        self.nwait = 0
        self.last_tok = {}
        self.dma_deps = []
        self.free_dsems = []

    def new_sem(self, tag):
        self.nsem += 1
        return self.semstack.enter_context(self.nc.semaphore("s%d_%s" % (self.nsem, tag)))

    def _need(self, e, tok):
        if tok is None:
            return
        pk, order, sem, val = tok
        if pk == "pe" and e == "pe":
            return
        if self.seen[e].get(pk, -1) >= order:
            return
        self.eng[e].wait_ge(sem, val)
        self.nwait += 1
        self.seen[e][pk] = order

    def _waits(self, e, reads, writes):
        for d in reads:
            self._need(e, d.w)
        for d in writes:
            self._need(e, d.w)
            for t in list(d.r.values()):
                self._need(e, t)

    def op(self, e, fn, reads=(), writes=(), sig=True):
        self._waits(e, reads, writes)
        ins = fn(self.eng[e])
        self.ninst += 1
        self.order[e] += 1
        if sig:
            self.cnt[e] += 1
            ins.then_inc(self.sem[e], 1)
            tok = (e, self.order[e], self.sem[e], self.cnt[e])
            self.last_tok[e] = tok
        else:
            tok = (e, self.order[e], self.sem[e], self.cnt[e] + 1)
        for d in writes:
            d.w = tok
            d.r = {}
        for d in reads:
            if d not in writes:
                d.r[e] = tok
        if sig and self.cnt[e] >= EPOCH:
            self.sem[e] = self.new_sem(e)
            self.cnt[e] = 0
        return ins

    def dma(self, q, out, in_, reads=(), writes=(), own=None, **kw):
        self._waits(q, reads, writes)
        own = own or writes[0]
        if own.dsem is not None and own.dval > 0:
            self._need(q, ("dma%d" % own.uid, own.dval, own.dsem, own.dval))
        if own.dsem is None:
            if self.free_dsems:
                own.dsem, own.dval = self.free_dsems.pop()
            else:
                own.dsem = self.new_sem("d")
                own.dval = 0
            self.dma_deps.append(own)
        own.dval += 16
        ins = self.eng[q].dma_start(out=out, in_=in_, **kw)
        ins.then_inc(own.dsem, 16)
        self.ninst += 1
        tok = ("dma%d" % own.uid, own.dval, own.dsem, own.dval)
        for d in writes:
            d.w = tok
            d.r = {}
        for d in reads:
            d.r[tok[0]] = tok
        return ins

    def barrier(self, engines=None):
        for e in (engines or list(self.eng)):
            for p, tok in self.last_tok.items():
                if not (p == e and e == "pe"):
                    self._need(e, tok)
            for d in self.dma_deps:
                self._need(e, ("dma%d" % d.uid, d.dval, d.dsem, d.dval))

    def retire(self):
        self.barrier()
        for d in self.dma_deps:
            self.free_dsems.append((d.dsem, d.dval))
            d.dsem = None
        self.dma_deps = []

    def allgather(self, in_ap, out_ap, groups):
        self.barrier()
        csem = self.new_sem("cc")
        ins = self.nc.gpsimd.collective_compute("AllGather", mybir.AluOpType.bypass, replica_groups=groups,
                                                ins=[in_ap.opt()], outs=[out_ap.opt()])
        ins.then_inc(csem, 1)
        self.ninst += 1
        for e in self.eng:
            self.eng[e].wait_ge(csem, 1)

    def wait_all(self, e, deps):
        for d in deps:
            self._need(e, d.w)
            for t in list(d.r.values()):
                self._need(e, t)


class SBT:
    def __init__(self, S, name, shape, dtype, psum=False, ndeps=1):
        nc = S.nc
        S.ntile = getattr(S, "ntile", 0) + 1
        name = "t%d_%s" % (S.ntile, name)
        if psum:
            self.t = S.stack.enter_context(nc.psum_tensor(name, shape, dtype))
        else:
            self.t = S.stack.enter_context(nc.sbuf_tensor(name, shape, dtype))
        self.deps = [Dep(name + str(i)) for i in range(ndeps)]
        self.d = self.deps[0]
        self.shape = shape

    def __getitem__(self, idx):
        return self.t[idx]


SBT_ = SBT


class DramT:
    def __init__(self, nc, name, shape, dtype, kind="Internal", blk=512):
        self.ap = nc.dram_tensor(name, shape, dtype, kind=kind).ap()
        self.blk = blk
        self.deps = [Dep("%s_%d" % (name, i)) for i in range((shape[-1] + blk - 1) // blk)]

    def dd(self, c0, c1):
        return self.deps[c0 // self.blk:(c1 - 1) // self.blk + 1]


import os
import numpy as np
import ml_dtypes
from contextlib import ExitStack

BF = ml_dtypes.bfloat16
D = 1024
T = 4096
NT = T // 128
HAL = 128
CTX = 256
EXT = T + 2 * HAL + CTX
COL_HL, COL_HR, COL_CTX = T, T + HAL, T + 2 * HAL
FF = 2816
NJ = FF // 128
EPS = 1e-6


class Ring:
    def __init__(self, S, name, shape, dtype, n, psum=False):
        self.tiles = [SBT(S, "%s%d" % (name, i), shape, dtype, psum=psum) for i in range(n)]
        self.i = 0

    def next(self):
        t = self.tiles[self.i % len(self.tiles)]
        self.i += 1
        return t


def rope_tables(s):
    pos_own = s * T + np.arange(T)
    pos_l = s * T - HAL + np.arange(HAL)
    pos_r = (s + 1) * T + np.arange(HAL)
    pos = np.concatenate([pos_own, pos_l, pos_r]).astype(np.int64)
    row = (pos // 64).astype(np.float32)
    col = (pos % 64).astype(np.float32)
    inv = (np.float32(10000.0) ** (-np.arange(16, dtype=np.float32) / np.float32(16))).astype(np.float32)
    ang = np.concatenate([row[:, None] * inv[None, :], col[:, None] * inv[None, :]], axis=1).astype(np.float32)
    cs = np.cos(ang).astype(np.float32).T
    sn = np.sin(ang).astype(np.float32).T
    cosT = np.concatenate([cs, cs, cs, cs], axis=0)
    sinT = np.concatenate([-sn, sn, -sn, sn], axis=0)
    return np.ascontiguousarray(cosT), np.ascontiguousarray(sinT)


def make_consts(s):
    c = {}
    c["ident_bf"] = np.eye(128, dtype=np.float32).astype(BF)
    c["ident_f"] = np.eye(128, dtype=np.float32)
    sw = np.zeros((128, 128), np.float32)
    for m in range(128):
        k = (m // 64) * 64 + ((m % 64) + 32) % 64
        sw[k, m] = 1.0
    c["swap_bf"] = sw.astype(BF)
    c["ones_bf"] = np.ones((128, 128), np.float32).astype(BF)
    c["ones_f"] = np.ones((128, 128), np.float32)
    m = np.arange(128)[:, None]
    n = np.arange(128)[None, :]
    mL = (m >= n).astype(np.float32)
    mR = (m <= n).astype(np.float32)
    masks = np.zeros((128, 4, 512), np.float32)
    masks[:, 0] = np.tile(mL, (1, 4))
    masks[:, 1] = np.tile(mR, (1, 4))
    masks[:, 2] = np.tile(mL, (1, 4)) * (0.0 if s == 0 else 1.0)
    masks[:, 3] = np.tile(mR, (1, 4)) * (0.0 if s == 3 else 1.0)
    c["amask"] = masks.astype(BF)
    gm = np.zeros((128, 4, 128), np.float32)
    gm[:, 0] = (n <= m)
    gm[:, 1] = (n < m)
    gm[:, 2] = (n >= m)
    gm[:, 3] = (n > m)
    c["gmask"] = gm
    cm = np.zeros((128, 4, 128), np.float32)
    cm[:, 0] = (m <= n)
    cm[:, 1] = (m > n)
    cm[:, 2] = (m >= n)
    cm[:, 3] = (m < n)
    c["cmat"] = cm
    lm = np.zeros((128, 14, 128), np.float32)
    for li, mm_ in enumerate((1, 2, 4, 8, 16, 32, 64)):
        low = ((m // mm_) % 2 == 1) & ((n // mm_) == (m // mm_) - 1)
        lm[:, li * 2] = low
        lm[:, li * 2 + 1] = low.T
    c["lvlmask"] = lm.astype(BF)
    hv = np.ones((128, 2), np.float32)
    if s == 0:
        hv[:, 0] = 0
    if s == 3:
        hv[:, 1] = 0
    c["hv"] = hv
    cosT, sinT = rope_tables(s)
    c["cosT"] = cosT
    c["sinT"] = sinT
    return c


def prep_core_inputs(inp, b, s):
    x = inp["x"]
    d = {}
    xo = x[b, s * T:(s + 1) * T, :]
    d["xT"] = np.ascontiguousarray(xo.T.reshape(8, 128, T))
    xh = np.zeros((2 * HAL, D), np.float32)
    if s > 0:
        xh[:HAL] = x[b, s * T - HAL:s * T]
    if s < 3:
        xh[HAL:] = x[b, (s + 1) * T:(s + 1) * T + HAL]
    d["xhT"] = np.ascontiguousarray(xh.T.reshape(8, 128, 2 * HAL))
    d["ctxT"] = np.ascontiguousarray(inp["ctx"][b].T.reshape(8, 128, CTX))
    cv = np.stack([inp["c"][b], inp["c_ctx"]], axis=1)
    d["cvec"] = np.ascontiguousarray(cv.reshape(8, 128, 2).transpose(1, 0, 2))
    d.update(make_consts(s))
    return d


def prep_shared_inputs(inp):
    d = {}
    d["ada_w"] = inp["ada_w"]
    d["ada_bT"] = np.ascontiguousarray(inp["ada_b"].reshape(2, 48, 128).transpose(2, 0, 1))
    g = np.stack([inp["pre_mix_g"], inp["post_mix_g"], inp["pre_ffn_g"], inp["post_ffn_g"]], axis=0)
    d["gains"] = np.ascontiguousarray(g.reshape(4, 2, 8, 128).transpose(3, 0, 1, 2))
    w = inp["hy_w_in"][0]
    wq = w[:, 0:512].reshape(D, 2, 4, 64).transpose(0, 2, 1, 3).reshape(D, 512)
    d["wq"] = np.ascontiguousarray(wq)
    d["wk"] = np.ascontiguousarray(w[:, 512:640])
    d["wvab"] = np.ascontiguousarray(np.concatenate([w[:, 640:768], w[:, 2816:2832]], axis=1))
    d["wg"] = np.ascontiguousarray(w[:, 768:2304])
    d["wz"] = np.ascontiguousarray(w[:, 2304:2816])
    d["hy_w_out"] = inp["hy_w_out"][0]
    sk = inp["attn_sink"][0].reshape(2, 4)
    d["sinkrow"] = np.ascontiguousarray(np.broadcast_to(sk[None, :, :, None], (64, 2, 4, 128)).reshape(64, 2, 512))
    d["gconv"] = np.ascontiguousarray(inp["gdn_conv_w"][0].reshape(3, 12, 128).transpose(2, 1, 0))
    d["galog"] = np.ascontiguousarray(np.broadcast_to(inp["gdn_a_log"][0].reshape(1, 8), (128, 8)))
    d["gdtb"] = np.ascontiguousarray(np.broadcast_to(inp["gdn_dt_bias"][0].reshape(1, 8), (128, 8)))
    d["gnormg"] = np.ascontiguousarray(inp["gdn_norm_g"][0].reshape(128, 1))
    d["sc_w_in"] = inp["sc_w_in"][0]
    d["sconv"] = np.ascontiguousarray(inp["sc_conv_w"][0].reshape(3, 8, 128).transpose(2, 1, 0))
    d["sc_w_out"] = inp["sc_w_out"][0]
    d["ffn_w_gate"] = inp["ffn_w_gate"]
    d["ffn_w_up"] = inp["ffn_w_up"]
    d["ffn_w_down"] = inp["ffn_w_down"]
    return d


INPUT_SPECS = {
    "xT": ([8, 128, T], F32), "xhT": ([8, 128, 2 * HAL], F32), "ctxT": ([8, 128, CTX], F32), "cvec": ([128, 8, 2], F32),
    "ident_bf": ([128, 128], BF16), "ident_f": ([128, 128], F32), "swap_bf": ([128, 128], BF16),
    "ones_bf": ([128, 128], BF16), "ones_f": ([128, 128], F32), "amask": ([128, 4, 512], BF16),
    "gmask": ([128, 4, 128], F32), "lvlmask": ([128, 14, 128], BF16), "cmat": ([128, 4, 128], F32), "hv": ([128, 2], F32),
    "cosT": ([128, T + 2 * HAL], F32), "sinT": ([128, T + 2 * HAL], F32),
    "ada_w": ([2, D, 6 * D], F32), "ada_bT": ([128, 2, 48], F32), "gains": ([128, 4, 2, 8], F32),
    "wq": ([D, 512], F32), "wk": ([D, 128], F32), "wvab": ([D, 144], F32), "wg": ([D, 1536], F32), "wz": ([D, 512], F32),
    "hy_w_out": ([D, D], F32), "sinkrow": ([64, 2, 512], F32), "gconv": ([128, 12, 3], F32),
    "galog": ([128, 8], F32), "gdtb": ([128, 8], F32), "gnormg": ([128, 1], F32),
    "sc_w_in": ([D, 3 * D], F32), "sconv": ([128, 8, 3], F32), "sc_w_out": ([D, D], F32),
    "sel": ([128, 4], F32), "xall": ([128, 4, 8, 2, 128], F32), "sctx_in": ([128, 8, 128], F32),
    "R_in0": ([128, 4, T], BF16), "R_in1": ([128, 4, T], BF16), "o0_in0": ([128, 4, T], F32), "o0_in1": ([128, 4, T], F32),
    "zs_in": ([128, 4, T], BF16), "attn_in": ([64, 8, T], BF16), "x2_in": ([128, 8, T], F32),
    "cu_ext": ([128, 8, T + 2], BF16), "bg_in": ([128, 8, T], BF16), "selL": ([128, 4], F32), "selR": ([128, 4], F32),
    "ffn_w_gate": ([2, D, FF], F32), "ffn_w_up": ([2, D, FF], F32), "ffn_w_down": ([2, FF, D], F32),
}


class Ctx:
    pass


def declare_inputs(nc, names):
    C = Ctx()
    C.inp = {}
    for n in names:
        shp, dt = INPUT_SPECS[n]
        C.inp[n] = nc.dram_tensor(n, shp, dt, kind="ExternalInput").ap()
    return C


def load_const(S, C, name, q="sp"):
    shp, dt = INPUT_SPECS[name]
    t = SBT(S, "c_" + name, shp, dt)
    S.dma(q, t[:], C.inp[name], writes=[t.d])
    return t


def load_weight_bf16(S, C, dst, dst_cols, src_ap, ncols, stage_ring, eng="pool"):
    for c0 in range(0, ncols, 256):
        n = min(256, ncols - c0)
        st = stage_ring.next()
        S.dma("sp", st[:, :, :n], src_ap[:, c0:c0 + n].rearrange("(kc p) n -> p kc n", p=128), writes=[st.d])
        S.op(eng, lambda e: e.tensor_copy(out=dst[:, :, dst_cols + c0:dst_cols + c0 + n], in_=st[:, :, :n]),
             reads=[st.d], writes=[dst.d])


def phase0(S, C):
    nc = S.nc
    cv = SBT(S, "cv", [128, 8, 2], F32)
    S.dma("sp", cv[:], C.inp["cvec"], writes=[cv.d])
    sc = SBT(S, "silu_c", [128, 8, 2], F32)
    S.op("act", lambda e: e.activation(out=sc[:], in_=cv[:], func=AF.Silu), reads=[cv.d], writes=[sc.d])
    adab = load_const(S, C, "ada_bT")
    gains = load_const(S, C, "gains")
    C.gains = gains
    C.mod = [SBT(S, "mod%d" % l, [128, 48, 2], F32) for l in range(2)]
    with ExitStack() as ph:
        old = S.stack
        S.stack = ph
        stage = Ring(S, "adast", [128, 8, 1536], F32, 2)
        psm = SBT(S, "ps_mod", [128, 48, 2], F32, psum=True)
        for l in range(2):
            for qq in range(4):
                st = stage.next()
                S.dma("sp", st[:], C.inp["ada_w"][l, :, qq * 1536:(qq + 1) * 1536].rearrange("(kc p) n -> p kc n", p=128),
                      writes=[st.d])
                for jj in range(12):
                    j = qq * 12 + jj
                    for kc in range(8):
                        S.op("pe", lambda e: e.matmul(psm[:, j, :], lhsT=st[:, kc, jj * 128:(jj + 1) * 128], rhs=sc[:, kc, :],
                                                      start=(kc == 0), stop=(kc == 7)),
                             reads=[st.d, sc.d], writes=[psm.d], sig=(kc == 7))
            mod = C.mod[l]
            S.op("dve", lambda e: e.tensor_tensor(out=mod[:], in0=psm[:], in1=adab[:, l, :, None].to_broadcast([128, 48, 2]),
                                                  op=ALU.add), reads=[psm.d, adab.d], writes=[mod.d])
        S.barrier()
        S.stack = old
    C.vec = {}
    for l in range(2):
        mod = C.mod[l]
        for (nm, gi, mi) in (("a1", 0, 1), ("a2", 2, 4)):
            for col, sfx in ((0, ""), (1, "c")):
                t = SBT(S, "%s%s_%d" % (nm, sfx, l), [128, 8], F32)
                S.op("dve", lambda e: e.scalar_tensor_tensor(out=t[:], in0=mod[:, mi * 8:(mi + 1) * 8, col], scalar=1.0,
                                                             in1=gains[:, gi, l, :], op0=ALU.add, op1=ALU.mult),
                     reads=[mod.d, gains.d], writes=[t.d])
                C.vec[(nm + sfx, l)] = t
        for (nm, gi, mi) in (("gg1", 1, 2), ("gg2", 3, 5)):
            t = SBT(S, "%s_%d" % (nm, l), [128, 8], F32)
            S.op("dve", lambda e: e.tensor_tensor(out=t[:], in0=mod[:, mi * 8:(mi + 1) * 8, 0], in1=gains[:, gi, l, :],
                                                  op=ALU.mult), reads=[mod.d, gains.d], writes=[t.d])
            C.vec[(nm, l)] = t
        for (nm, mi) in (("b1", 0), ("b2", 3)):
            for col, sfx in ((0, ""), (1, "c")):
                t = SBT(S, "%s%s_%d" % (nm, sfx, l), [128, 8], F32)
                S.op("dve", lambda e: e.tensor_copy(out=t[:], in_=mod[:, mi * 8:(mi + 1) * 8, col]),
                     reads=[mod.d], writes=[t.d])
                C.vec[(nm + sfx, l)] = t


def norm_mod_block(S, C, xb, ntok, a, b, hT, R):
    sq = R["sq"].next()
    for c in range(8):
        S.op("dve", lambda e: e.tensor_tensor(out=sq[:, c, :ntok], in0=xb[:, c, :ntok], in1=xb[:, c, :ntok], op=ALU.mult),
             reads=[xb.d], writes=[sq.d])
    psn = R["ps"].next()
    for c in range(8):
        S.op("pe", lambda e: e.matmul(psn[:, :ntok], lhsT=C.ones_bf[:], rhs=sq[:, c, :ntok], start=(c == 0), stop=(c == 7)),
             reads=[C.ones_bf.d, sq.d], writes=[psn.d], sig=(c == 7))
    rstd = R["rstd"].next()
    S.op("act", lambda e: e.activation(out=rstd[:, :ntok], in_=psn[:, :ntok], func=AF.Ln, bias=C.eps_t[:, 0:1], scale=1.0 / D),
         reads=[psn.d, C.eps_t.d], writes=[rstd.d])
    S.op("act", lambda e: e.activation(out=rstd[:, :ntok], in_=rstd[:, :ntok], func=AF.Exp, scale=-0.5),
         reads=[rstd.d], writes=[rstd.d])
    for c in range(8):
        tmp = R["tmp"].next()
        S.op("dve", lambda e: e.scalar_tensor_tensor(out=tmp[:, :ntok], in0=xb[:, c, :ntok], scalar=a[:, c:c + 1],
                                                     in1=rstd[:, :ntok], op0=ALU.mult, op1=ALU.mult),
             reads=[xb.d, a.d, rstd.d], writes=[tmp.d])
        S.op("act", lambda e: e.activation(out=hT[:, c, :ntok], in_=tmp[:, :ntok], func=AF.Identity, bias=b[:, c:c + 1], scale=1.0),
             reads=[tmp.d, b.d], writes=[hT.d])
    return rstd


def setup_common(S, C):
    C.ones_bf = load_const(S, C, "ones_bf")
    C.ident_bf = load_const(S, C, "ident_bf")
    C.eps_t = SBT(S, "eps_t", [128, 1], F32)
    S.op("pool", lambda e: e.memset(C.eps_t[:], EPS), writes=[C.eps_t.d])


def phase1(S, C, dbg=None):
    nc = S.nc
    l = 0
    C.kT = SBT(S, "kT", [128, EXT], BF16)
    C.V = SBT(S, "V", [128, 36, 128], BF16)
    C.ab = SBT(S, "ab", [128, 36, 16], F32)
    with ExitStack() as ph:
        old = S.stack
        S.stack = ph
        swap = load_const(S, C, "swap_bf")
        hv = load_const(S, C, "hv")
        wq = SBT(S, "wq", [128, 8, 512], BF16)
        wk = SBT(S, "wk", [128, 8, 128], BF16)
        wvab = SBT(S, "wvab", [128, 8, 144], BF16)
        wg = SBT(S, "wg", [128, 8, 1536], BF16)
        wz = SBT(S, "wz", [128, 8, 512], BF16)
        stage = Ring(S, "wst", [128, 8, 256], F32, 2)
        load_weight_bf16(S, C, wk, 0, C.inp["wk"], 128, stage)
        load_weight_bf16(S, C, wvab, 0, C.inp["wvab"], 144, stage)
        load_weight_bf16(S, C, wq, 0, C.inp["wq"], 512, stage)
        load_weight_bf16(S, C, wg, 0, C.inp["wg"], 1536, stage)
        load_weight_bf16(S, C, wz, 0, C.inp["wz"], 512, stage)
        R = {"sq": Ring(S, "sq", [128, 8, 512], BF16, 1), "ps": Ring(S, "psn", [128, 512], F32, 1, psum=True),
             "rstd": Ring(S, "rstd", [128, 512], F32, 2), "tmp": Ring(S, "ntmp", [128, 512], F32, 3)}
        xring = Ring(S, "xblk", [128, 8, 512], F32, 2)
        hring = Ring(S, "hT", [128, 8, 512], BF16, 2)
        psr = Ring(S, "psp", [128, 512], F32, 4, psum=True)
        psrot = Ring(S, "psrot", [128, 512], F32, 1, psum=True)
        pst = Ring(S, "pst", [128, 512], F32, 2, psum=True)
        cosr = Ring(S, "cosb", [128, 512], F32, 2)
        sinr = Ring(S, "sinb", [128, 512], F32, 2)
        t1r = Ring(S, "t1", [128, 512], F32, 2)
        q32r = Ring(S, "q32", [128, 512], F32, 2)
        t2r = Ring(S, "t2", [128, 512], F32, 2)
        qsr = Ring(S, "qs", [128, 512], BF16, 2)
        qblk = Ring(S, "qblk", [128, 4, 512], BF16, 2)
        preblk = Ring(S, "preblk", [128, 12, 512], BF16, 1)
        zblk = Ring(S, "zblk", [128, 4, 512], BF16, 1)
        blocks = [("ctx", 0), ("halo", 0)] + [("own", i) for i in range(T // 512)]
        import os
        CUT = int(os.environ.get("CUT", "99"))
        if CUT < 99:
            blocks = blocks[:1]
        for kind, bi in blocks:
            ntok = 512 if kind == "own" else 256
            if kind == "own":
                src = C.inp["xT"][:, :, bi * 512:(bi + 1) * 512]
                col0 = bi * 512
                a, b = C.vec[("a1", l)], C.vec[("b1", l)]
            elif kind == "halo":
                src = C.inp["xhT"]
                col0 = COL_HL
                a, b = C.vec[("a1", l)], C.vec[("b1", l)]
            else:
                src = C.inp["ctxT"]
                col0 = COL_CTX
                a, b = C.vec[("a1c", l)], C.vec[("b1c", l)]
            xb = xring.next()
            S.dma("sp", xb[:, :, :ntok], src.rearrange("c p t -> p c t"), writes=[xb.d])
            hT = hring.next()
            norm_mod_block(S, C, xb, ntok, a, b, hT, R)
            if CUT <= 1:
                continue
            if dbg is not None and kind == "own" and bi == 0 and "h" in dbg:
                S.dma("sp", dbg["h"].ap, hT[:], reads=[hT.d], writes=dbg["h"].deps, own=hT.d)
            rope = kind != "ctx"
            if rope:
                cb = cosr.next()
                sb = sinr.next()
                S.dma("sp", cb[:, :ntok], C.inp["cosT"][:, col0:col0 + ntok], writes=[cb.d])
                S.dma("sp", sb[:, :ntok], C.inp["sinT"][:, col0:col0 + ntok], writes=[sb.d])

            def proj(w, j):
                ps = psr.next()
                for kc in range(8):
                    S.op("pe", lambda e: e.matmul(ps[:, :ntok], lhsT=w[:, kc, j * 128:(j + 1) * 128], rhs=hT[:, kc, :ntok],
                                                  start=(kc == 0), stop=(kc == 7)),
                         reads=[w.d, hT.d], writes=[ps.d], sig=(kc == 7))
                return ps

            def rope_evac(ps, out_ap, out_dep):
                q32 = q32r.next()
                S.op("act", lambda e: e.copy(out=q32[:, :ntok], in_=ps[:, :ntok]), reads=[ps.d], writes=[q32.d])
                t1 = t1r.next()
                S.op("dve", lambda e: e.tensor_tensor(out=t1[:, :ntok], in0=q32[:, :ntok], in1=cb[:, :ntok], op=ALU.mult),
                     reads=[q32.d, cb.d], writes=[t1.d])
                qs = qsr.next()
                S.op("dve", lambda e: e.tensor_copy(out=qs[:, :ntok], in_=q32[:, :ntok]), reads=[q32.d], writes=[qs.d])
                pr = psrot.next()
                S.op("pe", lambda e: e.matmul(pr[:, :ntok], lhsT=swap[:], rhs=qs[:, :ntok], start=True, stop=True),
                     reads=[swap.d, qs.d], writes=[pr.d])
                t2 = t2r.next()
                S.op("dve", lambda e: e.tensor_tensor(out=t2[:, :ntok], in0=pr[:, :ntok], in1=sb[:, :ntok], op=ALU.mult),
                     reads=[pr.d, sb.d], writes=[t2.d])
                S.op("dve", lambda e: e.tensor_tensor(out=out_ap, in0=t1[:, :ntok], in1=t2[:, :ntok], op=ALU.add),
                     reads=[t1.d, t2.d], writes=[out_dep])

            ps = proj(wk, 0)
            if rope:
                rope_evac(ps, C.kT[:, col0:col0 + ntok], C.kT.d)
            else:
                S.op("act", lambda e: e.copy(out=C.kT[:, col0:col0 + ntok], in_=ps[:, :ntok]), reads=[ps.d], writes=[C.kT.d])
            if CUT <= 2:
                continue
            if kind == "own":
                qb = qblk.next()
                for j in range(4):
                    ps = proj(wq, j)
                    rope_evac(ps, qb[:, j, :], qb.d)
                S.dma("pool", C.q_d.ap[:, :, col0:col0 + 512], qb[:], reads=[qb.d], writes=C.q_d.dd(col0, col0 + 512), own=qb.d)
                zb = zblk.next()
                for j in range(4):
                    ps = proj(wz, j)
                    S.op("act", lambda e: e.activation(out=zb[:, j, :], in_=ps[:, :], func=AF.Silu), reads=[ps.d], writes=[zb.d])
                S.dma("pool", C.zs_d.ap[:, :, col0:col0 + 512], zb[:], reads=[zb.d], writes=C.zs_d.dd(col0, col0 + 512), own=zb.d)
            pb = preblk.next()
            for j in range(12):
                ps = proj(wg, j)
                if kind == "halo":
                    for hh in range(2):
                        S.op("dve", lambda e: e.tensor_scalar(out=pb[:, j, hh * 128:(hh + 1) * 128], in0=ps[:, hh * 128:(hh + 1) * 128],
                                                              scalar1=hv[:, hh:hh + 1], scalar2=None, op0=ALU.mult),
                             reads=[ps.d, hv.d], writes=[pb.d])
                else:
                    eng = "act" if j % 2 else "dve"
                    if eng == "act":
                        S.op("act", lambda e: e.copy(out=pb[:, j, :ntok], in_=ps[:, :ntok]), reads=[ps.d], writes=[pb.d])
                    else:
                        S.op("dve", lambda e: e.tensor_copy(out=pb[:, j, :ntok], in_=ps[:, :ntok]), reads=[ps.d], writes=[pb.d])
            if kind == "own":
                S.dma("pool", C.pre_d.ap[:, :, 128 + col0:128 + col0 + 512], pb[:], reads=[pb.d], writes=C.pre_d.dd(128 + col0, 128 + col0 + 512), own=pb.d)
            elif kind == "halo":
                S.dma("pool", C.pre_d.ap[:, :, 0:128], pb[:, :, 0:128], reads=[pb.d], writes=C.pre_d.dd(0, 128), own=pb.d)
                S.dma("pool", C.pre_d.ap[:, :, 128 + T:128 + T + 128], pb[:, :, 128:256], reads=[pb.d], writes=C.pre_d.dd(128 + T, 256 + T), own=pb.d)
            else:
                S.dma("pool", C.prec_d.ap[:, :, 1:257], pb[:, :, 0:256], reads=[pb.d], writes=C.prec_d.deps, own=pb.d)
            if CUT <= 3:
                continue
            for tt in range(ntok // 128):
                if kind == "own":
                    ti = bi * 4 + tt
                elif kind == "halo":
                    ti = 32 + tt
                else:
                    ti = 34 + tt
                pt = pst.next()
                for kc in range(8):
                    S.op("pe", lambda e: e.matmul(pt[:, 0:144], lhsT=hT[:, kc, tt * 128:(tt + 1) * 128], rhs=wvab[:, kc, :],
                                                  start=(kc == 0), stop=(kc == 7)),
                         reads=[wvab.d, hT.d], writes=[pt.d], sig=(kc == 7))
                VAR = os.environ.get("VAR", "")
                if "a" not in VAR:
                    S.op("act", lambda e: e.copy(out=C.V[:, ti, :], in_=pt[:, 0:128]), reads=[pt.d], writes=[C.V.d])
                if "b" not in VAR:
                    if True:
                        S.op("act", lambda e: e.copy(out=C.ab[:, ti, :], in_=pt[:, 128:144]), reads=[pt.d], writes=[C.ab.d])
                    else:
                        S.op("dve", lambda e: e.tensor_copy(out=C.ab[:, ti, :], in_=pt[:, 128:144]), reads=[pt.d] + ([C.V.d] if "s" in VAR else []), writes=[C.ab.d])
        S.barrier()
        S.stack = old


def phase2(S, C):
    with ExitStack() as ph:
        old = S.stack
        S.stack = ph
        amask = load_const(S, C, "amask")
        sinkrow = load_const(S, C, "sinkrow")
        esink = SBT(S, "esink", [64, 2, 512], F32)
        S.op("act", lambda e: e.activation(out=esink[:], in_=sinkrow[:], func=AF.Exp), reads=[sinkrow.d], writes=[esink.d])
        qring = Ring(S, "qld", [128, 4, 512], BF16, 2)
        pss = Ring(S, "pss", [128, 512], F32, 4, psum=True)
        pso = Ring(S, "pso", [64, 512], F32, 2, psum=True)
        psd = Ring(S, "psd", [64, 512], F32, 2, psum=True)
        ptr = Ring(S, "pT", [128, 512], BF16, 5)
        denr = Ring(S, "den", [64, 512], F32, 2)
        outr = Ring(S, "aout", [64, 2, 4, 512], BF16, 2)
        def kcols(ti):
            if ti < 32:
                return ti * 128
            return T + (ti - 32) * 128
        tasks = []
        for qblk_i in range(T // 512):
            for sub in range(4):
                qb = qblk_i * 4 + sub
                left = (32, 2) if qb == 0 else (qb - 1, 0)
                right = (33, 3) if qb == NT - 1 else (qb + 1, 1)
                keyblocks = [left, (qb, None), right, (34, None), (35, None)]
                for kap in range(2):
                    for n, (kt, mi) in enumerate(keyblocks):
                        tasks.append((qblk_i, sub, kap, n, kt, mi))
        state = {"qt": {}, "pT": {}, "po": None, "pd": None, "ot": {}}

        def emit_S(t):
            qblk_i, sub, kap, n, kt, mi = tasks[t]
            if qblk_i not in state["qt"]:
                qt = qring.next()
                S.dma("sp", qt[:], C.q_d.ap[:, :, qblk_i * 512:(qblk_i + 1) * 512], reads=C.q_d.dd(qblk_i * 512, qblk_i * 512 + 512), writes=[qt.d])
                state["qt"][qblk_i] = qt
            qt = state["qt"][qblk_i]
            ps = pss.next()
            kc0 = kcols(kt)
            S.op("pe", lambda e: e.matmul(ps[:].rearrange("p (g n) -> p g n", g=4),
                                          lhsT=C.kT[64 * kap:64 * kap + 64, kc0:kc0 + 128],
                                          rhs=qt[64 * kap:64 * kap + 64, :, sub * 128:(sub + 1) * 128],
                                          start=True, stop=True),
                 reads=[C.kT.d, qt.d], writes=[ps.d])
            pT = ptr.next()
            S.op("act", lambda e: e.activation(out=pT[:], in_=ps[:], func=AF.Exp, scale=0.125), reads=[ps.d], writes=[pT.d])
            if mi is not None:
                S.op("dve", lambda e: e.tensor_tensor(out=pT[:], in0=pT[:], in1=amask[:, mi, :], op=ALU.mult),
                     reads=[pT.d, amask.d], writes=[pT.d])
            state["pT"][t] = pT

        def emit_PV(t):
            qblk_i, sub, kap, n, kt, mi = tasks[t]
            pT = state["pT"].pop(t)
            if n == 0:
                state["po"], state["pd"] = pso.next(), psd.next()
            po, pd = state["po"], state["pd"]
            if qblk_i not in state["ot"]:
                state["ot"][qblk_i] = outr.next()
            ot = state["ot"][qblk_i]
            S.op("pe", lambda e: e.matmul(po[:], lhsT=C.V[:, kt, 64 * kap:64 * kap + 64], rhs=pT[:], start=(n == 0), stop=(n == 4)),
                 reads=[C.V.d, pT.d], writes=[po.d], sig=(n == 4))
            S.op("pe", lambda e: e.matmul(pd[:], lhsT=C.ones_bf[:, 0:64], rhs=pT[:], start=(n == 0), stop=(n == 4)),
                 reads=[C.ones_bf.d, pT.d], writes=[pd.d], sig=(n == 4))
            if n == 4:
                den = denr.next()
                S.op("dve", lambda e: e.tensor_tensor(out=den[:], in0=pd[:], in1=esink[:, kap, :], op=ALU.add),
                     reads=[pd.d, esink.d], writes=[den.d])
                S.op("dve", lambda e: e.reciprocal(out=den[:], in_=den[:]), reads=[den.d], writes=[den.d])
                S.op("dve", lambda e: e.tensor_tensor(out=ot[:, kap, :, sub * 128:(sub + 1) * 128],
                                                      in0=po[:].rearrange("p (g n) -> p g n", g=4),
                                                      in1=den[:].rearrange("p (g n) -> p g n", g=4), op=ALU.mult),
                     reads=[po.d, den.d], writes=[ot.d])
                if sub == 3 and kap == 1:
                    S.dma("pool", C.attn_d.ap[:, :, qblk_i * 512:(qblk_i + 1) * 512], ot[:].rearrange("p k g n -> p (k g) n"),
                          reads=[ot.d], writes=C.attn_d.dd(qblk_i * 512, qblk_i * 512 + 512), own=ot.d)
        LOOK = 3
        for t in range(min(LOOK, len(tasks))):
            emit_S(t)
        for t in range(len(tasks)):
            if t + LOOK < len(tasks):
                emit_S(t + LOOK)
            emit_PV(t)
        S.barrier()
        S.stack = old


class Slot:
    def __init__(self, ap, d):
        self.ap = ap
        self.d = d

    def __getitem__(self, idx):
        return self.ap[idx]


class SlotRing:
    def __init__(self, S, name, dtype, nslots, psum=False):
        per = 4 if dtype == F32 else 8
        if not psum:
            per = 1
        self.slots = []
        for b in range((nslots + per - 1) // per):
            t = SBT(S, "%s%d" % (name, b), [128, per, 128], dtype, psum=psum, ndeps=per)
            for i in range(per):
                self.slots.append(Slot(t[:, i, :], t.deps[0] if psum else t.deps[i]))
        self.slots = self.slots[:nslots]
        self.i = 0

    def next(self):
        s = self.slots[self.i % len(self.slots)]
        self.i += 1
        return s


class PairRing:
    def __init__(self, S, name, dtype, n):
        t = SBT(S, name, [128, n, 2, 128], dtype, psum=True)
        self.slots = [Slot(t[:, i, :, :], t.d) for i in range(n)]
        self.i = 0

    def next(self):
        s = self.slots[self.i % len(self.slots)]
        self.i += 1
        return s


def gdn_gates(S, C, G):
    ab = C.ab
    dtb = load_const(S, C, "gdtb")
    alog = load_const(S, C, "galog")
    cmat = load_const(S, C, "cmat")
    G.one_t = SBT(S, "one_t", [128, 1], F32)
    S.op("pool", lambda e: e.memset(G.one_t[:], 1.0), writes=[G.one_t.d])
    eal = SBT(S, "eal", [128, 8], F32)
    S.op("act", lambda e: e.activation(out=eal[:], in_=alog[:], func=AF.Exp), reads=[alog.d], writes=[eal.d])
    xg = SBT(S, "xg", [128, 36, 8], F32)
    S.op("dve", lambda e: e.tensor_tensor(out=xg[:], in0=ab[:, :, 0:8], in1=dtb[:, None, :].to_broadcast([128, 36, 8]), op=ALU.add),
         reads=[ab.d, dtb.d], writes=[xg.d])
    S.op("act", lambda e: e.activation(out=xg[:], in_=xg[:], func=AF.Exp), reads=[xg.d], writes=[xg.d])
    S.op("act", lambda e: e.activation(out=xg[:], in_=xg[:], func=AF.Ln, bias=G.one_t[:, 0:1], scale=1.0),
         reads=[xg.d, G.one_t.d], writes=[xg.d])
    g = SBT(S, "gg", [128, 36, 8], F32)
    S.op("dve", lambda e: e.scalar_tensor_tensor(out=g[:], in0=xg[:], scalar=-1.0, in1=eal[:, None, :].to_broadcast([128, 36, 8]),
                                                 op0=ALU.mult, op1=ALU.mult), reads=[xg.d, eal.d], writes=[g.d])
    G.beta = SBT(S, "beta", [128, 36, 8], F32)
    S.op("act", lambda e: e.activation(out=G.beta[:], in_=ab[:, :, 8:16], func=AF.Sigmoid), reads=[ab.d], writes=[G.beta.d])
    G.GT = SBT(S, "GT", [128, 36, 16], F32)
    _pst = ExitStack()
    _old = S.stack
    S.stack = _pst
    psg = Ring(S, "psg", [128, 18, 16], F32, 1, psum=True)
    S.stack = _old
    for half in range(2):
        pg = psg.next()
        for tl in range(18):
            ti = half * 18 + tl
            for m in range(4):
                dirn = m // 2
                S.op("pe", lambda e: e.matmul(pg[:, tl, m * 4:(m + 1) * 4], lhsT=cmat[:, m, :], rhs=g[:, ti, dirn * 4:(dirn + 1) * 4],
                                              start=True, stop=True),
                     reads=[cmat.d, g.d], writes=[pg.d], sig=(tl == 17 and m == 3))
        S.op("act", lambda e: e.copy(out=G.GT[:, half * 18:(half + 1) * 18, :], in_=pg[:]), reads=[pg.d], writes=[G.GT.d])
    S.barrier()
    _pst.close()
    G.negG = SBT(S, "negG", [128, 36, 16], F32)
    S.op("dve", lambda e: e.tensor_scalar(out=G.negG[:], in0=G.GT[:], scalar1=-1.0, scalar2=None, op0=ALU.mult),
         reads=[G.GT.d], writes=[G.negG.d])
    G.eGT = SBT(S, "eGT", [128, 36, 16], F32)
    S.op("act", lambda e: e.activation(out=G.eGT[:], in_=G.GT[:], func=AF.Exp), reads=[G.GT.d], writes=[G.eGT.d])
    G.bEG = SBT(S, "bEG", [128, 36, 8], F32)
    for dirn in range(2):
        S.op("dve", lambda e: e.tensor_tensor(out=G.bEG[:, :, dirn * 4:(dirn + 1) * 4], in0=G.beta[:, :, dirn * 4:(dirn + 1) * 4],
                                              in1=G.eGT[:, :, dirn * 8:dirn * 8 + 4], op=ALU.mult),
             reads=[G.beta.d, G.eGT.d], writes=[G.bEG.d])


def gdn_setup(S, C, G):
    G.ident_f = load_const(S, C, "ident_f")
    G.ones_f = load_const(S, C, "ones_f")
    G.gmask = load_const(S, C, "gmask")
    G.lvl = load_const(S, C, "lvlmask")
    gconv = load_const(S, C, "gconv")
    G.diag = SBT(S, "diag", [128, 36, 128], BF16)
    for c in range(12):
        for k in range(3):
            S.op("dve" if (c + k) % 2 else "pool",
                 lambda e: e.tensor_scalar(out=G.diag[:, c * 3 + k, :], in0=C.ident_bf[:], scalar1=gconv[:, c, k:k + 1], scalar2=None,
                                           op0=ALU.mult), reads=[C.ident_bf.d, gconv.d], writes=[G.diag.d])
    G.psC = SlotRing(S, "psC", F32, 8, psum=True)
    G.psFs = [SlotRing(S, "psF%d" % i, F32, 4, psum=True) for i in range(4)]
    G.psBs = [SlotRing(S, "psB%d" % i, BF16, 8, psum=True) for i in range(2)]
    G.psF = G.psFs[0]
    G.S32 = SBT(S, "S32", [128, 8, 128], F32, ndeps=8)
    G.P32 = SBT(S, "P32", [128, 8, 128], F32, ndeps=8)
    G.Sbf = SBT(S, "Sbf", [128, 8, 128], BF16, ndeps=8)
    G.Pbf = SBT(S, "Pbf", [128, 8, 128], BF16, ndeps=8)


def gdn_pass(S, C, G, dirn, heads, tiles, with_P, with_out, pre_src, tag, psF, psB):
    f32r = lambda nm, n=2: SlotRing(S, tag + nm, F32, n)
    bfr = lambda nm, n=2: SlotRing(S, tag + nm, BF16, n)
    prer = Ring(S, tag + "prew", [128, 12, 130], BF16, 1)
    csr = Ring(S, tag + "cs", [128, 128], F32, 2)
    sqr = Ring(S, tag + "sq", [128, 128], BF16, 2)
    rnr = Ring(S, tag + "rn", [128, 128], F32, 2)
    qkb = Ring(S, tag + "qkb", [128, 8, 128], BF16, 1)
    vbk = Ring(S, tag + "vbk", [128, 4, 128], BF16, 1)
    oblk = Ring(S, tag + "oblk", [128, 4, 128], F32, 1)
    rblk = Ring(S, tag + "rblk", [128, 4, 128], BF16, 1)
    r_ng, r_gs, r_dm, r_e, r_eg, r_ta, r_tq = (f32r(n) for n in ("ng", "gs", "dm", "e", "eg", "ta", "tq"))
    r_q = bfr("Q", 2)
    r_p = bfr("P", 2)
    r_m = bfr("M", 5)
    r_y = bfr("Y", 4)
    r_a, r_b, r_qk, r_qkT, r_kbg, r_kt, r_vb, r_qg, r_w, r_vn, r_vnP = (bfr(n) for n in
                                                                      ("A", "B", "qk", "qkT", "kbg", "kt", "vb", "qg", "w", "vn", "vnP"))
    gi = dirn * 8
    mi = dirn * 2
    last_col = 127 if dirn == 0 else 0
    hlo, hhi = min(heads), max(heads) + 1
    for bi in range(len(tiles)):
        btiles = tiles[bi:bi + 1]
        lo = btiles[0][1]
        ntok = 128
        src_ap, src_deps = pre_src(lo, ntok)
        pw = prer.next()
        S.dma("sp", pw[:, :, :ntok + 2], src_ap, reads=src_deps, writes=[pw.d])
        qk = qkb.next()
        vb_ = vbk.next()
        for c in [h for h in heads] + [4 + h for h in heads] + [8 + h for h in heads]:
            pc = G.psC.next()
            for k in range(3):
                S.op("pe", lambda e: e.matmul(pc[:, :ntok], lhsT=G.diag[:, c * 3 + k, :], rhs=pw[:, c, k:k + ntok],
                                              start=(k == 0), stop=(k == 2)),
                     reads=[G.diag.d, pw.d], writes=[pc.d], sig=(k == 2))
            if c >= 8:
                S.op("act", lambda e: e.activation(out=vb_[:, c - 8, :ntok], in_=pc[:, :ntok], func=AF.Silu), reads=[pc.d], writes=[vb_.d])
                yield
                continue
            cs = csr.next()
            S.op("act", lambda e: e.activation(out=cs[:, :ntok], in_=pc[:, :ntok], func=AF.Silu), reads=[pc.d], writes=[cs.d])
            sq = sqr.next()
            S.op("dve", lambda e: e.tensor_tensor(out=sq[:, :ntok], in0=cs[:, :ntok], in1=cs[:, :ntok], op=ALU.mult),
                 reads=[cs.d], writes=[sq.d])
            yield
            pn = G.psC.next()
            S.op("pe", lambda e: e.matmul(pn[:, :ntok], lhsT=C.ones_bf[:], rhs=sq[:, :ntok], start=True, stop=True),
                 reads=[C.ones_bf.d, sq.d], writes=[pn.d])
            rn = rnr.next()
            S.op("act", lambda e: e.activation(out=rn[:, :ntok], in_=pn[:, :ntok], func=AF.Ln, bias=C.eps_t[:, 0:1], scale=1.0),
                 reads=[pn.d, C.eps_t.d], writes=[rn.d])
            S.op("act", lambda e: e.activation(out=rn[:, :ntok], in_=rn[:, :ntok], func=AF.Exp, scale=-0.5), reads=[rn.d], writes=[rn.d])
            scl = (128.0 ** -0.5) if c < 4 else 1.0
            S.op("dve", lambda e: e.scalar_tensor_tensor(out=qk[:, c, :ntok], in0=cs[:, :ntok], scalar=scl, in1=rn[:, :ntok],
                                                         op0=ALU.mult, op1=ALU.mult), reads=[cs.d, rn.d], writes=[qk.d])
            yield
        ob = oblk.next() if with_out else None
        rb = rblk.next() if with_out else None
        for ti, col in btiles:
            c0 = col - lo
            cols = slice(c0, c0 + 128)
            for h in heads:
                sidx = dirn * 4 + h
                kT_t = qk[:, 4 + h, cols]
                qT_t = qk[:, h, cols]
                vT_t = vb_[:, h, cols]
                Gcol = G.GT[:, ti, gi + h:gi + h + 1]
                nGcol = G.negG[:, ti, gi + h:gi + h + 1]
                betac = G.beta[:, ti, dirn * 4 + h:dirn * 4 + h + 1]
                bEGc = G.bEG[:, ti, dirn * 4 + h:dirn * 4 + h + 1]
                etailc = G.eGT[:, ti, gi + 4 + h:gi + 4 + h + 1]
                kk = psF.next()
                S.op("pe", lambda e: e.matmul(kk[:], lhsT=kT_t, rhs=kT_t, start=True, stop=True), reads=[qk.d], writes=[kk.d])
                yield
                qkp = psF.next()
                S.op("pe", lambda e: e.matmul(qkp[:], lhsT=qT_t, rhs=kT_t, start=True, stop=True), reads=[qk.d], writes=[qkp.d])
                yield
                ktok = psB.next()
                S.op("pe", lambda e: e.transpose(ktok[:], kT_t, C.ident_bf[:]), reads=[qk.d, C.ident_bf.d], writes=[ktok.d])
                yield
                vtok = psB.next()
                S.op("pe", lambda e: e.transpose(vtok[:], vT_t, C.ident_bf[:]), reads=[vb_.d, C.ident_bf.d], writes=[vtok.d])
                ng = r_ng.next()
                S.op("dve", lambda e: e.tensor_scalar(out=ng[:], in0=G.ones_f[:], scalar1=nGcol, scalar2=None, op0=ALU.mult),
                     reads=[G.ones_f.d, G.negG.d], writes=[ng.d])
                yield
                gbc = psF.next()
                S.op("pe", lambda e: e.matmul(gbc[:], lhsT=ng[:], rhs=G.ident_f[:], start=True, stop=True),
                     reads=[ng.d, G.ident_f.d], writes=[gbc.d])
                gs = r_gs.next()
                S.op("dve", lambda e: e.tensor_copy(out=gs[:], in_=gbc[:]), reads=[gbc.d], writes=[gs.d])
                dm = r_dm.next()
                S.op("dve", lambda e: e.tensor_scalar(out=dm[:], in0=gs[:], scalar1=Gcol, scalar2=0.0, op0=ALU.add, op1=ALU.min),
                     reads=[gs.d, G.GT.d], writes=[dm.d])
                E = r_e.next()
                S.op("act", lambda e: e.activation(out=E[:], in_=dm[:], func=AF.Exp), reads=[dm.d], writes=[E.d])
                eg = r_eg.next()
                S.op("act", lambda e: e.activation(out=eg[:], in_=gs[:], func=AF.Exp, scale=-1.0), reads=[gs.d], writes=[eg.d])
                ta = r_ta.next()
                S.op("dve", lambda e: e.tensor_tensor(out=ta[:], in0=kk[:], in1=E[:], op=ALU.mult), reads=[kk.d, E.d], writes=[ta.d])
                A = r_q.next()
                S.op("dve", lambda e: e.scalar_tensor_tensor(out=A[:], in0=ta[:], scalar=betac, in1=G.gmask[:, mi + 1, :],
                                                             op0=ALU.mult, op1=ALU.mult),
                     reads=[ta.d, G.beta.d, G.gmask.d], writes=[A.d])
                tq = r_tq.next()
                S.op("dve", lambda e: e.tensor_tensor(out=tq[:], in0=qkp[:], in1=E[:], op=ALU.mult), reads=[qkp.d, E.d], writes=[tq.d])
                qkm = r_qk.next()
                S.op("dve", lambda e: e.tensor_tensor(out=qkm[:], in0=tq[:], in1=G.gmask[:, mi, :], op=ALU.mult),
                     reads=[tq.d, G.gmask.d], writes=[qkm.d])
                yield
                bps = psB.next()
                S.op("pe", lambda e: e.transpose(bps[:], A[:], C.ident_bf[:]), reads=[A.d, C.ident_bf.d], writes=[bps.d])
                Bm = r_p.next()
                S.op("act", lambda e: e.copy(out=Bm[:], in_=bps[:]), reads=[bps.d], writes=[Bm.d])
                yield
                qps = psB.next()
                S.op("pe", lambda e: e.transpose(qps[:], qkm[:], C.ident_bf[:]), reads=[qkm.d, C.ident_bf.d], writes=[qps.d])
                qkT = r_qkT.next()
                S.op("act", lambda e: e.copy(out=qkT[:], in_=qps[:]), reads=[qps.d], writes=[qkT.d])
                kbg = r_kbg.next()
                S.op("act", lambda e: e.activation(out=kbg[:], in_=ktok[:], func=AF.Identity, scale=bEGc), reads=[ktok.d, G.bEG.d], writes=[kbg.d])
                kt = r_kt.next()
                S.op("act", lambda e: e.activation(out=kt[:], in_=ktok[:], func=AF.Identity, scale=etailc), reads=[ktok.d, G.eGT.d], writes=[kt.d])
                vbt = r_vb.next()
                S.op("act", lambda e: e.activation(out=vbt[:], in_=vtok[:], func=AF.Identity, scale=betac), reads=[vtok.d, G.beta.d], writes=[vbt.d])
                qg = r_qg.next()
                S.op("dve", lambda e: e.tensor_tensor(out=qg[:], in0=qT_t, in1=eg[:], op=ALU.mult), reads=[qk.d, eg.d], writes=[qg.d])
                lo_i, up_i = (0, 1) if dirn == 0 else (1, 0)
                Tt = r_m.next()
                Xt = r_m.next()
                tmp1 = r_y.next()
                S.op("dve", lambda e: e.tensor_tensor(out=tmp1[:], in0=A[:], in1=G.lvl[:, lo_i, :], op=ALU.mult),
                     reads=[A.d, G.lvl.d], writes=[tmp1.d])
                S.op("dve", lambda e: e.tensor_tensor(out=Tt[:], in0=C.ident_bf[:], in1=tmp1[:], op=ALU.subtract),
                     reads=[C.ident_bf.d, tmp1.d], writes=[Tt.d])
                tmp2 = r_y.next()
                S.op("dve", lambda e: e.tensor_tensor(out=tmp2[:], in0=Bm[:], in1=G.lvl[:, up_i, :], op=ALU.mult),
                     reads=[Bm.d, G.lvl.d], writes=[tmp2.d])
                S.op("dve", lambda e: e.tensor_tensor(out=Xt[:], in0=C.ident_bf[:], in1=tmp2[:], op=ALU.subtract),
                     reads=[C.ident_bf.d, tmp2.d], writes=[Xt.d])
                for li in range(1, 7):
                    last = (li == 6)
                    yield
                    yp = psF.next()
                    S.op("pe", lambda e: e.matmul(yp[:], lhsT=A[:], rhs=Xt[:], start=True, stop=True), reads=[A.d, Xt.d], writes=[yp.d])
                    Y = r_y.next()
                    S.op("dve", lambda e: e.tensor_tensor(out=Y[:], in0=yp[:], in1=G.lvl[:, li * 2 + up_i, :], op=ALU.mult),
                         reads=[yp.d, G.lvl.d], writes=[Y.d])
                    if not last:
                        yield
                        ypp = psF.next()
                        S.op("pe", lambda e: e.matmul(ypp[:], lhsT=Bm[:], rhs=Tt[:], start=True, stop=True), reads=[Bm.d, Tt.d], writes=[ypp.d])
                        Y2 = r_y.next()
                        S.op("dve", lambda e: e.tensor_tensor(out=Y2[:], in0=ypp[:], in1=G.lvl[:, li * 2 + lo_i, :], op=ALU.mult),
                             reads=[ypp.d, G.lvl.d], writes=[Y2.d])
                    yield
                    zp = psF.next()
                    S.op("pe", lambda e: e.matmul(zp[:], lhsT=Tt[:], rhs=Y[:], start=True, stop=True), reads=[Tt.d, Y.d], writes=[zp.d])
                    Xn = r_m.next()
                    S.op("dve", lambda e: e.tensor_tensor(out=Xn[:], in0=Xt[:], in1=zp[:], op=ALU.subtract), reads=[zp.d, Xt.d], writes=[Xn.d])
                    if not last:
                        yield
                        zpp = psF.next()
                        S.op("pe", lambda e: e.matmul(zpp[:], lhsT=Xt[:], rhs=Y2[:], start=True, stop=True), reads=[Xt.d, Y2.d], writes=[zpp.d])
                        Tn = r_m.next()
                        S.op("dve", lambda e: e.tensor_tensor(out=Tn[:], in0=Tt[:], in1=zpp[:], op=ALU.subtract), reads=[zpp.d, Tt.d], writes=[Tn.d])
                        Tt = Tn
                    Xt = Xn
                M = Xt
                TT = M
                yield
                wps = psF.next()
                S.op("pe", lambda e: e.matmul(wps[:], lhsT=kbg[:], rhs=TT[:], start=True, stop=True), reads=[kbg.d, TT.d], writes=[wps.d])
                wn = r_w.next()
                S.op("act", lambda e: e.activation(out=wn[:], in_=wps[:], func=AF.Identity, scale=-1.0), reads=[wps.d], writes=[wn.d])
                Sb, Sd = G.Sbf[:, sidx, :], G.Sbf.deps[sidx]
                Pb, Pd = G.Pbf[:, sidx, :], G.Pbf.deps[sidx]
                S3, S3d = G.S32[:, sidx, :], G.S32.deps[sidx]
                P3, P3d = G.P32[:, sidx, :], G.P32.deps[sidx]
                yield
                vps = psF.next()
                S.op("pe", lambda e: e.matmul(vps[:], lhsT=TT[:], rhs=vbt[:], start=True, stop=False), reads=[TT.d, vbt.d], writes=[vps.d], sig=False)
                S.op("pe", lambda e: e.matmul(vps[:], lhsT=wn[:], rhs=Sb, start=False, stop=True), reads=[wn.d, Sd], writes=[vps.d])
                vn = r_vn.next()
                S.op("act", lambda e: e.copy(out=vn[:], in_=vps[:]), reads=[vps.d], writes=[vn.d])
                if with_P:
                    yield
                    vpp = psF.next()
                    S.op("pe", lambda e: e.matmul(vpp[:], lhsT=wn[:], rhs=Pb, start=True, stop=True), reads=[wn.d, Pd], writes=[vpp.d])
                    vnP = r_vnP.next()
                    S.op("dve", lambda e: e.tensor_copy(out=vnP[:], in_=vpp[:]), reads=[vpp.d], writes=[vnP.d])
                if with_out:
                    yield
                    ops_ = psF.next()
                    S.op("pe", lambda e: e.matmul(ops_[:], lhsT=Sb, rhs=qg[:], start=True, stop=False), reads=[Sd, qg.d], writes=[ops_.d], sig=False)
                    S.op("pe", lambda e: e.matmul(ops_[:], lhsT=vn[:], rhs=qkT[:], start=False, stop=True), reads=[vn.d, qkT.d], writes=[ops_.d])
                    S.op("act", lambda e: e.copy(out=ob[:, h, cols], in_=ops_[:]), reads=[ops_.d], writes=[ob.d])
                    yield
                    rps = psF.next()
                    S.op("pe", lambda e: e.matmul(rps[:], lhsT=Pb, rhs=qg[:], start=True, stop=False), reads=[Pd, qg.d], writes=[rps.d], sig=False)
                    S.op("pe", lambda e: e.matmul(rps[:], lhsT=vnP[:], rhs=qkT[:], start=False, stop=True), reads=[vnP.d, qkT.d], writes=[rps.d])
                    S.op("dve", lambda e: e.tensor_copy(out=rb[:, h, cols], in_=rps[:]), reads=[rps.d], writes=[rb.d])
                egl = eg[:, last_col:last_col + 1]
                yield
                sps = psF.next()
                S.op("pe", lambda e: e.matmul(sps[:], lhsT=kt[:], rhs=vn[:], start=True, stop=True), reads=[kt.d, vn.d], writes=[sps.d])
                S.op("dve", lambda e: e.scalar_tensor_tensor(out=S3, in0=S3, scalar=egl, in1=sps[:], op0=ALU.mult, op1=ALU.add),
                     reads=[S3d, eg.d, sps.d], writes=[S3d])
                S.op("act", lambda e: e.copy(out=Sb, in_=S3), reads=[S3d], writes=[Sd])
                if with_P:
                    yield
                    pps = psF.next()
                    S.op("pe", lambda e: e.matmul(pps[:], lhsT=kt[:], rhs=vnP[:], start=True, stop=True), reads=[kt.d, vnP.d], writes=[pps.d])
                    S.op("dve", lambda e: e.scalar_tensor_tensor(out=P3, in0=P3, scalar=egl, in1=pps[:], op0=ALU.mult, op1=ALU.add),
                         reads=[P3d, eg.d, pps.d], writes=[P3d])
                    S.op("act", lambda e: e.copy(out=Pb, in_=P3), reads=[P3d], writes=[Pd])
                yield
        if with_out:
            S.dma("pool", C.o0_d[dirn].ap[:, hlo:hhi, lo:lo + ntok], ob[:, hlo:hhi, :ntok], reads=[ob.d], writes=C.o0_d[dirn].dd(lo, lo + ntok), own=ob.d)
            S.dma("pool", C.R_d[dirn].ap[:, hlo:hhi, lo:lo + ntok], rb[:, hlo:hhi, :ntok], reads=[rb.d], writes=C.R_d[dirn].dd(lo, lo + ntok), own=rb.d)


def phase3(S, C):
    G = Ctx()
    C.sctx = SBT(S, "sctx", [128, 8, 128], F32)
    with ExitStack() as ph:
        old = S.stack
        S.stack = ph
        gdn_gates(S, C, G)
        gdn_setup(S, C, G)
        zt = SBT(S, "zt", [128, 12, 2], BF16)
        S.op("pool", lambda e: e.memset(zt[:], 0.0), writes=[zt.d])
        S.dma("pool", C.prec_d.ap[:, :, 0:1], zt[:, :, 0:1], reads=[zt.d], writes=C.prec_d.deps, own=zt.d, allow_slow_non_contiguous=True)
        S.dma("pool", C.prec_d.ap[:, :, 257:258], zt[:, :, 1:2], reads=[zt.d], writes=C.prec_d.deps, own=zt.d, allow_slow_non_contiguous=True)
        for i in range(8):
            S.op("pool", lambda e: e.memset(G.S32[:, i, :], 0.0), writes=[G.S32.deps[i]])
            S.op("pool", lambda e: e.memset(G.Sbf[:, i, :], 0.0), writes=[G.Sbf.deps[i]])

        def ctx_src(lo, ntok):
            return C.prec_d.ap[:, :, lo:lo + ntok + 2], C.prec_d.deps

        def lat_src(lo, ntok):
            return C.pre_d.ap[:, :, 128 + lo - 1:128 + lo + ntok + 1], C.pre_d.dd(128 + lo - 1, 128 + lo + ntok + 1)

        def run(gens):
            gens = list(gens)
            while gens:
                for g in list(gens):
                    try:
                        next(g)
                    except StopIteration:
                        gens.remove(g)
        with ExitStack() as ph2:
            S.stack = ph2
            HG = [(0, 1), (2, 3)]
            run([gdn_pass(S, C, G, d_, HG[k_], [(34, 0), (35, 128)] if d_ == 0 else [(35, 128), (34, 0)], False, False, ctx_src,
                          "c%d%d" % (d_, k_), G.psFs[d_ * 2 + k_], G.psBs[d_]) for d_ in range(2) for k_ in range(2)])
            for i in range(8):
                S.op("dve", lambda e: e.tensor_copy(out=C.sctx[:, i, :], in_=G.S32[:, i, :]), reads=[G.S32.deps[i]], writes=[C.sctx.d])
                S.op("pool", lambda e: e.memset(G.S32[:, i, :], 0.0), writes=[G.S32.deps[i]])
                S.op("pool", lambda e: e.memset(G.Sbf[:, i, :], 0.0), writes=[G.Sbf.deps[i]])
                S.op("pool", lambda e: e.tensor_copy(out=G.P32[:, i, :], in_=G.ident_f[:]), reads=[G.ident_f.d], writes=[G.P32.deps[i]])
                S.op("pool", lambda e: e.tensor_copy(out=G.Pbf[:, i, :], in_=C.ident_bf[:]), reads=[C.ident_bf.d], writes=[G.Pbf.deps[i]])
            S.barrier()
            S.stack = ph
        NTL = int(os.environ.get("NTL", str(NT)))
        with ExitStack() as ph2:
            S.stack = ph2
            ft = [(ti, ti * 128) for ti in range(NTL)]
            bt = [(ti, ti * 128) for ti in reversed(range(NT - NTL, NT))]
            gens = [gdn_pass(S, C, G, d_, HG[k_], ft if d_ == 0 else bt, True, True, lat_src,
                             "l%d%d" % (d_, k_), G.psFs[d_ * 2 + k_], G.psBs[d_]) for d_ in range(2) for k_ in range(2)]
            if getattr(C, "preconverted", False):
                gens.append(conv_gen(S, C))
            run(gens)
            xs = SBT(S, "xs", [128, 8, 2, 128], F32)
            for i in range(8):
                S.op("act", lambda e: e.copy(out=xs[:, i, 0, :], in_=G.S32[:, i, :]), reads=[G.S32.deps[i]], writes=[xs.d])
                pp = G.psF.next()
                S.op("pe", lambda e: e.matmul(pp[:], lhsT=G.P32[:, i, :], rhs=G.ident_f[:], start=True, stop=True),
                     reads=[G.P32.deps[i], G.ident_f.d], writes=[pp.d])
                S.op("act", lambda e: e.copy(out=xs[:, i, 1, :], in_=pp[:]), reads=[pp.d], writes=[xs.d])
            S.dma("pool", C.xch_d.ap, xs[:], reads=[xs.d], writes=C.xch_d.deps, own=xs.d)
            S.dma("pool", C.sctx_d.ap, C.sctx[:], reads=[C.sctx.d], writes=C.sctx_d.deps, own=C.sctx.d)
            S.barrier()
            S.stack = ph
        S.stack = old


def convert_ffn_weights(S, C, l):
    with ExitStack() as ph:
        old = S.stack
        S.stack = ph
        st = Ring(S, "cvst", [128, 22, 128], F32, 2)
        sb = Ring(S, "cvsb", [128, 22, 128], BF16, 2)
        k = 0
        for j in range(NJ):
            for gi, nm in enumerate(("ffn_w_gate", "ffn_w_up")):
                s_, b_ = st.next(), sb.next()
                S.dma("sp", s_[:, 0:8, :], C.inp[nm][l, :, j * 128:(j + 1) * 128].rearrange("(kc p) n -> p kc n", p=128), writes=[s_.d])
                S.op(("pool", "dve", "act")[k % 3], (lambda e: e.tensor_copy(out=b_[:, 0:8, :], in_=s_[:, 0:8, :])) if k % 3 != 2 else
                     (lambda e: e.copy(out=b_[:, 0:8, :], in_=s_[:, 0:8, :])), reads=[s_.d], writes=[b_.d])
                S.dma("pool", C.wgu_d.ap[j, :, gi, :, :], b_[:, 0:8, :], reads=[b_.d], writes=C.wgu_d.deps, own=b_.d)
                k += 1
        for n in range(8):
            s_, b_ = st.next(), sb.next()
            S.dma("sp", s_[:], C.inp["ffn_w_down"][l, :, n * 128:(n + 1) * 128].rearrange("(j p) n -> p j n", p=128), writes=[s_.d])
            S.op(("pool", "dve", "act")[k % 3], (lambda e: e.tensor_copy(out=b_[:], in_=s_[:])) if k % 3 != 2 else
                 (lambda e: e.copy(out=b_[:], in_=s_[:])), reads=[s_.d], writes=[b_.d])
            S.dma("pool", C.wd_d.ap[n], b_[:], reads=[b_.d], writes=C.wd_d.deps, own=b_.d)
            k += 1
        S.barrier()
        S.stack = old


def rstd_of(S, C, yT, ntok, R):
    sq = R["sq"].next()
    for c in range(8):
        S.op("dve", lambda e: e.tensor_tensor(out=sq[:, c, :ntok], in0=yT[:, c, :ntok], in1=yT[:, c, :ntok], op=ALU.mult),
             reads=[yT.d], writes=[sq.d])
    psn = R["ps"].next()
    for c in range(8):
        S.op("pe", lambda e: e.matmul(psn[:, :ntok], lhsT=C.ones_bf[:], rhs=sq[:, c, :ntok], start=(c == 0), stop=(c == 7)),
             reads=[C.ones_bf.d, sq.d], writes=[psn.d], sig=(c == 7))
    rstd = R["rstd"].next()
    S.op("act", lambda e: e.activation(out=rstd[:, :ntok], in_=psn[:, :ntok], func=AF.Ln, bias=C.eps_t[:, 0:1], scale=1.0 / D),
         reads=[psn.d, C.eps_t.d], writes=[rstd.d])
    S.op("act", lambda e: e.activation(out=rstd[:, :ntok], in_=rstd[:, :ntok], func=AF.Exp, scale=-0.5), reads=[rstd.d], writes=[rstd.d])
    return rstd


def postnorm_residual(S, C, xb, yT, gg, R):
    rstd = rstd_of(S, C, yT, 512, R)
    for c in range(8):
        tmp = R["tmp"].next()
        S.op("dve", lambda e: e.scalar_tensor_tensor(out=tmp[:], in0=yT[:, c, :], scalar=gg[:, c:c + 1], in1=rstd[:], op0=ALU.mult, op1=ALU.mult),
             reads=[yT.d, gg.d, rstd.d], writes=[tmp.d])
        S.op("dve", lambda e: e.tensor_tensor(out=xb[:, c, :], in0=xb[:, c, :], in1=tmp[:], op=ALU.add), reads=[xb.d, tmp.d], writes=[xb.d])


def ffn_block(S, C, xb, l, R, F):
    hT = F["hT"]
    norm_mod_block(S, C, xb, 512, C.vec[("a2", l)], C.vec[("b2", l)], hT, R)
    actT = F["actT"]
    for j in range(NJ):
        w = F["wgu"].next()
        wgu_d = C.wgu_d[l] if isinstance(C.wgu_d, list) else C.wgu_d
        S.dma("sp", w[:], wgu_d.ap[j], reads=wgu_d.deps, writes=[w.d])
        pg, pu = F["psg"].next(), F["psu"].next()
        for gi, ps in ((0, pg), (1, pu)):
            for kc in range(8):
                S.op("pe", lambda e: e.matmul(ps[:], lhsT=w[:, gi, kc, :], rhs=hT[:, kc, :], start=(kc == 0), stop=(kc == 7)),
                     reads=[w.d, hT.d], writes=[ps.d], sig=(kc == 7))
        sg = F["sg"].next()
        S.op("act", lambda e: e.activation(out=sg[:], in_=pg[:], func=AF.Silu), reads=[pg.d], writes=[sg.d])
        S.op("dve", lambda e: e.tensor_tensor(out=actT[:, j, :], in0=pu[:], in1=sg[:], op=ALU.mult), reads=[pu.d, sg.d], writes=[actT.d])
    fT = F["fT"]
    for n in range(8):
        w = F["wd"].next()
        wd_d = C.wd_d[l] if isinstance(C.wd_d, list) else C.wd_d
        S.dma("sp", w[:], wd_d.ap[n], reads=wd_d.deps, writes=[w.d])
        ps = F["psg"].next()
        for j in range(NJ):
            S.op("pe", lambda e: e.matmul(ps[:], lhsT=w[:, j, :], rhs=actT[:, j, :], start=(j == 0), stop=(j == NJ - 1)),
                 reads=[w.d, actT.d], writes=[ps.d], sig=(j == NJ - 1))
        S.op("act", lambda e: e.copy(out=fT[:, n, :], in_=ps[:]), reads=[ps.d], writes=[fT.d])
    postnorm_residual(S, C, xb, fT, C.vec[("gg2", l)], R)


def common_rings(S, deep=False):
    R = {"sq": Ring(S, "sq", [128, 8, 512], BF16, 1), "ps": Ring(S, "psn", [128, 512], F32, 1, psum=True),
         "rstd": Ring(S, "rstd", [128, 512], F32, 2), "tmp": Ring(S, "ntmp", [128, 512], F32, 3)}
    F = {"hT": SBT(S, "hT", [128, 8, 512], BF16), "actT": SBT(S, "actT", [128, NJ, 512], BF16),
         "fT": SBT(S, "fT", [128, 8, 512], F32), "wgu": Ring(S, "wgu", [128, 2, 8, 128], BF16, 4 if deep else 2),
         "wd": Ring(S, "wd", [128, NJ, 128], BF16, 3 if deep else 2), "psg": Ring(S, "psg", [128, 512], F32, 3, psum=True),
         "psu": Ring(S, "psu", [128, 512], F32, 3, psum=True), "sg": Ring(S, "sg", [128, 512], F32, 3 if deep else 2)}
    return R, F


def phaseB(S, C):
    if not getattr(C, "preconverted", False):
        convert_ffn_weights(S, C, 0)
    ident_f = load_const(S, C, "ident_f")
    gnormg = load_const(S, C, "gnormg")
    sel = load_const(S, C, "sel")
    Sst = SBT(S, "Sst", [128, 8, 128], BF16)
    with ExitStack() as ph:
        old = S.stack
        S.stack = ph
        xall = SBT(S, "xall", [128, 4, 8, 2, 128], F32)
        S.dma("sp", xall[:], C.inp["xall"], writes=[xall.d])
        sctx = SBT(S, "sctxB", [128, 8, 128], F32)
        S.dma("sp", sctx[:], C.inp["sctx_in"], writes=[sctx.d])
        cand = SBT(S, "cand", [128, 4, 128], F32, ndeps=4)
        acc = SBT(S, "accS", [128, 128], F32)
        psc = Ring(S, "psc", [128, 128], F32, 2, psum=True)
        for sidx in range(8):
            dirn = sidx // 4
            order = [0, 1, 2, 3] if dirn == 0 else [3, 2, 1, 0]
            first = order[0]
            S.op("dve", lambda e: e.tensor_copy(out=cand[:, first, :], in_=sctx[:, sidx, :]), reads=[sctx.d], writes=[cand.deps[first]])
            for a_, b_ in zip(order[:-1], order[1:]):
                ps = psc.next()
                S.op("pe", lambda e: e.matmul(ps[:], lhsT=xall[:, a_, sidx, 1, :], rhs=cand[:, a_, :], start=True, stop=True),
                     reads=[xall.d, cand.deps[a_]], writes=[ps.d])
                S.op("dve", lambda e: e.tensor_tensor(out=cand[:, b_, :], in0=ps[:], in1=xall[:, a_, sidx, 0, :], op=ALU.add),
                     reads=[ps.d, xall.d], writes=[cand.deps[b_]])
            S.op("dve", lambda e: e.tensor_scalar(out=acc[:], in0=cand[:, 0, :], scalar1=sel[:, 0:1], scalar2=None, op0=ALU.mult),
                 reads=[cand.deps[0], sel.d], writes=[acc.d])
            for s_ in range(1, 4):
                S.op("dve", lambda e: e.scalar_tensor_tensor(out=acc[:], in0=cand[:, s_, :], scalar=sel[:, s_:s_ + 1], in1=acc[:],
                                                             op0=ALU.mult, op1=ALU.add), reads=[cand.deps[s_], sel.d, acc.d], writes=[acc.d])
            S.op("dve", lambda e: e.tensor_copy(out=Sst[:, sidx, :], in_=acc[:]), reads=[acc.d], writes=[Sst.d])
        S.barrier()
        S.stack = old
    with ExitStack() as ph:
        old = S.stack
        S.stack = ph
        wst = Ring(S, "wst", [128, 8, 128], F32, 2)
        woa = SBT(S, "woa", [64, 8, 1024], BF16)
        wog = SBT(S, "wog", [128, 4, 1024], BF16)
        for c0 in range(0, 1024, 128):
            st = wst.next()
            S.dma("sp", st[0:64, :, :], C.inp["hy_w_out"][0:512, c0:c0 + 128].rearrange("(kg d) n -> d kg n", d=64), writes=[st.d])
            S.op("pool", lambda e: e.tensor_copy(out=woa[:, :, c0:c0 + 128], in_=st[0:64, :, :]), reads=[st.d], writes=[woa.d])
            st = wst.next()
            S.dma("sp", st[:, 0:4, :], C.inp["hy_w_out"][512:1024, c0:c0 + 128].rearrange("(h e) n -> e h n", e=128), writes=[st.d])
            S.op("pool", lambda e: e.tensor_copy(out=wog[:, :, c0:c0 + 128], in_=st[:, 0:4, :]), reads=[st.d], writes=[wog.d])
        cvs = Ring(S, "cvs2", [128, 8, 128], F32, 2)
        cvb = Ring(S, "cvb2", [128, 8, 128], BF16, 2)
        for j in range(0 if getattr(C, "preconverted", False) else 24):
            s_, b_ = cvs.next(), cvb.next()
            S.dma("sp", s_[:], C.inp["sc_w_in"][:, j * 128:(j + 1) * 128].rearrange("(kc p) n -> p kc n", p=128), writes=[s_.d])
            S.op("pool" if j % 2 else "dve", lambda e: e.tensor_copy(out=b_[:], in_=s_[:]), reads=[s_.d], writes=[b_.d])
            S.dma("pool", C.wsc_d.ap[j], b_[:], reads=[b_.d], writes=C.wsc_d.deps, own=b_.d)
        wscr = Ring(S, "wscr", [128, 8, 128], BF16, 3)
        R, F = common_rings(S)
        xring = Ring(S, "xblk", [128, 8, 512], F32, 1)
        rf = Ring(S, "rfb", [128, 2, 512], BF16, 2)
        o0 = Ring(S, "o0b", [128, 2, 512], F32, 2)
        zs = Ring(S, "zsb", [128, 4, 512], BF16, 1)
        at = Ring(S, "atb", [64, 8, 512], BF16, 1)
        mixT = SBT(S, "mixT", [128, 4, 512], BF16)
        o32 = Ring(S, "o32", [128, 512], F32, 2)
        yT = F["fT"]
        cur = Ring(S, "cub", [128, 512], BF16, 3)
        bgr_ = Ring(S, "bgb", [128, 512], BF16, 3)
        csb = Ring(S, "csb", [128, 512], F32, 2)
        pso = F["psu"]
        for bi in range(int(os.environ.get("NBLK", T // 512))):
            cs_ = slice(bi * 512, (bi + 1) * 512)
            xb = xring.next()
            S.dma("sp", xb[:], C.inp["xT"][:, :, cs_].rearrange("c p t -> p c t"), writes=[xb.d])
            z_, a_ = zs.next(), at.next()
            S.dma("sp", z_[:], C.inp["zs_in"][:, :, cs_], writes=[z_.d])
            S.dma("sp", a_[:], C.inp["attn_in"][:, :, cs_], writes=[a_.d])
            for h in range(4):
                r_, o_ = rf.next(), o0.next()
                for dirn in range(2):
                    S.dma("sp", r_[:, dirn], C.inp["R_in%d" % dirn][:, h, cs_], writes=[r_.d])
                    S.dma("sp", o_[:, dirn], C.inp["o0_in%d" % dirn][:, h, cs_], writes=[o_.d])
                ps = pso.next()
                S.op("pe", lambda e: e.matmul(ps[:], lhsT=Sst[:, h, :], rhs=r_[:, 0, :], start=True, stop=False), reads=[Sst.d, r_.d], writes=[ps.d], sig=False)
                S.op("pe", lambda e: e.matmul(ps[:], lhsT=Sst[:, 4 + h, :], rhs=r_[:, 1, :], start=False, stop=True), reads=[Sst.d, r_.d], writes=[ps.d])
                o = o32.next()
                S.op("dve", lambda e: e.tensor_tensor(out=o[:], in0=ps[:], in1=o_[:, 0, :], op=ALU.add), reads=[ps.d, o_.d], writes=[o.d])
                S.op("dve", lambda e: e.tensor_tensor(out=o[:], in0=o[:], in1=o_[:, 1, :], op=ALU.add), reads=[o.d, o_.d], writes=[o.d])
                sq = R["sq"].next()
                S.op("dve", lambda e: e.tensor_tensor(out=sq[:, 0, :], in0=o[:], in1=o[:], op=ALU.mult), reads=[o.d], writes=[sq.d])
                pn = R["ps"].next()
                S.op("pe", lambda e: e.matmul(pn[:], lhsT=C.ones_bf[:], rhs=sq[:, 0, :], start=True, stop=True), reads=[C.ones_bf.d, sq.d], writes=[pn.d])
                rn = R["rstd"].next()
                S.op("act", lambda e: e.activation(out=rn[:], in_=pn[:], func=AF.Ln, bias=C.eps_t[:, 0:1], scale=1.0 / 128), reads=[pn.d, C.eps_t.d], writes=[rn.d])
                S.op("act", lambda e: e.activation(out=rn[:], in_=rn[:], func=AF.Exp, scale=-0.5), reads=[rn.d], writes=[rn.d])
                tmp = R["tmp"].next()
                S.op("dve", lambda e: e.scalar_tensor_tensor(out=tmp[:], in0=o[:], scalar=gnormg[:, 0:1], in1=rn[:], op0=ALU.mult, op1=ALU.mult),
                     reads=[o.d, gnormg.d, rn.d], writes=[tmp.d])
                S.op("dve", lambda e: e.tensor_tensor(out=mixT[:, h, :], in0=tmp[:], in1=z_[:, h, :], op=ALU.mult), reads=[tmp.d, z_.d], writes=[mixT.d])
            for n in range(8):
                ps = F["psg"].next()
                for kg in range(8):
                    S.op("pe", lambda e: e.matmul(ps[:], lhsT=woa[:, kg, n * 128:(n + 1) * 128], rhs=a_[:, kg, :], start=(kg == 0), stop=False),
                         reads=[woa.d, a_.d], writes=[ps.d], sig=False)
                for h in range(4):
                    S.op("pe", lambda e: e.matmul(ps[:], lhsT=wog[:, h, n * 128:(n + 1) * 128], rhs=mixT[:, h, :], start=False, stop=(h == 3)),
                         reads=[wog.d, mixT.d], writes=[ps.d], sig=(h == 3))
                S.op("act", lambda e: e.copy(out=yT[:, n, :], in_=ps[:]), reads=[ps.d], writes=[yT.d])
            postnorm_residual(S, C, xb, yT, C.vec[("gg1", 0)], R)
            ffn_block(S, C, xb, 0, R, F)
            S.dma("pool", C.x2_d.ap[:, :, cs_], xb[:], reads=[xb.d], writes=C.x2_d.dd(bi * 512, bi * 512 + 512), own=xb.d)
            hT = F["hT"]
            norm_mod_block(S, C, xb, 512, C.vec[("a1", 1)], C.vec[("b1", 1)], hT, R)
            for c in range(8):
                def proj(j):
                    ps = F["psg"].next() if j < 16 else F["psu"].next()
                    wsc = wscr.next()
                    S.dma("sp", wsc[:], C.wsc_d.ap[j], reads=C.wsc_d.deps, writes=[wsc.d])
                    for kc in range(8):
                        S.op("pe", lambda e: e.matmul(ps[:], lhsT=wsc[:, kc, :], rhs=hT[:, kc, :], start=(kc == 0), stop=(kc == 7)),
                             reads=[wsc.d, hT.d], writes=[ps.d], sig=(kc == 7))
                    return ps
                pb_ = proj(c)
                bgb = bgr_.next()
                S.op("act", lambda e: e.copy(out=bgb[:], in_=pb_[:]), reads=[pb_.d], writes=[bgb.d])
                S.dma("pool", C.bg_d.ap[:, c, cs_], bgb[:], reads=[bgb.d], writes=C.bg_d.dd(bi * 512, bi * 512 + 512), own=bgb.d)
                pc_ = proj(8 + c)
                cs32 = csb.next()
                S.op("act", lambda e: e.copy(out=cs32[:], in_=pc_[:]), reads=[pc_.d], writes=[cs32.d])
                pu_ = proj(16 + c)
                cub = cur.next()
                S.op("dve", lambda e: e.tensor_tensor(out=cub[:], in0=pu_[:], in1=cs32[:], op=ALU.mult), reads=[pu_.d, cs32.d], writes=[cub.d])
                cuo = getattr(C, "cu_off", 0)
                S.dma("pool", C.cu_d.ap[:, c, cuo + bi * 512:cuo + bi * 512 + 512], cub[:], reads=[cub.d],
                      writes=C.cu_d.dd(bi * 512, bi * 512 + 512), own=cub.d)
                if getattr(C, "edge", None) is not None and bi in (0, T // 512 - 1):
                    ecol = 0 if bi == 0 else 511
                    eidx = c * 2 + (0 if bi == 0 else 1)
                    S.op("pool", lambda e: e.tensor_copy(out=C.edge[:, eidx:eidx + 1], in_=cub[:, ecol:ecol + 1]),
                         reads=[cub.d], writes=[C.edge.d])
                    if T // 512 == 1:
                        S.op("pool", lambda e: e.tensor_copy(out=C.edge[:, c * 2 + 1:c * 2 + 2], in_=cub[:, 511:512]),
                             reads=[cub.d], writes=[C.edge.d])
        S.barrier()
        S.stack = old


def phaseC(S, C):
    if not getattr(C, "preconverted", False):
        convert_ffn_weights(S, C, 1)
    with ExitStack() as ph:
        old = S.stack
        S.stack = ph
        sconv = load_const(S, C, "sconv")
        diag = SBT(S, "diagc", [128, 24, 128], BF16)
        for c in range(8):
            for k in range(3):
                S.op("dve", lambda e: e.tensor_scalar(out=diag[:, c * 3 + k, :], in0=C.ident_bf[:], scalar1=sconv[:, c, k:k + 1], scalar2=None,
                                                      op0=ALU.mult), reads=[C.ident_bf.d, sconv.d], writes=[diag.d])
        wso = SBT(S, "wso", [128, 8, 1024], BF16)
        with ExitStack() as phw:
            S.stack = phw
            wst = Ring(S, "wst", [128, 8, 256], F32, 2)
            load_weight_bf16(S, C, wso, 0, C.inp["sc_w_out"], 1024, wst)
            S.barrier()
            S.stack = ph
        R, F = common_rings(S, deep=True)
        xring = Ring(S, "xblk", [128, 8, 512], F32, 2)
        cuw = Ring(S, "cuw", [128, 8, 514], BF16, 1)
        bgr = Ring(S, "bgr", [128, 8, 512], BF16, 1)
        mT = SBT(S, "mT", [128, 8, 512], BF16)
        yT = F["fT"]
        for bi in range(int(os.environ.get("NBLK", T // 512))):
            cs_ = slice(bi * 512, (bi + 1) * 512)
            xb = xring.next()
            S.dma("sp", xb[:], C.inp["x2_in"][:, :, cs_], writes=[xb.d])
            cw, bg = cuw.next(), bgr.next()
            S.dma("sp", cw[:], C.inp["cu_ext"][:, :, bi * 512:bi * 512 + 514], writes=[cw.d])
            S.dma("sp", bg[:], C.inp["bg_in"][:, :, cs_], writes=[bg.d])
            for c in range(8):
                ps = F["psu"].next()
                for k in range(3):
                    S.op("pe", lambda e: e.matmul(ps[:], lhsT=diag[:, c * 3 + k, :], rhs=cw[:, c, k:k + 512], start=(k == 0), stop=(k == 2)),
                         reads=[diag.d, cw.d], writes=[ps.d], sig=(k == 2))
                S.op("dve", lambda e: e.tensor_tensor(out=mT[:, c, :], in0=ps[:], in1=bg[:, c, :], op=ALU.mult), reads=[ps.d, bg.d], writes=[mT.d])
            for n in range(8):
                ps = F["psg"].next()
                for kc in range(8):
                    S.op("pe", lambda e: e.matmul(ps[:], lhsT=wso[:, kc, n * 128:(n + 1) * 128], rhs=mT[:, kc, :], start=(kc == 0), stop=(kc == 7)),
                         reads=[wso.d, mT.d], writes=[ps.d], sig=(kc == 7))
                S.op("act", lambda e: e.copy(out=yT[:, n, :], in_=ps[:]), reads=[ps.d], writes=[yT.d])
            postnorm_residual(S, C, xb, yT, C.vec[("gg1", 1)], R)
            ffn_block(S, C, xb, 1, R, F)
            S.dma("pool", C.out_d.ap[:, :, cs_], xb[:], reads=[xb.d], writes=C.out_d.dd(bi * 512, bi * 512 + 512), own=xb.d)
        S.barrier()
        S.stack = old


def halo_exchange(S, C, groups):
    S.dma("pool", C.edge_d.ap, C.edge[:], reads=[C.edge.d], writes=C.edge_d.deps, own=C.edge.d)
    S.allgather(C.edge_d.ap, C.edges_g.ap, groups)
    with ExitStack() as ph:
        old = S.stack
        S.stack = ph
        selL = load_const(S, C, "selL")
        selR = load_const(S, C, "selR")
        eg = SBT_(S, "egath", [128, 4, 8, 2], BF16)
        S.dma("sp", eg[:].rearrange("p r c k -> p r (c k)"), C.edges_g.ap.rearrange("(r p) n -> p r n", p=128), writes=[eg.d])
        for nm, sel, k, col in (("hl", selL, 1, 0), ("hr", selR, 0, T + 1)):
            acc = SBT_(S, nm + "acc", [128, 8], F32)
            S.op("dve", lambda e: e.tensor_scalar(out=acc[:], in0=eg[:, 0, :, k], scalar1=sel[:, 0:1], scalar2=None, op0=ALU.mult),
                 reads=[eg.d, sel.d], writes=[acc.d])
            for r in range(1, 4):
                S.op("dve", lambda e: e.scalar_tensor_tensor(out=acc[:], in0=eg[:, r, :, k], scalar=sel[:, r:r + 1], in1=acc[:],
                                                             op0=ALU.mult, op1=ALU.add), reads=[eg.d, sel.d, acc.d], writes=[acc.d])
            hb = SBT_(S, nm + "bf", [128, 8, 1], BF16)
            S.op("dve", lambda e: e.tensor_copy(out=hb[:, :, 0], in_=acc[:]), reads=[acc.d], writes=[hb.d])
            S.dma("pool", C.cu_d.ap[:, :, col:col + 1], hb[:], reads=[hb.d], writes=C.cu_d.deps, own=hb.d, allow_slow_non_contiguous=True)
        S.barrier()
        S.stack = old


def conv_gen(S, C):
    st = Ring(S, "pcs", [128, 8, 128], F32, 2)
    sb = Ring(S, "pcb", [128, 8, 128], BF16, 2)
    for l in range(2):
        for j in range(NJ):
            for gi, nm in enumerate(("ffn_w_gate", "ffn_w_up")):
                s_, b_ = st.next(), sb.next()
                S.dma("sp", s_[:], C.inp[nm][l, :, j * 128:(j + 1) * 128].rearrange("(kc p) n -> p kc n", p=128), writes=[s_.d])
                S.op("pool", lambda e: e.tensor_copy(out=b_[:], in_=s_[:]), reads=[s_.d], writes=[b_.d])
                S.dma("pool", C.wgu_d[l].ap[j, :, gi, :, :], b_[:], reads=[b_.d], writes=C.wgu_d[l].deps, own=b_.d)
                for _sp in range(18):
                    yield
        for n in range(8):
            for j0 in range(0, NJ, 8):
                nj = min(8, NJ - j0)
                s_, b_ = st.next(), sb.next()
                S.dma("sp", s_[:, :nj, :], C.inp["ffn_w_down"][l, j0 * 128:(j0 + nj) * 128, n * 128:(n + 1) * 128].rearrange("(j p) n -> p j n", p=128),
                      writes=[s_.d])
                S.op("pool", lambda e: e.tensor_copy(out=b_[:, :nj, :], in_=s_[:, :nj, :]), reads=[s_.d], writes=[b_.d])
                S.dma("pool", C.wd_d[l].ap[n, :, j0:j0 + nj, :], b_[:, :nj, :], reads=[b_.d], writes=C.wd_d[l].deps, own=b_.d)
                for _sp in range(18):
                    yield
    for j in range(24):
        s_, b_ = st.next(), sb.next()
        S.dma("sp", s_[:], C.inp["sc_w_in"][:, j * 128:(j + 1) * 128].rearrange("(kc p) n -> p kc n", p=128), writes=[s_.d])
        S.op("pool", lambda e: e.tensor_copy(out=b_[:], in_=s_[:]), reads=[s_.d], writes=[b_.d])
        S.dma("pool", C.wsc_d.ap[j], b_[:], reads=[b_.d], writes=C.wsc_d.deps, own=b_.d)
        for _sp in range(18):
            yield


import sys, time
from concourse.bass_utils import run_bass_kernel_spmd

GROUPS = [[0, 1, 2, 3], [4, 5, 6, 7]]
NAMES_F = ["xT", "xhT", "ctxT", "cvec", "ident_bf", "ident_f", "swap_bf", "ones_bf", "ones_f", "amask", "gmask", "lvlmask", "cmat", "hv",
           "cosT", "sinT", "ada_w", "ada_bT", "gains", "wq", "wk", "wvab", "wg", "wz", "sinkrow", "gconv", "galog", "gdtb",
           "hy_w_out", "gnormg", "sel", "selL", "selR", "sc_w_in", "ffn_w_gate", "ffn_w_up", "ffn_w_down", "sconv", "sc_w_out"]


def build_fused():
    nc = bass.Bass("TRN2", target_bir_lowering=False)
    C = declare_inputs(nc, NAMES_F)
    C.q_d = DramT(nc, "q_d", [128, 4, T], BF16)
    C.zs_d = DramT(nc, "zs_d", [128, 4, T], BF16)
    C.pre_d = DramT(nc, "pre_d", [128, 12, T + 256], BF16)
    C.prec_d = DramT(nc, "prec_d", [128, 12, 258], BF16)
    C.attn_d = DramT(nc, "attn_d", [64, 8, T], BF16)
    C.o0_d = [DramT(nc, "o0_d%d" % i, [128, 4, T], F32) for i in range(2)]
    C.R_d = [DramT(nc, "R_d%d" % i, [128, 4, T], BF16) for i in range(2)]
    C.xch_d = DramT(nc, "xch_d", [128, 8, 2, 128], F32)
    C.sctx_d = DramT(nc, "sctx_d", [128, 8, 128], F32)
    C.xall_g = DramT(nc, "xall_g", [4 * 128, 2048], F32)
    C.wgu_d = [DramT(nc, "wgu_d%d" % l, [NJ, 128, 2, 8, 128], BF16, blk=10 ** 9) for l in range(2)]
    C.wd_d = [DramT(nc, "wd_d%d" % l, [8, 128, NJ, 128], BF16, blk=10 ** 9) for l in range(2)]
    C.preconverted = True
    C.wsc_d = DramT(nc, "wsc_d", [24, 128, 8, 128], BF16, blk=10 ** 9)
    C.x2_d = DramT(nc, "x2_d", [128, 8, T], F32)
    C.cu_d = DramT(nc, "cu_d", [128, 8, T + 2], BF16)
    C.bg_d = DramT(nc, "bg_d", [128, 8, T], BF16)
    C.edge_d = DramT(nc, "edge_d", [128, 16], BF16)
    C.edges_g = DramT(nc, "edges_g", [4 * 128, 16], BF16)
    C.out_d = DramT(nc, "out_d", [128, 8, T], F32, kind="ExternalOutput")
    C.cu_off = 1
    with ExitStack() as st:
        S = Sched(nc, st)
        setup_common(S, C)
        phase0(S, C)
        S.retire()
        with ExitStack() as stA:
            S.stack = stA
            phase1(S, C, None)
            S.retire()
            phase2(S, C)
            S.retire()
            phase3(S, C)
            S.retire()
            S.stack = st
        S.allgather(C.xch_d.ap.rearrange("p a b c -> p (a b c)"), C.xall_g.ap, GROUPS)
        C.inp["xall"] = C.xall_g.ap.rearrange("(r p) n -> p r n", p=128)
        C.inp["sctx_in"] = C.sctx_d.ap
        for i in range(2):
            C.inp["R_in%d" % i] = C.R_d[i].ap
            C.inp["o0_in%d" % i] = C.o0_d[i].ap
        C.inp["zs_in"] = C.zs_d.ap
        C.inp["attn_in"] = C.attn_d.ap
        with ExitStack() as stB:
            S.stack = stB
            C.edge = SBT_(S, "edge", [128, 16], BF16)
            phaseB(S, C)
            S.retire()
            halo_exchange(S, C, GROUPS)
            S.retire()
            S.stack = st
        C.inp["x2_in"] = C.x2_d.ap
        C.inp["cu_ext"] = C.cu_d.ap
        C.inp["bg_in"] = C.bg_d.ap
        with ExitStack() as stC:
            S.stack = stC
            phaseC(S, C)
            S.stack = st
        S.barrier()
        pass
    return nc


def kernel(**inp):
    inp = {k: np.asarray(v) for k, v in inp.items()}
    shared = prep_shared_inputs(inp)
    cores = list(range(8))
    maps = []
    for core in cores:
        b, s = core // 4, core % 4
        d = prep_core_inputs(inp, b, s)
        d.update(shared)
        for nm, idx in (("sel", s), ("selL", s - 1), ("selR", s + 1)):
            v = np.zeros((128, 4), np.float32)
            if 0 <= idx <= 3:
                v[:, idx] = 1.0
            d[nm] = v
        maps.append({k: d[k] for k in NAMES_F})
    nc = build_fused()
    res = run_bass_kernel_spmd(nc, maps, core_ids=cores).results
    out = np.zeros((2, 4 * T, D), np.float32)
    for c in cores:
        b, s = c // 4, c % 4
        o = np.asarray(res[c]["out_d"])
        out[b, s * T:(s + 1) * T, :] = o.transpose(2, 1, 0).reshape(T, D)
    return out
```

```python
import numpy as np
import concourse.bass as bass
import concourse.mybir as mybir

F32 = mybir.dt.float32
BF16 = mybir.dt.bfloat16
AF = mybir.ActivationFunctionType
ALU = mybir.AluOpType
AX = mybir.AxisListType

EPOCH = 12000
_UID = [0]


class Dep:
    __slots__ = ("w", "r", "dsem", "dval", "name", "uid")

    def __init__(self, name=""):
        _UID[0] += 1
        self.uid = _UID[0]
        self.w = None
        self.r = {}
        self.dsem = None
        self.dval = 0
        self.name = name


class Sched:
    def __init__(self, nc, stack):
        self.nc = nc
        self.stack = stack
        self.semstack = stack
        self.eng = {"pe": nc.tensor, "dve": nc.vector, "act": nc.scalar, "pool": nc.gpsimd, "sp": nc.sync}
        self.sem = {}
        self.cnt = {}
        self.order = {}
        self.seen = {k: {} for k in self.eng}
        self.nsem = 0
        for k in self.eng:
            self.sem[k] = self.new_sem(k)
            self.cnt[k] = 0
            self.order[k] = 0
        self.ninst = 0
        self.nwait = 0
        self.last_tok = {}
        self.dma_deps = []
        self.free_dsems = []

    def new_sem(self, tag):
        self.nsem += 1
        return self.semstack.enter_context(self.nc.semaphore("s%d_%s" % (self.nsem, tag)))

    def _need(self, e, tok):
        if tok is None:
            return
        pk, order, sem, val = tok
        if pk == "pe" and e == "pe":
            return
        if self.seen[e].get(pk, -1) >= order:
            return
        self.eng[e].wait_ge(sem, val)
        self.nwait += 1
        self.seen[e][pk] = order

    def _waits(self, e, reads, writes):
        for d in reads:
            self._need(e, d.w)
        for d in writes:
            self._need(e, d.w)
            for t in list(d.r.values()):
                self._need(e, t)

    def op(self, e, fn, reads=(), writes=(), sig=True):
        self._waits(e, reads, writes)
        ins = fn(self.eng[e])
        self.ninst += 1
        self.order[e] += 1
        if sig:
            self.cnt[e] += 1
            ins.then_inc(self.sem[e], 1)
            tok = (e, self.order[e], self.sem[e], self.cnt[e])
            self.last_tok[e] = tok
        else:
            tok = (e, self.order[e], self.sem[e], self.cnt[e] + 1)
        for d in writes:
            d.w = tok
            d.r = {}
        for d in reads:
            if d not in writes:
                d.r[e] = tok
        if sig and self.cnt[e] >= EPOCH:
            self.sem[e] = self.new_sem(e)
            self.cnt[e] = 0
        return ins

    def dma(self, q, out, in_, reads=(), writes=(), own=None, **kw):
        self._waits(q, reads, writes)
        own = own or writes[0]
        if own.dsem is not None and own.dval > 0:
            self._need(q, ("dma%d" % own.uid, own.dval, own.dsem, own.dval))
        if own.dsem is None:
            if self.free_dsems:
                own.dsem, own.dval = self.free_dsems.pop()
            else:
                own.dsem = self.new_sem("d")
                own.dval = 0
            self.dma_deps.append(own)
        own.dval += 16
        ins = self.eng[q].dma_start(out=out, in_=in_, **kw)
        ins.then_inc(own.dsem, 16)
        self.ninst += 1
        tok = ("dma%d" % own.uid, own.dval, own.dsem, own.dval)
        for d in writes:
            d.w = tok
            d.r = {}
        for d in reads:
            d.r[tok[0]] = tok
        return ins

    def barrier(self, engines=None):
        for e in (engines or list(self.eng)):
            for p, tok in self.last_tok.items():
                if not (p == e and e == "pe"):
                    self._need(e, tok)
            for d in self.dma_deps:
                self._need(e, ("dma%d" % d.uid, d.dval, d.dsem, d.dval))

    def retire(self):
        self.barrier()
        for d in self.dma_deps:
            self.free_dsems.append((d.dsem, d.dval))
            d.dsem = None
        self.dma_deps = []

    def allgather(self, in_ap, out_ap, groups):
        self.barrier()
        csem = self.new_sem("cc")
        ins = self.nc.gpsimd.collective_compute("AllGather", mybir.AluOpType.bypass, replica_groups=groups,
                                                ins=[in_ap.opt()], outs=[out_ap.opt()])
        ins.then_inc(csem, 1)
        self.ninst += 1
        for e in self.eng:
            self.eng[e].wait_ge(csem, 1)

    def wait_all(self, e, deps):
        for d in deps:
            self._need(e, d.w)
            for t in list(d.r.values()):
                self._need(e, t)


class SBT:
    def __init__(self, S, name, shape, dtype, psum=False, ndeps=1):
        nc = S.nc
        S.ntile = getattr(S, "ntile", 0) + 1
        name = "t%d_%s" % (S.ntile, name)
        if psum:
            self.t = S.stack.enter_context(nc.psum_tensor(name, shape, dtype))
        else:
            self.t = S.stack.enter_context(nc.sbuf_tensor(name, shape, dtype))
        self.deps = [Dep(name + str(i)) for i in range(ndeps)]
        self.d = self.deps[0]
        self.shape = shape

    def __getitem__(self, idx):
        return self.t[idx]


SBT_ = SBT


class DramT:
    def __init__(self, nc, name, shape, dtype, kind="Internal", blk=512):
        self.ap = nc.dram_tensor(name, shape, dtype, kind=kind).ap()
        self.blk = blk
        self.deps = [Dep("%s_%d" % (name, i)) for i in range((shape[-1] + blk - 1) // blk)]

    def dd(self, c0, c1):
        return self.deps[c0 // self.blk:(c1 - 1) // self.blk + 1]


import os
import numpy as np
import ml_dtypes
from contextlib import ExitStack

BF = ml_dtypes.bfloat16
D = 1024
T = 4096
NT = T // 128
HAL = 128
CTX = 256
EXT = T + 2 * HAL + CTX
COL_HL, COL_HR, COL_CTX = T, T + HAL, T + 2 * HAL
FF = 2816
NJ = FF // 128
EPS = 1e-6


class Ring:
    def __init__(self, S, name, shape, dtype, n, psum=False):
        self.tiles = [SBT(S, "%s%d" % (name, i), shape, dtype, psum=psum) for i in range(n)]
        self.i = 0

    def next(self):
        t = self.tiles[self.i % len(self.tiles)]
        self.i += 1
        return t


def rope_tables(s):
    pos_own = s * T + np.arange(T)
    pos_l = s * T - HAL + np.arange(HAL)
    pos_r = (s + 1) * T + np.arange(HAL)
    pos = np.concatenate([pos_own, pos_l, pos_r]).astype(np.int64)
    row = (pos // 64).astype(np.float32)
    col = (pos % 64).astype(np.float32)
    inv = (np.float32(10000.0) ** (-np.arange(16, dtype=np.float32) / np.float32(16))).astype(np.float32)
    ang = np.concatenate([row[:, None] * inv[None, :], col[:, None] * inv[None, :]], axis=1).astype(np.float32)
    cs = np.cos(ang).astype(np.float32).T
    sn = np.sin(ang).astype(np.float32).T
    cosT = np.concatenate([cs, cs, cs, cs], axis=0)
    sinT = np.concatenate([-sn, sn, -sn, sn], axis=0)
    return np.ascontiguousarray(cosT), np.ascontiguousarray(sinT)


def make_consts(s):
    c = {}
    c["ident_bf"] = np.eye(128, dtype=np.float32).astype(BF)
    c["ident_f"] = np.eye(128, dtype=np.float32)
    sw = np.zeros((128, 128), np.float32)
    for m in range(128):
        k = (m // 64) * 64 + ((m % 64) + 32) % 64
        sw[k, m] = 1.0
    c["swap_bf"] = sw.astype(BF)
    c["ones_bf"] = np.ones((128, 128), np.float32).astype(BF)
    c["ones_f"] = np.ones((128, 128), np.float32)
    m = np.arange(128)[:, None]
    n = np.arange(128)[None, :]
    mL = (m >= n).astype(np.float32)
    mR = (m <= n).astype(np.float32)
    masks = np.zeros((128, 4, 512), np.float32)
    masks[:, 0] = np.tile(mL, (1, 4))
    masks[:, 1] = np.tile(mR, (1, 4))
    masks[:, 2] = np.tile(mL, (1, 4)) * (0.0 if s == 0 else 1.0)
    masks[:, 3] = np.tile(mR, (1, 4)) * (0.0 if s == 3 else 1.0)
    c["amask"] = masks.astype(BF)
    gm = np.zeros((128, 4, 128), np.float32)
    gm[:, 0] = (n <= m)
    gm[:, 1] = (n < m)
    gm[:, 2] = (n >= m)
    gm[:, 3] = (n > m)
    c["gmask"] = gm
    cm = np.zeros((128, 4, 128), np.float32)
    cm[:, 0] = (m <= n)
    cm[:, 1] = (m > n)
    cm[:, 2] = (m >= n)
    cm[:, 3] = (m < n)
    c["cmat"] = cm
    lm = np.zeros((128, 14, 128), np.float32)
    for li, mm_ in enumerate((1, 2, 4, 8, 16, 32, 64)):
        low = ((m // mm_) % 2 == 1) & ((n // mm_) == (m // mm_) - 1)
        lm[:, li * 2] = low
        lm[:, li * 2 + 1] = low.T
    c["lvlmask"] = lm.astype(BF)
    hv = np.ones((128, 2), np.float32)
    if s == 0:
        hv[:, 0] = 0
    if s == 3:
        hv[:, 1] = 0
    c["hv"] = hv
    cosT, sinT = rope_tables(s)
    c["cosT"] = cosT
    c["sinT"] = sinT
    return c


def prep_core_inputs(inp, b, s):
    x = inp["x"]
    d = {}
    xo = x[b, s * T:(s + 1) * T, :]
    d["xT"] = np.ascontiguousarray(xo.T.reshape(8, 128, T))
    xh = np.zeros((2 * HAL, D), np.float32)
    if s > 0:
        xh[:HAL] = x[b, s * T - HAL:s * T]
    if s < 3:
        xh[HAL:] = x[b, (s + 1) * T:(s + 1) * T + HAL]
    d["xhT"] = np.ascontiguousarray(xh.T.reshape(8, 128, 2 * HAL))
    d["ctxT"] = np.ascontiguousarray(inp["ctx"][b].T.reshape(8, 128, CTX))
    cv = np.stack([inp["c"][b], inp["c_ctx"]], axis=1)
    d["cvec"] = np.ascontiguousarray(cv.reshape(8, 128, 2).transpose(1, 0, 2))
    d.update(make_consts(s))
    return d


def prep_shared_inputs(inp):
    d = {}
    d["ada_w"] = inp["ada_w"]
    d["ada_bT"] = np.ascontiguousarray(inp["ada_b"].reshape(2, 48, 128).transpose(2, 0, 1))
    g = np.stack([inp["pre_mix_g"], inp["post_mix_g"], inp["pre_ffn_g"], inp["post_ffn_g"]], axis=0)
    d["gains"] = np.ascontiguousarray(g.reshape(4, 2, 8, 128).transpose(3, 0, 1, 2))
    w = inp["hy_w_in"][0]
    wq = w[:, 0:512].reshape(D, 2, 4, 64).transpose(0, 2, 1, 3).reshape(D, 512)
    d["wq"] = np.ascontiguousarray(wq)
    d["wk"] = np.ascontiguousarray(w[:, 512:640])
    d["wvab"] = np.ascontiguousarray(np.concatenate([w[:, 640:768], w[:, 2816:2832]], axis=1))
    d["wg"] = np.ascontiguousarray(w[:, 768:2304])
    d["wz"] = np.ascontiguousarray(w[:, 2304:2816])
    d["hy_w_out"] = inp["hy_w_out"][0]
    sk = inp["attn_sink"][0].reshape(2, 4)
    d["sinkrow"] = np.ascontiguousarray(np.broadcast_to(sk[None, :, :, None], (64, 2, 4, 128)).reshape(64, 2, 512))
    d["gconv"] = np.ascontiguousarray(inp["gdn_conv_w"][0].reshape(3, 12, 128).transpose(2, 1, 0))
    d["galog"] = np.ascontiguousarray(np.broadcast_to(inp["gdn_a_log"][0].reshape(1, 8), (128, 8)))
    d["gdtb"] = np.ascontiguousarray(np.broadcast_to(inp["gdn_dt_bias"][0].reshape(1, 8), (128, 8)))
    d["gnormg"] = np.ascontiguousarray(inp["gdn_norm_g"][0].reshape(128, 1))
    d["sc_w_in"] = inp["sc_w_in"][0]
    d["sconv"] = np.ascontiguousarray(inp["sc_conv_w"][0].reshape(3, 8, 128).transpose(2, 1, 0))
    d["sc_w_out"] = inp["sc_w_out"][0]
    d["ffn_w_gate"] = inp["ffn_w_gate"]
    d["ffn_w_up"] = inp["ffn_w_up"]
    d["ffn_w_down"] = inp["ffn_w_down"]
    return d


INPUT_SPECS = {
    "xT": ([8, 128, T], F32), "xhT": ([8, 128, 2 * HAL], F32), "ctxT": ([8, 128, CTX], F32), "cvec": ([128, 8, 2], F32),
    "ident_bf": ([128, 128], BF16), "ident_f": ([128, 128], F32), "swap_bf": ([128, 128], BF16),
    "ones_bf": ([128, 128], BF16), "ones_f": ([128, 128], F32), "amask": ([128, 4, 512], BF16),
    "gmask": ([128, 4, 128], F32), "lvlmask": ([128, 14, 128], BF16), "cmat": ([128, 4, 128], F32), "hv": ([128, 2], F32),
    "cosT": ([128, T + 2 * HAL], F32), "sinT": ([128, T + 2 * HAL], F32),
    "ada_w": ([2, D, 6 * D], F32), "ada_bT": ([128, 2, 48], F32), "gains": ([128, 4, 2, 8], F32),
    "wq": ([D, 512], F32), "wk": ([D, 128], F32), "wvab": ([D, 144], F32), "wg": ([D, 1536], F32), "wz": ([D, 512], F32),
    "hy_w_out": ([D, D], F32), "sinkrow": ([64, 2, 512], F32), "gconv": ([128, 12, 3], F32),
    "galog": ([128, 8], F32), "gdtb": ([128, 8], F32), "gnormg": ([128, 1], F32),
    "sc_w_in": ([D, 3 * D], F32), "sconv": ([128, 8, 3], F32), "sc_w_out": ([D, D], F32),
    "sel": ([128, 4], F32), "xall": ([128, 4, 8, 2, 128], F32), "sctx_in": ([128, 8, 128], F32),
    "R_in0": ([128, 4, T], BF16), "R_in1": ([128, 4, T], BF16), "o0_in0": ([128, 4, T], F32), "o0_in1": ([128, 4, T], F32),
    "zs_in": ([128, 4, T], BF16), "attn_in": ([64, 8, T], BF16), "x2_in": ([128, 8, T], F32),
    "cu_ext": ([128, 8, T + 2], BF16), "bg_in": ([128, 8, T], BF16), "selL": ([128, 4], F32), "selR": ([128, 4], F32),
    "ffn_w_gate": ([2, D, FF], F32), "ffn_w_up": ([2, D, FF], F32), "ffn_w_down": ([2, FF, D], F32),
}


class Ctx:
    pass


def declare_inputs(nc, names):
    C = Ctx()
    C.inp = {}
    for n in names:
        shp, dt = INPUT_SPECS[n]
        C.inp[n] = nc.dram_tensor(n, shp, dt, kind="ExternalInput").ap()
    return C


def load_const(S, C, name, q="sp"):
    shp, dt = INPUT_SPECS[name]
    t = SBT(S, "c_" + name, shp, dt)
    S.dma(q, t[:], C.inp[name], writes=[t.d])
    return t


def load_weight_bf16(S, C, dst, dst_cols, src_ap, ncols, stage_ring, eng="pool"):
    for c0 in range(0, ncols, 256):
        n = min(256, ncols - c0)
        st = stage_ring.next()
        S.dma("sp", st[:, :, :n], src_ap[:, c0:c0 + n].rearrange("(kc p) n -> p kc n", p=128), writes=[st.d])
        S.op(eng, lambda e: e.tensor_copy(out=dst[:, :, dst_cols + c0:dst_cols + c0 + n], in_=st[:, :, :n]),
             reads=[st.d], writes=[dst.d])


def phase0(S, C):
    nc = S.nc
    cv = SBT(S, "cv", [128, 8, 2], F32)
    S.dma("sp", cv[:], C.inp["cvec"], writes=[cv.d])
    sc = SBT(S, "silu_c", [128, 8, 2], F32)
    S.op("act", lambda e: e.activation(out=sc[:], in_=cv[:], func=AF.Silu), reads=[cv.d], writes=[sc.d])
    adab = load_const(S, C, "ada_bT")
    gains = load_const(S, C, "gains")
    C.gains = gains
    C.mod = [SBT(S, "mod%d" % l, [128, 48, 2], F32) for l in range(2)]
    with ExitStack() as ph:
        old = S.stack
        S.stack = ph
        stage = Ring(S, "adast", [128, 8, 1536], F32, 2)
        psm = SBT(S, "ps_mod", [128, 48, 2], F32, psum=True)
        for l in range(2):
            for qq in range(4):
                st = stage.next()
                S.dma("sp", st[:], C.inp["ada_w"][l, :, qq * 1536:(qq + 1) * 1536].rearrange("(kc p) n -> p kc n", p=128),
                      writes=[st.d])
                for jj in range(12):
                    j = qq * 12 + jj
                    for kc in range(8):
                        S.op("pe", lambda e: e.matmul(psm[:, j, :], lhsT=st[:, kc, jj * 128:(jj + 1) * 128], rhs=sc[:, kc, :],
                                                      start=(kc == 0), stop=(kc == 7)),
                             reads=[st.d, sc.d], writes=[psm.d], sig=(kc == 7))
            mod = C.mod[l]
            S.op("dve", lambda e: e.tensor_tensor(out=mod[:], in0=psm[:], in1=adab[:, l, :, None].to_broadcast([128, 48, 2]),
                                                  op=ALU.add), reads=[psm.d, adab.d], writes=[mod.d])
        S.barrier()
        S.stack = old
    C.vec = {}
    for l in range(2):
        mod = C.mod[l]
        for (nm, gi, mi) in (("a1", 0, 1), ("a2", 2, 4)):
            for col, sfx in ((0, ""), (1, "c")):
                t = SBT(S, "%s%s_%d" % (nm, sfx, l), [128, 8], F32)
                S.op("dve", lambda e: e.scalar_tensor_tensor(out=t[:], in0=mod[:, mi * 8:(mi + 1) * 8, col], scalar=1.0,
                                                             in1=gains[:, gi, l, :], op0=ALU.add, op1=ALU.mult),
                     reads=[mod.d, gains.d], writes=[t.d])
                C.vec[(nm + sfx, l)] = t
        for (nm, gi, mi) in (("gg1", 1, 2), ("gg2", 3, 5)):
            t = SBT(S, "%s_%d" % (nm, l), [128, 8], F32)
            S.op("dve", lambda e: e.tensor_tensor(out=t[:], in0=mod[:, mi * 8:(mi + 1) * 8, 0], in1=gains[:, gi, l, :],
                                                  op=ALU.mult), reads=[mod.d, gains.d], writes=[t.d])
            C.vec[(nm, l)] = t
        for (nm, mi) in (("b1", 0), ("b2", 3)):
            for col, sfx in ((0, ""), (1, "c")):
                t = SBT(S, "%s%s_%d" % (nm, sfx, l), [128, 8], F32)
                S.op("dve", lambda e: e.tensor_copy(out=t[:], in_=mod[:, mi * 8:(mi + 1) * 8, col]),
                     reads=[mod.d], writes=[t.d])
                C.vec[(nm + sfx, l)] = t


def norm_mod_block(S, C, xb, ntok, a, b, hT, R):
    sq = R["sq"].next()
    for c in range(8):
        S.op("dve", lambda e: e.tensor_tensor(out=sq[:, c, :ntok], in0=xb[:, c, :ntok], in1=xb[:, c, :ntok], op=ALU.mult),
             reads=[xb.d], writes=[sq.d])
    psn = R["ps"].next()
    for c in range(8):
        S.op("pe", lambda e: e.matmul(psn[:, :ntok], lhsT=C.ones_bf[:], rhs=sq[:, c, :ntok], start=(c == 0), stop=(c == 7)),
             reads=[C.ones_bf.d, sq.d], writes=[psn.d], sig=(c == 7))
    rstd = R["rstd"].next()
    S.op("act", lambda e: e.activation(out=rstd[:, :ntok], in_=psn[:, :ntok], func=AF.Ln, bias=C.eps_t[:, 0:1], scale=1.0 / D),
         reads=[psn.d, C.eps_t.d], writes=[rstd.d])
    S.op("act", lambda e: e.activation(out=rstd[:, :ntok], in_=rstd[:, :ntok], func=AF.Exp, scale=-0.5),
         reads=[rstd.d], writes=[rstd.d])
    for c in range(8):
        tmp = R["tmp"].next()
        S.op("dve", lambda e: e.scalar_tensor_tensor(out=tmp[:, :ntok], in0=xb[:, c, :ntok], scalar=a[:, c:c + 1],
                                                     in1=rstd[:, :ntok], op0=ALU.mult, op1=ALU.mult),
             reads=[xb.d, a.d, rstd.d], writes=[tmp.d])
        S.op("act", lambda e: e.activation(out=hT[:, c, :ntok], in_=tmp[:, :ntok], func=AF.Identity, bias=b[:, c:c + 1], scale=1.0),
             reads=[tmp.d, b.d], writes=[hT.d])
    return rstd


def setup_common(S, C):
    C.ones_bf = load_const(S, C, "ones_bf")
    C.ident_bf = load_const(S, C, "ident_bf")
    C.eps_t = SBT(S, "eps_t", [128, 1], F32)
    S.op("pool", lambda e: e.memset(C.eps_t[:], EPS), writes=[C.eps_t.d])


def phase1(S, C, dbg=None):
    nc = S.nc
    l = 0
    C.kT = SBT(S, "kT", [128, EXT], BF16)
    C.V = SBT(S, "V", [128, 36, 128], BF16)
    C.ab = SBT(S, "ab", [128, 36, 16], F32)
    with ExitStack() as ph:
        old = S.stack
        S.stack = ph
        swap = load_const(S, C, "swap_bf")
        hv = load_const(S, C, "hv")
        wq = SBT(S, "wq", [128, 8, 512], BF16)
        wk = SBT(S, "wk", [128, 8, 128], BF16)
        wvab = SBT(S, "wvab", [128, 8, 144], BF16)
        wg = SBT(S, "wg", [128, 8, 1536], BF16)
        wz = SBT(S, "wz", [128, 8, 512], BF16)
        stage = Ring(S, "wst", [128, 8, 256], F32, 2)
        load_weight_bf16(S, C, wk, 0, C.inp["wk"], 128, stage)
        load_weight_bf16(S, C, wvab, 0, C.inp["wvab"], 144, stage)
        load_weight_bf16(S, C, wq, 0, C.inp["wq"], 512, stage)
        load_weight_bf16(S, C, wg, 0, C.inp["wg"], 1536, stage)
        load_weight_bf16(S, C, wz, 0, C.inp["wz"], 512, stage)
        R = {"sq": Ring(S, "sq", [128, 8, 512], BF16, 1), "ps": Ring(S, "psn", [128, 512], F32, 1, psum=True),
             "rstd": Ring(S, "rstd", [128, 512], F32, 2), "tmp": Ring(S, "ntmp", [128, 512], F32, 3)}
        xring = Ring(S, "xblk", [128, 8, 512], F32, 2)
        hring = Ring(S, "hT", [128, 8, 512], BF16, 2)
        psr = Ring(S, "psp", [128, 512], F32, 4, psum=True)
        psrot = Ring(S, "psrot", [128, 512], F32, 1, psum=True)
        pst = Ring(S, "pst", [128, 512], F32, 2, psum=True)
        cosr = Ring(S, "cosb", [128, 512], F32, 2)
        sinr = Ring(S, "sinb", [128, 512], F32, 2)
        t1r = Ring(S, "t1", [128, 512], F32, 2)
        q32r = Ring(S, "q32", [128, 512], F32, 2)
        t2r = Ring(S, "t2", [128, 512], F32, 2)
        qsr = Ring(S, "qs", [128, 512], BF16, 2)
        qblk = Ring(S, "qblk", [128, 4, 512], BF16, 2)
        preblk = Ring(S, "preblk", [128, 12, 512], BF16, 1)
        zblk = Ring(S, "zblk", [128, 4, 512], BF16, 1)
        blocks = [("ctx", 0), ("halo", 0)] + [("own", i) for i in range(T // 512)]
        import os
        CUT = int(os.environ.get("CUT", "99"))
        if CUT < 99:
            blocks = blocks[:1]
        for kind, bi in blocks:
            ntok = 512 if kind == "own" else 256
            if kind == "own":
                src = C.inp["xT"][:, :, bi * 512:(bi + 1) * 512]
                col0 = bi * 512
                a, b = C.vec[("a1", l)], C.vec[("b1", l)]
            elif kind == "halo":
                src = C.inp["xhT"]
                col0 = COL_HL
                a, b = C.vec[("a1", l)], C.vec[("b1", l)]
            else:
                src = C.inp["ctxT"]
                col0 = COL_CTX
                a, b = C.vec[("a1c", l)], C.vec[("b1c", l)]
            xb = xring.next()
            S.dma("sp", xb[:, :, :ntok], src.rearrange("c p t -> p c t"), writes=[xb.d])
            hT = hring.next()
            norm_mod_block(S, C, xb, ntok, a, b, hT, R)
            if CUT <= 1:
                continue
            if dbg is not None and kind == "own" and bi == 0 and "h" in dbg:
                S.dma("sp", dbg["h"].ap, hT[:], reads=[hT.d], writes=dbg["h"].deps, own=hT.d)
            rope = kind != "ctx"
            if rope:
                cb = cosr.next()
                sb = sinr.next()
                S.dma("sp", cb[:, :ntok], C.inp["cosT"][:, col0:col0 + ntok], writes=[cb.d])
                S.dma("sp", sb[:, :ntok], C.inp["sinT"][:, col0:col0 + ntok], writes=[sb.d])

            def proj(w, j):
                ps = psr.next()
                for kc in range(8):
                    S.op("pe", lambda e: e.matmul(ps[:, :ntok], lhsT=w[:, kc, j * 128:(j + 1) * 128], rhs=hT[:, kc, :ntok],
                                                  start=(kc == 0), stop=(kc == 7)),
                         reads=[w.d, hT.d], writes=[ps.d], sig=(kc == 7))
                return ps

            def rope_evac(ps, out_ap, out_dep):
                q32 = q32r.next()
                S.op("act", lambda e: e.copy(out=q32[:, :ntok], in_=ps[:, :ntok]), reads=[ps.d], writes=[q32.d])
                t1 = t1r.next()
                S.op("dve", lambda e: e.tensor_tensor(out=t1[:, :ntok], in0=q32[:, :ntok], in1=cb[:, :ntok], op=ALU.mult),
                     reads=[q32.d, cb.d], writes=[t1.d])
                qs = qsr.next()
                S.op("dve", lambda e: e.tensor_copy(out=qs[:, :ntok], in_=q32[:, :ntok]), reads=[q32.d], writes=[qs.d])
                pr = psrot.next()
                S.op("pe", lambda e: e.matmul(pr[:, :ntok], lhsT=swap[:], rhs=qs[:, :ntok], start=True, stop=True),
                     reads=[swap.d, qs.d], writes=[pr.d])
                t2 = t2r.next()
                S.op("dve", lambda e: e.tensor_tensor(out=t2[:, :ntok], in0=pr[:, :ntok], in1=sb[:, :ntok], op=ALU.mult),
                     reads=[pr.d, sb.d], writes=[t2.d])
                S.op("dve", lambda e: e.tensor_tensor(out=out_ap, in0=t1[:, :ntok], in1=t2[:, :ntok], op=ALU.add),
                     reads=[t1.d, t2.d], writes=[out_dep])

            ps = proj(wk, 0)
            if rope:
                rope_evac(ps, C.kT[:, col0:col0 + ntok], C.kT.d)
            else:
                S.op("act", lambda e: e.copy(out=C.kT[:, col0:col0 + ntok], in_=ps[:, :ntok]), reads=[ps.d], writes=[C.kT.d])
            if CUT <= 2:
                continue
            if kind == "own":
                qb = qblk.next()
                for j in range(4):
                    ps = proj(wq, j)
                    rope_evac(ps, qb[:, j, :], qb.d)
                S.dma("pool", C.q_d.ap[:, :, col0:col0 + 512], qb[:], reads=[qb.d], writes=C.q_d.dd(col0, col0 + 512), own=qb.d)
                zb = zblk.next()
                for j in range(4):
                    ps = proj(wz, j)
                    S.op("act", lambda e: e.activation(out=zb[:, j, :], in_=ps[:, :], func=AF.Silu), reads=[ps.d], writes=[zb.d])
                S.dma("pool", C.zs_d.ap[:, :, col0:col0 + 512], zb[:], reads=[zb.d], writes=C.zs_d.dd(col0, col0 + 512), own=zb.d)
            pb = preblk.next()
            for j in range(12):
                ps = proj(wg, j)
                if kind == "halo":
                    for hh in range(2):
                        S.op("dve", lambda e: e.tensor_scalar(out=pb[:, j, hh * 128:(hh + 1) * 128], in0=ps[:, hh * 128:(hh + 1) * 128],
                                                              scalar1=hv[:, hh:hh + 1], scalar2=None, op0=ALU.mult),
                             reads=[ps.d, hv.d], writes=[pb.d])
                else:
                    eng = "act" if j % 2 else "dve"
                    if eng == "act":
                        S.op("act", lambda e: e.copy(out=pb[:, j, :ntok], in_=ps[:, :ntok]), reads=[ps.d], writes=[pb.d])
                    else:
                        S.op("dve", lambda e: e.tensor_copy(out=pb[:, j, :ntok], in_=ps[:, :ntok]), reads=[ps.d], writes=[pb.d])
            if kind == "own":
                S.dma("pool", C.pre_d.ap[:, :, 128 + col0:128 + col0 + 512], pb[:], reads=[pb.d], writes=C.pre_d.dd(128 + col0, 128 + col0 + 512), own=pb.d)
            elif kind == "halo":
                S.dma("pool", C.pre_d.ap[:, :, 0:128], pb[:, :, 0:128], reads=[pb.d], writes=C.pre_d.dd(0, 128), own=pb.d)
                S.dma("pool", C.pre_d.ap[:, :, 128 + T:128 + T + 128], pb[:, :, 128:256], reads=[pb.d], writes=C.pre_d.dd(128 + T, 256 + T), own=pb.d)
            else:
                S.dma("pool", C.prec_d.ap[:, :, 1:257], pb[:, :, 0:256], reads=[pb.d], writes=C.prec_d.deps, own=pb.d)
            if CUT <= 3:
                continue
            for tt in range(ntok // 128):
                if kind == "own":
                    ti = bi * 4 + tt
                elif kind == "halo":
                    ti = 32 + tt
                else:
                    ti = 34 + tt
                pt = pst.next()
                for kc in range(8):
                    S.op("pe", lambda e: e.matmul(pt[:, 0:144], lhsT=hT[:, kc, tt * 128:(tt + 1) * 128], rhs=wvab[:, kc, :],
                                                  start=(kc == 0), stop=(kc == 7)),
                         reads=[wvab.d, hT.d], writes=[pt.d], sig=(kc == 7))
                VAR = os.environ.get("VAR", "")
                if "a" not in VAR:
                    S.op("act", lambda e: e.copy(out=C.V[:, ti, :], in_=pt[:, 0:128]), reads=[pt.d], writes=[C.V.d])
                if "b" not in VAR:
                    if True:
                        S.op("act", lambda e: e.copy(out=C.ab[:, ti, :], in_=pt[:, 128:144]), reads=[pt.d], writes=[C.ab.d])
                    else:
                        S.op("dve", lambda e: e.tensor_copy(out=C.ab[:, ti, :], in_=pt[:, 128:144]), reads=[pt.d] + ([C.V.d] if "s" in VAR else []), writes=[C.ab.d])
        S.barrier()
        S.stack = old


def phase2(S, C):
    with ExitStack() as ph:
        old = S.stack
        S.stack = ph
        amask = load_const(S, C, "amask")
        sinkrow = load_const(S, C, "sinkrow")
        esink = SBT(S, "esink", [64, 2, 512], F32)
        S.op("act", lambda e: e.activation(out=esink[:], in_=sinkrow[:], func=AF.Exp), reads=[sinkrow.d], writes=[esink.d])
        qring = Ring(S, "qld", [128, 4, 512], BF16, 2)
        pss = Ring(S, "pss", [128, 512], F32, 4, psum=True)
        pso = Ring(S, "pso", [64, 512], F32, 2, psum=True)
        psd = Ring(S, "psd", [64, 512], F32, 2, psum=True)
        ptr = Ring(S, "pT", [128, 512], BF16, 5)
        denr = Ring(S, "den", [64, 512], F32, 2)
        outr = Ring(S, "aout", [64, 2, 4, 512], BF16, 2)
        def kcols(ti):
            if ti < 32:
                return ti * 128
            return T + (ti - 32) * 128
        tasks = []
        for qblk_i in range(T // 512):
            for sub in range(4):
                qb = qblk_i * 4 + sub
                left = (32, 2) if qb == 0 else (qb - 1, 0)
                right = (33, 3) if qb == NT - 1 else (qb + 1, 1)
                keyblocks = [left, (qb, None), right, (34, None), (35, None)]
                for kap in range(2):
                    for n, (kt, mi) in enumerate(keyblocks):
                        tasks.append((qblk_i, sub, kap, n, kt, mi))
        state = {"qt": {}, "pT": {}, "po": None, "pd": None, "ot": {}}

        def emit_S(t):
            qblk_i, sub, kap, n, kt, mi = tasks[t]
            if qblk_i not in state["qt"]:
                qt = qring.next()
                S.dma("sp", qt[:], C.q_d.ap[:, :, qblk_i * 512:(qblk_i + 1) * 512], reads=C.q_d.dd(qblk_i * 512, qblk_i * 512 + 512), writes=[qt.d])
                state["qt"][qblk_i] = qt
            qt = state["qt"][qblk_i]
            ps = pss.next()
            kc0 = kcols(kt)
            S.op("pe", lambda e: e.matmul(ps[:].rearrange("p (g n) -> p g n", g=4),
                                          lhsT=C.kT[64 * kap:64 * kap + 64, kc0:kc0 + 128],
                                          rhs=qt[64 * kap:64 * kap + 64, :, sub * 128:(sub + 1) * 128],
                                          start=True, stop=True),
                 reads=[C.kT.d, qt.d], writes=[ps.d])
            pT = ptr.next()
            S.op("act", lambda e: e.activation(out=pT[:], in_=ps[:], func=AF.Exp, scale=0.125), reads=[ps.d], writes=[pT.d])
            if mi is not None:
                S.op("dve", lambda e: e.tensor_tensor(out=pT[:], in0=pT[:], in1=amask[:, mi, :], op=ALU.mult),
                     reads=[pT.d, amask.d], writes=[pT.d])
            state["pT"][t] = pT

        def emit_PV(t):
            qblk_i, sub, kap, n, kt, mi = tasks[t]
            pT = state["pT"].pop(t)
            if n == 0:
                state["po"], state["pd"] = pso.next(), psd.next()
            po, pd = state["po"], state["pd"]
            if qblk_i not in state["ot"]:
                state["ot"][qblk_i] = outr.next()
            ot = state["ot"][qblk_i]
            S.op("pe", lambda e: e.matmul(po[:], lhsT=C.V[:, kt, 64 * kap:64 * kap + 64], rhs=pT[:], start=(n == 0), stop=(n == 4)),
                 reads=[C.V.d, pT.d], writes=[po.d], sig=(n == 4))
            S.op("pe", lambda e: e.matmul(pd[:], lhsT=C.ones_bf[:, 0:64], rhs=pT[:], start=(n == 0), stop=(n == 4)),
                 reads=[C.ones_bf.d, pT.d], writes=[pd.d], sig=(n == 4))
            if n == 4:
                den = denr.next()
                S.op("dve", lambda e: e.tensor_tensor(out=den[:], in0=pd[:], in1=esink[:, kap, :], op=ALU.add),
                     reads=[pd.d, esink.d], writes=[den.d])
                S.op("dve", lambda e: e.reciprocal(out=den[:], in_=den[:]), reads=[den.d], writes=[den.d])
                S.op("dve", lambda e: e.tensor_tensor(out=ot[:, kap, :, sub * 128:(sub + 1) * 128],
                                                      in0=po[:].rearrange("p (g n) -> p g n", g=4),
                                                      in1=den[:].rearrange("p (g n) -> p g n", g=4), op=ALU.mult),
                     reads=[po.d, den.d], writes=[ot.d])
                if sub == 3 and kap == 1:
                    S.dma("pool", C.attn_d.ap[:, :, qblk_i * 512:(qblk_i + 1) * 512], ot[:].rearrange("p k g n -> p (k g) n"),
                          reads=[ot.d], writes=C.attn_d.dd(qblk_i * 512, qblk_i * 512 + 512), own=ot.d)
        LOOK = 3
        for t in range(min(LOOK, len(tasks))):
            emit_S(t)
        for t in range(len(tasks)):
            if t + LOOK < len(tasks):
                emit_S(t + LOOK)
            emit_PV(t)
        S.barrier()
        S.stack = old


class Slot:
    def __init__(self, ap, d):
        self.ap = ap
        self.d = d

    def __getitem__(self, idx):
        return self.ap[idx]


class SlotRing:
    def __init__(self, S, name, dtype, nslots, psum=False):
        per = 4 if dtype == F32 else 8
        if not psum:
            per = 1
        self.slots = []
        for b in range((nslots + per - 1) // per):
            t = SBT(S, "%s%d" % (name, b), [128, per, 128], dtype, psum=psum, ndeps=per)
            for i in range(per):
                self.slots.append(Slot(t[:, i, :], t.deps[0] if psum else t.deps[i]))
        self.slots = self.slots[:nslots]
        self.i = 0

    def next(self):
        s = self.slots[self.i % len(self.slots)]
        self.i += 1
        return s


class PairRing:
    def __init__(self, S, name, dtype, n):
        t = SBT(S, name, [128, n, 2, 128], dtype, psum=True)
        self.slots = [Slot(t[:, i, :, :], t.d) for i in range(n)]
        self.i = 0

    def next(self):
        s = self.slots[self.i % len(self.slots)]
        self.i += 1
        return s


def gdn_gates(S, C, G):
    ab = C.ab
    dtb = load_const(S, C, "gdtb")
    alog = load_const(S, C, "galog")
    cmat = load_const(S, C, "cmat")
    G.one_t = SBT(S, "one_t", [128, 1], F32)
    S.op("pool", lambda e: e.memset(G.one_t[:], 1.0), writes=[G.one_t.d])
    eal = SBT(S, "eal", [128, 8], F32)
    S.op("act", lambda e: e.activation(out=eal[:], in_=alog[:], func=AF.Exp), reads=[alog.d], writes=[eal.d])
    xg = SBT(S, "xg", [128, 36, 8], F32)
    S.op("dve", lambda e: e.tensor_tensor(out=xg[:], in0=ab[:, :, 0:8], in1=dtb[:, None, :].to_broadcast([128, 36, 8]), op=ALU.add),
         reads=[ab.d, dtb.d], writes=[xg.d])
    S.op("act", lambda e: e.activation(out=xg[:], in_=xg[:], func=AF.Exp), reads=[xg.d], writes=[xg.d])
    S.op("act", lambda e: e.activation(out=xg[:], in_=xg[:], func=AF.Ln, bias=G.one_t[:, 0:1], scale=1.0),
         reads=[xg.d, G.one_t.d], writes=[xg.d])
    g = SBT(S, "gg", [128, 36, 8], F32)
    S.op("dve", lambda e: e.scalar_tensor_tensor(out=g[:], in0=xg[:], scalar=-1.0, in1=eal[:, None, :].to_broadcast([128, 36, 8]),
                                                 op0=ALU.mult, op1=ALU.mult), reads=[xg.d, eal.d], writes=[g.d])
    G.beta = SBT(S, "beta", [128, 36, 8], F32)
    S.op("act", lambda e: e.activation(out=G.beta[:], in_=ab[:, :, 8:16], func=AF.Sigmoid), reads=[ab.d], writes=[G.beta.d])
    G.GT = SBT(S, "GT", [128, 36, 16], F32)
    _pst = ExitStack()
    _old = S.stack
    S.stack = _pst
    psg = Ring(S, "psg", [128, 18, 16], F32, 1, psum=True)
    S.stack = _old
    for half in range(2):
        pg = psg.next()
        for tl in range(18):
            ti = half * 18 + tl
            for m in range(4):
                dirn = m // 2
                S.op("pe", lambda e: e.matmul(pg[:, tl, m * 4:(m + 1) * 4], lhsT=cmat[:, m, :], rhs=g[:, ti, dirn * 4:(dirn + 1) * 4],
                                              start=True, stop=True),
                     reads=[cmat.d, g.d], writes=[pg.d], sig=(tl == 17 and m == 3))
        S.op("act", lambda e: e.copy(out=G.GT[:, half * 18:(half + 1) * 18, :], in_=pg[:]), reads=[pg.d], writes=[G.GT.d])
    S.barrier()
    _pst.close()
    G.negG = SBT(S, "negG", [128, 36, 16], F32)
    S.op("dve", lambda e: e.tensor_scalar(out=G.negG[:], in0=G.GT[:], scalar1=-1.0, scalar2=None, op0=ALU.mult),
         reads=[G.GT.d], writes=[G.negG.d])
    G.eGT = SBT(S, "eGT", [128, 36, 16], F32)
    S.op("act", lambda e: e.activation(out=G.eGT[:], in_=G.GT[:], func=AF.Exp), reads=[G.GT.d], writes=[G.eGT.d])
    G.bEG = SBT(S, "bEG", [128, 36, 8], F32)
    for dirn in range(2):
        S.op("dve", lambda e: e.tensor_tensor(out=G.bEG[:, :, dirn * 4:(dirn + 1) * 4], in0=G.beta[:, :, dirn * 4:(dirn + 1) * 4],
                                              in1=G.eGT[:, :, dirn * 8:dirn * 8 + 4], op=ALU.mult),
             reads=[G.beta.d, G.eGT.d], writes=[G.bEG.d])


def gdn_setup(S, C, G):
    G.ident_f = load_const(S, C, "ident_f")
    G.ones_f = load_const(S, C, "ones_f")
    G.gmask = load_const(S, C, "gmask")
    G.lvl = load_const(S, C, "lvlmask")
    gconv = load_const(S, C, "gconv")
    G.diag = SBT(S, "diag", [128, 36, 128], BF16)
    for c in range(12):
        for k in range(3):
            S.op("dve" if (c + k) % 2 else "pool",
                 lambda e: e.tensor_scalar(out=G.diag[:, c * 3 + k, :], in0=C.ident_bf[:], scalar1=gconv[:, c, k:k + 1], scalar2=None,
                                           op0=ALU.mult), reads=[C.ident_bf.d, gconv.d], writes=[G.diag.d])
    G.psC = SlotRing(S, "psC", F32, 8, psum=True)
    G.psFs = [SlotRing(S, "psF%d" % i, F32, 4, psum=True) for i in range(4)]
    G.psBs = [SlotRing(S, "psB%d" % i, BF16, 8, psum=True) for i in range(2)]
    G.psF = G.psFs[0]
    G.S32 = SBT(S, "S32", [128, 8, 128], F32, ndeps=8)
    G.P32 = SBT(S, "P32", [128, 8, 128], F32, ndeps=8)
    G.Sbf = SBT(S, "Sbf", [128, 8, 128], BF16, ndeps=8)
    G.Pbf = SBT(S, "Pbf", [128, 8, 128], BF16, ndeps=8)


def gdn_pass(S, C, G, dirn, heads, tiles, with_P, with_out, pre_src, tag, psF, psB):
    f32r = lambda nm, n=2: SlotRing(S, tag + nm, F32, n)
    bfr = lambda nm, n=2: SlotRing(S, tag + nm, BF16, n)
    prer = Ring(S, tag + "prew", [128, 12, 130], BF16, 1)
    csr = Ring(S, tag + "cs", [128, 128], F32, 2)
    sqr = Ring(S, tag + "sq", [128, 128], BF16, 2)
    rnr = Ring(S, tag + "rn", [128, 128], F32, 2)
    qkb = Ring(S, tag + "qkb", [128, 8, 128], BF16, 1)
    vbk = Ring(S, tag + "vbk", [128, 4, 128], BF16, 1)
    oblk = Ring(S, tag + "oblk", [128, 4, 128], F32, 1)
    rblk = Ring(S, tag + "rblk", [128, 4, 128], BF16, 1)
    r_ng, r_gs, r_dm, r_e, r_eg, r_ta, r_tq = (f32r(n) for n in ("ng", "gs", "dm", "e", "eg", "ta", "tq"))
    r_q = bfr("Q", 2)
    r_p = bfr("P", 2)
    r_m = bfr("M", 5)
    r_y = bfr("Y", 4)
    r_a, r_b, r_qk, r_qkT, r_kbg, r_kt, r_vb, r_qg, r_w, r_vn, r_vnP = (bfr(n) for n in
                                                                      ("A", "B", "qk", "qkT", "kbg", "kt", "vb", "qg", "w", "vn", "vnP"))
    gi = dirn * 8
    mi = dirn * 2
    last_col = 127 if dirn == 0 else 0
    hlo, hhi = min(heads), max(heads) + 1
    for bi in range(len(tiles)):
        btiles = tiles[bi:bi + 1]
        lo = btiles[0][1]
        ntok = 128
        src_ap, src_deps = pre_src(lo, ntok)
        pw = prer.next()
        S.dma("sp", pw[:, :, :ntok + 2], src_ap, reads=src_deps, writes=[pw.d])
        qk = qkb.next()
        vb_ = vbk.next()
        for c in [h for h in heads] + [4 + h for h in heads] + [8 + h for h in heads]:
            pc = G.psC.next()
            for k in range(3):
                S.op("pe", lambda e: e.matmul(pc[:, :ntok], lhsT=G.diag[:, c * 3 + k, :], rhs=pw[:, c, k:k + ntok],
                                              start=(k == 0), stop=(k == 2)),
                     reads=[G.diag.d, pw.d], writes=[pc.d], sig=(k == 2))
            if c >= 8:
                S.op("act", lambda e: e.activation(out=vb_[:, c - 8, :ntok], in_=pc[:, :ntok], func=AF.Silu), reads=[pc.d], writes=[vb_.d])
                yield
                continue
            cs = csr.next()
            S.op("act", lambda e: e.activation(out=cs[:, :ntok], in_=pc[:, :ntok], func=AF.Silu), reads=[pc.d], writes=[cs.d])
            sq = sqr.next()
            S.op("dve", lambda e: e.tensor_tensor(out=sq[:, :ntok], in0=cs[:, :ntok], in1=cs[:, :ntok], op=ALU.mult),
                 reads=[cs.d], writes=[sq.d])
            yield
            pn = G.psC.next()
            S.op("pe", lambda e: e.matmul(pn[:, :ntok], lhsT=C.ones_bf[:], rhs=sq[:, :ntok], start=True, stop=True),
                 reads=[C.ones_bf.d, sq.d], writes=[pn.d])
            rn = rnr.next()
            S.op("act", lambda e: e.activation(out=rn[:, :ntok], in_=pn[:, :ntok], func=AF.Ln, bias=C.eps_t[:, 0:1], scale=1.0),
                 reads=[pn.d, C.eps_t.d], writes=[rn.d])
            S.op("act", lambda e: e.activation(out=rn[:, :ntok], in_=rn[:, :ntok], func=AF.Exp, scale=-0.5), reads=[rn.d], writes=[rn.d])
            scl = (128.0 ** -0.5) if c < 4 else 1.0
            S.op("dve", lambda e: e.scalar_tensor_tensor(out=qk[:, c, :ntok], in0=cs[:, :ntok], scalar=scl, in1=rn[:, :ntok],
                                                         op0=ALU.mult, op1=ALU.mult), reads=[cs.d, rn.d], writes=[qk.d])
            yield
        ob = oblk.next() if with_out else None
        rb = rblk.next() if with_out else None
        for ti, col in btiles:
            c0 = col - lo
            cols = slice(c0, c0 + 128)
            for h in heads:
                sidx = dirn * 4 + h
                kT_t = qk[:, 4 + h, cols]
                qT_t = qk[:, h, cols]
                vT_t = vb_[:, h, cols]
                Gcol = G.GT[:, ti, gi + h:gi + h + 1]
                nGcol = G.negG[:, ti, gi + h:gi + h + 1]
                betac = G.beta[:, ti, dirn * 4 + h:dirn * 4 + h + 1]
                bEGc = G.bEG[:, ti, dirn * 4 + h:dirn * 4 + h + 1]
                etailc = G.eGT[:, ti, gi + 4 + h:gi + 4 + h + 1]
                kk = psF.next()
                S.op("pe", lambda e: e.matmul(kk[:], lhsT=kT_t, rhs=kT_t, start=True, stop=True), reads=[qk.d], writes=[kk.d])
                yield
                qkp = psF.next()
                S.op("pe", lambda e: e.matmul(qkp[:], lhsT=qT_t, rhs=kT_t, start=True, stop=True), reads=[qk.d], writes=[qkp.d])
                yield
                ktok = psB.next()
                S.op("pe", lambda e: e.transpose(ktok[:], kT_t, C.ident_bf[:]), reads=[qk.d, C.ident_bf.d], writes=[ktok.d])
                yield
                vtok = psB.next()
                S.op("pe", lambda e: e.transpose(vtok[:], vT_t, C.ident_bf[:]), reads=[vb_.d, C.ident_bf.d], writes=[vtok.d])
                ng = r_ng.next()
                S.op("dve", lambda e: e.tensor_scalar(out=ng[:], in0=G.ones_f[:], scalar1=nGcol, scalar2=None, op0=ALU.mult),
                     reads=[G.ones_f.d, G.negG.d], writes=[ng.d])
                yield
                gbc = psF.next()
                S.op("pe", lambda e: e.matmul(gbc[:], lhsT=ng[:], rhs=G.ident_f[:], start=True, stop=True),
                     reads=[ng.d, G.ident_f.d], writes=[gbc.d])
                gs = r_gs.next()
                S.op("dve", lambda e: e.tensor_copy(out=gs[:], in_=gbc[:]), reads=[gbc.d], writes=[gs.d])
                dm = r_dm.next()
                S.op("dve", lambda e: e.tensor_scalar(out=dm[:], in0=gs[:], scalar1=Gcol, scalar2=0.0, op0=ALU.add, op1=ALU.min),
                     reads=[gs.d, G.GT.d], writes=[dm.d])
                E = r_e.next()
                S.op("act", lambda e: e.activation(out=E[:], in_=dm[:], func=AF.Exp), reads=[dm.d], writes=[E.d])
                eg = r_eg.next()
                S.op("act", lambda e: e.activation(out=eg[:], in_=gs[:], func=AF.Exp, scale=-1.0), reads=[gs.d], writes=[eg.d])
                ta = r_ta.next()
                S.op("dve", lambda e: e.tensor_tensor(out=ta[:], in0=kk[:], in1=E[:], op=ALU.mult), reads=[kk.d, E.d], writes=[ta.d])
                A = r_q.next()
                S.op("dve", lambda e: e.scalar_tensor_tensor(out=A[:], in0=ta[:], scalar=betac, in1=G.gmask[:, mi + 1, :],
                                                             op0=ALU.mult, op1=ALU.mult),
                     reads=[ta.d, G.beta.d, G.gmask.d], writes=[A.d])
                tq = r_tq.next()
                S.op("dve", lambda e: e.tensor_tensor(out=tq[:], in0=qkp[:], in1=E[:], op=ALU.mult), reads=[qkp.d, E.d], writes=[tq.d])
                qkm = r_qk.next()
                S.op("dve", lambda e: e.tensor_tensor(out=qkm[:], in0=tq[:], in1=G.gmask[:, mi, :], op=ALU.mult),
                     reads=[tq.d, G.gmask.d], writes=[qkm.d])
                yield
                bps = psB.next()
                S.op("pe", lambda e: e.transpose(bps[:], A[:], C.ident_bf[:]), reads=[A.d, C.ident_bf.d], writes=[bps.d])
                Bm = r_p.next()
                S.op("act", lambda e: e.copy(out=Bm[:], in_=bps[:]), reads=[bps.d], writes=[Bm.d])
                yield
                qps = psB.next()
                S.op("pe", lambda e: e.transpose(qps[:], qkm[:], C.ident_bf[:]), reads=[qkm.d, C.ident_bf.d], writes=[qps.d])
                qkT = r_qkT.next()
                S.op("act", lambda e: e.copy(out=qkT[:], in_=qps[:]), reads=[qps.d], writes=[qkT.d])
                kbg = r_kbg.next()
                S.op("act", lambda e: e.activation(out=kbg[:], in_=ktok[:], func=AF.Identity, scale=bEGc), reads=[ktok.d, G.bEG.d], writes=[kbg.d])
                kt = r_kt.next()
                S.op("act", lambda e: e.activation(out=kt[:], in_=ktok[:], func=AF.Identity, scale=etailc), reads=[ktok.d, G.eGT.d], writes=[kt.d])
                vbt = r_vb.next()
                S.op("act", lambda e: e.activation(out=vbt[:], in_=vtok[:], func=AF.Identity, scale=betac), reads=[vtok.d, G.beta.d], writes=[vbt.d])
                qg = r_qg.next()
                S.op("dve", lambda e: e.tensor_tensor(out=qg[:], in0=qT_t, in1=eg[:], op=ALU.mult), reads=[qk.d, eg.d], writes=[qg.d])
                lo_i, up_i = (0, 1) if dirn == 0 else (1, 0)
                Tt = r_m.next()
                Xt = r_m.next()
                tmp1 = r_y.next()
                S.op("dve", lambda e: e.tensor_tensor(out=tmp1[:], in0=A[:], in1=G.lvl[:, lo_i, :], op=ALU.mult),
                     reads=[A.d, G.lvl.d], writes=[tmp1.d])
                S.op("dve", lambda e: e.tensor_tensor(out=Tt[:], in0=C.ident_bf[:], in1=tmp1[:], op=ALU.subtract),
                     reads=[C.ident_bf.d, tmp1.d], writes=[Tt.d])
                tmp2 = r_y.next()
                S.op("dve", lambda e: e.tensor_tensor(out=tmp2[:], in0=Bm[:], in1=G.lvl[:, up_i, :], op=ALU.mult),
                     reads=[Bm.d, G.lvl.d], writes=[tmp2.d])
                S.op("dve", lambda e: e.tensor_tensor(out=Xt[:], in0=C.ident_bf[:], in1=tmp2[:], op=ALU.subtract),
                     reads=[C.ident_bf.d, tmp2.d], writes=[Xt.d])
                for li in range(1, 7):
                    last = (li == 6)
                    yield
                    yp = psF.next()
                    S.op("pe", lambda e: e.matmul(yp[:], lhsT=A[:], rhs=Xt[:], start=True, stop=True), reads=[A.d, Xt.d], writes=[yp.d])
                    Y = r_y.next()
                    S.op("dve", lambda e: e.tensor_tensor(out=Y[:], in0=yp[:], in1=G.lvl[:, li * 2 + up_i, :], op=ALU.mult),
                         reads=[yp.d, G.lvl.d], writes=[Y.d])
                    if not last:
                        yield
                        ypp = psF.next()
                        S.op("pe", lambda e: e.matmul(ypp[:], lhsT=Bm[:], rhs=Tt[:], start=True, stop=True), reads=[Bm.d, Tt.d], writes=[ypp.d])
                        Y2 = r_y.next()
                        S.op("dve", lambda e: e.tensor_tensor(out=Y2[:], in0=ypp[:], in1=G.lvl[:, li * 2 + lo_i, :], op=ALU.mult),
                             reads=[ypp.d, G.lvl.d], writes=[Y2.d])
                    yield
                    zp = psF.next()
                    S.op("pe", lambda e: e.matmul(zp[:], lhsT=Tt[:], rhs=Y[:], start=True, stop=True), reads=[Tt.d, Y.d], writes=[zp.d])
                    Xn = r_m.next()
                    S.op("dve", lambda e: e.tensor_tensor(out=Xn[:], in0=Xt[:], in1=zp[:], op=ALU.subtract), reads=[zp.d, Xt.d], writes=[Xn.d])
                    if not last:
                        yield
                        zpp = psF.next()
                        S.op("pe", lambda e: e.matmul(zpp[:], lhsT=Xt[:], rhs=Y2[:], start=True, stop=True), reads=[Xt.d, Y2.d], writes=[zpp.d])
                        Tn = r_m.next()
                        S.op("dve", lambda e: e.tensor_tensor(out=Tn[:], in0=Tt[:], in1=zpp[:], op=ALU.subtract), reads=[zpp.d, Tt.d], writes=[Tn.d])
                        Tt = Tn
                    Xt = Xn
                M = Xt
                TT = M
                yield
                wps = psF.next()
                S.op("pe", lambda e: e.matmul(wps[:], lhsT=kbg[:], rhs=TT[:], start=True, stop=True), reads=[kbg.d, TT.d], writes=[wps.d])
                wn = r_w.next()
                S.op("act", lambda e: e.activation(out=wn[:], in_=wps[:], func=AF.Identity, scale=-1.0), reads=[wps.d], writes=[wn.d])
                Sb, Sd = G.Sbf[:, sidx, :], G.Sbf.deps[sidx]
                Pb, Pd = G.Pbf[:, sidx, :], G.Pbf.deps[sidx]
                S3, S3d = G.S32[:, sidx, :], G.S32.deps[sidx]
                P3, P3d = G.P32[:, sidx, :], G.P32.deps[sidx]
                yield
                vps = psF.next()
                S.op("pe", lambda e: e.matmul(vps[:], lhsT=TT[:], rhs=vbt[:], start=True, stop=False), reads=[TT.d, vbt.d], writes=[vps.d], sig=False)
                S.op("pe", lambda e: e.matmul(vps[:], lhsT=wn[:], rhs=Sb, start=False, stop=True), reads=[wn.d, Sd], writes=[vps.d])
                vn = r_vn.next()
                S.op("act", lambda e: e.copy(out=vn[:], in_=vps[:]), reads=[vps.d], writes=[vn.d])
                if with_P:
                    yield
                    vpp = psF.next()
                    S.op("pe", lambda e: e.matmul(vpp[:], lhsT=wn[:], rhs=Pb, start=True, stop=True), reads=[wn.d, Pd], writes=[vpp.d])
                    vnP = r_vnP.next()
                    S.op("dve", lambda e: e.tensor_copy(out=vnP[:], in_=vpp[:]), reads=[vpp.d], writes=[vnP.d])
                if with_out:
                    yield
                    ops_ = psF.next()
                    S.op("pe", lambda e: e.matmul(ops_[:], lhsT=Sb, rhs=qg[:], start=True, stop=False), reads=[Sd, qg.d], writes=[ops_.d], sig=False)
                    S.op("pe", lambda e: e.matmul(ops_[:], lhsT=vn[:], rhs=qkT[:], start=False, stop=True), reads=[vn.d, qkT.d], writes=[ops_.d])
                    S.op("act", lambda e: e.copy(out=ob[:, h, cols], in_=ops_[:]), reads=[ops_.d], writes=[ob.d])
                    yield
                    rps = psF.next()
                    S.op("pe", lambda e: e.matmul(rps[:], lhsT=Pb, rhs=qg[:], start=True, stop=False), reads=[Pd, qg.d], writes=[rps.d], sig=False)
                    S.op("pe", lambda e: e.matmul(rps[:], lhsT=vnP[:], rhs=qkT[:], start=False, stop=True), reads=[vnP.d, qkT.d], writes=[rps.d])
                    S.op("dve", lambda e: e.tensor_copy(out=rb[:, h, cols], in_=rps[:]), reads=[rps.d], writes=[rb.d])
                egl = eg[:, last_col:last_col + 1]
                yield
                sps = psF.next()
                S.op("pe", lambda e: e.matmul(sps[:], lhsT=kt[:], rhs=vn[:], start=True, stop=True), reads=[kt.d, vn.d], writes=[sps.d])
                S.op("dve", lambda e: e.scalar_tensor_tensor(out=S3, in0=S3, scalar=egl, in1=sps[:], op0=ALU.mult, op1=ALU.add),
                     reads=[S3d, eg.d, sps.d], writes=[S3d])
                S.op("act", lambda e: e.copy(out=Sb, in_=S3), reads=[S3d], writes=[Sd])
                if with_P:
                    yield
                    pps = psF.next()
                    S.op("pe", lambda e: e.matmul(pps[:], lhsT=kt[:], rhs=vnP[:], start=True, stop=True), reads=[kt.d, vnP.d], writes=[pps.d])
                    S.op("dve", lambda e: e.scalar_tensor_tensor(out=P3, in0=P3, scalar=egl, in1=pps[:], op0=ALU.mult, op1=ALU.add),
                         reads=[P3d, eg.d, pps.d], writes=[P3d])
                    S.op("act", lambda e: e.copy(out=Pb, in_=P3), reads=[P3d], writes=[Pd])
                yield
        if with_out:
            S.dma("pool", C.o0_d[dirn].ap[:, hlo:hhi, lo:lo + ntok], ob[:, hlo:hhi, :ntok], reads=[ob.d], writes=C.o0_d[dirn].dd(lo, lo + ntok), own=ob.d)
            S.dma("pool", C.R_d[dirn].ap[:, hlo:hhi, lo:lo + ntok], rb[:, hlo:hhi, :ntok], reads=[rb.d], writes=C.R_d[dirn].dd(lo, lo + ntok), own=rb.d)


def phase3(S, C):
    G = Ctx()
    C.sctx = SBT(S, "sctx", [128, 8, 128], F32)
    with ExitStack() as ph:
        old = S.stack
        S.stack = ph
        gdn_gates(S, C, G)
        gdn_setup(S, C, G)
        zt = SBT(S, "zt", [128, 12, 2], BF16)
        S.op("pool", lambda e: e.memset(zt[:], 0.0), writes=[zt.d])
        S.dma("pool", C.prec_d.ap[:, :, 0:1], zt[:, :, 0:1], reads=[zt.d], writes=C.prec_d.deps, own=zt.d, allow_slow_non_contiguous=True)
        S.dma("pool", C.prec_d.ap[:, :, 257:258], zt[:, :, 1:2], reads=[zt.d], writes=C.prec_d.deps, own=zt.d, allow_slow_non_contiguous=True)
        for i in range(8):
            S.op("pool", lambda e: e.memset(G.S32[:, i, :], 0.0), writes=[G.S32.deps[i]])
            S.op("pool", lambda e: e.memset(G.Sbf[:, i, :], 0.0), writes=[G.Sbf.deps[i]])

        def ctx_src(lo, ntok):
            return C.prec_d.ap[:, :, lo:lo + ntok + 2], C.prec_d.deps

        def lat_src(lo, ntok):
            return C.pre_d.ap[:, :, 128 + lo - 1:128 + lo + ntok + 1], C.pre_d.dd(128 + lo - 1, 128 + lo + ntok + 1)

        def run(gens):
            gens = list(gens)
            while gens:
                for g in list(gens):
                    try:
                        next(g)
                    except StopIteration:
                        gens.remove(g)
        with ExitStack() as ph2:
            S.stack = ph2
            HG = [(0, 1), (2, 3)]
            run([gdn_pass(S, C, G, d_, HG[k_], [(34, 0), (35, 128)] if d_ == 0 else [(35, 128), (34, 0)], False, False, ctx_src,
                          "c%d%d" % (d_, k_), G.psFs[d_ * 2 + k_], G.psBs[d_]) for d_ in range(2) for k_ in range(2)])
            for i in range(8):
                S.op("dve", lambda e: e.tensor_copy(out=C.sctx[:, i, :], in_=G.S32[:, i, :]), reads=[G.S32.deps[i]], writes=[C.sctx.d])
                S.op("pool", lambda e: e.memset(G.S32[:, i, :], 0.0), writes=[G.S32.deps[i]])
                S.op("pool", lambda e: e.memset(G.Sbf[:, i, :], 0.0), writes=[G.Sbf.deps[i]])
                S.op("pool", lambda e: e.tensor_copy(out=G.P32[:, i, :], in_=G.ident_f[:]), reads=[G.ident_f.d], writes=[G.P32.deps[i]])
                S.op("pool", lambda e: e.tensor_copy(out=G.Pbf[:, i, :], in_=C.ident_bf[:]), reads=[C.ident_bf.d], writes=[G.Pbf.deps[i]])
            S.barrier()
            S.stack = ph
        NTL = int(os.environ.get("NTL", str(NT)))
        with ExitStack() as ph2:
            S.stack = ph2
            ft = [(ti, ti * 128) for ti in range(NTL)]
            bt = [(ti, ti * 128) for ti in reversed(range(NT - NTL, NT))]
            gens = [gdn_pass(S, C, G, d_, HG[k_], ft if d_ == 0 else bt, True, True, lat_src,
                             "l%d%d" % (d_, k_), G.psFs[d_ * 2 + k_], G.psBs[d_]) for d_ in range(2) for k_ in range(2)]
            if getattr(C, "preconverted", False):
                gens.append(conv_gen(S, C))
            run(gens)
            xs = SBT(S, "xs", [128, 8, 2, 128], F32)
            for i in range(8):
                S.op("act", lambda e: e.copy(out=xs[:, i, 0, :], in_=G.S32[:, i, :]), reads=[G.S32.deps[i]], writes=[xs.d])
                pp = G.psF.next()
                S.op("pe", lambda e: e.matmul(pp[:], lhsT=G.P32[:, i, :], rhs=G.ident_f[:], start=True, stop=True),
                     reads=[G.P32.deps[i], G.ident_f.d], writes=[pp.d])
                S.op("act", lambda e: e.copy(out=xs[:, i, 1, :], in_=pp[:]), reads=[pp.d], writes=[xs.d])
            S.dma("pool", C.xch_d.ap, xs[:], reads=[xs.d], writes=C.xch_d.deps, own=xs.d)
            S.dma("pool", C.sctx_d.ap, C.sctx[:], reads=[C.sctx.d], writes=C.sctx_d.deps, own=C.sctx.d)
            S.barrier()
            S.stack = ph
        S.stack = old


def convert_ffn_weights(S, C, l):
    with ExitStack() as ph:
        old = S.stack
        S.stack = ph
        st = Ring(S, "cvst", [128, 22, 128], F32, 2)
        sb = Ring(S, "cvsb", [128, 22, 128], BF16, 2)
        k = 0
        for j in range(NJ):
            for gi, nm in enumerate(("ffn_w_gate", "ffn_w_up")):
                s_, b_ = st.next(), sb.next()
                S.dma("sp", s_[:, 0:8, :], C.inp[nm][l, :, j * 128:(j + 1) * 128].rearrange("(kc p) n -> p kc n", p=128), writes=[s_.d])
                S.op(("pool", "dve", "act")[k % 3], (lambda e: e.tensor_copy(out=b_[:, 0:8, :], in_=s_[:, 0:8, :])) if k % 3 != 2 else
                     (lambda e: e.copy(out=b_[:, 0:8, :], in_=s_[:, 0:8, :])), reads=[s_.d], writes=[b_.d])
                S.dma("pool", C.wgu_d.ap[j, :, gi, :, :], b_[:, 0:8, :], reads=[b_.d], writes=C.wgu_d.deps, own=b_.d)
                k += 1
        for n in range(8):
            s_, b_ = st.next(), sb.next()
            S.dma("sp", s_[:], C.inp["ffn_w_down"][l, :, n * 128:(n + 1) * 128].rearrange("(j p) n -> p j n", p=128), writes=[s_.d])
            S.op(("pool", "dve", "act")[k % 3], (lambda e: e.tensor_copy(out=b_[:], in_=s_[:])) if k % 3 != 2 else
                 (lambda e: e.copy(out=b_[:], in_=s_[:])), reads=[s_.d], writes=[b_.d])
            S.dma("pool", C.wd_d.ap[n], b_[:], reads=[b_.d], writes=C.wd_d.deps, own=b_.d)
            k += 1
        S.barrier()
        S.stack = old


def rstd_of(S, C, yT, ntok, R):
    sq = R["sq"].next()
    for c in range(8):
        S.op("dve", lambda e: e.tensor_tensor(out=sq[:, c, :ntok], in0=yT[:, c, :ntok], in1=yT[:, c, :ntok], op=ALU.mult),
             reads=[yT.d], writes=[sq.d])
    psn = R["ps"].next()
    for c in range(8):
        S.op("pe", lambda e: e.matmul(psn[:, :ntok], lhsT=C.ones_bf[:], rhs=sq[:, c, :ntok], start=(c == 0), stop=(c == 7)),
             reads=[C.ones_bf.d, sq.d], writes=[psn.d], sig=(c == 7))
    rstd = R["rstd"].next()
    S.op("act", lambda e: e.activation(out=rstd[:, :ntok], in_=psn[:, :ntok], func=AF.Ln, bias=C.eps_t[:, 0:1], scale=1.0 / D),
         reads=[psn.d, C.eps_t.d], writes=[rstd.d])
    S.op("act", lambda e: e.activation(out=rstd[:, :ntok], in_=rstd[:, :ntok], func=AF.Exp, scale=-0.5), reads=[rstd.d], writes=[rstd.d])
    return rstd


def postnorm_residual(S, C, xb, yT, gg, R):
    rstd = rstd_of(S, C, yT, 512, R)
    for c in range(8):
        tmp = R["tmp"].next()
        S.op("dve", lambda e: e.scalar_tensor_tensor(out=tmp[:], in0=yT[:, c, :], scalar=gg[:, c:c + 1], in1=rstd[:], op0=ALU.mult, op1=ALU.mult),
             reads=[yT.d, gg.d, rstd.d], writes=[tmp.d])
        S.op("dve", lambda e: e.tensor_tensor(out=xb[:, c, :], in0=xb[:, c, :], in1=tmp[:], op=ALU.add), reads=[xb.d, tmp.d], writes=[xb.d])


def evac_sq_scaled(S, ps, yT, sq, n, gg):
    S.op("act", lambda e: e.activation(out=sq[:, n, :], in_=ps[:], func=AF.Square), reads=[ps.d], writes=[sq.d])
    S.op("act", lambda e: e.activation(out=yT[:, n, :], in_=ps[:], func=AF.Identity, scale=gg[:, n:n + 1]), reads=[ps.d, gg.d], writes=[yT.d])


def postnorm_residual_pre(S, C, xb, ys, sq, R):
    psn = R["ps"].next()
    for c in range(8):
        S.op("pe", lambda e: e.matmul(psn[:], lhsT=C.ones_bf[:], rhs=sq[:, c, :], start=(c == 0), stop=(c == 7)),
             reads=[C.ones_bf.d, sq.d], writes=[psn.d], sig=(c == 7))
    rstd = R["rstd"].next()
    S.op("act", lambda e: e.activation(out=rstd[:], in_=psn[:], func=AF.Ln, bias=C.eps_t[:, 0:1], scale=1.0 / D),
         reads=[psn.d, C.eps_t.d], writes=[rstd.d])
    S.op("act", lambda e: e.activation(out=rstd[:], in_=rstd[:], func=AF.Exp, scale=-0.5), reads=[rstd.d], writes=[rstd.d])
    for c in range(8):
        tmp = R["tmp"].next()
        S.op("dve", lambda e: e.tensor_tensor(out=tmp[:], in0=ys[:, c, :], in1=rstd[:], op=ALU.mult), reads=[ys.d, rstd.d], writes=[tmp.d])
        S.op("dve", lambda e: e.tensor_tensor(out=xb[:, c, :], in0=xb[:, c, :], in1=tmp[:], op=ALU.add), reads=[xb.d, tmp.d], writes=[xb.d])


def ffn_block(S, C, xb, l, R, F):
    hT = F["hT"]
    norm_mod_block(S, C, xb, 512, C.vec[("a2", l)], C.vec[("b2", l)], hT, R)
    actT = F["actT"]
    for j in range(NJ):
        w = F["wgu"].next()
        wgu_d = C.wgu_d[l] if isinstance(C.wgu_d, list) else C.wgu_d
        S.dma("sp", w[:], wgu_d.ap[j], reads=wgu_d.deps, writes=[w.d])
        pg, pu = F["psg"].next(), F["psu"].next()
        for gi, ps in ((0, pg), (1, pu)):
            for kc in range(8):
                S.op("pe", lambda e: e.matmul(ps[:], lhsT=w[:, gi, kc, :], rhs=hT[:, kc, :], start=(kc == 0), stop=(kc == 7)),
                     reads=[w.d, hT.d], writes=[ps.d], sig=(kc == 7))
        sg = F["sg"].next()
        S.op("act", lambda e: e.activation(out=sg[:], in_=pg[:], func=AF.Silu), reads=[pg.d], writes=[sg.d])
        S.op("dve", lambda e: e.tensor_tensor(out=actT[:, j, :], in0=pu[:], in1=sg[:], op=ALU.mult), reads=[pu.d, sg.d], writes=[actT.d])
    fT = F["fT"]
    for n in range(8):
        w = F["wd"].next()
        wd_d = C.wd_d[l] if isinstance(C.wd_d, list) else C.wd_d
        S.dma("sp", w[:], wd_d.ap[n], reads=wd_d.deps, writes=[w.d])
        ps = F["psg"].next()
        for j in range(NJ):
            S.op("pe", lambda e: e.matmul(ps[:], lhsT=w[:, j, :], rhs=actT[:, j, :], start=(j == 0), stop=(j == NJ - 1)),
                 reads=[w.d, actT.d], writes=[ps.d], sig=(j == NJ - 1))
        if n == 0:
            sqf = R["sq"].next()
        evac_sq_scaled(S, ps, fT, sqf, n, C.vec[("gg2", l)])
    postnorm_residual_pre(S, C, xb, fT, sqf, R)


def common_rings(S, deep=False):
    R = {"sq": Ring(S, "sq", [128, 8, 512], BF16, 1), "ps": Ring(S, "psn", [128, 512], F32, 1, psum=True),
         "rstd": Ring(S, "rstd", [128, 512], F32, 2), "tmp": Ring(S, "ntmp", [128, 512], F32, 3)}
    F = {"hT": SBT(S, "hT", [128, 8, 512], BF16), "actT": SBT(S, "actT", [128, NJ, 512], BF16),
         "fT": SBT(S, "fT", [128, 8, 512], F32), "wgu": Ring(S, "wgu", [128, 2, 8, 128], BF16, 4 if deep else 2),
         "wd": Ring(S, "wd", [128, NJ, 128], BF16, 3 if deep else 2), "psg": Ring(S, "psg", [128, 512], F32, 3, psum=True),
         "psu": Ring(S, "psu", [128, 512], F32, 3, psum=True), "sg": Ring(S, "sg", [128, 512], F32, 3 if deep else 2)}
    return R, F


def phaseB(S, C):
    if not getattr(C, "preconverted", False):
        convert_ffn_weights(S, C, 0)
    ident_f = load_const(S, C, "ident_f")
    gnormg = load_const(S, C, "gnormg")
    sel = load_const(S, C, "sel")
    Sst = SBT(S, "Sst", [128, 8, 128], BF16)
    with ExitStack() as ph:
        old = S.stack
        S.stack = ph
        xall = SBT(S, "xall", [128, 4, 8, 2, 128], F32)
        S.dma("sp", xall[:], C.inp["xall"], writes=[xall.d])
        sctx = SBT(S, "sctxB", [128, 8, 128], F32)
        S.dma("sp", sctx[:], C.inp["sctx_in"], writes=[sctx.d])
        cand = SBT(S, "cand", [128, 4, 128], F32, ndeps=4)
        acc = SBT(S, "accS", [128, 128], F32)
        psc = Ring(S, "psc", [128, 128], F32, 2, psum=True)
        for sidx in range(8):
            dirn = sidx // 4
            order = [0, 1, 2, 3] if dirn == 0 else [3, 2, 1, 0]
            first = order[0]
            S.op("dve", lambda e: e.tensor_copy(out=cand[:, first, :], in_=sctx[:, sidx, :]), reads=[sctx.d], writes=[cand.deps[first]])
            for a_, b_ in zip(order[:-1], order[1:]):
                ps = psc.next()
                S.op("pe", lambda e: e.matmul(ps[:], lhsT=xall[:, a_, sidx, 1, :], rhs=cand[:, a_, :], start=True, stop=True),
                     reads=[xall.d, cand.deps[a_]], writes=[ps.d])
                S.op("dve", lambda e: e.tensor_tensor(out=cand[:, b_, :], in0=ps[:], in1=xall[:, a_, sidx, 0, :], op=ALU.add),
                     reads=[ps.d, xall.d], writes=[cand.deps[b_]])
            S.op("dve", lambda e: e.tensor_scalar(out=acc[:], in0=cand[:, 0, :], scalar1=sel[:, 0:1], scalar2=None, op0=ALU.mult),
                 reads=[cand.deps[0], sel.d], writes=[acc.d])
            for s_ in range(1, 4):
                S.op("dve", lambda e: e.scalar_tensor_tensor(out=acc[:], in0=cand[:, s_, :], scalar=sel[:, s_:s_ + 1], in1=acc[:],
                                                             op0=ALU.mult, op1=ALU.add), reads=[cand.deps[s_], sel.d, acc.d], writes=[acc.d])
            S.op("dve", lambda e: e.tensor_copy(out=Sst[:, sidx, :], in_=acc[:]), reads=[acc.d], writes=[Sst.d])
        S.barrier()
        S.stack = old
    with ExitStack() as ph:
        old = S.stack
        S.stack = ph
        wst = Ring(S, "wst", [128, 8, 128], F32, 2)
        woa = SBT(S, "woa", [64, 8, 1024], BF16)
        wog = SBT(S, "wog", [128, 4, 1024], BF16)
        for c0 in range(0, 1024, 128):
            st = wst.next()
            S.dma("sp", st[0:64, :, :], C.inp["hy_w_out"][0:512, c0:c0 + 128].rearrange("(kg d) n -> d kg n", d=64), writes=[st.d])
            S.op("pool", lambda e: e.tensor_copy(out=woa[:, :, c0:c0 + 128], in_=st[0:64, :, :]), reads=[st.d], writes=[woa.d])
            st = wst.next()
            S.dma("sp", st[:, 0:4, :], C.inp["hy_w_out"][512:1024, c0:c0 + 128].rearrange("(h e) n -> e h n", e=128), writes=[st.d])
            S.op("pool", lambda e: e.tensor_copy(out=wog[:, :, c0:c0 + 128], in_=st[:, 0:4, :]), reads=[st.d], writes=[wog.d])
        cvs = Ring(S, "cvs2", [128, 8, 128], F32, 2)
        cvb = Ring(S, "cvb2", [128, 8, 128], BF16, 2)
        for j in range(0 if getattr(C, "preconverted", False) else 24):
            s_, b_ = cvs.next(), cvb.next()
            S.dma("sp", s_[:], C.inp["sc_w_in"][:, j * 128:(j + 1) * 128].rearrange("(kc p) n -> p kc n", p=128), writes=[s_.d])
            S.op("pool" if j % 2 else "dve", lambda e: e.tensor_copy(out=b_[:], in_=s_[:]), reads=[s_.d], writes=[b_.d])
            S.dma("pool", C.wsc_d.ap[j], b_[:], reads=[b_.d], writes=C.wsc_d.deps, own=b_.d)
        wscr = Ring(S, "wscr", [128, 8, 128], BF16, 3)
        R, F = common_rings(S)
        xring = Ring(S, "xblk", [128, 8, 512], F32, 1)
        rf = Ring(S, "rfb", [128, 2, 512], BF16, 2)
        o0 = Ring(S, "o0b", [128, 2, 512], F32, 2)
        zs = Ring(S, "zsb", [128, 4, 512], BF16, 1)
        at = Ring(S, "atb", [64, 8, 512], BF16, 1)
        mixT = SBT(S, "mixT", [128, 4, 512], BF16)
        o32 = Ring(S, "o32", [128, 512], F32, 2)
        yT = F["fT"]
        cur = Ring(S, "cub", [128, 512], BF16, 3)
        bgr_ = Ring(S, "bgb", [128, 512], BF16, 3)
        csb = Ring(S, "csb", [128, 512], F32, 2)
        pso = F["psu"]
        for bi in range(int(os.environ.get("NBLK", T // 512))):
            cs_ = slice(bi * 512, (bi + 1) * 512)
            xb = xring.next()
            S.dma("sp", xb[:], C.inp["xT"][:, :, cs_].rearrange("c p t -> p c t"), writes=[xb.d])
            z_, a_ = zs.next(), at.next()
            S.dma("sp", z_[:], C.inp["zs_in"][:, :, cs_], writes=[z_.d])
            S.dma("sp", a_[:], C.inp["attn_in"][:, :, cs_], writes=[a_.d])
            for h in range(4):
                r_, o_ = rf.next(), o0.next()
                for dirn in range(2):
                    S.dma("sp", r_[:, dirn], C.inp["R_in%d" % dirn][:, h, cs_], writes=[r_.d])
                    S.dma("sp", o_[:, dirn], C.inp["o0_in%d" % dirn][:, h, cs_], writes=[o_.d])
                ps = pso.next()
                S.op("pe", lambda e: e.matmul(ps[:], lhsT=Sst[:, h, :], rhs=r_[:, 0, :], start=True, stop=False), reads=[Sst.d, r_.d], writes=[ps.d], sig=False)
                S.op("pe", lambda e: e.matmul(ps[:], lhsT=Sst[:, 4 + h, :], rhs=r_[:, 1, :], start=False, stop=True), reads=[Sst.d, r_.d], writes=[ps.d])
                o = o32.next()
                S.op("dve", lambda e: e.tensor_tensor(out=o[:], in0=ps[:], in1=o_[:, 0, :], op=ALU.add), reads=[ps.d, o_.d], writes=[o.d])
                S.op("dve", lambda e: e.tensor_tensor(out=o[:], in0=o[:], in1=o_[:, 1, :], op=ALU.add), reads=[o.d, o_.d], writes=[o.d])
                sq = R["sq"].next()
                S.op("dve", lambda e: e.tensor_tensor(out=sq[:, 0, :], in0=o[:], in1=o[:], op=ALU.mult), reads=[o.d], writes=[sq.d])
                pn = R["ps"].next()
                S.op("pe", lambda e: e.matmul(pn[:], lhsT=C.ones_bf[:], rhs=sq[:, 0, :], start=True, stop=True), reads=[C.ones_bf.d, sq.d], writes=[pn.d])
                rn = R["rstd"].next()
                S.op("act", lambda e: e.activation(out=rn[:], in_=pn[:], func=AF.Ln, bias=C.eps_t[:, 0:1], scale=1.0 / 128), reads=[pn.d, C.eps_t.d], writes=[rn.d])
                S.op("act", lambda e: e.activation(out=rn[:], in_=rn[:], func=AF.Exp, scale=-0.5), reads=[rn.d], writes=[rn.d])
                tmp = R["tmp"].next()
                S.op("dve", lambda e: e.scalar_tensor_tensor(out=tmp[:], in0=o[:], scalar=gnormg[:, 0:1], in1=rn[:], op0=ALU.mult, op1=ALU.mult),
                     reads=[o.d, gnormg.d, rn.d], writes=[tmp.d])
                S.op("dve", lambda e: e.tensor_tensor(out=mixT[:, h, :], in0=tmp[:], in1=z_[:, h, :], op=ALU.mult), reads=[tmp.d, z_.d], writes=[mixT.d])
            for n in range(8):
                ps = F["psg"].next()
                for kg in range(8):
                    S.op("pe", lambda e: e.matmul(ps[:], lhsT=woa[:, kg, n * 128:(n + 1) * 128], rhs=a_[:, kg, :], start=(kg == 0), stop=False),
                         reads=[woa.d, a_.d], writes=[ps.d], sig=False)
                for h in range(4):
                    S.op("pe", lambda e: e.matmul(ps[:], lhsT=wog[:, h, n * 128:(n + 1) * 128], rhs=mixT[:, h, :], start=False, stop=(h == 3)),
                         reads=[wog.d, mixT.d], writes=[ps.d], sig=(h == 3))
                if n == 0:
                    sqy = R["sq"].next()
                evac_sq_scaled(S, ps, yT, sqy, n, C.vec[("gg1", 0)])
            postnorm_residual_pre(S, C, xb, yT, sqy, R)
            ffn_block(S, C, xb, 0, R, F)
            S.dma("pool", C.x2_d.ap[:, :, cs_], xb[:], reads=[xb.d], writes=C.x2_d.dd(bi * 512, bi * 512 + 512), own=xb.d)
            hT = F["hT"]
            norm_mod_block(S, C, xb, 512, C.vec[("a1", 1)], C.vec[("b1", 1)], hT, R)
            for c in range(8):
                def proj(j):
                    ps = F["psg"].next() if j < 16 else F["psu"].next()
                    wsc = wscr.next()
                    S.dma("sp", wsc[:], C.wsc_d.ap[j], reads=C.wsc_d.deps, writes=[wsc.d])
                    for kc in range(8):
                        S.op("pe", lambda e: e.matmul(ps[:], lhsT=wsc[:, kc, :], rhs=hT[:, kc, :], start=(kc == 0), stop=(kc == 7)),
                             reads=[wsc.d, hT.d], writes=[ps.d], sig=(kc == 7))
                    return ps
                pb_ = proj(c)
                bgb = bgr_.next()
                S.op("act", lambda e: e.copy(out=bgb[:], in_=pb_[:]), reads=[pb_.d], writes=[bgb.d])
                S.dma("pool", C.bg_d.ap[:, c, cs_], bgb[:], reads=[bgb.d], writes=C.bg_d.dd(bi * 512, bi * 512 + 512), own=bgb.d)
                pc_ = proj(8 + c)
                cs32 = csb.next()
                S.op("act", lambda e: e.copy(out=cs32[:], in_=pc_[:]), reads=[pc_.d], writes=[cs32.d])
                pu_ = proj(16 + c)
                cub = cur.next()
                S.op("dve", lambda e: e.tensor_tensor(out=cub[:], in0=pu_[:], in1=cs32[:], op=ALU.mult), reads=[pu_.d, cs32.d], writes=[cub.d])
                cuo = getattr(C, "cu_off", 0)
                S.dma("pool", C.cu_d.ap[:, c, cuo + bi * 512:cuo + bi * 512 + 512], cub[:], reads=[cub.d],
                      writes=C.cu_d.dd(bi * 512, bi * 512 + 512), own=cub.d)
                if getattr(C, "edge", None) is not None and bi in (0, T // 512 - 1):
                    ecol = 0 if bi == 0 else 511
                    eidx = c * 2 + (0 if bi == 0 else 1)
                    S.op("pool", lambda e: e.tensor_copy(out=C.edge[:, eidx:eidx + 1], in_=cub[:, ecol:ecol + 1]),
                         reads=[cub.d], writes=[C.edge.d])
                    if T // 512 == 1:
                        S.op("pool", lambda e: e.tensor_copy(out=C.edge[:, c * 2 + 1:c * 2 + 2], in_=cub[:, 511:512]),
                             reads=[cub.d], writes=[C.edge.d])
        S.barrier()
        S.stack = old


def phaseC(S, C):
    if not getattr(C, "preconverted", False):
        convert_ffn_weights(S, C, 1)
    with ExitStack() as ph:
        old = S.stack
        S.stack = ph
        sconv = load_const(S, C, "sconv")
        diag = SBT(S, "diagc", [128, 24, 128], BF16)
        for c in range(8):
            for k in range(3):
                S.op("dve", lambda e: e.tensor_scalar(out=diag[:, c * 3 + k, :], in0=C.ident_bf[:], scalar1=sconv[:, c, k:k + 1], scalar2=None,
                                                      op0=ALU.mult), reads=[C.ident_bf.d, sconv.d], writes=[diag.d])
        wso = SBT(S, "wso", [128, 8, 1024], BF16)
        with ExitStack() as phw:
            S.stack = phw
            wst = Ring(S, "wst", [128, 8, 256], F32, 2)
            load_weight_bf16(S, C, wso, 0, C.inp["sc_w_out"], 1024, wst)
            S.barrier()
            S.stack = ph
        R, F = common_rings(S, deep=True)
        xring = Ring(S, "xblk", [128, 8, 512], F32, 2)
        cuw = Ring(S, "cuw", [128, 8, 514], BF16, 1)
        bgr = Ring(S, "bgr", [128, 8, 512], BF16, 1)
        mT = SBT(S, "mT", [128, 8, 512], BF16)
        yT = F["fT"]
        for bi in range(int(os.environ.get("NBLK", T // 512))):
            cs_ = slice(bi * 512, (bi + 1) * 512)
            xb = xring.next()
            S.dma("sp", xb[:], C.inp["x2_in"][:, :, cs_], writes=[xb.d])
            cw, bg = cuw.next(), bgr.next()
            S.dma("sp", cw[:], C.inp["cu_ext"][:, :, bi * 512:bi * 512 + 514], writes=[cw.d])
            S.dma("sp", bg[:], C.inp["bg_in"][:, :, cs_], writes=[bg.d])
            for c in range(8):
                ps = F["psu"].next()
                for k in range(3):
                    S.op("pe", lambda e: e.matmul(ps[:], lhsT=diag[:, c * 3 + k, :], rhs=cw[:, c, k:k + 512], start=(k == 0), stop=(k == 2)),
                         reads=[diag.d, cw.d], writes=[ps.d], sig=(k == 2))
                S.op("dve", lambda e: e.tensor_tensor(out=mT[:, c, :], in0=ps[:], in1=bg[:, c, :], op=ALU.mult), reads=[ps.d, bg.d], writes=[mT.d])
            for n in range(8):
                ps = F["psg"].next()
                for kc in range(8):
                    S.op("pe", lambda e: e.matmul(ps[:], lhsT=wso[:, kc, n * 128:(n + 1) * 128], rhs=mT[:, kc, :], start=(kc == 0), stop=(kc == 7)),
                         reads=[wso.d, mT.d], writes=[ps.d], sig=(kc == 7))
                if n == 0:
                    sqy = R["sq"].next()
                evac_sq_scaled(S, ps, yT, sqy, n, C.vec[("gg1", 1)])
            postnorm_residual_pre(S, C, xb, yT, sqy, R)
            ffn_block(S, C, xb, 1, R, F)
            S.dma("pool", C.out_d.ap[:, :, cs_], xb[:], reads=[xb.d], writes=C.out_d.dd(bi * 512, bi * 512 + 512), own=xb.d)
        S.barrier()
        S.stack = old


def halo_exchange(S, C, groups):
    S.dma("pool", C.edge_d.ap, C.edge[:], reads=[C.edge.d], writes=C.edge_d.deps, own=C.edge.d)
    S.allgather(C.edge_d.ap, C.edges_g.ap, groups)
    with ExitStack() as ph:
        old = S.stack
        S.stack = ph
        selL = load_const(S, C, "selL")
        selR = load_const(S, C, "selR")
        eg = SBT_(S, "egath", [128, 4, 8, 2], BF16)
        S.dma("sp", eg[:].rearrange("p r c k -> p r (c k)"), C.edges_g.ap.rearrange("(r p) n -> p r n", p=128), writes=[eg.d])
        for nm, sel, k, col in (("hl", selL, 1, 0), ("hr", selR, 0, T + 1)):
            acc = SBT_(S, nm + "acc", [128, 8], F32)
            S.op("dve", lambda e: e.tensor_scalar(out=acc[:], in0=eg[:, 0, :, k], scalar1=sel[:, 0:1], scalar2=None, op0=ALU.mult),
                 reads=[eg.d, sel.d], writes=[acc.d])
            for r in range(1, 4):
                S.op("dve", lambda e: e.scalar_tensor_tensor(out=acc[:], in0=eg[:, r, :, k], scalar=sel[:, r:r + 1], in1=acc[:],
                                                             op0=ALU.mult, op1=ALU.add), reads=[eg.d, sel.d, acc.d], writes=[acc.d])
            hb = SBT_(S, nm + "bf", [128, 8, 1], BF16)
            S.op("dve", lambda e: e.tensor_copy(out=hb[:, :, 0], in_=acc[:]), reads=[acc.d], writes=[hb.d])
            S.dma("pool", C.cu_d.ap[:, :, col:col + 1], hb[:], reads=[hb.d], writes=C.cu_d.deps, own=hb.d, allow_slow_non_contiguous=True)
        S.barrier()
        S.stack = old


def conv_gen(S, C):
    st = Ring(S, "pcs", [128, 8, 128], F32, 2)
    sb = Ring(S, "pcb", [128, 8, 128], BF16, 2)
    for l in range(2):
        for j in range(NJ):
            for gi, nm in enumerate(("ffn_w_gate", "ffn_w_up")):
                s_, b_ = st.next(), sb.next()
                S.dma("sp", s_[:], C.inp[nm][l, :, j * 128:(j + 1) * 128].rearrange("(kc p) n -> p kc n", p=128), writes=[s_.d])
                S.op("pool", lambda e: e.tensor_copy(out=b_[:], in_=s_[:]), reads=[s_.d], writes=[b_.d])
                S.dma("pool", C.wgu_d[l].ap[j, :, gi, :, :], b_[:], reads=[b_.d], writes=C.wgu_d[l].deps, own=b_.d)
                for _sp in range(18):
                    yield
        for n in range(8):
            for j0 in range(0, NJ, 8):
                nj = min(8, NJ - j0)
                s_, b_ = st.next(), sb.next()
                S.dma("sp", s_[:, :nj, :], C.inp["ffn_w_down"][l, j0 * 128:(j0 + nj) * 128, n * 128:(n + 1) * 128].rearrange("(j p) n -> p j n", p=128),
                      writes=[s_.d])
                S.op("pool", lambda e: e.tensor_copy(out=b_[:, :nj, :], in_=s_[:, :nj, :]), reads=[s_.d], writes=[b_.d])
                S.dma("pool", C.wd_d[l].ap[n, :, j0:j0 + nj, :], b_[:, :nj, :], reads=[b_.d], writes=C.wd_d[l].deps, own=b_.d)
                for _sp in range(18):
                    yield
    for j in range(24):
        s_, b_ = st.next(), sb.next()
        S.dma("sp", s_[:], C.inp["sc_w_in"][:, j * 128:(j + 1) * 128].rearrange("(kc p) n -> p kc n", p=128), writes=[s_.d])
        S.op("pool", lambda e: e.tensor_copy(out=b_[:], in_=s_[:]), reads=[s_.d], writes=[b_.d])
        S.dma("pool", C.wsc_d.ap[j], b_[:], reads=[b_.d], writes=C.wsc_d.deps, own=b_.d)
        for _sp in range(18):
            yield


import sys, time
from concourse.bass_utils import run_bass_kernel_spmd

GROUPS = [[0, 1, 2, 3], [4, 5, 6, 7]]
NAMES_F = ["xT", "xhT", "ctxT", "cvec", "ident_bf", "ident_f", "swap_bf", "ones_bf", "ones_f", "amask", "gmask", "lvlmask", "cmat", "hv",
           "cosT", "sinT", "ada_w", "ada_bT", "gains", "wq", "wk", "wvab", "wg", "wz", "sinkrow", "gconv", "galog", "gdtb",
           "hy_w_out", "gnormg", "sel", "selL", "selR", "sc_w_in", "ffn_w_gate", "ffn_w_up", "ffn_w_down", "sconv", "sc_w_out"]


def build_fused():
    nc = bass.Bass("TRN2", target_bir_lowering=False)
    C = declare_inputs(nc, NAMES_F)
    C.q_d = DramT(nc, "q_d", [128, 4, T], BF16)
    C.zs_d = DramT(nc, "zs_d", [128, 4, T], BF16)
    C.pre_d = DramT(nc, "pre_d", [128, 12, T + 256], BF16)
    C.prec_d = DramT(nc, "prec_d", [128, 12, 258], BF16)
    C.attn_d = DramT(nc, "attn_d", [64, 8, T], BF16)
    C.o0_d = [DramT(nc, "o0_d%d" % i, [128, 4, T], F32) for i in range(2)]
    C.R_d = [DramT(nc, "R_d%d" % i, [128, 4, T], BF16) for i in range(2)]
    C.xch_d = DramT(nc, "xch_d", [128, 8, 2, 128], F32)
    C.sctx_d = DramT(nc, "sctx_d", [128, 8, 128], F32)
    C.xall_g = DramT(nc, "xall_g", [4 * 128, 2048], F32)
    C.wgu_d = [DramT(nc, "wgu_d%d" % l, [NJ, 128, 2, 8, 128], BF16, blk=10 ** 9) for l in range(2)]
    C.wd_d = [DramT(nc, "wd_d%d" % l, [8, 128, NJ, 128], BF16, blk=10 ** 9) for l in range(2)]
    C.preconverted = True
    C.wsc_d = DramT(nc, "wsc_d", [24, 128, 8, 128], BF16, blk=10 ** 9)
    C.x2_d = DramT(nc, "x2_d", [128, 8, T], F32)
    C.cu_d = DramT(nc, "cu_d", [128, 8, T + 2], BF16)
    C.bg_d = DramT(nc, "bg_d", [128, 8, T], BF16)
    C.edge_d = DramT(nc, "edge_d", [128, 16], BF16)
    C.edges_g = DramT(nc, "edges_g", [4 * 128, 16], BF16)
    C.out_d = DramT(nc, "out_d", [128, 8, T], F32, kind="ExternalOutput")
    C.cu_off = 1
    with ExitStack() as st:
        S = Sched(nc, st)
        setup_common(S, C)
        phase0(S, C)
        S.retire()
        with ExitStack() as stA:
            S.stack = stA
            phase1(S, C, None)
            S.retire()
            phase2(S, C)
            S.retire()
            phase3(S, C)
            S.retire()
            S.stack = st
        S.allgather(C.xch_d.ap.rearrange("p a b c -> p (a b c)"), C.xall_g.ap, GROUPS)
        C.inp["xall"] = C.xall_g.ap.rearrange("(r p) n -> p r n", p=128)
        C.inp["sctx_in"] = C.sctx_d.ap
        for i in range(2):
            C.inp["R_in%d" % i] = C.R_d[i].ap
            C.inp["o0_in%d" % i] = C.o0_d[i].ap
        C.inp["zs_in"] = C.zs_d.ap
        C.inp["attn_in"] = C.attn_d.ap
        with ExitStack() as stB:
            S.stack = stB
            C.edge = SBT_(S, "edge", [128, 16], BF16)
            phaseB(S, C)
            S.retire()
            halo_exchange(S, C, GROUPS)
            S.retire()
            S.stack = st
        C.inp["x2_in"] = C.x2_d.ap
        C.inp["cu_ext"] = C.cu_d.ap
        C.inp["bg_in"] = C.bg_d.ap
        with ExitStack() as stC:
            S.stack = stC
            phaseC(S, C)
            S.stack = st
        S.barrier()
        pass
    return nc


def kernel(**inp):
    inp = {k: np.asarray(v) for k, v in inp.items()}
    shared = prep_shared_inputs(inp)
    cores = list(range(8))
    maps = []
    for core in cores:
        b, s = core // 4, core % 4
        d = prep_core_inputs(inp, b, s)
        d.update(shared)
        for nm, idx in (("sel", s), ("selL", s - 1), ("selR", s + 1)):
            v = np.zeros((128, 4), np.float32)
            if 0 <= idx <= 3:
                v[:, idx] = 1.0
            d[nm] = v
        maps.append({k: d[k] for k in NAMES_F})
    nc = build_fused()
    res = run_bass_kernel_spmd(nc, maps, core_ids=cores).results
    out = np.zeros((2, 4 * T, D), np.float32)
    for c in cores:
        b, s = c // 4, c % 4
        o = np.asarray(res[c]["out_d"])
        out[b, s * T:(s + 1) * T, :] = o.transpose(2, 1, 0).reshape(T, D)
    return out
```
